# Optimizing a Trainium2 kernel written in Bass

```python
import math
import jax, jax.numpy as jnp
from jax import lax
import numpy as np

D_MODEL = 1024
BATCH = 4
SEQ = 8192
DEPTH = 2

N_META = 16
MLA_HEADS = 8
Q_LORA = 768
KV_LORA = 256
QK_NOPE = 128
QK_ROPE = 64
V_HEAD = 128
ROPE_THETA = 10000.0
Q_BLOCK = 128
NEG_INF = -1e30
SSD_INNER = 2 * D_MODEL
SSD_HEAD_DIM = 64
SSD_HEADS = SSD_INNER // SSD_HEAD_DIM
SSD_GROUPS = 4
SSD_HEADS_PER_GROUP = SSD_HEADS // SSD_GROUPS
SSD_STATE = 128
SSD_CONV = 4
SSD_CONV_DIM = SSD_INNER + 2 * SSD_GROUPS * SSD_STATE
CHUNK = 128
DT_MIN = 0.001
DT_MAX = 0.1
D_FF = 2816
FFN_CONV = 3
LN_EPS = 1e-5
RMS_EPS = 1e-6
DEEPNORM_ALPHA = (2 * DEPTH) ** 0.25
DEEPNORM_BETA = (8 * DEPTH) ** -0.25
IN_SIZES = (Q_LORA, KV_LORA, QK_ROPE, SSD_INNER, SSD_CONV_DIM, SSD_HEADS, D_MODEL, D_MODEL)
IN_COLS = sum(IN_SIZES)

kernel_name = "hybrid_mla_ssd_gated_deepnorm"


def layer_norm(x, g, b):
    xf = x.astype(jnp.float32)
    mu = jnp.mean(xf, axis=-1, keepdims=True)
    var = jnp.mean(jnp.square(xf - mu), axis=-1, keepdims=True)
    y = (xf - mu) * lax.rsqrt(var + LN_EPS) * g.astype(jnp.float32) + b.astype(jnp.float32)
    return y.astype(x.dtype)


def rms_norm(x, g):
    xf = x.astype(jnp.float32)
    y = xf * lax.rsqrt(jnp.mean(xf * xf, axis=-1, keepdims=True) + RMS_EPS) * g.astype(jnp.float32)
    return y.astype(x.dtype)


def causal_dwconv(x, w, b):
    k = w.shape[0]
    y = lax.conv_general_dilated(
        x, w[:, None, :].astype(x.dtype), window_strides=(1,), padding=((k - 1, 0),),
        dimension_numbers=("NWC", "WIO", "NWC"), feature_group_count=x.shape[-1])
    return y + b.astype(x.dtype)


def front_pad(t, pad):
    return jnp.pad(t, [(0, 0), (pad, 0)] + [(0, 0)] * (t.ndim - 2))


def rope_tables(length):
    inv_freq = 1.0 / (ROPE_THETA ** (jnp.arange(0, QK_ROPE, 2, dtype=jnp.float32) / QK_ROPE))
    ang = jnp.arange(length, dtype=jnp.float32)[:, None] * inv_freq[None, :]
    ang = jnp.concatenate([ang, ang], axis=-1)
    return jnp.cos(ang), jnp.sin(ang)


def apply_rope(x, cos, sin):
    xf = x.astype(jnp.float32)
    x1, x2 = jnp.split(xf, 2, axis=-1)
    rot = jnp.concatenate([-x2, x1], axis=-1)
    return (xf * cos + rot * sin).astype(x.dtype)


def mla_branch(q_lat, kv_lat, k_pe, cos, sin, q_norm_g, w_q_b, kv_norm_g, w_kv_b, w_o):
    b, l, _ = q_lat.shape
    q = (rms_norm(q_lat, q_norm_g) @ w_q_b).reshape(b, l, MLA_HEADS, QK_NOPE + QK_ROPE)
    q_nope = q[..., :QK_NOPE]
    q_pe = apply_rope(q[..., QK_NOPE:], cos[:, None, :], sin[:, None, :])
    kv = (rms_norm(kv_lat, kv_norm_g) @ w_kv_b).reshape(b, l, MLA_HEADS, QK_NOPE + V_HEAD)
    k_nope, v = kv[..., :QK_NOPE], kv[..., QK_NOPE:]
    k_pe = apply_rope(k_pe, cos, sin)
    pad = (-l) % Q_BLOCK
    q_nope, q_pe, k_nope, k_pe, v = [front_pad(t, pad) for t in (q_nope, q_pe, k_nope, k_pe, v)]
    lp = l + pad
    scale = (QK_NOPE + QK_ROPE) ** -0.5
    key_pos = jnp.arange(lp)

    def attend_block(blk):
        start = blk * Q_BLOCK
        qn = lax.dynamic_slice_in_dim(q_nope, start, Q_BLOCK, axis=1)
        qr = lax.dynamic_slice_in_dim(q_pe, start, Q_BLOCK, axis=1)
        s = (jnp.einsum("bqhd,bkhd->bhqk", qn, k_nope).astype(jnp.float32)
             + jnp.einsum("bqhr,bkr->bhqk", qr, k_pe).astype(jnp.float32))
        q_pos = start + jnp.arange(Q_BLOCK)
        visible = (key_pos[None, :] <= q_pos[:, None]) & (key_pos[None, :] >= pad)
        p = jax.nn.softmax(jnp.where(visible, s * scale, NEG_INF), axis=-1)
        return jnp.einsum("bhqk,bkhd->bqhd", p.astype(v.dtype), v)

    o = lax.map(attend_block, jnp.arange(lp // Q_BLOCK))
    o = jnp.moveaxis(o, 0, 1).reshape(b, lp, MLA_HEADS * V_HEAD)[:, pad:]
    return o @ w_o


def ssd_branch(z, xbc, dt_raw, conv_w, conv_b, dt_bias, a_log, d_skip, norm_g, w_o):
    b, l, _ = xbc.shape
    dtype = xbc.dtype
    f32 = jnp.float32
    xbc = jax.nn.silu(causal_dwconv(xbc, conv_w, conv_b)).astype(f32)
    xs, bm, cm = jnp.split(xbc, [SSD_INNER, SSD_INNER + SSD_GROUPS * SSD_STATE], axis=-1)
    dt = jax.nn.softplus(dt_raw.astype(f32) + dt_bias.astype(f32))
    a = -jnp.exp(a_log.astype(f32))
    pad = (-l) % CHUNK
    lp = l + pad
    nc = lp // CHUNK
    g, e = SSD_GROUPS, SSD_HEADS_PER_GROUP
    dt_c = front_pad(dt, pad).reshape(b, nc, CHUNK, g, e)
    x_c = front_pad(xs, pad).reshape(b, nc, CHUNK, g, e, SSD_HEAD_DIM) * dt_c[..., None]
    b_c = front_pad(bm, pad).reshape(b, nc, CHUNK, g, SSD_STATE)
    c_c = front_pad(cm, pad).reshape(b, nc, CHUNK, g, SSD_STATE)
    a_c = jnp.transpose(dt_c * a.reshape(g, e), (0, 3, 4, 1, 2))
    a_cs = jnp.cumsum(a_c, axis=-1)
    causal = jnp.tril(jnp.ones((CHUNK, CHUNK), dtype=bool))
    decay = jnp.exp(jnp.where(causal, a_cs[..., :, None] - a_cs[..., None, :], -jnp.inf))
    cb = jnp.einsum("bclgn,bcsgn->bgcls", c_c, b_c)
    y_diag = jnp.einsum("bgecls,bcsgep->bclgep", cb[:, :, None] * decay, x_c)
    decay_states = jnp.exp(a_cs[..., -1:] - a_cs)
    states = jnp.einsum("bclgn,bgecl,bclgep->cbgepn", b_c, decay_states, x_c)
    chunk_decay = jnp.moveaxis(jnp.exp(a_cs[..., -1]), -1, 0)

    def carry_state(h, inp):
        s_c, d_c = inp
        return d_c[..., None, None] * h + s_c, h

    h0 = jnp.zeros(states.shape[1:], f32)
    _, prev = lax.scan(carry_state, h0, (states, chunk_decay))
    y_off = jnp.einsum("bclgn,cbgepn,bgecl->bclgep", c_c, prev, jnp.exp(a_cs))
    y = (y_diag + y_off).reshape(b, lp, SSD_HEADS, SSD_HEAD_DIM)[:, pad:]
    y = y + xs.reshape(b, l, SSD_HEADS, SSD_HEAD_DIM) * d_skip.astype(f32)[:, None]
    gs = SSD_INNER // SSD_GROUPS
    y = y.reshape(b, l, SSD_GROUPS, gs) * jax.nn.silu(z.astype(f32)).reshape(b, l, SSD_GROUPS, gs)
    y = y * lax.rsqrt(jnp.mean(y * y, axis=-1, keepdims=True) + RMS_EPS)
    y = y.reshape(b, l, SSD_INNER) * norm_g.astype(f32)
    return y.astype(dtype) @ w_o


def conv_glu_ffn(h, w_up, conv_w, conv_b, w_down):
    u = causal_dwconv(h @ w_up, conv_w, conv_b)
    gate, val = jnp.split(u, 2, axis=-1)
    return (jax.nn.silu(gate) * val) @ w_down


def setup_inputs(seed: int = 0) -> dict:
    key = jax.random.key(seed)
    ks = iter(jax.random.split(key, 40))
    nrm = lambda shape, scale: jax.random.normal(next(ks), shape, jnp.float32) * scale
    gain = lambda shape: 1.0 + nrm(shape, 0.02)
    bias = lambda shape: nrm(shape, 0.02)
    L = DEPTH
    x = nrm((BATCH, SEQ, D_MODEL), 1.0)
    meta_tokens = nrm((N_META, D_MODEL), 1.0)
    emb_ln_g = gain((D_MODEL,))
    emb_ln_b = bias((D_MODEL,))
    w_in = nrm((L, D_MODEL, IN_COLS), D_MODEL ** -0.5)
    q_norm_g = gain((L, Q_LORA))
    w_q_b = nrm((L, Q_LORA, MLA_HEADS * (QK_NOPE + QK_ROPE)), Q_LORA ** -0.5)
    kv_norm_g = gain((L, KV_LORA))
    w_kv_b = nrm((L, KV_LORA, MLA_HEADS * (QK_NOPE + V_HEAD)), KV_LORA ** -0.5)
    w_o_attn = nrm((L, MLA_HEADS * V_HEAD, D_MODEL), (MLA_HEADS * V_HEAD) ** -0.5)
    ssd_conv_w = nrm((L, SSD_CONV, SSD_CONV_DIM), SSD_CONV ** -0.5)
    ssd_conv_b = bias((L, SSD_CONV_DIM))
    u = jax.random.uniform(next(ks), (L, SSD_HEADS), jnp.float32)
    dt0 = jnp.exp(u * (math.log(DT_MAX) - math.log(DT_MIN)) + math.log(DT_MIN))
    dt_bias = dt0 + jnp.log(-jnp.expm1(-dt0))
    a_log = jnp.log(jax.random.uniform(next(ks), (L, SSD_HEADS), jnp.float32, 1.0, 16.0))
    d_skip = gain((L, SSD_HEADS))
    ssd_norm_g = gain((L, SSD_INNER))
    w_o_ssd = nrm((L, SSD_INNER, D_MODEL), SSD_INNER ** -0.5)
    w_out = nrm((L, D_MODEL, D_MODEL), DEEPNORM_BETA * D_MODEL ** -0.5)
    ln1_g = gain((L, D_MODEL))
    ln1_b = bias((L, D_MODEL))
    w_up = nrm((L, D_MODEL, 2 * D_FF), D_MODEL ** -0.5)
    ffn_conv_w = nrm((L, FFN_CONV, 2 * D_FF), FFN_CONV ** -0.5)
    ffn_conv_b = bias((L, 2 * D_FF))
    w_down = nrm((L, D_FF, D_MODEL), DEEPNORM_BETA * D_FF ** -0.5)
    ln2_g = gain((L, D_MODEL))
    ln2_b = bias((L, D_MODEL))
    return {"x": x, "meta_tokens": meta_tokens, "emb_ln_g": emb_ln_g, "emb_ln_b": emb_ln_b,
            "w_in": w_in, "q_norm_g": q_norm_g, "w_q_b": w_q_b, "kv_norm_g": kv_norm_g,
            "w_kv_b": w_kv_b, "w_o_attn": w_o_attn, "ssd_conv_w": ssd_conv_w, "ssd_conv_b": ssd_conv_b,
            "dt_bias": dt_bias, "a_log": a_log, "d_skip": d_skip, "ssd_norm_g": ssd_norm_g,
            "w_o_ssd": w_o_ssd, "w_out": w_out, "ln1_g": ln1_g, "ln1_b": ln1_b, "w_up": w_up,
            "ffn_conv_w": ffn_conv_w, "ffn_conv_b": ffn_conv_b, "w_down": w_down,
            "ln2_g": ln2_g, "ln2_b": ln2_b}


def reference(x, meta_tokens, emb_ln_g, emb_ln_b, w_in, q_norm_g, w_q_b, kv_norm_g, w_kv_b,
              w_o_attn, ssd_conv_w, ssd_conv_b, dt_bias, a_log, d_skip, ssd_norm_g, w_o_ssd,
              w_out, ln1_g, ln1_b, w_up, ffn_conv_w, ffn_conv_b, w_down, ln2_g, ln2_b):
    b = x.shape[0]
    meta = jnp.broadcast_to(meta_tokens.astype(x.dtype)[None], (b, N_META, D_MODEL))
    h = layer_norm(jnp.concatenate([meta, x], axis=1), emb_ln_g, emb_ln_b)
    cos, sin = rope_tables(h.shape[1])
    splits = np.cumsum(IN_SIZES)[:-1].tolist()
    for i in range(DEPTH):
        proj = h @ w_in[i]
        q_lat, kv_lat, k_pe, z, xbc, dt_raw, g_attn, g_ssd = jnp.split(proj, splits, axis=-1)
        y_attn = mla_branch(q_lat, kv_lat, k_pe, cos, sin, q_norm_g[i], w_q_b[i],
                            kv_norm_g[i], w_kv_b[i], w_o_attn[i])
        y_ssd = ssd_branch(z, xbc, dt_raw, ssd_conv_w[i], ssd_conv_b[i], dt_bias[i], a_log[i],
                           d_skip[i], ssd_norm_g[i], w_o_ssd[i])
        mixed = jax.nn.sigmoid(g_attn) * y_attn + jax.nn.sigmoid(g_ssd) * y_ssd
        h = layer_norm(DEEPNORM_ALPHA * h + mixed @ w_out[i], ln1_g[i], ln1_b[i])
        ffn = conv_glu_ffn(h, w_up[i], ffn_conv_w[i], ffn_conv_b[i], w_down[i])
        h = layer_norm(DEEPNORM_ALPHA * h + ffn, ln2_g[i], ln2_b[i])
    return h[:, N_META:]
```

```python
import math
from contextlib import ExitStack

import numpy as np
import concourse.bass as bass
import concourse.mybir as mybir
from concourse.bass_utils import run_bass_kernel_spmd

F32 = mybir.dt.float32
BF16 = mybir.dt.bfloat16
AF = mybir.ActivationFunctionType
ALU = mybir.AluOpType

D = 1024
SEQ = 8192
NMETA = 16
PAD = 112
LP = 8320
NBLK = 65
NH = 8
QL = 768
KVL = 256
DFF = 2816
DEPTH = 2
ALPHA = (2 * DEPTH) ** 0.25
LN_EPS = 1e-5
RMS_EPS = 1e-6
SCALE = 192 ** -0.5

ENGS = ("sync", "scalar", "vector", "gpsimd", "tensor")
COMPUTE = ("scalar", "vector", "gpsimd", "tensor")


class Buf:
    __slots__ = ("w", "r")

    def __init__(self):
        self.w = None
        self.r = {}


class DSem:
    def __init__(self, sem):
        self.sem = sem
        self.count = 0


class TB:
    def __init__(self, t, n=1, ds=None, ss=None):
        self.t = t
        self.b = [Buf() for _ in range(n)]
        self.ds = ds
        self.ss = ss


class Ring:
    def __init__(self, items):
        self.items = items
        self.i = 0

    def next(self):
        it = self.items[self.i % len(self.items)]
        self.i += 1
        return it


class Prog:
    def __init__(self, nc, gstack):
        self.nc = nc
        self.gstack = gstack
        self.pstack = None
        self.q = {e: [] for e in ENGS}
        self.cnt = {e: 0 for e in ENGS}
        self.waited = {e: {} for e in ENGS}
        self.pending = {e: [] for e in ENGS}
        self.psem = {}
        self.nsem = 0
        for e in COMPUTE:
            self.psem[e] = self._new_sem()
        self.pool = []
        self.pool_i = 0
        self.ninstr = 0
        self.phase_i = 0
        self.nt = 0

    def _new_sem(self):
        self.nsem += 1
        return self.gstack.enter_context(self.nc.semaphore(f"sem{self.nsem}"))

    def dsem(self):
        if self.pool_i == len(self.pool):
            self.pool.append(DSem(self._new_sem()))
        d = self.pool[self.pool_i]
        self.pool_i += 1
        return d

    def sbuf(self, shape, dtype, n=1, dma=False):
        self.nt += 1
        t = self.pstack.enter_context(self.nc.sbuf_tensor(f"sb{self.nt}", list(shape), dtype))
        return TB(t, n, self.dsem() if dma else None, self.dsem() if dma else None)

    def psum(self, shape, dtype, n=1):
        self.nt += 1
        t = self.pstack.enter_context(self.nc.psum_tensor(f"ps{self.nt}", list(shape), dtype))
        return TB(t, n)

    def _wait(self, eng, tok):
        if tok is None:
            return
        sem, val, src = tok
        if src == eng and eng == "tensor":
            return
        key = id(sem)
        if self.waited[eng].get(key, 0) >= val:
            return
        self.waited[eng][key] = val
        self.q[eng].append(lambda e, s=sem, v=val: e.wait_ge(s, v))

    def _deps(self, eng, reads, writes, nowaw=False):
        for b in reads:
            self._wait(eng, b.w)
        for b in writes:
            if not nowaw:
                self._wait(eng, b.w)
            for t in b.r.values():
                self._wait(eng, t)

    def _commit(self, tok, reads, writes):
        k = id(tok[0])
        for b in reads:
            b.r[k] = tok
        for b in writes:
            b.w = tok
            b.r = {}

    def op(self, eng, fn, reads=(), writes=(), mark=True):
        self.ninstr += 1
        self._deps(eng, reads, writes)
        if not mark:
            self.pending[eng].append((tuple(reads), tuple(writes)))
            self.q[eng].append(fn)
            return None
        self.cnt[eng] += 1
        v = self.cnt[eng]
        sem = self.psem[eng]
        self.q[eng].append(lambda e, f=fn, s=sem: f(e).then_inc(s, 1))
        tok = (sem, v, eng)
        for (r, w) in self.pending[eng]:
            self._commit(tok, r, w)
        self.pending[eng] = []
        self._commit(tok, reads, writes)
        return tok

    def dma(self, eng, pairs, reads, writes, ds, nowaw=False):
        self._deps(eng, reads, writes, nowaw)
        for (o, i) in pairs:
            self.ninstr += 1
            ds.count += 16
            self.q[eng].append(lambda e, o=o, i=i, s=ds.sem: e.dma_start(out=o, in_=i).then_inc(s, 16))
        tok = (ds.sem, ds.count, "dma")
        self._commit(tok, reads, writes)
        return tok

    def begin_phase(self, st):
        self.pstack = st
        self.pool_i = 0

    def end_phase(self):
        for d in self.pool:
            if d.count:
                self._wait("sync", (d.sem, d.count, "dma"))
        for e in COMPUTE:
            assert not self.pending[e], e
            if self.cnt[e]:
                self._wait("sync", (self.psem[e], self.cnt[e], e))
        nc = self.nc
        self.phase_i += 1
        with nc.Block() as block:
            for ename in ENGS:
                lst = self.q[ename]
                if not lst:
                    continue

                def body(e, lst=lst):
                    for f in lst:
                        f(e)
                getattr(block, ename)(body)
        self.q = {e: [] for e in ENGS}


def MM(out, lhsT, rhs, start=True, stop=True):
    return lambda e: e.matmul(out, lhsT=lhsT, rhs=rhs, start=start, stop=stop)


def TR(out, in_, ident):
    return lambda e: e.transpose(out=out, in_=in_, identity=ident)


def ACT(out, in_, func, bias=None, scale=None):
    kw = {}
    if bias is not None:
        kw["bias"] = bias
    if scale is not None:
        kw["scale"] = scale
    return lambda e: e.activation(out=out, in_=in_, func=func, **kw)


def TT(out, a, b, op):
    return lambda e: e.tensor_tensor(out=out, in0=a, in1=b, op=op)


def TS(out, a, s1, op0, s2=None, op1=None):
    if op1 is None:
        return lambda e: e.tensor_scalar(out=out, in0=a, scalar1=s1, scalar2=None, op0=op0)
    return lambda e: e.tensor_scalar(out=out, in0=a, scalar1=s1, scalar2=s2, op0=op0, op1=op1)


def STT(out, in0, scalar, in1, op0, op1):
    return lambda e: e.scalar_tensor_tensor(out=out, in0=in0, scalar=scalar, in1=in1, op0=op0, op1=op1)


def CP(out, in_):
    return lambda e: e.tensor_copy(out=out, in_=in_)


def MEMSET(ap, v):
    return lambda e: e.memset(ap, v)


def tiles_of(width):
    t = [(0, 128)]
    c = 128
    while c < LP:
        t.append((c, width))
        c += width
    return t


class K:
    def __init__(self, nc, P, n_layers, dbg):
        self.nc = nc
        self.P = P
        self.n_layers = n_layers
        self.dbg = dbg or ()
        self.ext = {}
        self.scr = {}

    def din(self, name, shape, dt=F32):
        ap = self.nc.dram_tensor(name, list(shape), dt, kind="ExternalInput").ap()
        self.ext[name] = ap
        return ap

    def dscr(self, name, shape, dt):
        kind = "ExternalOutput" if name in self.dbg else "Internal"
        ap = self.nc.dram_tensor(name, list(shape), dt, kind=kind).ap()
        self.scr[name] = TB(ap, 1)
        return self.scr[name]

    def load_consts(self, st):
        P = self.P
        P.pstack = st
        c = self.ext
        self.identf = P.sbuf([128, 128], F32, dma=True)
        self.tri_f = P.sbuf([128, 128], F32, dma=True)
        self.sellast = P.sbuf([128, 128], F32, dma=True)
        self.mask0 = P.sbuf([128, 128], F32, dma=True)
        self.kmask0 = P.sbuf([128, 1], F32, dma=True)
        self.identb = P.sbuf([128, 128], BF16)
        self.ones_b = P.sbuf([128, 128], BF16)
        self.tri_b = P.sbuf([128, 128], BF16)
        self.mask0_b = P.sbuf([128, 128], BF16)
        for tb, nm in ((self.identf, "c_ident"), (self.tri_f, "c_tri"), (self.sellast, "c_sellast"),
                       (self.mask0, "c_mask0")):
            P.dma("sync", [(tb.t[:], c[nm][:, :])], [], tb.b, tb.ds)
        P.dma("sync", [(self.kmask0.t[:], c["c_kmask0"][:, :])], [], self.kmask0.b, self.kmask0.ds)
        P.op("vector", CP(self.identb.t[:], self.identf.t[:]), self.identf.b, self.identb.b)
        P.op("vector", CP(self.tri_b.t[:], self.tri_f.t[:]), self.tri_f.b, self.tri_b.b)
        P.op("vector", CP(self.mask0_b.t[:], self.mask0.t[:]), self.mask0.b, self.mask0_b.b)
        P.op("vector", MEMSET(self.ones_b.t[:], 1.0), [], self.ones_b.b)

    def load_w(self, dst, src, kchunks, per=4):
        P = self.P
        for k0 in range(0, kchunks, per):
            k1 = min(kchunks, k0 + per)
            P.dma("gpsimd", [(dst.t[:, k0:k1, :], src[k0 * 128:k1 * 128, :].rearrange("(k p) n -> p k n", p=128))],
                  [], dst.b, dst.ds)

    def ln_fm(self, s, out, W, g, b, R):
        P = self.P
        ones = self.ones_b
        sum_ps, ssq_ps = R["sum"], R["ssq"]
        for c in range(8):
            sb = R["sb"].next()
            sq = R["sq"].next()
            P.op("scalar", ACT(sb.t[:, 0:W], s.t[:, c, 0:W], AF.Copy), [s.b[c]], sb.b)
            P.op("scalar", ACT(sq.t[:, 0:W], s.t[:, c, 0:W], AF.Square), [s.b[c]], sq.b)
            P.op("tensor", MM(sum_ps.t[:, 0:W], ones.t[:], sb.t[:, 0:W], c == 0, c == 7), sb.b + ones.b, sum_ps.b,
                 mark=False)
            P.op("tensor", MM(ssq_ps.t[:, 0:W], ones.t[:], sq.t[:, 0:W], c == 0, c == 7), sq.b + ones.b, ssq_ps.b,
                 mark=True)
        mean, var, rstd = R["mean"], R["var"], R["rstd"]
        P.op("vector", TS(mean.t[:, 0:W], sum_ps.t[:, 0:W], 1.0 / D, ALU.mult), sum_ps.b, mean.b)
        P.op("vector", TS(var.t[:, 0:W], ssq_ps.t[:, 0:W], 1.0 / D, ALU.mult), ssq_ps.b, var.b)
        msq = R["msq"]
        P.op("vector", TT(msq.t[:, 0:W], mean.t[:, 0:W], mean.t[:, 0:W], ALU.mult), mean.b, msq.b)
        P.op("vector", TT(var.t[:, 0:W], var.t[:, 0:W], msq.t[:, 0:W], ALU.subtract), var.b + msq.b, var.b)
        P.op("vector", TS(var.t[:, 0:W], var.t[:, 0:W], LN_EPS, ALU.add), var.b, var.b)
        P.op("scalar", ACT(rstd.t[:, 0:W], var.t[:, 0:W], AF.Ln), var.b, rstd.b)
        P.op("scalar", ACT(rstd.t[:, 0:W], rstd.t[:, 0:W], AF.Exp, scale=-0.5), rstd.b, rstd.b)
        for c in range(8):
            tmp = R["tmp"].next()
            P.op("vector", TT(tmp.t[:, 0:W], s.t[:, c, 0:W], mean.t[:, 0:W], ALU.subtract), [s.b[c]] + mean.b, tmp.b)
            P.op("vector", TT(tmp.t[:, 0:W], tmp.t[:, 0:W], rstd.t[:, 0:W], ALU.mult), tmp.b + rstd.b, tmp.b)
            P.op("scalar", ACT(out.t[:, c, 0:W], tmp.t[:, 0:W], AF.Identity, bias=b[:, c:c + 1], scale=g[:, c:c + 1]),
                 tmp.b, [out.b[c]])

    def ln_resources(self, wmax=512):
        P = self.P
        R = {}
        R["sum"] = P.psum([128, 512], F32)
        R["ssq"] = P.psum([128, 512], F32)
        R["sb"] = Ring([P.sbuf([128, wmax], BF16) for _ in range(2)])
        R["sq"] = Ring([P.sbuf([128, wmax], BF16) for _ in range(2)])
        for nm in ("mean", "var", "msq", "rstd"):
            R[nm] = P.sbuf([128, wmax], F32)
        R["tmp"] = Ring([P.sbuf([128, wmax], F32) for _ in range(2)])
        return R

    def phase_E(self):
        P = self.P
        hT = self.scr["hT"]
        with ExitStack() as st:
            P.begin_phase(st)
            xin = self.ext["xin"]
            gb = P.sbuf([128, 16], F32, dma=True)
            P.dma("sync", [(gb.t[:, 0:8], self.ext["emb_g"][:, :]), (gb.t[:, 8:16], self.ext["emb_b"][:, :])],
                  [], gb.b, gb.ds)
            xr = Ring([P.sbuf([128, 4, 1024], F32, dma=True) for _ in range(2)])
            sr = Ring([P.sbuf([128, 8, 512], F32, n=8) for _ in range(2)])
            orr = Ring([P.sbuf([128, 8, 512], F32, n=8, dma=True) for _ in range(2)])
            tpr = Ring([P.psum([128, 512], F32) for _ in range(2)])
            R = self.ln_resources()
            for (c0, W) in tiles_of(512):
                nb = W // 128
                x = xr.next()
                P.dma("sync", [(x.t[:, 0:nb, :], xin[c0:c0 + W, :].rearrange("(b p) f -> p b f", p=128))],
                      [], x.b, x.ds)
                s = sr.next()
                for c in range(8):
                    tp = tpr.next()
                    for bi in range(nb):
                        P.op("tensor", TR(tp.t[:, bi * 128:(bi + 1) * 128], x.t[:, bi, c * 128:(c + 1) * 128],
                                          self.identf.t[:]), x.b + self.identf.b, tp.b, mark=(bi == nb - 1))
                    P.op("vector" if c % 2 else "scalar",
                         CP(s.t[:, c, 0:W], tp.t[:, 0:W]) if c % 2 else ACT(s.t[:, c, 0:W], tp.t[:, 0:W], AF.Copy),
                         tp.b, [s.b[c]])
                o = orr.next()
                self.ln_fm(s, o, W, gb.t[:, 0:8], gb.t[:, 8:16], R)
                P.dma("gpsimd", [(hT.t[:, :, c0:c0 + W].rearrange("c p t -> p c t"), o.t[:, :, 0:W])],
                      o.b, hT.b, o.ss, nowaw=True)
            P.end_phase()


    def phase_A1(self, l):
        P = self.P
        E = self.ext
        hT, qT, kT, kpeT, vS = (self.scr[n] for n in ("hT", "qT", "kT", "kpeT", "v"))
        with ExitStack() as st:
            P.begin_phase(st)
            w_in = E["w_in"]
            Wql = P.sbuf([128, 8, QL], BF16, dma=True)
            Wkvl = P.sbuf([128, 8, KVL], BF16, dma=True)
            Wkpe = P.sbuf([128, 8, 64], BF16, dma=True)
            Wkpes = P.sbuf([128, 8, 64], BF16, dma=True)
            Wqn = P.sbuf([128, 6, 1024], BF16, dma=True)
            Wqp = P.sbuf([128, 6, 512], BF16, dma=True)
            Wqps = P.sbuf([128, 6, 512], BF16, dma=True)
            Wkn = P.sbuf([128, 2, 1024], BF16, dma=True)
            Wv = P.sbuf([128, 2, 1024], BF16, dma=True)
            self.load_w(Wql, w_in[l, :, 0:768], 8)
            self.load_w(Wkvl, w_in[l, :, 768:1024], 8, per=8)
            self.load_w(Wkpe, w_in[l, :, 1024:1088], 8, per=8)
            self.load_w(Wkpes, E["w_kpes"][l, :, :], 8, per=8)
            self.load_w(Wqn, E["wqb_n"][l, :, :], 6, per=3)
            self.load_w(Wqp, E["wqb_p"][l, :, :], 6, per=6)
            self.load_w(Wqps, E["wqb_ps"][l, :, :], 6, per=6)
            self.load_w(Wkn, E["wkvb_kn"][l, :, :], 2)
            self.load_w(Wv, E["wkvb_v"][l, :, :], 2)
            ng = P.sbuf([128, 8], F32, dma=True)
            P.dma("sync", [(ng.t[:, 0:6], E["qng"][l, :, :]), (ng.t[:, 6:8], E["kvng"][l, :, :])], [], ng.b, ng.ds)
            hr = Ring([P.sbuf([128, 8, 512], F32, dma=True) for _ in range(2)])
            hbr = Ring([P.sbuf([128, 8, 512], BF16, n=8) for _ in range(2)])
            csr = Ring([P.sbuf([128, 2, 512], F32, dma=True) for _ in range(2)])
            qlat = P.sbuf([128, 6, 512], F32, n=6)
            qn = P.sbuf([128, 6, 512], BF16, n=6)
            kvlat = P.sbuf([128, 2, 512], F32, n=2)
            kvn = P.sbuf([128, 2, 512], BF16, n=2)
            sqr = Ring([P.sbuf([128, 512], BF16) for _ in range(2)])
            rstd = P.sbuf([128, 512], F32)
            t1r = Ring([P.sbuf([128, 512], F32) for _ in range(2)])
            t2r = Ring([P.sbuf([128, 512], F32) for _ in range(2)])
            stg = Ring([P.sbuf([128, 512], BF16, dma=True) for _ in range(4)])
            vst = Ring([P.sbuf([128, 1024], BF16, n=2, dma=True) for _ in range(2)])
            accr = Ring([P.psum([128, 512], F32) for _ in range(4)])
            ss_ps = P.psum([128, 512], F32)
            ones = self.ones_b
            evi = [0]

            def evac(out_ap, in_ap, reads, writes):
                evi[0] += 1
                if evi[0] % 2:
                    P.op("scalar", ACT(out_ap, in_ap, AF.Copy), reads, writes)
                else:
                    P.op("vector", CP(out_ap, in_ap), reads, writes)

            def rms(acc_list_fn, nch, lat, dst, W, gcol0, nfeat):
                for c in range(nch):
                    acc = acc_list_fn(c)
                    sq = sqr.next()
                    P.op("scalar", ACT(lat.t[:, c, 0:W], acc.t[:, 0:W], AF.Copy), acc.b, [lat.b[c]])
                    P.op("scalar", ACT(sq.t[:, 0:W], acc.t[:, 0:W], AF.Square), acc.b, sq.b)
                    P.op("tensor", MM(ss_ps.t[:, 0:W], ones.t[:], sq.t[:, 0:W], c == 0, c == nch - 1),
                         sq.b + ones.b, ss_ps.b, mark=(c == nch - 1))
                P.op("vector", TS(rstd.t[:, 0:W], ss_ps.t[:, 0:W], 1.0 / nfeat, ALU.mult, RMS_EPS, ALU.add),
                     ss_ps.b, rstd.b)
                P.op("scalar", ACT(rstd.t[:, 0:W], rstd.t[:, 0:W], AF.Ln), rstd.b, rstd.b)
                P.op("scalar", ACT(rstd.t[:, 0:W], rstd.t[:, 0:W], AF.Exp, scale=-0.5), rstd.b, rstd.b)
                for c in range(nch):
                    P.op("vector", STT(dst.t[:, c, 0:W], lat.t[:, c, 0:W], ng.t[:, gcol0 + c:gcol0 + c + 1],
                                       rstd.t[:, 0:W], ALU.mult, ALU.mult), [lat.b[c]] + rstd.b + ng.b, [dst.b[c]])

            def proj(acc, W_tb, ncols0, ncols, K, rhs_tb, W, M=128):
                for k in range(K):
                    P.op("tensor", MM(acc.t[0:M, 0:W], W_tb.t[:, k, ncols0:ncols0 + ncols], rhs_tb.t[:, k, 0:W],
                                      k == 0, k == K - 1), W_tb.b + [rhs_tb.b[k]], acc.b, mark=(k == K - 1))

            def rope(acc1, acc2, cs, W, M, outs):
                t1 = t1r.next()
                t2 = t2r.next()
                P.op("vector", TT(t1.t[0:M, 0:W], acc1.t[0:M, 0:W], cs.t[0:M, 0, 0:W], ALU.mult), acc1.b + cs.b, t1.b)
                P.op("vector", TT(t2.t[0:M, 0:W], acc2.t[0:M, 0:W], cs.t[0:M, 1, 0:W], ALU.mult), acc2.b + cs.b, t2.b)
                sg = stg.next()
                P.op("vector", TT(sg.t[0:M, 0:W], t1.t[0:M, 0:W], t2.t[0:M, 0:W], ALU.add), t1.b + t2.b, sg.b)
                for (dst_tb, dst_ap, p0, p1) in outs:
                    P.dma("gpsimd", [(dst_ap, sg.t[p0:p1, 0:W])], sg.b, dst_tb.b, sg.ss, nowaw=True)

            for (c0, W) in tiles_of(512):
                nb = W // 128
                h = hr.next()
                P.dma("sync", [(h.t[:, :, 0:W], hT.t[:, :, c0:c0 + W].rearrange("c p t -> p c t"))], hT.b, h.b, h.ds)
                cs = csr.next()
                P.dma("sync", [(cs.t[:, 0, 0:W], E["cosT2"][:, c0:c0 + W]), (cs.t[:, 1, 0:W], E["sinT2"][:, c0:c0 + W])],
                      [], cs.b, cs.ds)
                hb = hbr.next()
                for c in range(8):
                    evac(hb.t[:, c, 0:W], h.t[:, c, 0:W], h.b, [hb.b[c]])

                def ql_acc(c):
                    acc = accr.next()
                    proj(acc, Wql, c * 128, 128, 8, hb, W)
                    return acc
                rms(ql_acc, 6, qlat, qn, W, 0, QL)
                for hd in range(NH):
                    acc = accr.next()
                    proj(acc, Wqn, hd * 128, 128, 6, qn, W)
                    sg = stg.next()
                    evac(sg.t[:, 0:W], acc.t[:, 0:W], acc.b, sg.b)
                    P.dma("gpsimd", [(qT.t[hd, 0:128, c0:c0 + W], sg.t[:, 0:W])], sg.b, qT.b, sg.ss, nowaw=True)
                for pr in range(4):
                    a1 = accr.next()
                    proj(a1, Wqp, pr * 128, 128, 6, qn, W)
                    a2 = accr.next()
                    proj(a2, Wqps, pr * 128, 128, 6, qn, W)
                    rope(a1, a2, cs, W, 128, [(qT, qT.t[2 * pr, 128:192, c0:c0 + W], 0, 64),
                                              (qT, qT.t[2 * pr + 1, 128:192, c0:c0 + W], 64, 128)])

                def kv_acc(c):
                    acc = accr.next()
                    proj(acc, Wkvl, c * 128, 128, 8, hb, W)
                    return acc
                rms(kv_acc, 2, kvlat, kvn, W, 6, KVL)
                for hd in range(NH):
                    acc = accr.next()
                    proj(acc, Wkn, hd * 128, 128, 2, kvn, W)
                    sg = stg.next()
                    evac(sg.t[:, 0:W], acc.t[:, 0:W], acc.b, sg.b)
                    P.dma("gpsimd", [(kT.t[hd, :, c0:c0 + W], sg.t[:, 0:W])], sg.b, kT.b, sg.ss, nowaw=True)
                for bi in range(nb):
                    vs = vst.next()
                    for half in range(2):
                        acc = accr.next()
                        for k in range(2):
                            P.op("tensor", MM(acc.t[:, :], kvn.t[:, k, bi * 128:(bi + 1) * 128],
                                              Wv.t[:, k, half * 512:(half + 1) * 512], k == 0, k == 1),
                                 Wv.b + [kvn.b[k]], acc.b, mark=(k == 1))
                        evac(vs.t[:, half * 512:(half + 1) * 512], acc.t[:, :], acc.b, [vs.b[half]])
                    blk = c0 // 128 + bi
                    P.dma("gpsimd", [(vS.t[:, :, blk, :].rearrange("h p d -> p h d"),
                                      vs.t[:, :].rearrange("p (h d) -> p h d", h=NH))], vs.b, vS.b, vs.ss, nowaw=True)
                a1 = accr.next()
                proj(a1, Wkpe, 0, 64, 8, hb, W, M=64)
                a2 = accr.next()
                proj(a2, Wkpes, 0, 64, 8, hb, W, M=64)
                rope(a1, a2, cs, W, 64, [(kpeT, kpeT.t[:, c0:c0 + W], 0, 64)])
            P.end_phase()


    def phase_B(self, l):
        P = self.P
        qT, kT, kpeT, vS, oT = (self.scr[n] for n in ("qT", "kT", "kpeT", "v", "oT"))
        ones = self.ones_b
        with ExitStack() as st:
            P.begin_phase(st)
            kpe = P.sbuf([64, LP], BF16, dma=True)
            P.dma("sync", [(kpe.t[:, :], kpeT.t[:, :])], kpeT.b, kpe.b, kpe.ds)
            knr = Ring([P.sbuf([128, LP], BF16, dma=True) for _ in range(2)])
            vr = Ring([P.sbuf([128, NBLK, 128], BF16, dma=True) for _ in range(2)])
            qr = Ring([P.sbuf([128, 2, 512], BF16, dma=True) for _ in range(3)])
            ptr = Ring([P.sbuf([128, 512], BF16) for _ in range(4)])
            rsr = Ring([P.sbuf([128, 512], F32) for _ in range(2)])
            osr = Ring([P.sbuf([128, 512], BF16, dma=True) for _ in range(2)])
            spr = Ring([P.psum([128, 512], F32) for _ in range(3)])
            opr = Ring([P.psum([128, 512], F32) for _ in range(2)])
            smr = Ring([P.psum([128, 512], F32) for _ in range(2)])
            tiles = tiles_of(512)
            for hd in range(NH):
                kn = knr.next()
                P.dma("sync", [(kn.t[:, :], kT.t[hd, :, :])], kT.b, kn.b, kn.ds)
                v = vr.next()
                P.dma("sync", [(v.t[:, :, :], vS.t[hd, :, :, :])], vS.b, v.b, v.ds)
                for (c0, W) in tiles:
                    q = qr.next()
                    P.dma("sync", [(q.t[:, 0, 0:W], qT.t[hd, 0:128, c0:c0 + W]),
                                   (q.t[0:64, 1, 0:W], qT.t[hd, 128:192, c0:c0 + W])], qT.b, q.b, q.ds)
                    o_ps = opr.next()
                    sm_ps = smr.next()
                    nkb = (c0 + W) // 128
                    units = []
                    for kb in range(nkb):
                        qo = kb * 128 - c0 if kb * 128 >= c0 else 0
                        units.append((kb, qo))

                    def qk(u):
                        kb, qo = u
                        sp = spr.next()
                        P.op("tensor", MM(sp.t[:, qo:W], kn.t[:, kb * 128:(kb + 1) * 128], q.t[:, 0, qo:W], True, False),
                             kn.b + q.b, sp.b, mark=False)
                        P.op("tensor", MM(sp.t[:, qo:W], kpe.t[0:64, kb * 128:(kb + 1) * 128], q.t[0:64, 1, qo:W],
                                          False, True), kpe.b + q.b, sp.b, mark=True)
                        return sp

                    def soft(u, sp):
                        kb, qo = u
                        pt = ptr.next()
                        P.op("scalar", ACT(pt.t[:, qo:W], sp.t[:, qo:W], AF.Exp, scale=SCALE), sp.b, pt.b)
                        if kb == 0 and c0 == 0:
                            P.op("vector", TT(pt.t[:, 0:128], pt.t[:, 0:128], self.mask0_b.t[:], ALU.mult),
                                 pt.b + self.mask0_b.b, pt.b)
                        elif kb == 0:
                            P.op("vector", TS(pt.t[:, 0:W], pt.t[:, 0:W], self.kmask0.t[:, 0:1], ALU.mult),
                                 pt.b + self.kmask0.b, pt.b)
                        elif kb * 128 >= c0:
                            P.op("vector", TT(pt.t[:, qo:qo + 128], pt.t[:, qo:qo + 128], self.tri_b.t[:], ALU.mult),
                                 pt.b + self.tri_b.b, pt.b)
                        return pt

                    def pv(u, pt, first, last):
                        kb, qo = u
                        P.op("tensor", MM(o_ps.t[:, qo:W], v.t[:, kb, :], pt.t[:, qo:W], first, last),
                             v.b + pt.b, o_ps.b, mark=False)
                        P.op("tensor", MM(sm_ps.t[:, qo:W], ones.t[:], pt.t[:, qo:W], first, last),
                             ones.b + pt.b, sm_ps.b, mark=True)

                    sps = [qk(units[0])]
                    for i, u in enumerate(units):
                        if i + 1 < len(units):
                            sps.append(qk(units[i + 1]))
                        pt = soft(u, sps[i])
                        pv(u, pt, i == 0, i == len(units) - 1)
                    rs = rsr.next()
                    P.op("vector", lambda e, o=rs.t[:, 0:W], i=sm_ps.t[:, 0:W]: e.reciprocal(out=o, in_=i), sm_ps.b, rs.b)
                    og = osr.next()
                    P.op("vector", TT(og.t[:, 0:W], o_ps.t[:, 0:W], rs.t[:, 0:W], ALU.mult), o_ps.b + rs.b, og.b)
                    P.dma("gpsimd", [(oT.t[hd, :, c0:c0 + W], og.t[:, 0:W])], og.b, oT.b, og.ss, nowaw=True)
            P.end_phase()


    def phase_A2(self, l):
        P = self.P
        E = self.ext
        hT, xsS, BtokS, BTS, CTS, dtS, zsS = (self.scr[n] for n in ("hT", "xs", "Btok", "BT", "CT", "dt", "zs"))
        with ExitStack() as st:
            P.begin_phase(st)
            w_in = E["w_in"]
            Wz = P.sbuf([128, 8, 2048], BF16, dma=True)
            Wx = P.sbuf([128, 8, 3072], BF16, dma=True)
            Wdt = P.sbuf([128, 8, 32], BF16, dma=True)
            self.load_w(Wz, w_in[l, :, 1088:3136], 8, per=2)
            self.load_w(Wx, w_in[l, :, 3136:6208], 8, per=2)
            self.load_w(Wdt, w_in[l, :, 6208:6240], 8, per=8)
            cw = P.sbuf([128, 24, 4], F32, dma=True)
            cb = P.sbuf([128, 24], F32, dma=True)
            dtb = P.sbuf([128, 32], F32, dma=True)
            P.dma("sync", [(cw.t[:], E["ssd_cw"][l, :, :, :])], [], cw.b, cw.ds)
            P.dma("sync", [(cb.t[:], E["ssd_cb"][l, :, :])], [], cb.b, cb.ds)
            P.dma("sync", [(dtb.t[:], E["dtb_bc"][l, :, :])], [], dtb.b, dtb.ds)
            h = P.sbuf([128, 8, 512], F32, dma=True)
            hbr = Ring([P.sbuf([128, 8, 512], BF16, n=8) for _ in range(2)])
            xc = P.sbuf([128, 24, 512], BF16, n=24, dma=True)
            xbr = Ring([P.sbuf([128, 515], F32) for _ in range(2)])
            halo = P.sbuf([128, 24, 3], F32, n=24)
            tcr = Ring([P.sbuf([128, 512], F32) for _ in range(2)])
            zst = Ring([P.sbuf([128, 2048], BF16, n=4, dma=True) for _ in range(2)])
            dtr = Ring([P.sbuf([128, 32], F32, dma=True) for _ in range(2)])
            tst = Ring([P.sbuf([128, 2560], BF16, n=5, dma=True) for _ in range(2)])
            accr = Ring([P.psum([128, 512], F32) for _ in range(3)])
            tpr = Ring([P.psum([128, 512], BF16) for _ in range(2)])
            P.op("vector", MEMSET(halo.t[:], 0.0), [], halo.b)
            evi = [0]

            def evac(out_ap, in_ap, reads, writes):
                evi[0] += 1
                if evi[0] % 2:
                    P.op("scalar", ACT(out_ap, in_ap, AF.Copy), reads, writes)
                else:
                    P.op("vector", CP(out_ap, in_ap), reads, writes)

            for (c0, W) in tiles_of(512):
                nb = W // 128
                P.dma("sync", [(h.t[:, :, 0:W], hT.t[:, :, c0:c0 + W].rearrange("c p t -> p c t"))], hT.b, h.b, h.ds)
                hb = hbr.next()
                for c in range(8):
                    evac(hb.t[:, c, 0:W], h.t[:, c, 0:W], h.b, [hb.b[c]])
                for bi in range(nb):
                    zt = zst.next()
                    for cg in range(4):
                        acc = accr.next()
                        for k in range(8):
                            P.op("tensor", MM(acc.t[:, :], hb.t[:, k, bi * 128:(bi + 1) * 128],
                                              Wz.t[:, k, cg * 512:(cg + 1) * 512], k == 0, k == 7),
                                 Wz.b + [hb.b[k]], acc.b, mark=(k == 7))
                        P.op("scalar", ACT(zt.t[:, cg * 512:(cg + 1) * 512], acc.t[:, :], AF.Silu), acc.b, [zt.b[cg]])
                    r0 = c0 + bi * 128
                    P.dma("gpsimd", [(zsS.t[r0:r0 + 128, :], zt.t[:, :])], zt.b, zsS.b, zt.ss, nowaw=True)
                    acc = accr.next()
                    for k in range(8):
                        P.op("tensor", MM(acc.t[:, 0:32], hb.t[:, k, bi * 128:(bi + 1) * 128], Wdt.t[:, k, 0:32],
                                          k == 0, k == 7), Wdt.b + [hb.b[k]], acc.b, mark=(k == 7))
                    dtt = dtr.next()
                    P.op("vector", TT(dtt.t[:, :], acc.t[:, 0:32], dtb.t[:, :], ALU.add), acc.b + dtb.b, dtt.b)
                    P.op("scalar", ACT(dtt.t[:, :], dtt.t[:, :], AF.Exp), dtt.b, dtt.b)
                    P.op("scalar", ACT(dtt.t[:, :], dtt.t[:, :], AF.Ln, bias=1.0), dtt.b, dtt.b)
                    P.dma("gpsimd", [(dtS.t[r0:r0 + 128, :], dtt.t[:, :])], dtt.b, dtS.b, dtt.ss, nowaw=True)
                for c in range(24):
                    acc = accr.next()
                    for k in range(8):
                        P.op("tensor", MM(acc.t[:, 0:W], Wx.t[:, k, c * 128:(c + 1) * 128], hb.t[:, k, 0:W],
                                          k == 0, k == 7), Wx.b + [hb.b[k]], acc.b, mark=(k == 7))
                    xb = xbr.next()
                    P.op("scalar", ACT(xb.t[:, 3:3 + W], acc.t[:, 0:W], AF.Copy), acc.b, xb.b)
                    P.op("vector", CP(xb.t[:, 0:3], halo.t[:, c, :]), [halo.b[c]], xb.b)
                    if c0 == 0:
                        P.op("vector", MEMSET(xb.t[:, 0:3 + PAD], 0.0), [], xb.b)
                    P.op("vector", CP(halo.t[:, c, :], xb.t[:, W:W + 3]), xb.b, [halo.b[c]])
                    tc_ = tcr.next()
                    P.op("vector", TS(tc_.t[:, 0:W], xb.t[:, 0:W], cw.t[:, c, 0:1], ALU.mult, cb.t[:, c:c + 1], ALU.add),
                         xb.b + cw.b + cb.b, tc_.b)
                    for tap in range(1, 4):
                        P.op("vector", STT(tc_.t[:, 0:W], xb.t[:, tap:tap + W], cw.t[:, c, tap:tap + 1], tc_.t[:, 0:W],
                                           ALU.mult, ALU.add), xb.b + cw.b + tc_.b, tc_.b)
                    P.op("scalar", ACT(xc.t[:, c, 0:W], tc_.t[:, 0:W], AF.Silu), tc_.b, [xc.b[c]])
                    if c0 == 0:
                        P.op("vector", MEMSET(xc.t[:, c, 0:PAD], 0.0), [], [xc.b[c]])
                prs = []
                for g in range(4):
                    prs.append((BTS.t[g, :, c0:c0 + W], xc.t[:, 16 + g, 0:W]))
                    prs.append((CTS.t[g, :, c0:c0 + W], xc.t[:, 20 + g, 0:W]))
                P.dma("gpsimd", prs, xc.b[16:24], BTS.b + CTS.b, xc.ss, nowaw=True)
                for bi in range(nb):
                    ts_ = tst.next()
                    for q4 in range(5):
                        tp = tpr.next()
                        for j in range(4):
                            c = q4 * 4 + j
                            P.op("tensor", TR(tp.t[:, j * 128:(j + 1) * 128], xc.t[:, c, bi * 128:(bi + 1) * 128],
                                              self.identb.t[:]), [xc.b[c]] + self.identb.b, tp.b, mark=(j == 3))
                        evac(ts_.t[:, q4 * 512:(q4 + 1) * 512], tp.t[:, :], tp.b, [ts_.b[q4]])
                    r0 = c0 + bi * 128
                    P.dma("gpsimd", [(xsS.t[r0:r0 + 128, :], ts_.t[:, 0:2048]), (BtokS.t[r0:r0 + 128, :], ts_.t[:, 2048:2560])],
                          ts_.b, xsS.b + BtokS.b, ts_.ss, nowaw=True)
            P.end_phase()

    def phase_S(self, l):
        P = self.P
        E = self.ext
        xsS, BtokS, BTS, CTS, dtS, zsS, ynTS = (self.scr[n] for n in ("xs", "Btok", "BT", "CT", "dt", "zs", "ynT"))
        tri = self.tri_f
        with ExitStack() as st:
            P.begin_phase(st)
            abc = P.sbuf([128, 32], F32, dma=True)
            dsk = P.sbuf([128, 2048], F32, dma=True)
            ngb = P.sbuf([128, 2048], F32, dma=True)
            P.dma("sync", [(abc.t[:], E["alog_bc"][l, :, :])], [], abc.b, abc.ds)
            P.dma("sync", [(dsk.t[:], E["dskip_bc"][l, :, :])], [], dsk.b, dsk.ds)
            P.dma("sync", [(ngb.t[:], E["ssdng_bc"][l, :, :])], [], ngb.b, ngb.ds)
            P.op("scalar", ACT(abc.t[:], abc.t[:], AF.Exp), abc.b, abc.b)
            P.op("vector", TS(abc.t[:], abc.t[:], -1.0, ALU.mult), abc.b, abc.b)
            xsr = Ring([P.sbuf([128, 2048], BF16, dma=True) for _ in range(2)])
            btr = Ring([P.sbuf([128, 512], BF16, dma=True) for _ in range(2)])
            bTr = Ring([P.sbuf([128, 4, 128], BF16, dma=True) for _ in range(2)])
            cTr = Ring([P.sbuf([128, 4, 128], BF16, dma=True) for _ in range(2)])
            dtr = Ring([P.sbuf([128, 32], F32, dma=True) for _ in range(2)])
            zsr = Ring([P.sbuf([128, 2048], BF16, dma=True) for _ in range(2)])
            prev = P.sbuf([128, 2048], F32, n=4)
            prevb = P.sbuf([128, 2048], BF16, n=4)
            P.op("vector", MEMSET(prev.t[:], 0.0), [], prev.b)
            P.op("vector", MEMSET(prevb.t[:], 0.0), [], prevb.b)
            at = P.sbuf([128, 32], F32)
            ar = P.sbuf([128, 32], F32)
            a3 = [P.sbuf([128, 32], BF16) for _ in range(3)]
            trib = self.tri_b
            abig = [P.sbuf([128, 32, 128], BF16) for _ in range(2)]
            acs = P.sbuf([128, 32], F32)
            eacs = P.sbuf([128, 32], F32)
            dst = P.sbuf([128, 32], F32)
            cdec = P.sbuf([128, 32], F32)
            xdt = P.sbuf([128, 2048], BF16)
            xdts = P.sbuf([128, 2048], BF16)
            y = P.sbuf([128, 2048], F32, n=4)
            yn = P.sbuf([128, 2048], BF16, n=4)
            junk = P.sbuf([128, 512], F32)
            ss = P.sbuf([128, 4], F32)
            rstd = P.sbuf([128, 4], F32)
            t1r = Ring([P.sbuf([128, 512], F32) for _ in range(2)])
            t2r = Ring([P.sbuf([128, 512], F32) for _ in range(2)])
            cbmr = Ring([P.sbuf([128, 128], F32) for _ in range(2)])
            dmr = Ring([P.sbuf([128, 128], F32) for _ in range(3)])
            er = Ring([P.sbuf([128, 128], F32) for _ in range(3)])
            mtr = Ring([P.sbuf([128, 128], BF16) for _ in range(3)])
            ynst = Ring([P.sbuf([128, 16, 128], BF16, n=4, dma=True) for _ in range(2)])
            dbg8 = P.sbuf([128, 168], F32)
            dd = P.dsem()
            misc = P.psum([128, 512], F32)
            acs_ps = TB(misc.t[:, 0:32])
            last_ps = TB(misc.t[:, 32:64])
            acs_ps.b = misc.b
            last_ps.b = misc.b
            cbr = Ring([P.psum([128, 128], F32)])
            dpr = Ring([P.psum([128, 128], F32) for i in range(2)])
            ydr = Ring([P.psum([128, 512], F32) for _ in range(1)])
            yo_ps = P.psum([128, 512], F32)
            st_ps = P.psum([128, 512], F32)
            tpr = Ring([P.psum([128, 512], BF16) for _ in range(1)])
            bc3 = lambda ap2, n: ap2.unsqueeze(2).to_broadcast([128, n, 64])
            v3 = lambda ap2: ap2.rearrange("p (h d) -> p h d", d=64)
            for c in range(getattr(self, 'S_NCH', NBLK)):
                r0 = c * 128
                xs = xsr.next(); bt = btr.next(); bT = bTr.next(); cT = cTr.next(); dt = dtr.next(); zs = zsr.next()
                P.dma("sync", [(xs.t[:], xsS.t[r0:r0 + 128, :])], xsS.b, xs.b, xs.ds)
                P.dma("sync", [(bt.t[:], BtokS.t[r0:r0 + 128, :])], BtokS.b, bt.b, bt.ds)
                P.dma("sync", [(bT.t[:], BTS.t[:, :, r0:r0 + 128].rearrange("g p t -> p g t"))], BTS.b, bT.b, bT.ds)
                P.dma("sync", [(cT.t[:], CTS.t[:, :, r0:r0 + 128].rearrange("g p t -> p g t"))], CTS.b, cT.b, cT.ds)
                P.dma("sync", [(dt.t[:], dtS.t[r0:r0 + 128, :])], dtS.b, dt.b, dt.ds)
                P.dma("sync", [(zs.t[:], zsS.t[r0:r0 + 128, :])], zsS.b, zs.b, zs.ds)
                P.op("vector", TT(at.t[:], dt.t[:], abc.t[:], ALU.mult), dt.b + abc.b, at.b)
                P.op("vector", CP(a3[0].t[:], at.t[:]), at.b, a3[0].b)
                P.op("vector", TT(ar.t[:], at.t[:], a3[0].t[:], ALU.subtract), at.b + a3[0].b, ar.b)
                P.op("vector", CP(a3[1].t[:], ar.t[:]), ar.b, a3[1].b)
                P.op("vector", TT(ar.t[:], ar.t[:], a3[1].t[:], ALU.subtract), a3[1].b, ar.b)
                P.op("vector", CP(a3[2].t[:], ar.t[:]), ar.b, a3[2].b)
                import os
                n3 = 1 if "one" in os.environ.get("S_VAR", "") else 3
                for i3 in range(n3):
                    P.op("tensor", MM(acs_ps.t, trib.t[:], a3[i3].t[:], i3 == 0, i3 == n3 - 1), trib.b + a3[i3].b, acs_ps.b,
                         mark=(i3 == n3 - 1))
                for i3 in range(n3):
                    P.op("tensor", MM(last_ps.t, self.ones_b.t[:], a3[i3].t[:], i3 == 0, i3 == n3 - 1),
                         self.ones_b.b + a3[i3].b, last_ps.b, mark=(i3 == n3 - 1))
                P.op("vector", CP(acs.t[:], acs_ps.t), [], acs.b + misc.b)
                for i3 in range(2):
                    P.op("vector", CP(abig[i3].t[:], a3[i3].t[:, :].unsqueeze(2).to_broadcast([128, 32, 128])),
                         a3[i3].b, abig[i3].b)
                P.op("scalar", ACT(eacs.t[:], acs.t[:], AF.Exp), acs.b, eacs.b)
                P.op("scalar", ACT(cdec.t[:], last_ps.t, AF.Exp), [], cdec.b + misc.b)
                P.op("vector", TT(dst.t[:], last_ps.t, acs.t[:], ALU.subtract), acs.b, dst.b + misc.b)
                P.op("scalar", ACT(dst.t[:], dst.t[:], AF.Exp), dst.b, dst.b)
                P.op("vector", TT(v3(xdt.t[:]), v3(xs.t[:]), bc3(dt.t[:, :], 32), ALU.mult), xs.b + dt.b, xdt.b)
                P.op("vector", TT(v3(xdts.t[:]), v3(xdt.t[:]), bc3(dst.t[:, :], 32), ALU.mult), xdt.b + dst.b, xdts.b)
                cut = int(os.environ.get("S_CUT", "9"))
                if cut <= 1:
                    continue
                for g in range(4):
                    gs = slice(g * 512, (g + 1) * 512)
                    cb_ps = cbr.next()
                    P.op("tensor", MM(cb_ps.t[:, :], bT.t[:, g, :], cT.t[:, g, :]), bT.b + cT.b, cb_ps.b)
                    cbm = cbmr.next()
                    P.op("vector", TT(cbm.t[:], cb_ps.t[:, :], tri.t[:], ALU.mult), cb_ps.b + tri.b, cbm.b)
                    yd = ydr.next()
                    for hh in range(8 if cut >= 3 else 0):
                        hd = g * 8 + hh
                        dps = dpr.next()
                        import os
                        nsp = 1 if "one" in os.environ.get("S_VAR", "") else 2
                        for i3 in range(nsp):
                            P.op("tensor", MM(dps.t[:, :], abig[i3].t[:, hd, :], trib.t[:], i3 == 0, i3 == nsp - 1),
                                 abig[i3].b + trib.b, dps.b, mark=(i3 == nsp - 1))
                        dm = dmr.next()
                        P.op("vector", TS(dm.t[:], dps.t[:, :], acs.t[:, hd:hd + 1], ALU.subtract, 0.0, ALU.min),
                             dps.b + acs.b, dm.b)
                        ee = er.next()
                        P.op("scalar", ACT(ee.t[:], dm.t[:], AF.Exp), dm.b, ee.b)
                        mt = mtr.next()
                        P.op("vector", TT(mt.t[:], ee.t[:], cbm.t[:], ALU.mult), ee.b + cbm.b, mt.b)
                        P.op("tensor", MM(yd.t[:, hh * 64:(hh + 1) * 64], mt.t[:], xdt.t[:, hd * 64:(hd + 1) * 64]),
                             mt.b + xdt.b, yd.b, mark=(hh == 7))
                    if cut == 2:
                        P.op("tensor", MM(yd.t[:, :], cT.t[:, g, :], prevb.t[:, gs]), cT.b + [prevb.b[g]], yd.b)
                    P.op("tensor", MM(yo_ps.t[:, :], cT.t[:, g, :], prevb.t[:, gs]), cT.b + [prevb.b[g]], yo_ps.b)
                    P.op("tensor", MM(st_ps.t[:, :], bt.t[:, g * 128:(g + 1) * 128], xdts.t[:, gs]), bt.b + xdts.b, st_ps.b)
                    t1 = t1r.next()
                    P.op("vector", TT(v3(t1.t[:]), v3(yo_ps.t[:, :]), bc3(eacs.t[:, g * 8:(g + 1) * 8], 8), ALU.mult),
                         yo_ps.b + eacs.b, t1.b)
                    P.op("vector", TT(y.t[:, gs], yd.t[:, :], t1.t[:], ALU.add), yd.b + t1.b, [y.b[g]])
                    t2 = t2r.next()
                    P.op("vector", TT(t2.t[:], xs.t[:, gs], dsk.t[:, gs], ALU.mult), xs.b + dsk.b, t2.b)
                    P.op("vector", TT(y.t[:, gs], y.t[:, gs], t2.t[:], ALU.add), t2.b, [y.b[g]])
                    P.op("vector", TT(y.t[:, gs], y.t[:, gs], zs.t[:, gs], ALU.mult), zs.b, [y.b[g]])
                    P.op("vector", TT(junk.t[:], y.t[:, gs], y.t[:, gs], ALU.mult), [y.b[g]], junk.b)
                    P.op("vector", lambda e, o=junk.t[:], acc=ss.t[:, g:g + 1]: e.tensor_scalar(
                        out=o, in0=o, scalar1=1.0, scalar2=0.0, op0=ALU.mult, op1=ALU.add, accum_out=acc),
                        junk.b, junk.b + ss.b)
                    P.op("vector", TT(v3(prev.t[:, gs]), v3(prev.t[:, gs]), bc3(cdec.t[:, g * 8:(g + 1) * 8], 8), ALU.mult),
                         cdec.b, [prev.b[g]])
                    P.op("vector", TT(prev.t[:, gs], prev.t[:, gs], st_ps.t[:, :], ALU.add), st_ps.b, [prev.b[g]])
                    P.op("scalar", ACT(prevb.t[:, gs], prev.t[:, gs], AF.Copy), [prev.b[g]], [prevb.b[g]])
                P.op("vector", TS(rstd.t[:], ss.t[:], 1.0 / 512, ALU.mult, RMS_EPS, ALU.add), ss.b, rstd.b)
                P.op("scalar", ACT(rstd.t[:], rstd.t[:], AF.Ln), rstd.b, rstd.b)
                P.op("scalar", ACT(rstd.t[:], rstd.t[:], AF.Exp, scale=-0.5), rstd.b, rstd.b)
                if "dbgY" in self.dbg:
                    P.op("vector", CP(dbg8.t[:, 0:4], ss.t[:]), ss.b, dbg8.b)
                    P.op("vector", CP(dbg8.t[:, 4:8], rstd.t[:]), rstd.b, dbg8.b)
                    for ii, tb_ in enumerate((at, acs, eacs, dst, cdec)):
                        P.op("vector", CP(dbg8.t[:, 8 + ii * 32:40 + ii * 32], tb_.t[:]), tb_.b, dbg8.b)
                    P.dma("sync", [(self.scr["dbgY"].t[r0:r0 + 128, :], y.t[:]), (self.scr["dbgS"].t[r0:r0 + 128, :], dbg8.t[:]),
                                   (self.scr["dbgP"].t[c, :, :], prev.t[:])], y.b + dbg8.b + prev.b, [], dd)
                    P._wait("vector", (dd.sem, dd.count, "dma"))
                yst = ynst.next()
                for g in range(4):
                    gs = slice(g * 512, (g + 1) * 512)
                    P.op("vector", STT(yn.t[:, gs], y.t[:, gs], rstd.t[:, g:g + 1], ngb.t[:, gs], ALU.mult, ALU.mult),
                         [y.b[g]] + rstd.b + ngb.b, [yn.b[g]])
                    tp = tpr.next()
                    for j in range(4):
                        cc = g * 4 + j
                        P.op("tensor", TR(tp.t[:, j * 128:(j + 1) * 128], yn.t[:, cc * 128:(cc + 1) * 128], self.identb.t[:]),
                             [yn.b[g]] + self.identb.b, tp.b, mark=(j == 3))
                    P.op("scalar", ACT(yst.t[:, g * 4:(g + 1) * 4, :], tp.t[:, :].rearrange("p (c t) -> p c t", c=4), AF.Copy),
                         tp.b, [yst.b[g]])
                P.dma("gpsimd", [(ynTS.t[:, :, r0:r0 + 128].rearrange("c p t -> p c t"), yst.t[:, :, :])],
                      yst.b, ynTS.b, yst.ss, nowaw=True)
            P.end_phase()


    def phase_C1(self, l):
        P = self.P
        E = self.ext
        hT, oTS, ynTS, h1T = (self.scr[n] for n in ("hT", "oT", "ynT", "h1T"))
        WT = 256
        with ExitStack() as st:
            P.begin_phase(st)
            w_in = E["w_in"]
            Woa = P.sbuf([128, 8, 1024], BF16, dma=True)
            Wos = P.sbuf([128, 16, 1024], BF16, dma=True)
            Wout = P.sbuf([128, 8, 1024], BF16, dma=True)
            Wga = P.sbuf([128, 8, 1024], BF16, dma=True)
            Wgs = P.sbuf([128, 8, 1024], BF16, dma=True)
            self.load_w(Woa, E["w_o_attn"][l, :, :], 8)
            self.load_w(Wos, E["w_o_ssd"][l, :, :], 16)
            self.load_w(Wout, E["w_out"][l, :, :], 8)
            self.load_w(Wga, w_in[l, :, 6240:7264], 8)
            self.load_w(Wgs, w_in[l, :, 7264:8288], 8)
            gb = P.sbuf([128, 16], F32, dma=True)
            P.dma("sync", [(gb.t[:, 0:8], E["ln1_g"][l, :, :]), (gb.t[:, 8:16], E["ln1_b"][l, :, :])], [], gb.b, gb.ds)
            otr = Ring([P.sbuf([128, 8, WT], BF16, dma=True) for _ in range(2)])
            ynr = Ring([P.sbuf([128, 16, WT], BF16, dma=True) for _ in range(2)])
            hr = Ring([P.sbuf([128, 8, WT], F32, n=8, dma=True) for _ in range(2)])
            hb = P.sbuf([128, 8, WT], BF16, n=8)
            mixed = P.sbuf([128, 8, WT], BF16, n=8)
            orr = Ring([P.sbuf([128, 8, WT], F32, n=8, dma=True) for _ in range(1)])
            sgr = Ring([P.sbuf([128, WT], F32) for _ in range(4)])
            tr_ = Ring([P.sbuf([128, WT], F32) for _ in range(4)])
            accr = Ring([P.psum([128, 512], F32) for _ in range(4)])
            R = self.ln_resources(WT)
            evi = [0]

            def proj(acc, W_tb, oc, K, rhs_tb, W):
                for k in range(K):
                    P.op("tensor", MM(acc.t[:, 0:W], W_tb.t[:, k, oc * 128:(oc + 1) * 128], rhs_tb.t[:, k, 0:W],
                                      k == 0, k == K - 1), W_tb.b + [rhs_tb.b[k % len(rhs_tb.b)]], acc.b, mark=(k == K - 1))

            for (c0, W) in tiles_of(WT):
                ot = otr.next(); yt = ynr.next(); h = hr.next()
                P.dma("sync", [(ot.t[:, :, 0:W], oTS.t[:, :, c0:c0 + W].rearrange("h p t -> p h t"))], oTS.b, ot.b, ot.ds)
                P.dma("sync", [(yt.t[:, :, 0:W], ynTS.t[:, :, c0:c0 + W].rearrange("c p t -> p c t"))], ynTS.b, yt.b, yt.ds)
                P.dma("sync", [(h.t[:, :, 0:W], hT.t[:, :, c0:c0 + W].rearrange("c p t -> p c t"))], hT.b, h.b, h.ds)
                for c in range(8):
                    evi[0] += 1
                    if evi[0] % 2:
                        P.op("scalar", ACT(hb.t[:, c, 0:W], h.t[:, c, 0:W], AF.Copy), h.b, [hb.b[c]])
                    else:
                        P.op("vector", CP(hb.t[:, c, 0:W], h.t[:, c, 0:W]), h.b, [hb.b[c]])
                for oc in range(8):
                    ga = accr.next()
                    proj(ga, Wga, oc, 8, hb, W)
                    sga = sgr.next()
                    P.op("scalar", ACT(sga.t[:, 0:W], ga.t[:, 0:W], AF.Sigmoid), ga.b, sga.b)
                    gs_ = accr.next()
                    proj(gs_, Wgs, oc, 8, hb, W)
                    sgs = sgr.next()
                    P.op("scalar", ACT(sgs.t[:, 0:W], gs_.t[:, 0:W], AF.Sigmoid), gs_.b, sgs.b)
                    ya = accr.next()
                    proj(ya, Woa, oc, 8, ot, W)
                    t1 = tr_.next()
                    P.op("vector", TT(t1.t[:, 0:W], ya.t[:, 0:W], sga.t[:, 0:W], ALU.mult), ya.b + sga.b, t1.b)
                    ys_ = accr.next()
                    proj(ys_, Wos, oc, 16, yt, W)
                    t2 = tr_.next()
                    P.op("vector", TT(t2.t[:, 0:W], ys_.t[:, 0:W], sgs.t[:, 0:W], ALU.mult), ys_.b + sgs.b, t2.b)
                    P.op("vector", TT(mixed.t[:, oc, 0:W], t1.t[:, 0:W], t2.t[:, 0:W], ALU.add), t1.b + t2.b, [mixed.b[oc]])
                for oc in range(8):
                    r = accr.next()
                    proj(r, Wout, oc, 8, mixed, W)
                    P.op("vector", STT(h.t[:, oc, 0:W], h.t[:, oc, 0:W], ALPHA, r.t[:, 0:W], ALU.mult, ALU.add),
                         r.b + [hb.b[oc]], [h.b[oc]])
                o = orr.next()
                self.ln_fm(h, o, W, gb.t[:, 0:8], gb.t[:, 8:16], R)
                P.dma("gpsimd", [(h1T.t[:, :, c0:c0 + W].rearrange("c p t -> p c t"), o.t[:, :, 0:W])],
                      o.b, h1T.b, o.ss, nowaw=True)
            P.end_phase()

    def phase_C2(self, l):
        P = self.P
        E = self.ext
        hT, h1T = (self.scr[n] for n in ("hT", "h1T"))
        last = (l == self.n_layers - 1)
        WT = 256
        with ExitStack() as st:
            P.begin_phase(st)
            Wup = P.sbuf([128, 8, 2 * DFF], BF16, dma=True)
            Wdn = P.sbuf([128, 22, 1024], BF16, dma=True)
            self.load_w(Wup, E["w_up"][l, :, :], 8, per=1)
            self.load_w(Wdn, E["w_down"][l, :, :], 22, per=4)
            gb = P.sbuf([128, 16], F32, dma=True)
            P.dma("sync", [(gb.t[:, 0:8], E["ln2_g"][l, :, :]), (gb.t[:, 8:16], E["ln2_b"][l, :, :])], [], gb.b, gb.ds)
            cw = P.sbuf([128, 44, 3], F32, dma=True)
            cb = P.sbuf([128, 44], F32, dma=True)
            P.dma("sync", [(cw.t[:], E["ffn_cw"][l, :, :, :])], [], cw.b, cw.ds)
            P.dma("sync", [(cb.t[:], E["ffn_cb"][l, :, :])], [], cb.b, cb.ds)
            hr = Ring([P.sbuf([128, 8, WT], F32, n=8, dma=True) for _ in range(2)])
            hb = P.sbuf([128, 8, WT], BF16, n=8)
            a = P.sbuf([128, 22, WT], BF16, n=22)
            orr = Ring([P.sbuf([128, 8, WT], F32, n=8, dma=True) for _ in range(1)])
            xbr = Ring([P.sbuf([128, WT + 2], F32) for _ in range(4)])
            tcr = Ring([P.sbuf([128, WT], F32) for _ in range(4)])
            sgr = Ring([P.sbuf([128, WT], F32) for _ in range(2)])
            halo = P.sbuf([128, 44, 2], F32, n=44)
            ostr = Ring([P.sbuf([128, 1024], F32, n=2, dma=True) for _ in range(1)]) if last else None
            accr = Ring([P.psum([128, 512], F32) for _ in range(4)])
            tpr = Ring([P.psum([128, 512], F32) for _ in range(2)]) if last else None
            R = self.ln_resources(WT)
            P.op("vector", MEMSET(halo.t[:], 0.0), [], halo.b)
            evi = [0]

            def proj(acc, W_tb, col0, K, rhs_tb, W):
                for k in range(K):
                    P.op("tensor", MM(acc.t[:, 0:W], W_tb.t[:, k, col0:col0 + 128], rhs_tb.t[:, k, 0:W],
                                      k == 0, k == K - 1), W_tb.b + [rhs_tb.b[k]], acc.b, mark=(k == K - 1))

            def conv(acc, ci, W, first):
                xb = xbr.next()
                P.op("scalar", ACT(xb.t[:, 2:2 + W], acc.t[:, 0:W], AF.Copy), acc.b, xb.b)
                P.op("vector", CP(xb.t[:, 0:2], halo.t[:, ci, :]), [halo.b[ci]], xb.b)
                if first:
                    P.op("vector", MEMSET(xb.t[:, 0:2 + PAD], 0.0), [], xb.b)
                P.op("vector", CP(halo.t[:, ci, :], xb.t[:, W:W + 2]), xb.b, [halo.b[ci]])
                tc_ = tcr.next()
                P.op("vector", TS(tc_.t[:, 0:W], xb.t[:, 0:W], cw.t[:, ci, 0:1], ALU.mult, cb.t[:, ci:ci + 1], ALU.add),
                     xb.b + cw.b + cb.b, tc_.b)
                for tap in (1, 2):
                    P.op("vector", STT(tc_.t[:, 0:W], xb.t[:, tap:tap + W], cw.t[:, ci, tap:tap + 1], tc_.t[:, 0:W],
                                       ALU.mult, ALU.add), xb.b + cw.b, tc_.b)
                return tc_

            for (c0, W) in tiles_of(WT):
                nb = W // 128
                h = hr.next()
                P.dma("sync", [(h.t[:, :, 0:W], h1T.t[:, :, c0:c0 + W].rearrange("c p t -> p c t"))], h1T.b, h.b, h.ds)
                for c in range(8):
                    evi[0] += 1
                    if evi[0] % 2:
                        P.op("scalar", ACT(hb.t[:, c, 0:W], h.t[:, c, 0:W], AF.Copy), h.b, [hb.b[c]])
                    else:
                        P.op("vector", CP(hb.t[:, c, 0:W], h.t[:, c, 0:W]), h.b, [hb.b[c]])
                for c in range(22):
                    ug = accr.next()
                    proj(ug, Wup, c * 128, 8, hb, W)
                    uv = accr.next()
                    proj(uv, Wup, DFF + c * 128, 8, hb, W)
                    tg = conv(ug, c, W, c0 == 0)
                    tv = conv(uv, 22 + c, W, c0 == 0)
                    sg = sgr.next()
                    P.op("scalar", ACT(sg.t[:, 0:W], tg.t[:, 0:W], AF.Silu), tg.b, sg.b)
                    P.op("vector", TT(a.t[:, c, 0:W], sg.t[:, 0:W], tv.t[:, 0:W], ALU.mult), sg.b + tv.b, [a.b[c]])
                for oc in range(8):
                    f = accr.next()
                    proj(f, Wdn, oc * 128, 22, a, W)
                    P.op("vector", STT(h.t[:, oc, 0:W], h.t[:, oc, 0:W], ALPHA, f.t[:, 0:W], ALU.mult, ALU.add),
                         f.b + [hb.b[oc]], [h.b[oc]])
                o = orr.next()
                self.ln_fm(h, o, W, gb.t[:, 0:8], gb.t[:, 8:16], R)
                if not last:
                    P.dma("gpsimd", [(hT.t[:, :, c0:c0 + W].rearrange("c p t -> p c t"), o.t[:, :, 0:W])],
                          o.b, hT.b, o.ss, nowaw=True)
                elif c0 > 0:
                    for bi in range(nb):
                        os_ = ostr.next()
                        for half in range(2):
                            tp = tpr.next()
                            for j in range(4):
                                cc = half * 4 + j
                                P.op("tensor", TR(tp.t[:, j * 128:(j + 1) * 128], o.t[:, cc, bi * 128:(bi + 1) * 128],
                                                  self.identf.t[:]), [o.b[cc]] + self.identf.b, tp.b, mark=(j == 3))
                            P.op("scalar", ACT(os_.t[:, half * 512:(half + 1) * 512], tp.t[:, :], AF.Copy), tp.b, [os_.b[half]])
                        r0 = c0 + bi * 128 - 128
                        P.dma("gpsimd", [(self.out[r0:r0 + 128, :], os_.t[:, :])], os_.b, [], os_.ss)
            P.end_phase()


INPUT_SHAPES = {
    "xin": [LP, D], "emb_g": [128, 8], "emb_b": [128, 8],
    "c_ident": [128, 128], "c_tri": [128, 128], "c_sellast": [128, 128], "c_mask0": [128, 128], "c_kmask0": [128, 1],
    "cosT2": [128, LP], "sinT2": [128, LP],
    "w_in": [DEPTH, D, 8288], "w_kpes": [DEPTH, D, 64],
    "wqb_n": [DEPTH, QL, 1024], "wqb_p": [DEPTH, QL, 512], "wqb_ps": [DEPTH, QL, 512],
    "wkvb_kn": [DEPTH, KVL, 1024], "wkvb_v": [DEPTH, KVL, 1024],
    "qng": [DEPTH, 128, 6], "kvng": [DEPTH, 128, 2],
    "w_o_attn": [DEPTH, 1024, D], "w_o_ssd": [DEPTH, 2048, D], "w_out": [DEPTH, D, D],
    "w_up": [DEPTH, D, 2 * DFF], "w_down": [DEPTH, DFF, D],
    "ssd_cw": [DEPTH, 128, 24, 4], "ssd_cb": [DEPTH, 128, 24],
    "dtb_bc": [DEPTH, 128, 32], "alog_bc": [DEPTH, 128, 32], "dskip_bc": [DEPTH, 128, 2048], "ssdng_bc": [DEPTH, 128, 2048],
    "ln1_g": [DEPTH, 128, 8], "ln1_b": [DEPTH, 128, 8], "ln2_g": [DEPTH, 128, 8], "ln2_b": [DEPTH, 128, 8],
    "ffn_cw": [DEPTH, 128, 44, 3], "ffn_cb": [DEPTH, 128, 44],
}

SCRATCH = {
    "hT": ([8, 128, LP], F32), "h1T": ([8, 128, LP], F32),
    "qT": ([NH, 192, LP], BF16), "kT": ([NH, 128, LP], BF16), "kpeT": ([64, LP], BF16),
    "v": ([NH, 128, NBLK, 128], BF16), "oT": ([NH, 128, LP], BF16),
    "xs": ([LP, 2048], BF16), "Btok": ([LP, 512], BF16), "BT": ([4, 128, LP], BF16), "CT": ([4, 128, LP], BF16),
    "dt": ([LP, 32], F32), "zs": ([LP, 2048], BF16), "ynT": ([16, 128, LP], BF16),
    "dbgY": ([LP, 2048], F32), "dbgS": ([LP, 168], F32), "dbgP": ([NBLK, 128, 2048], F32),
}


def build(n_layers=2, dbg=None, stop=None):
    nc = bass.Bass("TRN2", target_bir_lowering=False)
    with ExitStack() as gst:
        P = Prog(nc, gst)
        k = K(nc, P, n_layers, dbg)
        for nm, shp in INPUT_SHAPES.items():
            k.din(nm, shp)
        for nm, (shp, dt) in SCRATCH.items():
            k.dscr(nm, shp, dt)
        k.out = nc.dram_tensor("out", [SEQ, D], F32, kind="ExternalOutput").ap()
        k.load_consts(gst)
        seq = [("E", k.phase_E, ())]
        for l in range(n_layers):
            for nm in ("A1", "A2", "B", "S", "C1", "C2"):
                fn = getattr(k, "phase_" + nm, None)
                if fn is not None:
                    seq.append((f"{nm}_{l}", fn, (l,)))
        for (nm, fn, args) in seq:
            fn(*args)
            if stop == nm:
                break
    return nc, k


def host_consts():
    c = {}
    c["c_ident"] = np.eye(128, dtype=np.float32)
    kk = np.arange(128)[:, None]
    qq = np.arange(128)[None, :]
    c["c_tri"] = (kk <= qq).astype(np.float32)
    c["c_sellast"] = np.broadcast_to((kk == 127), (128, 128)).astype(np.float32).copy()
    m0 = ((qq >= PAD) & (kk >= PAD) & (kk <= qq)) | ((qq < PAD) & (kk == qq))
    c["c_mask0"] = m0.astype(np.float32)
    c["c_kmask0"] = (np.arange(128) >= PAD).astype(np.float32)[:, None].copy()
    return c


def pm(v, nchunk):
    return np.ascontiguousarray(np.asarray(v, np.float32).reshape(nchunk, 128).T)


def rope_tables_T():
    inv_freq = (1.0 / (np.float32(10000.0) ** (np.arange(0, 64, 2, dtype=np.float32) / np.float32(64)))).astype(np.float32)
    pos = np.maximum(np.arange(LP, dtype=np.float32) - np.float32(PAD), np.float32(0))
    ang = (pos[:, None] * inv_freq[None, :]).astype(np.float32)
    ang = np.concatenate([ang, ang], axis=-1)
    cos = np.cos(ang).astype(np.float32).T
    sin = np.sin(ang).astype(np.float32).T
    sgn = np.concatenate([-np.ones(32, np.float32), np.ones(32, np.float32)])[:, None]
    sins = sin * sgn
    return (np.ascontiguousarray(np.concatenate([cos, cos], 0)), np.ascontiguousarray(np.concatenate([sins, sins], 0)))


def prep_shared(inp):
    f = lambda a: np.ascontiguousarray(np.asarray(a, np.float32))
    sh = dict(host_consts())
    sh["cosT2"], sh["sinT2"] = rope_tables_T()
    sh["emb_g"] = pm(inp["emb_ln_g"], 8)
    sh["emb_b"] = pm(inp["emb_ln_b"], 8)
    w_in = f(inp["w_in"])
    sh["w_in"] = w_in
    sh["w_kpes"] = f(np.concatenate([w_in[:, :, 1056:1088], w_in[:, :, 1024:1056]], axis=-1))
    wqb = f(inp["w_q_b"]).reshape(DEPTH, QL, NH, 192)
    sh["wqb_n"] = f(wqb[..., :128].reshape(DEPTH, QL, 1024))
    sh["wqb_p"] = f(wqb[..., 128:].reshape(DEPTH, QL, 512))
    sh["wqb_ps"] = f(np.concatenate([wqb[..., 160:192], wqb[..., 128:160]], axis=-1).reshape(DEPTH, QL, 512))
    wkv = f(inp["w_kv_b"]).reshape(DEPTH, KVL, NH, 256)
    sh["wkvb_kn"] = f(wkv[..., :128].reshape(DEPTH, KVL, 1024))
    sh["wkvb_v"] = f(wkv[..., 128:].reshape(DEPTH, KVL, 1024))
    sh["qng"] = f(np.stack([pm(inp["q_norm_g"][l], 6) for l in range(DEPTH)]))
    sh["kvng"] = f(np.stack([pm(inp["kv_norm_g"][l], 2) for l in range(DEPTH)]))
    for nm in ("w_o_attn", "w_o_ssd", "w_out", "w_up", "w_down"):
        sh[nm] = f(inp[nm])
    cw = f(inp["ssd_conv_w"])
    sh["ssd_cw"] = f(cw.reshape(DEPTH, 4, 24, 128).transpose(0, 3, 2, 1))
    sh["ssd_cb"] = f(f(inp["ssd_conv_b"]).reshape(DEPTH, 24, 128).transpose(0, 2, 1))
    bc = lambda a: f(np.broadcast_to(f(a)[:, None, :], (DEPTH, 128, a.shape[-1])))
    sh["dtb_bc"] = bc(inp["dt_bias"])
    sh["alog_bc"] = bc(inp["a_log"])
    sh["dskip_bc"] = bc(np.repeat(f(inp["d_skip"]), 64, axis=-1))
    sh["ssdng_bc"] = bc(inp["ssd_norm_g"])
    for nm in ("ln1_g", "ln1_b", "ln2_g", "ln2_b"):
        sh[nm] = f(np.stack([pm(inp[nm][l], 8) for l in range(DEPTH)]))
    fw = f(inp["ffn_conv_w"])
    sh["ffn_cw"] = f(fw.reshape(DEPTH, 3, 44, 128).transpose(0, 3, 2, 1))
    sh["ffn_cb"] = f(f(inp["ffn_conv_b"]).reshape(DEPTH, 44, 128).transpose(0, 2, 1))
    return sh


def xin_of(inp, b):
    xin = np.zeros((LP, D), np.float32)
    xin[PAD:PAD + NMETA] = inp["meta_tokens"]
    xin[128:] = inp["x"][b]
    return xin


def kernel(**inp):
    sh = prep_shared(inp)
    nc, k = build()
    in_maps = []
    for c in range(8):
        m = dict(sh)
        m["xin"] = xin_of(inp, c % 4)
        in_maps.append(m)
    res = run_bass_kernel_spmd(nc, in_maps, core_ids=list(range(8)))
    out = np.stack([np.asarray(res.results[b]["out"], np.float32) for b in range(4)], axis=0)
    return out
```

```python
import math
from contextlib import ExitStack

import numpy as np
import concourse.bass as bass
import concourse.mybir as mybir
from concourse.bass_utils import run_bass_kernel_spmd

F32 = mybir.dt.float32
BF16 = mybir.dt.bfloat16
AF = mybir.ActivationFunctionType
ALU = mybir.AluOpType

D = 1024
SEQ = 8192
NMETA = 16
PAD = 112
LP = 8320
NBLK = 65
NH = 8
QL = 768
KVL = 256
DFF = 2816
DEPTH = 2
ALPHA = (2 * DEPTH) ** 0.25
LN_EPS = 1e-5
RMS_EPS = 1e-6
SCALE = 192 ** -0.5

ENGS = ("sync", "scalar", "vector", "gpsimd", "tensor")
COMPUTE = ("scalar", "vector", "gpsimd", "tensor")


class Buf:
    __slots__ = ("w", "r")

    def __init__(self):
        self.w = None
        self.r = {}


class DSem:
    def __init__(self, sem):
        self.sem = sem
        self.count = 0


class TB:
    def __init__(self, t, n=1, ds=None, ss=None):
        self.t = t
        self.b = [Buf() for _ in range(n)]
        self.ds = ds
        self.ss = ss


class Ring:
    def __init__(self, items):
        self.items = items
        self.i = 0

    def next(self):
        it = self.items[self.i % len(self.items)]
        self.i += 1
        return it


class Prog:
    def __init__(self, nc, gstack):
        self.nc = nc
        self.gstack = gstack
        self.pstack = None
        self.q = {e: [] for e in ENGS}
        self.cnt = {e: 0 for e in ENGS}
        self.waited = {e: {} for e in ENGS}
        self.pending = {e: [] for e in ENGS}
        self.psem = {}
        self.nsem = 0
        for e in COMPUTE:
            self.psem[e] = self._new_sem()
        self.pool = []
        self.pool_i = 0
        self.ninstr = 0
        self.phase_i = 0
        self.nt = 0

    def _new_sem(self):
        self.nsem += 1
        return self.gstack.enter_context(self.nc.semaphore(f"sem{self.nsem}"))

    def dsem(self):
        if self.pool_i == len(self.pool):
            self.pool.append(DSem(self._new_sem()))
        d = self.pool[self.pool_i]
        self.pool_i += 1
        return d

    def sbuf(self, shape, dtype, n=1, dma=False):
        self.nt += 1
        t = self.pstack.enter_context(self.nc.sbuf_tensor(f"sb{self.nt}", list(shape), dtype))
        return TB(t, n, self.dsem() if dma else None, self.dsem() if dma else None)

    def psum(self, shape, dtype, n=1):
        self.nt += 1
        t = self.pstack.enter_context(self.nc.psum_tensor(f"ps{self.nt}", list(shape), dtype))
        return TB(t, n)

    def _wait(self, eng, tok):
        if tok is None:
            return
        sem, val, src = tok
        if src == eng and eng == "tensor":
            return
        key = id(sem)
        if self.waited[eng].get(key, 0) >= val:
            return
        self.waited[eng][key] = val
        self.q[eng].append(lambda e, s=sem, v=val: e.wait_ge(s, v))

    def _deps(self, eng, reads, writes, nowaw=False):
        for b in reads:
            self._wait(eng, b.w)
        for b in writes:
            if not nowaw:
                self._wait(eng, b.w)
            for t in b.r.values():
                self._wait(eng, t)

    def _commit(self, tok, reads, writes):
        k = id(tok[0])
        for b in reads:
            b.r[k] = tok
        for b in writes:
            b.w = tok
            b.r = {}

    def op(self, eng, fn, reads=(), writes=(), mark=True):
        self.ninstr += 1
        self._deps(eng, reads, writes)
        if not mark:
            self.pending[eng].append((tuple(reads), tuple(writes)))
            self.q[eng].append(fn)
            return None
        self.cnt[eng] += 1
        v = self.cnt[eng]
        sem = self.psem[eng]
        self.q[eng].append(lambda e, f=fn, s=sem: f(e).then_inc(s, 1))
        tok = (sem, v, eng)
        for (r, w) in self.pending[eng]:
            self._commit(tok, r, w)
        self.pending[eng] = []
        self._commit(tok, reads, writes)
        return tok

    def dma(self, eng, pairs, reads, writes, ds, nowaw=False):
        self._deps(eng, reads, writes, nowaw)
        for (o, i) in pairs:
            self.ninstr += 1
            ds.count += 16
            self.q[eng].append(lambda e, o=o, i=i, s=ds.sem: e.dma_start(out=o, in_=i).then_inc(s, 16))
        tok = (ds.sem, ds.count, "dma")
        self._commit(tok, reads, writes)
        return tok

    def begin_phase(self, st):
        self.pstack = st
        self.pool_i = 0

    def end_phase(self):
        for d in self.pool:
            if d.count:
                self._wait("sync", (d.sem, d.count, "dma"))
        for e in COMPUTE:
            assert not self.pending[e], e
            if self.cnt[e]:
                self._wait("sync", (self.psem[e], self.cnt[e], e))
        nc = self.nc
        self.phase_i += 1
        with nc.Block() as block:
            for ename in ENGS:
                lst = self.q[ename]
                if not lst:
                    continue

                def body(e, lst=lst):
                    for f in lst:
                        f(e)
                getattr(block, ename)(body)
        self.q = {e: [] for e in ENGS}


def MM(out, lhsT, rhs, start=True, stop=True):
    return lambda e: e.matmul(out, lhsT=lhsT, rhs=rhs, start=start, stop=stop)


def TR(out, in_, ident):
    return lambda e: e.transpose(out=out, in_=in_, identity=ident)


def ACT(out, in_, func, bias=None, scale=None):
    kw = {}
    if bias is not None:
        kw["bias"] = bias
    if scale is not None:
        kw["scale"] = scale
    return lambda e: e.activation(out=out, in_=in_, func=func, **kw)


def TT(out, a, b, op):
    return lambda e: e.tensor_tensor(out=out, in0=a, in1=b, op=op)


def TS(out, a, s1, op0, s2=None, op1=None):
    if op1 is None:
        return lambda e: e.tensor_scalar(out=out, in0=a, scalar1=s1, scalar2=None, op0=op0)
    return lambda e: e.tensor_scalar(out=out, in0=a, scalar1=s1, scalar2=s2, op0=op0, op1=op1)


def STT(out, in0, scalar, in1, op0, op1):
    return lambda e: e.scalar_tensor_tensor(out=out, in0=in0, scalar=scalar, in1=in1, op0=op0, op1=op1)


def CP(out, in_):
    return lambda e: e.tensor_copy(out=out, in_=in_)


def MEMSET(ap, v):
    return lambda e: e.memset(ap, v)


def tiles_of(width):
    t = [(0, 128)]
    c = 128
    while c < LP:
        t.append((c, width))
        c += width
    return t


class K:
    def __init__(self, nc, P, n_layers, dbg):
        self.nc = nc
        self.P = P
        self.n_layers = n_layers
        self.dbg = dbg or ()
        self.ext = {}
        self.scr = {}

    def din(self, name, shape, dt=F32):
        ap = self.nc.dram_tensor(name, list(shape), dt, kind="ExternalInput").ap()
        self.ext[name] = ap
        return ap

    def dscr(self, name, shape, dt):
        kind = "ExternalOutput" if name in self.dbg else "Internal"
        ap = self.nc.dram_tensor(name, list(shape), dt, kind=kind).ap()
        self.scr[name] = TB(ap, 1)
        return self.scr[name]

    def load_consts(self, st):
        P = self.P
        P.pstack = st
        c = self.ext
        self.identf = P.sbuf([128, 128], F32, dma=True)
        self.tri_f = P.sbuf([128, 128], F32, dma=True)
        self.sellast = P.sbuf([128, 128], F32, dma=True)
        self.mask0 = P.sbuf([128, 128], F32, dma=True)
        self.kmask0 = P.sbuf([128, 1], F32, dma=True)
        self.identb = P.sbuf([128, 128], BF16)
        self.ones_b = P.sbuf([128, 128], BF16)
        self.tri_b = P.sbuf([128, 128], BF16)
        self.mask0_b = P.sbuf([128, 128], BF16)
        for tb, nm in ((self.identf, "c_ident"), (self.tri_f, "c_tri"), (self.sellast, "c_sellast"),
                       (self.mask0, "c_mask0")):
            P.dma("sync", [(tb.t[:], c[nm][:, :])], [], tb.b, tb.ds)
        P.dma("sync", [(self.kmask0.t[:], c["c_kmask0"][:, :])], [], self.kmask0.b, self.kmask0.ds)
        P.op("vector", CP(self.identb.t[:], self.identf.t[:]), self.identf.b, self.identb.b)
        P.op("vector", CP(self.tri_b.t[:], self.tri_f.t[:]), self.tri_f.b, self.tri_b.b)
        P.op("vector", CP(self.mask0_b.t[:], self.mask0.t[:]), self.mask0.b, self.mask0_b.b)
        P.op("vector", MEMSET(self.ones_b.t[:], 1.0), [], self.ones_b.b)

    def load_w(self, dst, src, kchunks, per=4):
        P = self.P
        for k0 in range(0, kchunks, per):
            k1 = min(kchunks, k0 + per)
            P.dma("gpsimd", [(dst.t[:, k0:k1, :], src[k0 * 128:k1 * 128, :].rearrange("(k p) n -> p k n", p=128))],
                  [], dst.b, dst.ds)

    def ln_fm(self, s, out, W, g, b, R):
        P = self.P
        ones = self.ones_b
        sum_ps, ssq_ps = R["sum"], R["ssq"]
        for c in range(8):
            sb = R["sb"].next()
            sq = R["sq"].next()
            P.op("scalar", ACT(sb.t[:, 0:W], s.t[:, c, 0:W], AF.Copy), [s.b[c]], sb.b)
            P.op("scalar", ACT(sq.t[:, 0:W], s.t[:, c, 0:W], AF.Square), [s.b[c]], sq.b)
            P.op("tensor", MM(sum_ps.t[:, 0:W], ones.t[:], sb.t[:, 0:W], c == 0, c == 7), sb.b + ones.b, sum_ps.b,
                 mark=False)
            P.op("tensor", MM(ssq_ps.t[:, 0:W], ones.t[:], sq.t[:, 0:W], c == 0, c == 7), sq.b + ones.b, ssq_ps.b,
                 mark=True)
        mean, var, rstd = R["mean"], R["var"], R["rstd"]
        P.op("vector", TS(mean.t[:, 0:W], sum_ps.t[:, 0:W], 1.0 / D, ALU.mult), sum_ps.b, mean.b)
        P.op("vector", TS(var.t[:, 0:W], ssq_ps.t[:, 0:W], 1.0 / D, ALU.mult), ssq_ps.b, var.b)
        msq = R["msq"]
        P.op("vector", TT(msq.t[:, 0:W], mean.t[:, 0:W], mean.t[:, 0:W], ALU.mult), mean.b, msq.b)
        P.op("vector", TT(var.t[:, 0:W], var.t[:, 0:W], msq.t[:, 0:W], ALU.subtract), var.b + msq.b, var.b)
        P.op("vector", TS(var.t[:, 0:W], var.t[:, 0:W], LN_EPS, ALU.add), var.b, var.b)
        P.op("scalar", ACT(rstd.t[:, 0:W], var.t[:, 0:W], AF.Ln), var.b, rstd.b)
        P.op("scalar", ACT(rstd.t[:, 0:W], rstd.t[:, 0:W], AF.Exp, scale=-0.5), rstd.b, rstd.b)
        for c in range(8):
            tmp = R["tmp"].next()
            P.op("vector", TT(tmp.t[:, 0:W], s.t[:, c, 0:W], mean.t[:, 0:W], ALU.subtract), [s.b[c]] + mean.b, tmp.b)
            P.op("vector", TT(tmp.t[:, 0:W], tmp.t[:, 0:W], rstd.t[:, 0:W], ALU.mult), tmp.b + rstd.b, tmp.b)
            P.op("scalar", ACT(out.t[:, c, 0:W], tmp.t[:, 0:W], AF.Identity, bias=b[:, c:c + 1], scale=g[:, c:c + 1]),
                 tmp.b, [out.b[c]])

    def ln_resources(self, wmax=512):
        P = self.P
        R = {}
        R["sum"] = P.psum([128, 512], F32)
        R["ssq"] = P.psum([128, 512], F32)
        R["sb"] = Ring([P.sbuf([128, wmax], BF16) for _ in range(2)])
        R["sq"] = Ring([P.sbuf([128, wmax], BF16) for _ in range(2)])
        for nm in ("mean", "var", "msq", "rstd"):
            R[nm] = P.sbuf([128, wmax], F32)
        R["tmp"] = Ring([P.sbuf([128, wmax], F32) for _ in range(2)])
        return R

    def phase_E(self):
        P = self.P
        hT = self.scr["hT"]
        with ExitStack() as st:
            P.begin_phase(st)
            xin = self.ext["xin"]
            gb = P.sbuf([128, 16], F32, dma=True)
            P.dma("sync", [(gb.t[:, 0:8], self.ext["emb_g"][:, :]), (gb.t[:, 8:16], self.ext["emb_b"][:, :])],
                  [], gb.b, gb.ds)
            xr = Ring([P.sbuf([128, 4, 1024], F32, dma=True) for _ in range(2)])
            sr = Ring([P.sbuf([128, 8, 512], F32, n=8) for _ in range(2)])
            orr = Ring([P.sbuf([128, 8, 512], F32, n=8, dma=True) for _ in range(2)])
            tpr = Ring([P.psum([128, 512], F32) for _ in range(2)])
            R = self.ln_resources()
            for (c0, W) in tiles_of(512):
                nb = W // 128
                x = xr.next()
                P.dma("sync", [(x.t[:, 0:nb, :], xin[c0:c0 + W, :].rearrange("(b p) f -> p b f", p=128))],
                      [], x.b, x.ds)
                s = sr.next()
                for c in range(8):
                    tp = tpr.next()
                    for bi in range(nb):
                        P.op("tensor", TR(tp.t[:, bi * 128:(bi + 1) * 128], x.t[:, bi, c * 128:(c + 1) * 128],
                                          self.identf.t[:]), x.b + self.identf.b, tp.b, mark=(bi == nb - 1))
                    P.op("vector" if c % 2 else "scalar",
                         CP(s.t[:, c, 0:W], tp.t[:, 0:W]) if c % 2 else ACT(s.t[:, c, 0:W], tp.t[:, 0:W], AF.Copy),
                         tp.b, [s.b[c]])
                o = orr.next()
                self.ln_fm(s, o, W, gb.t[:, 0:8], gb.t[:, 8:16], R)
                P.dma("gpsimd", [(hT.t[:, :, c0:c0 + W].rearrange("c p t -> p c t"), o.t[:, :, 0:W])],
                      o.b, hT.b, o.ss, nowaw=True)
            P.end_phase()


    def phase_A1(self, l):
        P = self.P
        E = self.ext
        hT, qT, kT, kpeT, vS = (self.scr[n] for n in ("hT", "qT", "kT", "kpeT", "v"))
        with ExitStack() as st:
            P.begin_phase(st)
            w_in = E["w_in"]
            Wql = P.sbuf([128, 8, QL], BF16, dma=True)
            Wkvl = P.sbuf([128, 8, KVL], BF16, dma=True)
            Wkpe = P.sbuf([128, 8, 64], BF16, dma=True)
            Wkpes = P.sbuf([128, 8, 64], BF16, dma=True)
            Wqn = P.sbuf([128, 6, 1024], BF16, dma=True)
            Wqp = P.sbuf([128, 6, 512], BF16, dma=True)
            Wqps = P.sbuf([128, 6, 512], BF16, dma=True)
            Wkn = P.sbuf([128, 2, 1024], BF16, dma=True)
            Wv = P.sbuf([128, 2, 1024], BF16, dma=True)
            self.load_w(Wql, w_in[l, :, 0:768], 8)
            self.load_w(Wkvl, w_in[l, :, 768:1024], 8, per=8)
            self.load_w(Wkpe, w_in[l, :, 1024:1088], 8, per=8)
            self.load_w(Wkpes, E["w_kpes"][l, :, :], 8, per=8)
            self.load_w(Wqn, E["wqb_n"][l, :, :], 6, per=3)
            self.load_w(Wqp, E["wqb_p"][l, :, :], 6, per=6)
            self.load_w(Wqps, E["wqb_ps"][l, :, :], 6, per=6)
            self.load_w(Wkn, E["wkvb_kn"][l, :, :], 2)
            self.load_w(Wv, E["wkvb_v"][l, :, :], 2)
            ng = P.sbuf([128, 8], F32, dma=True)
            P.dma("sync", [(ng.t[:, 0:6], E["qng"][l, :, :]), (ng.t[:, 6:8], E["kvng"][l, :, :])], [], ng.b, ng.ds)
            hr = Ring([P.sbuf([128, 8, 512], F32, dma=True) for _ in range(2)])
            hbr = Ring([P.sbuf([128, 8, 512], BF16, n=8) for _ in range(2)])
            csr = Ring([P.sbuf([128, 2, 512], F32, dma=True) for _ in range(2)])
            qlat = P.sbuf([128, 6, 512], F32, n=6)
            qn = P.sbuf([128, 6, 512], BF16, n=6)
            kvlat = P.sbuf([128, 2, 512], F32, n=2)
            kvn = P.sbuf([128, 2, 512], BF16, n=2)
            sqr = Ring([P.sbuf([128, 512], BF16) for _ in range(2)])
            rstd = P.sbuf([128, 512], F32)
            t1r = Ring([P.sbuf([128, 512], F32) for _ in range(2)])
            t2r = Ring([P.sbuf([128, 512], F32) for _ in range(2)])
            stg = Ring([P.sbuf([128, 512], BF16, dma=True) for _ in range(4)])
            vst = Ring([P.sbuf([128, 1024], BF16, n=2, dma=True) for _ in range(2)])
            accr = Ring([P.psum([128, 512], F32) for _ in range(4)])
            ss_ps = P.psum([128, 512], F32)
            ones = self.ones_b
            evi = [0]

            def evac(out_ap, in_ap, reads, writes):
                evi[0] += 1
                if evi[0] % 2:
                    P.op("scalar", ACT(out_ap, in_ap, AF.Copy), reads, writes)
                else:
                    P.op("vector", CP(out_ap, in_ap), reads, writes)

            def rms(acc_list_fn, nch, lat, dst, W, gcol0, nfeat):
                for c in range(nch):
                    acc = acc_list_fn(c)
                    sq = sqr.next()
                    P.op("scalar", ACT(lat.t[:, c, 0:W], acc.t[:, 0:W], AF.Copy), acc.b, [lat.b[c]])
                    P.op("scalar", ACT(sq.t[:, 0:W], acc.t[:, 0:W], AF.Square), acc.b, sq.b)
                    P.op("tensor", MM(ss_ps.t[:, 0:W], ones.t[:], sq.t[:, 0:W], c == 0, c == nch - 1),
                         sq.b + ones.b, ss_ps.b, mark=(c == nch - 1))
                P.op("vector", TS(rstd.t[:, 0:W], ss_ps.t[:, 0:W], 1.0 / nfeat, ALU.mult, RMS_EPS, ALU.add),
                     ss_ps.b, rstd.b)
                P.op("scalar", ACT(rstd.t[:, 0:W], rstd.t[:, 0:W], AF.Ln), rstd.b, rstd.b)
                P.op("scalar", ACT(rstd.t[:, 0:W], rstd.t[:, 0:W], AF.Exp, scale=-0.5), rstd.b, rstd.b)
                for c in range(nch):
                    P.op("vector", STT(dst.t[:, c, 0:W], lat.t[:, c, 0:W], ng.t[:, gcol0 + c:gcol0 + c + 1],
                                       rstd.t[:, 0:W], ALU.mult, ALU.mult), [lat.b[c]] + rstd.b + ng.b, [dst.b[c]])

            def proj(acc, W_tb, ncols0, ncols, K, rhs_tb, W, M=128):
                for k in range(K):
                    P.op("tensor", MM(acc.t[0:M, 0:W], W_tb.t[:, k, ncols0:ncols0 + ncols], rhs_tb.t[:, k, 0:W],
                                      k == 0, k == K - 1), W_tb.b + [rhs_tb.b[k]], acc.b, mark=(k == K - 1))

            def rope(acc1, acc2, cs, W, M, outs):
                t1 = t1r.next()
                t2 = t2r.next()
                P.op("vector", TT(t1.t[0:M, 0:W], acc1.t[0:M, 0:W], cs.t[0:M, 0, 0:W], ALU.mult), acc1.b + cs.b, t1.b)
                P.op("vector", TT(t2.t[0:M, 0:W], acc2.t[0:M, 0:W], cs.t[0:M, 1, 0:W], ALU.mult), acc2.b + cs.b, t2.b)
                sg = stg.next()
                P.op("vector", TT(sg.t[0:M, 0:W], t1.t[0:M, 0:W], t2.t[0:M, 0:W], ALU.add), t1.b + t2.b, sg.b)
                for (dst_tb, dst_ap, p0, p1) in outs:
                    P.dma("gpsimd", [(dst_ap, sg.t[p0:p1, 0:W])], sg.b, dst_tb.b, sg.ss, nowaw=True)

            for (c0, W) in tiles_of(512):
                nb = W // 128
                h = hr.next()
                P.dma("sync", [(h.t[:, :, 0:W], hT.t[:, :, c0:c0 + W].rearrange("c p t -> p c t"))], hT.b, h.b, h.ds)
                cs = csr.next()
                P.dma("sync", [(cs.t[:, 0, 0:W], E["cosT2"][:, c0:c0 + W]), (cs.t[:, 1, 0:W], E["sinT2"][:, c0:c0 + W])],
                      [], cs.b, cs.ds)
                hb = hbr.next()
                for c in range(8):
                    evac(hb.t[:, c, 0:W], h.t[:, c, 0:W], h.b, [hb.b[c]])

                def ql_acc(c):
                    acc = accr.next()
                    proj(acc, Wql, c * 128, 128, 8, hb, W)
                    return acc
                rms(ql_acc, 6, qlat, qn, W, 0, QL)
                for hd in range(NH):
                    acc = accr.next()
                    proj(acc, Wqn, hd * 128, 128, 6, qn, W)
                    sg = stg.next()
                    evac(sg.t[:, 0:W], acc.t[:, 0:W], acc.b, sg.b)
                    P.dma("gpsimd", [(qT.t[hd, 0:128, c0:c0 + W], sg.t[:, 0:W])], sg.b, qT.b, sg.ss, nowaw=True)
                for pr in range(4):
                    a1 = accr.next()
                    proj(a1, Wqp, pr * 128, 128, 6, qn, W)
                    a2 = accr.next()
                    proj(a2, Wqps, pr * 128, 128, 6, qn, W)
                    rope(a1, a2, cs, W, 128, [(qT, qT.t[2 * pr, 128:192, c0:c0 + W], 0, 64),
                                              (qT, qT.t[2 * pr + 1, 128:192, c0:c0 + W], 64, 128)])

                def kv_acc(c):
                    acc = accr.next()
                    proj(acc, Wkvl, c * 128, 128, 8, hb, W)
                    return acc
                rms(kv_acc, 2, kvlat, kvn, W, 6, KVL)
                for hd in range(NH):
                    acc = accr.next()
                    proj(acc, Wkn, hd * 128, 128, 2, kvn, W)
                    sg = stg.next()
                    evac(sg.t[:, 0:W], acc.t[:, 0:W], acc.b, sg.b)
                    P.dma("gpsimd", [(kT.t[hd, :, c0:c0 + W], sg.t[:, 0:W])], sg.b, kT.b, sg.ss, nowaw=True)
                for bi in range(nb):
                    vs = vst.next()
                    for half in range(2):
                        acc = accr.next()
                        for k in range(2):
                            P.op("tensor", MM(acc.t[:, :], kvn.t[:, k, bi * 128:(bi + 1) * 128],
                                              Wv.t[:, k, half * 512:(half + 1) * 512], k == 0, k == 1),
                                 Wv.b + [kvn.b[k]], acc.b, mark=(k == 1))
                        evac(vs.t[:, half * 512:(half + 1) * 512], acc.t[:, :], acc.b, [vs.b[half]])
                    blk = c0 // 128 + bi
                    P.dma("gpsimd", [(vS.t[:, :, blk, :].rearrange("h p d -> p h d"),
                                      vs.t[:, :].rearrange("p (h d) -> p h d", h=NH))], vs.b, vS.b, vs.ss, nowaw=True)
                a1 = accr.next()
                proj(a1, Wkpe, 0, 64, 8, hb, W, M=64)
                a2 = accr.next()
                proj(a2, Wkpes, 0, 64, 8, hb, W, M=64)
                rope(a1, a2, cs, W, 64, [(kpeT, kpeT.t[:, c0:c0 + W], 0, 64)])
            P.end_phase()


    def phase_B(self, l):
        P = self.P
        qT, kT, kpeT, vS, oT = (self.scr[n] for n in ("qT", "kT", "kpeT", "v", "oT"))
        ones = self.ones_b
        with ExitStack() as st:
            P.begin_phase(st)
            kpe = P.sbuf([64, LP], BF16, dma=True)
            P.dma("sync", [(kpe.t[:, :], kpeT.t[:, :])], kpeT.b, kpe.b, kpe.ds)
            knr = Ring([P.sbuf([128, LP], BF16, dma=True) for _ in range(2)])
            vr = Ring([P.sbuf([128, NBLK, 128], BF16, dma=True) for _ in range(2)])
            qr = Ring([P.sbuf([128, 2, 512], BF16, dma=True) for _ in range(3)])
            ptr = Ring([P.sbuf([128, 512], BF16) for _ in range(6)])
            rsr = Ring([P.sbuf([128, 512], F32) for _ in range(2)])
            osr = Ring([P.sbuf([128, 512], BF16, dma=True) for _ in range(2)])
            spr = Ring([P.psum([128, 512], F32) for _ in range(4)])
            opr = Ring([P.psum([128, 512], F32) for _ in range(2)])
            smr = Ring([P.psum([128, 512], F32) for _ in range(2)])
            tiles = tiles_of(512)
            for hd in range(NH):
                kn = knr.next()
                P.dma("sync", [(kn.t[:, :], kT.t[hd, :, :])], kT.b, kn.b, kn.ds)
                v = vr.next()
                P.dma("sync", [(v.t[:, :, :], vS.t[hd, :, :, :])], vS.b, v.b, v.ds)
                for (c0, W) in tiles:
                    q = qr.next()
                    P.dma("sync", [(q.t[:, 0, 0:W], qT.t[hd, 0:128, c0:c0 + W]),
                                   (q.t[0:64, 1, 0:W], qT.t[hd, 128:192, c0:c0 + W])], qT.b, q.b, q.ds)
                    o_ps = opr.next()
                    sm_ps = smr.next()
                    nkb = (c0 + W) // 128
                    units = []
                    for kb in range(nkb):
                        qo = kb * 128 - c0 if kb * 128 >= c0 else 0
                        units.append((kb, qo))

                    def qk(u):
                        kb, qo = u
                        sp = spr.next()
                        P.op("tensor", MM(sp.t[:, qo:W], kn.t[:, kb * 128:(kb + 1) * 128], q.t[:, 0, qo:W], True, False),
                             kn.b + q.b, sp.b, mark=False)
                        P.op("tensor", MM(sp.t[:, qo:W], kpe.t[0:64, kb * 128:(kb + 1) * 128], q.t[0:64, 1, qo:W],
                                          False, True), kpe.b + q.b, sp.b, mark=True)
                        return sp

                    def soft(u, sp):
                        kb, qo = u
                        pt = ptr.next()
                        P.op("scalar", ACT(pt.t[:, qo:W], sp.t[:, qo:W], AF.Exp, scale=SCALE), sp.b, pt.b)
                        if kb == 0 and c0 == 0:
                            P.op("vector", TT(pt.t[:, 0:128], pt.t[:, 0:128], self.mask0_b.t[:], ALU.mult),
                                 pt.b + self.mask0_b.b, pt.b)
                        elif kb == 0:
                            P.op("vector", TS(pt.t[:, 0:W], pt.t[:, 0:W], self.kmask0.t[:, 0:1], ALU.mult),
                                 pt.b + self.kmask0.b, pt.b)
                        elif kb * 128 >= c0:
                            P.op("vector", TT(pt.t[:, qo:qo + 128], pt.t[:, qo:qo + 128], self.tri_b.t[:], ALU.mult),
                                 pt.b + self.tri_b.b, pt.b)
                        return pt

                    def pv(u, pt, first, last):
                        kb, qo = u
                        P.op("tensor", MM(o_ps.t[:, qo:W], v.t[:, kb, :], pt.t[:, qo:W], first, last),
                             v.b + pt.b, o_ps.b, mark=False)
                        P.op("tensor", MM(sm_ps.t[:, qo:W], ones.t[:], pt.t[:, qo:W], first, last),
                             ones.b + pt.b, sm_ps.b, mark=True)

                    LA = 2
                    sps = [qk(units[i]) for i in range(min(LA, len(units)))]
                    for i, u in enumerate(units):
                        if i + LA < len(units):
                            sps.append(qk(units[i + LA]))
                        pt = soft(u, sps[i])
                        pv(u, pt, i == 0, i == len(units) - 1)
                    rs = rsr.next()
                    P.op("vector", lambda e, o=rs.t[:, 0:W], i=sm_ps.t[:, 0:W]: e.reciprocal(out=o, in_=i), sm_ps.b, rs.b)
                    og = osr.next()
                    P.op("vector", TT(og.t[:, 0:W], o_ps.t[:, 0:W], rs.t[:, 0:W], ALU.mult), o_ps.b + rs.b, og.b)
                    P.dma("gpsimd", [(oT.t[hd, :, c0:c0 + W], og.t[:, 0:W])], og.b, oT.b, og.ss, nowaw=True)
            P.end_phase()


    def phase_A2(self, l):
        P = self.P
        E = self.ext
        hT, xsS, BtokS, BTS, CTS, dtS, zsS = (self.scr[n] for n in ("hT", "xs", "Btok", "BT", "CT", "dt", "zs"))
        with ExitStack() as st:
            P.begin_phase(st)
            w_in = E["w_in"]
            Wz = P.sbuf([128, 8, 2048], BF16, dma=True)
            Wx = P.sbuf([128, 8, 3072], BF16, dma=True)
            Wdt = P.sbuf([128, 8, 32], BF16, dma=True)
            self.load_w(Wz, w_in[l, :, 1088:3136], 8, per=2)
            self.load_w(Wx, w_in[l, :, 3136:6208], 8, per=2)
            self.load_w(Wdt, w_in[l, :, 6208:6240], 8, per=8)
            cw = P.sbuf([128, 24, 4], F32, dma=True)
            cb = P.sbuf([128, 24], F32, dma=True)
            dtb = P.sbuf([128, 32], F32, dma=True)
            P.dma("sync", [(cw.t[:], E["ssd_cw"][l, :, :, :])], [], cw.b, cw.ds)
            P.dma("sync", [(cb.t[:], E["ssd_cb"][l, :, :])], [], cb.b, cb.ds)
            P.dma("sync", [(dtb.t[:], E["dtb_bc"][l, :, :])], [], dtb.b, dtb.ds)
            h = P.sbuf([128, 8, 512], F32, dma=True)
            hbr = Ring([P.sbuf([128, 8, 512], BF16, n=8) for _ in range(2)])
            xc = P.sbuf([128, 24, 512], BF16, n=24, dma=True)
            xbr = Ring([P.sbuf([128, 515], F32) for _ in range(2)])
            halo = P.sbuf([128, 24, 3], F32, n=24)
            tcr = Ring([P.sbuf([128, 512], F32) for _ in range(2)])
            zst = Ring([P.sbuf([128, 2048], BF16, n=4, dma=True) for _ in range(2)])
            dtr = Ring([P.sbuf([128, 32], F32, dma=True) for _ in range(2)])
            tst = Ring([P.sbuf([128, 2560], BF16, n=5, dma=True) for _ in range(2)])
            accr = Ring([P.psum([128, 512], F32) for _ in range(3)])
            tpr = Ring([P.psum([128, 512], BF16) for _ in range(2)])
            P.op("vector", MEMSET(halo.t[:], 0.0), [], halo.b)
            evi = [0]

            def evac(out_ap, in_ap, reads, writes):
                evi[0] += 1
                if evi[0] % 2:
                    P.op("scalar", ACT(out_ap, in_ap, AF.Copy), reads, writes)
                else:
                    P.op("vector", CP(out_ap, in_ap), reads, writes)

            for (c0, W) in tiles_of(512):
                nb = W // 128
                P.dma("sync", [(h.t[:, :, 0:W], hT.t[:, :, c0:c0 + W].rearrange("c p t -> p c t"))], hT.b, h.b, h.ds)
                hb = hbr.next()
                for c in range(8):
                    evac(hb.t[:, c, 0:W], h.t[:, c, 0:W], h.b, [hb.b[c]])
                for bi in range(nb):
                    zt = zst.next()
                    for cg in range(4):
                        acc = accr.next()
                        for k in range(8):
                            P.op("tensor", MM(acc.t[:, :], hb.t[:, k, bi * 128:(bi + 1) * 128],
                                              Wz.t[:, k, cg * 512:(cg + 1) * 512], k == 0, k == 7),
                                 Wz.b + [hb.b[k]], acc.b, mark=(k == 7))
                        P.op("scalar", ACT(zt.t[:, cg * 512:(cg + 1) * 512], acc.t[:, :], AF.Silu), acc.b, [zt.b[cg]])
                    r0 = c0 + bi * 128
                    P.dma("gpsimd", [(zsS.t[r0:r0 + 128, :], zt.t[:, :])], zt.b, zsS.b, zt.ss, nowaw=True)
                    acc = accr.next()
                    for k in range(8):
                        P.op("tensor", MM(acc.t[:, 0:32], hb.t[:, k, bi * 128:(bi + 1) * 128], Wdt.t[:, k, 0:32],
                                          k == 0, k == 7), Wdt.b + [hb.b[k]], acc.b, mark=(k == 7))
                    dtt = dtr.next()
                    P.op("vector", TT(dtt.t[:, :], acc.t[:, 0:32], dtb.t[:, :], ALU.add), acc.b + dtb.b, dtt.b)
                    P.op("scalar", ACT(dtt.t[:, :], dtt.t[:, :], AF.Exp), dtt.b, dtt.b)
                    P.op("scalar", ACT(dtt.t[:, :], dtt.t[:, :], AF.Ln, bias=1.0), dtt.b, dtt.b)
                    P.dma("gpsimd", [(dtS.t[r0:r0 + 128, :], dtt.t[:, :])], dtt.b, dtS.b, dtt.ss, nowaw=True)
                for c in range(24):
                    acc = accr.next()
                    for k in range(8):
                        P.op("tensor", MM(acc.t[:, 0:W], Wx.t[:, k, c * 128:(c + 1) * 128], hb.t[:, k, 0:W],
                                          k == 0, k == 7), Wx.b + [hb.b[k]], acc.b, mark=(k == 7))
                    xb = xbr.next()
                    P.op("scalar", ACT(xb.t[:, 3:3 + W], acc.t[:, 0:W], AF.Copy), acc.b, xb.b)
                    P.op("vector", CP(xb.t[:, 0:3], halo.t[:, c, :]), [halo.b[c]], xb.b)
                    if c0 == 0:
                        P.op("vector", MEMSET(xb.t[:, 0:3 + PAD], 0.0), [], xb.b)
                    P.op("vector", CP(halo.t[:, c, :], xb.t[:, W:W + 3]), xb.b, [halo.b[c]])
                    tc_ = tcr.next()
                    P.op("scalar", ACT(tc_.t[:, 0:W], xb.t[:, 0:W], AF.Identity, bias=cb.t[:, c:c + 1], scale=cw.t[:, c, 0:1]),
                         xb.b + cw.b + cb.b, tc_.b)
                    for tap in range(1, 4):
                        P.op("vector", STT(tc_.t[:, 0:W], xb.t[:, tap:tap + W], cw.t[:, c, tap:tap + 1], tc_.t[:, 0:W],
                                           ALU.mult, ALU.add), xb.b + cw.b + tc_.b, tc_.b)
                    P.op("scalar", ACT(xc.t[:, c, 0:W], tc_.t[:, 0:W], AF.Silu), tc_.b, [xc.b[c]])
                    if c0 == 0:
                        P.op("vector", MEMSET(xc.t[:, c, 0:PAD], 0.0), [], [xc.b[c]])
                prs = []
                for g in range(4):
                    prs.append((BTS.t[g, :, c0:c0 + W], xc.t[:, 16 + g, 0:W]))
                    prs.append((CTS.t[g, :, c0:c0 + W], xc.t[:, 20 + g, 0:W]))
                P.dma("gpsimd", prs, xc.b[16:24], BTS.b + CTS.b, xc.ss, nowaw=True)
                for bi in range(nb):
                    ts_ = tst.next()
                    for q4 in range(5):
                        tp = tpr.next()
                        for j in range(4):
                            c = q4 * 4 + j
                            P.op("tensor", TR(tp.t[:, j * 128:(j + 1) * 128], xc.t[:, c, bi * 128:(bi + 1) * 128],
                                              self.identb.t[:]), [xc.b[c]] + self.identb.b, tp.b, mark=(j == 3))
                        evac(ts_.t[:, q4 * 512:(q4 + 1) * 512], tp.t[:, :], tp.b, [ts_.b[q4]])
                    r0 = c0 + bi * 128
                    P.dma("gpsimd", [(xsS.t[r0:r0 + 128, :], ts_.t[:, 0:2048]), (BtokS.t[r0:r0 + 128, :], ts_.t[:, 2048:2560])],
                          ts_.b, xsS.b + BtokS.b, ts_.ss, nowaw=True)
            P.end_phase()

    def phase_S(self, l):
        P = self.P
        E = self.ext
        xsS, BtokS, BTS, CTS, dtS, zsS, ynTS = (self.scr[n] for n in ("xs", "Btok", "BT", "CT", "dt", "zs", "ynT"))
        trib = self.tri_b
        ones = self.ones_b
        identb = self.identb
        with ExitStack() as st:
            P.begin_phase(st)
            abc = P.sbuf([128, 32], F32, dma=True)
            dsk = P.sbuf([128, 2048], F32, dma=True)
            ngb = P.sbuf([128, 2048], F32, dma=True)
            negm_f = P.sbuf([128, 128], F32, dma=True)
            P.dma("sync", [(abc.t[:], E["alog_bc"][l, :, :])], [], abc.b, abc.ds)
            P.dma("sync", [(dsk.t[:], E["dskip_bc"][l, :, :])], [], dsk.b, dsk.ds)
            P.dma("sync", [(ngb.t[:], E["ssdng_bc"][l, :, :])], [], ngb.b, ngb.ds)
            P.dma("sync", [(negm_f.t[:], E["c_negmask"][:, :])], [], negm_f.b, negm_f.ds)
            P.op("scalar", ACT(abc.t[:], abc.t[:], AF.Exp), abc.b, abc.b)
            P.op("vector", TS(abc.t[:], abc.t[:], -1.0, ALU.mult), abc.b, abc.b)
            negm = P.sbuf([128, 128], BF16)
            trin = P.sbuf([128, 128], BF16)
            P.op("vector", CP(negm.t[:], negm_f.t[:]), negm_f.b, negm.b)
            P.op("vector", TS(trin.t[:], self.tri_f.t[:], -1.0, ALU.mult), self.tri_f.b, trin.b)
            NB_ = 3
            xsr = Ring([P.sbuf([128, 2048], BF16, dma=True) for _ in range(NB_)])
            btr = Ring([P.sbuf([128, 512], BF16, dma=True) for _ in range(NB_)])
            bTr = Ring([P.sbuf([128, 4, 128], BF16, dma=True) for _ in range(NB_)])
            cTr = Ring([P.sbuf([128, 4, 128], BF16, dma=True) for _ in range(NB_)])
            dtr = Ring([P.sbuf([128, 32], F32, dma=True) for _ in range(NB_)])
            zsr = Ring([P.sbuf([128, 2048], BF16, dma=True) for _ in range(NB_)])
            prev = P.sbuf([128, 2048], F32, n=4)
            prevb = P.sbuf([128, 2048], BF16, n=4)
            P.op("vector", MEMSET(prev.t[:], 0.0), [], prev.b)
            P.op("vector", MEMSET(prevb.t[:], 0.0), [], prevb.b)
            at = P.sbuf([128, 32], F32)
            ar = P.sbuf([128, 32], F32)
            a3 = [P.sbuf([128, 32], BF16) for _ in range(3)]
            def mk_ctx():
                return dict(abig=[P.sbuf([128, 32, 128], BF16) for _ in range(2)],
                            abm=[P.sbuf([128, 32, 128], BF16) for _ in range(2)],
                            eacs=P.sbuf([128, 32], F32), cdec=P.sbuf([128, 32], F32),
                            xdt=P.sbuf([128, 2048], BF16), xdts=P.sbuf([128, 2048], BF16), xsd=P.sbuf([128, 2048], BF16))
            ctxr = Ring([mk_ctx() for _ in range(2)])
            acs = P.sbuf([128, 32], F32)
            dst = P.sbuf([128, 32], F32)
            y = P.sbuf([128, 2048], F32, n=4)
            yn = P.sbuf([128, 2048], BF16, n=4)
            junkr = Ring([P.sbuf([128, 512], F32) for _ in range(2)])
            ss = P.sbuf([128, 4], F32)
            rstd = P.sbuf([128, 4], F32)
            t1r = Ring([P.sbuf([128, 512], F32) for _ in range(2)])
            cbsr = Ring([P.sbuf([128, 128], F32) for _ in range(2)])
            er = Ring([P.sbuf([128, 4, 128], F32) for _ in range(2)])
            mtr = Ring([P.sbuf([128, 4, 128], BF16) for _ in range(2)])
            ynst = Ring([P.sbuf([128, 16, 128], BF16, n=4, dma=True) for _ in range(2)])
            misc = P.psum([128, 512], F32)
            acs_ps = TB(misc.t[:, 0:32]); acs_ps.b = misc.b
            last_ps = TB(misc.t[:, 32:64]); last_ps.b = misc.b
            cb_ps = P.psum([128, 128], F32)
            dpr = Ring([P.psum([128, 512], F32) for i in range(2)])
            yd = P.psum([128, 512], F32)
            yo_ps = P.psum([128, 512], F32)
            st_ps = P.psum([128, 512], F32)
            tpr = Ring([P.psum([128, 512], BF16) for _ in range(1)])
            bc3 = lambda ap2, n: ap2.unsqueeze(2).to_broadcast([128, n, 64])
            v3 = lambda ap2: ap2.rearrange("p (h d) -> p h d", d=64)
            nch = getattr(self, 'S_NCH', NBLK)
            loaded = {}
            ctxs = {}

            def load(c):
                r0 = c * 128
                t = dict(xs=xsr.next(), bt=btr.next(), bT=bTr.next(), cT=cTr.next(), dt=dtr.next(), zs=zsr.next())
                P.dma("sync", [(t["xs"].t[:], xsS.t[r0:r0 + 128, :])], xsS.b, t["xs"].b, t["xs"].ds)
                P.dma("sync", [(t["bt"].t[:], BtokS.t[r0:r0 + 128, :])], BtokS.b, t["bt"].b, t["bt"].ds)
                P.dma("sync", [(t["bT"].t[:], BTS.t[:, :, r0:r0 + 128].rearrange("g p t -> p g t"))], BTS.b, t["bT"].b, t["bT"].ds)
                P.dma("sync", [(t["cT"].t[:], CTS.t[:, :, r0:r0 + 128].rearrange("g p t -> p g t"))], CTS.b, t["cT"].b, t["cT"].ds)
                P.dma("sync", [(t["dt"].t[:], dtS.t[r0:r0 + 128, :])], dtS.b, t["dt"].b, t["dt"].ds)
                P.dma("sync", [(t["zs"].t[:], zsS.t[r0:r0 + 128, :])], zsS.b, t["zs"].b, t["zs"].ds)
                loaded[c] = t

            def front(c):
                t = loaded[c]
                xs, dt = t["xs"], t["dt"]
                X = ctxr.next()
                ctxs[c] = X
                abig, abm, eacs, cdec, xdt, xdts, xsd = (X[k_] for k_ in ("abig", "abm", "eacs", "cdec", "xdt", "xdts", "xsd"))
                P.op("vector", TT(at.t[:], dt.t[:], abc.t[:], ALU.mult), dt.b + abc.b, at.b)
                P.op("vector", CP(a3[0].t[:], at.t[:]), at.b, a3[0].b)
                P.op("vector", TT(ar.t[:], at.t[:], a3[0].t[:], ALU.subtract), at.b + a3[0].b, ar.b)
                P.op("vector", CP(a3[1].t[:], ar.t[:]), ar.b, a3[1].b)
                P.op("vector", TT(ar.t[:], ar.t[:], a3[1].t[:], ALU.subtract), a3[1].b, ar.b)
                P.op("vector", CP(a3[2].t[:], ar.t[:]), ar.b, a3[2].b)
                for i3 in range(3):
                    P.op("tensor", MM(acs_ps.t, trib.t[:], a3[i3].t[:], i3 == 0, i3 == 2), trib.b + a3[i3].b, misc.b,
                         mark=(i3 == 2))
                for i3 in range(3):
                    P.op("tensor", MM(last_ps.t, ones.t[:], a3[i3].t[:], i3 == 0, i3 == 2), ones.b + a3[i3].b, misc.b,
                         mark=(i3 == 2))
                P.op("vector", CP(acs.t[:], acs_ps.t), [], acs.b + misc.b)
                for i3 in range(2):
                    P.op("scalar", ACT(abig[i3].t[:], a3[i3].t[:, :].unsqueeze(2).to_broadcast([128, 32, 128]), AF.Copy),
                         a3[i3].b, abig[i3].b)
                    P.op("gpsimd", TT(abm[i3].t[:], a3[i3].t[:, :].unsqueeze(2).to_broadcast([128, 32, 128]),
                                      trin.t[:, :].unsqueeze(1).to_broadcast([128, 32, 128]), ALU.mult),
                         a3[i3].b + trin.b, abm[i3].b)
                P.op("scalar", ACT(eacs.t[:], acs.t[:], AF.Exp), acs.b, eacs.b)
                P.op("scalar", ACT(cdec.t[:], last_ps.t, AF.Exp), [], cdec.b + misc.b)
                P.op("vector", TT(dst.t[:], last_ps.t, acs.t[:], ALU.subtract), acs.b, dst.b + misc.b)
                P.op("scalar", ACT(dst.t[:], dst.t[:], AF.Exp), dst.b, dst.b)
                P.op("vector", TT(v3(xdt.t[:]), v3(xs.t[:]), bc3(dt.t[:, :], 32), ALU.mult), xs.b + dt.b, xdt.b)
                P.op("gpsimd", TT(v3(xdts.t[:]), v3(xdt.t[:]), bc3(dst.t[:, :], 32), ALU.mult), xdt.b + dst.b, xdts.b)
                P.op("gpsimd", TT(xsd.t[:], xs.t[:], dsk.t[:], ALU.mult), xs.b + dsk.b, xsd.b)

            load(0)
            front(0)
            for c in range(nch):
                r0 = c * 128
                if c + 1 < nch:
                    load(c + 1)
                    front(c + 1)
                t = loaded.pop(c)
                X = ctxs.pop(c)
                abig, abm, eacs, cdec, xdt, xdts, xsd = (X[k_] for k_ in ("abig", "abm", "eacs", "cdec", "xdt", "xdts", "xsd"))
                xs, bt, bT, cT, dt, zs = t["xs"], t["bt"], t["bT"], t["cT"], t["dt"], t["zs"]
                for g in range(4):
                    gs = slice(g * 512, (g + 1) * 512)
                    P.op("tensor", MM(cb_ps.t[:, :], bT.t[:, g, :], cT.t[:, g, :]), bT.b + cT.b, cb_ps.b)
                    cbs = cbsr.next()
                    P.op("scalar", ACT(cbs.t[:], cb_ps.t[:, :], AF.Copy), cb_ps.b, cbs.b)
                    for hb4 in range(2):
                        dps = dpr.next()
                        for j in range(4):
                            hd = g * 8 + hb4 * 4 + j
                            sl = dps.t[:, j * 128:(j + 1) * 128]
                            P.op("tensor", MM(sl, abig[0].t[:, hd, :], trib.t[:], True, False), abig[0].b + trib.b, dps.b, mark=False)
                            P.op("tensor", MM(sl, abig[1].t[:, hd, :], trib.t[:], False, False), abig[1].b, dps.b, mark=False)
                            P.op("tensor", MM(sl, abm[0].t[:, hd, :], ones.t[:], False, False), abm[0].b + ones.b, dps.b, mark=False)
                            P.op("tensor", MM(sl, abm[1].t[:, hd, :], ones.t[:], False, False), abm[1].b, dps.b, mark=False)
                            P.op("tensor", MM(sl, identb.t[:], negm.t[:], False, True), identb.b + negm.b, dps.b, mark=(j == 3))
                        ee = er.next()
                        P.op("scalar", ACT(ee.t[:].rearrange("p a b -> p (a b)"), dps.t[:, :], AF.Exp), dps.b, ee.b)
                        mt = mtr.next()
                        P.op("vector", TT(mt.t[:], ee.t[:], cbs.t[:, :].unsqueeze(1).to_broadcast([128, 4, 128]), ALU.mult),
                             ee.b + cbs.b, mt.b)
                        for j in range(4):
                            hh = hb4 * 4 + j
                            hd = g * 8 + hh
                            P.op("tensor", MM(yd.t[:, hh * 64:(hh + 1) * 64], mt.t[:, j, :], xdt.t[:, hd * 64:(hd + 1) * 64], True, False),
                                 mt.b + xdt.b, yd.b, mark=False)
                            P.op("tensor", MM(yd.t[:, hh * 64:(hh + 1) * 64], identb.t[:], xsd.t[:, hd * 64:(hd + 1) * 64], False, True),
                                 identb.b + xsd.b, yd.b, mark=(hh == 7))
                    P.op("tensor", MM(yo_ps.t[:, :], cT.t[:, g, :], prevb.t[:, gs]), cT.b + [prevb.b[g]], yo_ps.b)
                    P.op("tensor", MM(st_ps.t[:, :], bt.t[:, g * 128:(g + 1) * 128], xdts.t[:, gs]), bt.b + xdts.b, st_ps.b)
                    t1 = t1r.next()
                    P.op("vector", TT(v3(t1.t[:]), v3(yo_ps.t[:, :]), bc3(eacs.t[:, g * 8:(g + 1) * 8], 8), ALU.mult),
                         yo_ps.b + eacs.b, t1.b)
                    P.op("vector", TT(y.t[:, gs], yd.t[:, :], t1.t[:], ALU.add), yd.b + t1.b, [y.b[g]])
                    P.op("vector", TT(y.t[:, gs], y.t[:, gs], zs.t[:, gs], ALU.mult), zs.b, [y.b[g]])
                    junk = junkr.next()
                    P.op("gpsimd", TT(junk.t[:], y.t[:, gs], y.t[:, gs], ALU.mult), [y.b[g]], junk.b)
                    P.op("vector", lambda e, o=junk.t[:], acc=ss.t[:, g:g + 1]: e.tensor_scalar(
                        out=o, in0=o, scalar1=1.0, scalar2=0.0, op0=ALU.mult, op1=ALU.add, accum_out=acc),
                        [], junk.b + ss.b)
                    P.op("gpsimd", TT(v3(prev.t[:, gs]), v3(prev.t[:, gs]), bc3(cdec.t[:, g * 8:(g + 1) * 8], 8), ALU.mult),
                         cdec.b, [prev.b[g]])
                    P.op("vector", TT(prev.t[:, gs], prev.t[:, gs], st_ps.t[:, :], ALU.add), st_ps.b, [prev.b[g]])
                    P.op("scalar", ACT(prevb.t[:, gs], prev.t[:, gs], AF.Copy), [prev.b[g]], [prevb.b[g]])
                P.op("vector", TS(rstd.t[:], ss.t[:], 1.0 / 512, ALU.mult, RMS_EPS, ALU.add), ss.b, rstd.b)
                P.op("scalar", ACT(rstd.t[:], rstd.t[:], AF.Ln), rstd.b, rstd.b)
                P.op("scalar", ACT(rstd.t[:], rstd.t[:], AF.Exp, scale=-0.5), rstd.b, rstd.b)
                yst = ynst.next()
                for g in range(4):
                    gs = slice(g * 512, (g + 1) * 512)
                    P.op("vector", STT(yn.t[:, gs], y.t[:, gs], rstd.t[:, g:g + 1], ngb.t[:, gs], ALU.mult, ALU.mult),
                         [y.b[g]] + rstd.b + ngb.b, [yn.b[g]])
                    tp = tpr.next()
                    for j in range(4):
                        cc = g * 4 + j
                        P.op("tensor", TR(tp.t[:, j * 128:(j + 1) * 128], yn.t[:, cc * 128:(cc + 1) * 128], identb.t[:]),
                             [yn.b[g]] + identb.b, tp.b, mark=(j == 3))
                    P.op("scalar", ACT(yst.t[:, g * 4:(g + 1) * 4, :], tp.t[:, :].rearrange("p (c t) -> p c t", c=4), AF.Copy),
                         tp.b, [yst.b[g]])
                P.dma("sync", [(ynTS.t[:, :, r0:r0 + 128].rearrange("c p t -> p c t"), yst.t[:, :, :])],
                      yst.b, ynTS.b, yst.ss, nowaw=True)
            P.end_phase()

    def phase_C1(self, l):
        P = self.P
        E = self.ext
        hT, oTS, ynTS, h1T = (self.scr[n] for n in ("hT", "oT", "ynT", "h1T"))
        WT = 256
        with ExitStack() as st:
            P.begin_phase(st)
            w_in = E["w_in"]
            Woa = P.sbuf([128, 8, 1024], BF16, dma=True)
            Wos = P.sbuf([128, 16, 1024], BF16, dma=True)
            Wout = P.sbuf([128, 8, 1024], BF16, dma=True)
            Wga = P.sbuf([128, 8, 1024], BF16, dma=True)
            Wgs = P.sbuf([128, 8, 1024], BF16, dma=True)
            self.load_w(Woa, E["w_o_attn"][l, :, :], 8)
            self.load_w(Wos, E["w_o_ssd"][l, :, :], 16)
            self.load_w(Wout, E["w_out"][l, :, :], 8)
            self.load_w(Wga, w_in[l, :, 6240:7264], 8)
            self.load_w(Wgs, w_in[l, :, 7264:8288], 8)
            gb = P.sbuf([128, 16], F32, dma=True)
            P.dma("sync", [(gb.t[:, 0:8], E["ln1_g"][l, :, :]), (gb.t[:, 8:16], E["ln1_b"][l, :, :])], [], gb.b, gb.ds)
            otr = Ring([P.sbuf([128, 8, WT], BF16, dma=True) for _ in range(2)])
            ynr = Ring([P.sbuf([128, 16, WT], BF16, dma=True) for _ in range(2)])
            hr = Ring([P.sbuf([128, 8, WT], F32, n=8, dma=True) for _ in range(2)])
            hb = P.sbuf([128, 8, WT], BF16, n=8)
            mixed = P.sbuf([128, 8, WT], BF16, n=8)
            orr = Ring([P.sbuf([128, 8, WT], F32, n=8, dma=True) for _ in range(1)])
            sgr = Ring([P.sbuf([128, WT], F32) for _ in range(4)])
            tr_ = Ring([P.sbuf([128, WT], F32) for _ in range(4)])
            accr = Ring([P.psum([128, 512], F32) for _ in range(4)])
            R = self.ln_resources(WT)
            evi = [0]

            def proj(acc, W_tb, oc, K, rhs_tb, W):
                for k in range(K):
                    P.op("tensor", MM(acc.t[:, 0:W], W_tb.t[:, k, oc * 128:(oc + 1) * 128], rhs_tb.t[:, k, 0:W],
                                      k == 0, k == K - 1), W_tb.b + [rhs_tb.b[k % len(rhs_tb.b)]], acc.b, mark=(k == K - 1))

            for (c0, W) in tiles_of(WT):
                ot = otr.next(); yt = ynr.next(); h = hr.next()
                P.dma("sync", [(ot.t[:, :, 0:W], oTS.t[:, :, c0:c0 + W].rearrange("h p t -> p h t"))], oTS.b, ot.b, ot.ds)
                P.dma("sync", [(yt.t[:, :, 0:W], ynTS.t[:, :, c0:c0 + W].rearrange("c p t -> p c t"))], ynTS.b, yt.b, yt.ds)
                P.dma("sync", [(h.t[:, :, 0:W], hT.t[:, :, c0:c0 + W].rearrange("c p t -> p c t"))], hT.b, h.b, h.ds)
                for c in range(8):
                    evi[0] += 1
                    if evi[0] % 2:
                        P.op("scalar", ACT(hb.t[:, c, 0:W], h.t[:, c, 0:W], AF.Copy), h.b, [hb.b[c]])
                    else:
                        P.op("vector", CP(hb.t[:, c, 0:W], h.t[:, c, 0:W]), h.b, [hb.b[c]])
                for oc in range(8):
                    ga = accr.next()
                    proj(ga, Wga, oc, 8, hb, W)
                    sga = sgr.next()
                    P.op("scalar", ACT(sga.t[:, 0:W], ga.t[:, 0:W], AF.Sigmoid), ga.b, sga.b)
                    gs_ = accr.next()
                    proj(gs_, Wgs, oc, 8, hb, W)
                    sgs = sgr.next()
                    P.op("scalar", ACT(sgs.t[:, 0:W], gs_.t[:, 0:W], AF.Sigmoid), gs_.b, sgs.b)
                    ya = accr.next()
                    proj(ya, Woa, oc, 8, ot, W)
                    t1 = tr_.next()
                    P.op("vector", TT(t1.t[:, 0:W], ya.t[:, 0:W], sga.t[:, 0:W], ALU.mult), ya.b + sga.b, t1.b)
                    ys_ = accr.next()
                    proj(ys_, Wos, oc, 16, yt, W)
                    t2 = tr_.next()
                    P.op("vector", TT(t2.t[:, 0:W], ys_.t[:, 0:W], sgs.t[:, 0:W], ALU.mult), ys_.b + sgs.b, t2.b)
                    P.op("vector", TT(mixed.t[:, oc, 0:W], t1.t[:, 0:W], t2.t[:, 0:W], ALU.add), t1.b + t2.b, [mixed.b[oc]])
                for oc in range(8):
                    r = accr.next()
                    proj(r, Wout, oc, 8, mixed, W)
                    P.op("vector", STT(h.t[:, oc, 0:W], h.t[:, oc, 0:W], ALPHA, r.t[:, 0:W], ALU.mult, ALU.add),
                         r.b + [hb.b[oc]], [h.b[oc]])
                o = orr.next()
                self.ln_fm(h, o, W, gb.t[:, 0:8], gb.t[:, 8:16], R)
                P.dma("gpsimd", [(h1T.t[:, :, c0:c0 + W].rearrange("c p t -> p c t"), o.t[:, :, 0:W])],
                      o.b, h1T.b, o.ss, nowaw=True)
            P.end_phase()

    def phase_C2(self, l):
        P = self.P
        E = self.ext
        hT, h1T = (self.scr[n] for n in ("hT", "h1T"))
        last = (l == self.n_layers - 1)
        WT = 256
        with ExitStack() as st:
            P.begin_phase(st)
            Wup = P.sbuf([128, 8, 2 * DFF], BF16, dma=True)
            Wdn = P.sbuf([128, 22, 1024], BF16, dma=True)
            self.load_w(Wup, E["w_up"][l, :, :], 8, per=1)
            self.load_w(Wdn, E["w_down"][l, :, :], 22, per=4)
            gb = P.sbuf([128, 16], F32, dma=True)
            P.dma("sync", [(gb.t[:, 0:8], E["ln2_g"][l, :, :]), (gb.t[:, 8:16], E["ln2_b"][l, :, :])], [], gb.b, gb.ds)
            cw = P.sbuf([128, 44, 3], F32, dma=True)
            cb = P.sbuf([128, 44], F32, dma=True)
            P.dma("sync", [(cw.t[:], E["ffn_cw"][l, :, :, :])], [], cw.b, cw.ds)
            P.dma("sync", [(cb.t[:], E["ffn_cb"][l, :, :])], [], cb.b, cb.ds)
            hr = Ring([P.sbuf([128, 8, WT], F32, n=8, dma=True) for _ in range(2)])
            hb = P.sbuf([128, 8, WT], BF16, n=8)
            a = P.sbuf([128, 22, WT], BF16, n=22)
            orr = Ring([P.sbuf([128, 8, WT], F32, n=8, dma=True) for _ in range(1)])
            xbr = Ring([P.sbuf([128, WT + 2], F32) for _ in range(4)])
            tcr = Ring([P.sbuf([128, WT], F32) for _ in range(4)])
            sgr = Ring([P.sbuf([128, WT], F32) for _ in range(2)])
            halo = P.sbuf([128, 44, 2], F32, n=44)
            ostr = Ring([P.sbuf([128, 1024], F32, n=2, dma=True) for _ in range(1)]) if last else None
            accr = Ring([P.psum([128, 512], F32) for _ in range(4)])
            tpr = Ring([P.psum([128, 512], F32) for _ in range(2)]) if last else None
            R = self.ln_resources(WT)
            P.op("vector", MEMSET(halo.t[:], 0.0), [], halo.b)
            evi = [0]

            def proj(acc, W_tb, col0, K, rhs_tb, W):
                for k in range(K):
                    P.op("tensor", MM(acc.t[:, 0:W], W_tb.t[:, k, col0:col0 + 128], rhs_tb.t[:, k, 0:W],
                                      k == 0, k == K - 1), W_tb.b + [rhs_tb.b[k]], acc.b, mark=(k == K - 1))

            def conv(acc, ci, W, first):
                xb = xbr.next()
                P.op("scalar", ACT(xb.t[:, 2:2 + W], acc.t[:, 0:W], AF.Copy), acc.b, xb.b)
                P.op("vector", CP(xb.t[:, 0:2], halo.t[:, ci, :]), [halo.b[ci]], xb.b)
                if first:
                    P.op("vector", MEMSET(xb.t[:, 0:2 + PAD], 0.0), [], xb.b)
                P.op("vector", CP(halo.t[:, ci, :], xb.t[:, W:W + 2]), xb.b, [halo.b[ci]])
                tc_ = tcr.next()
                P.op("scalar", ACT(tc_.t[:, 0:W], xb.t[:, 0:W], AF.Identity, bias=cb.t[:, ci:ci + 1], scale=cw.t[:, ci, 0:1]),
                     xb.b + cw.b + cb.b, tc_.b)
                for tap in (1, 2):
                    P.op("vector", STT(tc_.t[:, 0:W], xb.t[:, tap:tap + W], cw.t[:, ci, tap:tap + 1], tc_.t[:, 0:W],
                                       ALU.mult, ALU.add), xb.b + cw.b, tc_.b)
                return tc_

            for (c0, W) in tiles_of(WT):
                nb = W // 128
                h = hr.next()
                P.dma("sync", [(h.t[:, :, 0:W], h1T.t[:, :, c0:c0 + W].rearrange("c p t -> p c t"))], h1T.b, h.b, h.ds)
                for c in range(8):
                    evi[0] += 1
                    if evi[0] % 2:
                        P.op("scalar", ACT(hb.t[:, c, 0:W], h.t[:, c, 0:W], AF.Copy), h.b, [hb.b[c]])
                    else:
                        P.op("vector", CP(hb.t[:, c, 0:W], h.t[:, c, 0:W]), h.b, [hb.b[c]])
                for c in range(22):
                    ug = accr.next()
                    proj(ug, Wup, c * 128, 8, hb, W)
                    uv = accr.next()
                    proj(uv, Wup, DFF + c * 128, 8, hb, W)
                    tg = conv(ug, c, W, c0 == 0)
                    tv = conv(uv, 22 + c, W, c0 == 0)
                    sg = sgr.next()
                    P.op("scalar", ACT(sg.t[:, 0:W], tg.t[:, 0:W], AF.Silu), tg.b, sg.b)
                    P.op("vector", TT(a.t[:, c, 0:W], sg.t[:, 0:W], tv.t[:, 0:W], ALU.mult), sg.b + tv.b, [a.b[c]])
                for oc in range(8):
                    f = accr.next()
                    proj(f, Wdn, oc * 128, 22, a, W)
                    P.op("vector", STT(h.t[:, oc, 0:W], h.t[:, oc, 0:W], ALPHA, f.t[:, 0:W], ALU.mult, ALU.add),
                         f.b + [hb.b[oc]], [h.b[oc]])
                o = orr.next()
                self.ln_fm(h, o, W, gb.t[:, 0:8], gb.t[:, 8:16], R)
                if not last:
                    P.dma("gpsimd", [(hT.t[:, :, c0:c0 + W].rearrange("c p t -> p c t"), o.t[:, :, 0:W])],
                          o.b, hT.b, o.ss, nowaw=True)
                elif c0 > 0:
                    for bi in range(nb):
                        os_ = ostr.next()
                        for half in range(2):
                            tp = tpr.next()
                            for j in range(4):
                                cc = half * 4 + j
                                P.op("tensor", TR(tp.t[:, j * 128:(j + 1) * 128], o.t[:, cc, bi * 128:(bi + 1) * 128],
                                                  self.identf.t[:]), [o.b[cc]] + self.identf.b, tp.b, mark=(j == 3))
                            P.op("scalar", ACT(os_.t[:, half * 512:(half + 1) * 512], tp.t[:, :], AF.Copy), tp.b, [os_.b[half]])
                        r0 = c0 + bi * 128 - 128
                        P.dma("gpsimd", [(self.out[r0:r0 + 128, :], os_.t[:, :])], os_.b, [], os_.ss)
            P.end_phase()


INPUT_SHAPES = {
    "xin": [LP, D], "emb_g": [128, 8], "emb_b": [128, 8],
    "c_ident": [128, 128], "c_tri": [128, 128], "c_sellast": [128, 128], "c_mask0": [128, 128], "c_kmask0": [128, 1], "c_negmask": [128, 128],
    "cosT2": [128, LP], "sinT2": [128, LP],
    "w_in": [DEPTH, D, 8288], "w_kpes": [DEPTH, D, 64],
    "wqb_n": [DEPTH, QL, 1024], "wqb_p": [DEPTH, QL, 512], "wqb_ps": [DEPTH, QL, 512],
    "wkvb_kn": [DEPTH, KVL, 1024], "wkvb_v": [DEPTH, KVL, 1024],
    "qng": [DEPTH, 128, 6], "kvng": [DEPTH, 128, 2],
    "w_o_attn": [DEPTH, 1024, D], "w_o_ssd": [DEPTH, 2048, D], "w_out": [DEPTH, D, D],
    "w_up": [DEPTH, D, 2 * DFF], "w_down": [DEPTH, DFF, D],
    "ssd_cw": [DEPTH, 128, 24, 4], "ssd_cb": [DEPTH, 128, 24],
    "dtb_bc": [DEPTH, 128, 32], "alog_bc": [DEPTH, 128, 32], "dskip_bc": [DEPTH, 128, 2048], "ssdng_bc": [DEPTH, 128, 2048],
    "ln1_g": [DEPTH, 128, 8], "ln1_b": [DEPTH, 128, 8], "ln2_g": [DEPTH, 128, 8], "ln2_b": [DEPTH, 128, 8],
    "ffn_cw": [DEPTH, 128, 44, 3], "ffn_cb": [DEPTH, 128, 44],
}

SCRATCH = {
    "hT": ([8, 128, LP], F32), "h1T": ([8, 128, LP], F32),
    "qT": ([NH, 192, LP], BF16), "kT": ([NH, 128, LP], BF16), "kpeT": ([64, LP], BF16),
    "v": ([NH, 128, NBLK, 128], BF16), "oT": ([NH, 128, LP], BF16),
    "xs": ([LP, 2048], BF16), "Btok": ([LP, 512], BF16), "BT": ([4, 128, LP], BF16), "CT": ([4, 128, LP], BF16),
    "dt": ([LP, 32], F32), "zs": ([LP, 2048], BF16), "ynT": ([16, 128, LP], BF16),
    "dbgY": ([LP, 2048], F32), "dbgS": ([LP, 168], F32), "dbgP": ([NBLK, 128, 2048], F32),
}


def build(n_layers=2, dbg=None, stop=None):
    nc = bass.Bass("TRN2", target_bir_lowering=False)
    with ExitStack() as gst:
        P = Prog(nc, gst)
        k = K(nc, P, n_layers, dbg)
        for nm, shp in INPUT_SHAPES.items():
            k.din(nm, shp)
        for nm, (shp, dt) in SCRATCH.items():
            k.dscr(nm, shp, dt)
        k.out = nc.dram_tensor("out", [SEQ, D], F32, kind="ExternalOutput").ap()
        k.load_consts(gst)
        seq = [("E", k.phase_E, ())]
        for l in range(n_layers):
            for nm in ("A1", "A2", "B", "S", "C1", "C2"):
                fn = getattr(k, "phase_" + nm, None)
                if fn is not None:
                    seq.append((f"{nm}_{l}", fn, (l,)))
        import os
        skip = os.environ.get("K_SKIP", "").split(",")
        for (nm, fn, args) in seq:
            if nm.split("_")[0] in skip:
                continue
            fn(*args)
            if stop == nm:
                break
    return nc, k


def host_consts():
    c = {}
    c["c_ident"] = np.eye(128, dtype=np.float32)
    kk = np.arange(128)[:, None]
    qq = np.arange(128)[None, :]
    c["c_tri"] = (kk <= qq).astype(np.float32)
    c["c_sellast"] = np.broadcast_to((kk == 127), (128, 128)).astype(np.float32).copy()
    m0 = ((qq >= PAD) & (kk >= PAD) & (kk <= qq)) | ((qq < PAD) & (kk == qq))
    c["c_mask0"] = m0.astype(np.float32)
    c["c_kmask0"] = (np.arange(128) >= PAD).astype(np.float32)[:, None].copy()
    c["c_negmask"] = np.where(qq < kk, np.float32(-30000.0), np.float32(0.0)).astype(np.float32)
    return c


def pm(v, nchunk):
    return np.ascontiguousarray(np.asarray(v, np.float32).reshape(nchunk, 128).T)


def rope_tables_T():
    inv_freq = (1.0 / (np.float32(10000.0) ** (np.arange(0, 64, 2, dtype=np.float32) / np.float32(64)))).astype(np.float32)
    pos = np.maximum(np.arange(LP, dtype=np.float32) - np.float32(PAD), np.float32(0))
    ang = (pos[:, None] * inv_freq[None, :]).astype(np.float32)
    ang = np.concatenate([ang, ang], axis=-1)
    cos = np.cos(ang).astype(np.float32).T
    sin = np.sin(ang).astype(np.float32).T
    sgn = np.concatenate([-np.ones(32, np.float32), np.ones(32, np.float32)])[:, None]
    sins = sin * sgn
    return (np.ascontiguousarray(np.concatenate([cos, cos], 0)), np.ascontiguousarray(np.concatenate([sins, sins], 0)))


def prep_shared(inp):
    f = lambda a: np.ascontiguousarray(np.asarray(a, np.float32))
    sh = dict(host_consts())
    sh["cosT2"], sh["sinT2"] = rope_tables_T()
    sh["emb_g"] = pm(inp["emb_ln_g"], 8)
    sh["emb_b"] = pm(inp["emb_ln_b"], 8)
    w_in = f(inp["w_in"])
    sh["w_in"] = w_in
    sh["w_kpes"] = f(np.concatenate([w_in[:, :, 1056:1088], w_in[:, :, 1024:1056]], axis=-1))
    wqb = f(inp["w_q_b"]).reshape(DEPTH, QL, NH, 192)
    sh["wqb_n"] = f(wqb[..., :128].reshape(DEPTH, QL, 1024))
    sh["wqb_p"] = f(wqb[..., 128:].reshape(DEPTH, QL, 512))
    sh["wqb_ps"] = f(np.concatenate([wqb[..., 160:192], wqb[..., 128:160]], axis=-1).reshape(DEPTH, QL, 512))
    wkv = f(inp["w_kv_b"]).reshape(DEPTH, KVL, NH, 256)
    sh["wkvb_kn"] = f(wkv[..., :128].reshape(DEPTH, KVL, 1024))
    sh["wkvb_v"] = f(wkv[..., 128:].reshape(DEPTH, KVL, 1024))
    sh["qng"] = f(np.stack([pm(inp["q_norm_g"][l], 6) for l in range(DEPTH)]))
    sh["kvng"] = f(np.stack([pm(inp["kv_norm_g"][l], 2) for l in range(DEPTH)]))
    for nm in ("w_o_attn", "w_o_ssd", "w_out", "w_up", "w_down"):
        sh[nm] = f(inp[nm])
    cw = f(inp["ssd_conv_w"])
    sh["ssd_cw"] = f(cw.reshape(DEPTH, 4, 24, 128).transpose(0, 3, 2, 1))
    sh["ssd_cb"] = f(f(inp["ssd_conv_b"]).reshape(DEPTH, 24, 128).transpose(0, 2, 1))
    bc = lambda a: f(np.broadcast_to(f(a)[:, None, :], (DEPTH, 128, a.shape[-1])))
    sh["dtb_bc"] = bc(inp["dt_bias"])
    sh["alog_bc"] = bc(inp["a_log"])
    sh["dskip_bc"] = bc(np.repeat(f(inp["d_skip"]), 64, axis=-1))
    sh["ssdng_bc"] = bc(inp["ssd_norm_g"])
    for nm in ("ln1_g", "ln1_b", "ln2_g", "ln2_b"):
        sh[nm] = f(np.stack([pm(inp[nm][l], 8) for l in range(DEPTH)]))
    fw = f(inp["ffn_conv_w"])
    sh["ffn_cw"] = f(fw.reshape(DEPTH, 3, 44, 128).transpose(0, 3, 2, 1))
    sh["ffn_cb"] = f(f(inp["ffn_conv_b"]).reshape(DEPTH, 44, 128).transpose(0, 2, 1))
    return sh


def xin_of(inp, b):
    xin = np.zeros((LP, D), np.float32)
    xin[PAD:PAD + NMETA] = inp["meta_tokens"]
    xin[128:] = inp["x"][b]
    return xin


def kernel(**inp):
    sh = prep_shared(inp)
    nc, k = build()
    in_maps = []
    for c in range(8):
        m = dict(sh)
        m["xin"] = xin_of(inp, c % 4)
        in_maps.append(m)
    res = run_bass_kernel_spmd(nc, in_maps, core_ids=list(range(8)))
    out = np.stack([np.asarray(res.results[b]["out"], np.float32) for b in range(4)], axis=0)
    return out
```

```python
import math
from contextlib import ExitStack

import numpy as np
import concourse.bass as bass
import concourse.mybir as mybir
from concourse.bass_utils import run_bass_kernel_spmd

F32 = mybir.dt.float32
BF16 = mybir.dt.bfloat16
AF = mybir.ActivationFunctionType
ALU = mybir.AluOpType

D = 1024
SEQ = 8192
NMETA = 16
PAD = 112
LP = 8320
NBLK = 65
NH = 8
QL = 768
KVL = 256
DFF = 2816
DEPTH = 2
ALPHA = (2 * DEPTH) ** 0.25
LN_EPS = 1e-5
RMS_EPS = 1e-6
SCALE = 192 ** -0.5

ENGS = ("sync", "scalar", "vector", "gpsimd", "tensor")
COMPUTE = ("scalar", "vector", "gpsimd", "tensor")


class Buf:
    __slots__ = ("w", "r")

    def __init__(self):
        self.w = None
        self.r = {}


class DSem:
    def __init__(self, sem):
        self.sem = sem
        self.count = 0


class TB:
    def __init__(self, t, n=1, ds=None, ss=None):
        self.t = t
        self.b = [Buf() for _ in range(n)]
        self.ds = ds
        self.ss = ss


class Ring:
    def __init__(self, items):
        self.items = items
        self.i = 0

    def next(self):
        it = self.items[self.i % len(self.items)]
        self.i += 1
        return it


class Prog:
    def __init__(self, nc, gstack):
        self.nc = nc
        self.gstack = gstack
        self.pstack = None
        self.q = {e: [] for e in ENGS}
        self.cnt = {e: 0 for e in ENGS}
        self.waited = {e: {} for e in ENGS}
        self.pending = {e: [] for e in ENGS}
        self.psem = {}
        self.nsem = 0
        for e in COMPUTE:
            self.psem[e] = self._new_sem()
        self.pool = []
        self.pool_i = 0
        self.ninstr = 0
        self.phase_i = 0
        self.nt = 0

    def _new_sem(self):
        self.nsem += 1
        return self.gstack.enter_context(self.nc.semaphore(f"sem{self.nsem}"))

    def dsem(self):
        if self.pool_i == len(self.pool):
            self.pool.append(DSem(self._new_sem()))
        d = self.pool[self.pool_i]
        self.pool_i += 1
        return d

    def sbuf(self, shape, dtype, n=1, dma=False):
        self.nt += 1
        t = self.pstack.enter_context(self.nc.sbuf_tensor(f"sb{self.nt}", list(shape), dtype))
        return TB(t, n, self.dsem() if dma else None, self.dsem() if dma else None)

    def psum(self, shape, dtype, n=1):
        self.nt += 1
        t = self.pstack.enter_context(self.nc.psum_tensor(f"ps{self.nt}", list(shape), dtype))
        return TB(t, n)

    def _wait(self, eng, tok):
        if tok is None:
            return
        sem, val, src = tok
        if src == eng and eng == "tensor":
            return
        key = id(sem)
        if self.waited[eng].get(key, 0) >= val:
            return
        self.waited[eng][key] = val
        self.q[eng].append(lambda e, s=sem, v=val: e.wait_ge(s, v))

    def _deps(self, eng, reads, writes, nowaw=False):
        for b in reads:
            self._wait(eng, b.w)
        for b in writes:
            if not nowaw:
                self._wait(eng, b.w)
            for t in b.r.values():
                self._wait(eng, t)

    def _commit(self, tok, reads, writes):
        k = id(tok[0])
        for b in reads:
            b.r[k] = tok
        for b in writes:
            b.w = tok
            b.r = {}

    def op(self, eng, fn, reads=(), writes=(), mark=True):
        self.ninstr += 1
        self._deps(eng, reads, writes)
        if not mark:
            self.pending[eng].append((tuple(reads), tuple(writes)))
            self.q[eng].append(fn)
            return None
        self.cnt[eng] += 1
        v = self.cnt[eng]
        sem = self.psem[eng]
        self.q[eng].append(lambda e, f=fn, s=sem: f(e).then_inc(s, 1))
        tok = (sem, v, eng)
        for (r, w) in self.pending[eng]:
            self._commit(tok, r, w)
        self.pending[eng] = []
        self._commit(tok, reads, writes)
        return tok

    def dma(self, eng, pairs, reads, writes, ds, nowaw=False):
        self._deps(eng, reads, writes, nowaw)
        for (o, i) in pairs:
            self.ninstr += 1
            ds.count += 16
            self.q[eng].append(lambda e, o=o, i=i, s=ds.sem: e.dma_start(out=o, in_=i).then_inc(s, 16))
        tok = (ds.sem, ds.count, "dma")
        self._commit(tok, reads, writes)
        return tok

    def begin_phase(self, st):
        self.pstack = st
        self.pool_i = 0

    def end_phase(self):
        for d in self.pool:
            if d.count:
                self._wait("sync", (d.sem, d.count, "dma"))
        for e in COMPUTE:
            assert not self.pending[e], e
            if self.cnt[e]:
                self._wait("sync", (self.psem[e], self.cnt[e], e))
        nc = self.nc
        self.phase_i += 1
        with nc.Block() as block:
            for ename in ENGS:
                lst = self.q[ename]
                if not lst:
                    continue

                def body(e, lst=lst):
                    for f in lst:
                        f(e)
                getattr(block, ename)(body)
        self.q = {e: [] for e in ENGS}


def MM(out, lhsT, rhs, start=True, stop=True):
    return lambda e: e.matmul(out, lhsT=lhsT, rhs=rhs, start=start, stop=stop)


def TR(out, in_, ident):
    return lambda e: e.transpose(out=out, in_=in_, identity=ident)


def ACT(out, in_, func, bias=None, scale=None):
    kw = {}
    if bias is not None:
        kw["bias"] = bias
    if scale is not None:
        kw["scale"] = scale
    return lambda e: e.activation(out=out, in_=in_, func=func, **kw)


def TT(out, a, b, op):
    return lambda e: e.tensor_tensor(out=out, in0=a, in1=b, op=op)


def TS(out, a, s1, op0, s2=None, op1=None):
    if op1 is None:
        return lambda e: e.tensor_scalar(out=out, in0=a, scalar1=s1, scalar2=None, op0=op0)
    return lambda e: e.tensor_scalar(out=out, in0=a, scalar1=s1, scalar2=s2, op0=op0, op1=op1)


def STT(out, in0, scalar, in1, op0, op1):
    return lambda e: e.scalar_tensor_tensor(out=out, in0=in0, scalar=scalar, in1=in1, op0=op0, op1=op1)


def CP(out, in_):
    return lambda e: e.tensor_copy(out=out, in_=in_)


def MEMSET(ap, v):
    return lambda e: e.memset(ap, v)


def tiles_of(width):
    t = [(0, 128)]
    c = 128
    while c < LP:
        t.append((c, width))
        c += width
    return t


class K:
    def __init__(self, nc, P, n_layers, dbg):
        self.nc = nc
        self.P = P
        self.n_layers = n_layers
        self.dbg = dbg or ()
        self.ext = {}
        self.scr = {}

    def din(self, name, shape, dt=F32):
        ap = self.nc.dram_tensor(name, list(shape), dt, kind="ExternalInput").ap()
        self.ext[name] = ap
        return ap

    def dscr(self, name, shape, dt):
        kind = "ExternalOutput" if name in self.dbg else "Internal"
        ap = self.nc.dram_tensor(name, list(shape), dt, kind=kind).ap()
        self.scr[name] = TB(ap, 1)
        return self.scr[name]

    def load_consts(self, st):
        P = self.P
        P.pstack = st
        c = self.ext
        self.identf = P.sbuf([128, 128], F32, dma=True)
        self.tri_f = P.sbuf([128, 128], F32, dma=True)
        self.sellast = P.sbuf([128, 128], F32, dma=True)
        self.mask0 = P.sbuf([128, 128], F32, dma=True)
        self.kmask0 = P.sbuf([128, 1], F32, dma=True)
        self.identb = P.sbuf([128, 128], BF16)
        self.ones_b = P.sbuf([128, 128], BF16)
        self.tri_b = P.sbuf([128, 128], BF16)
        self.mask0_b = P.sbuf([128, 128], BF16)
        for tb, nm in ((self.identf, "c_ident"), (self.tri_f, "c_tri"), (self.sellast, "c_sellast"),
                       (self.mask0, "c_mask0")):
            P.dma("sync", [(tb.t[:], c[nm][:, :])], [], tb.b, tb.ds)
        P.dma("sync", [(self.kmask0.t[:], c["c_kmask0"][:, :])], [], self.kmask0.b, self.kmask0.ds)
        P.op("vector", CP(self.identb.t[:], self.identf.t[:]), self.identf.b, self.identb.b)
        P.op("vector", CP(self.tri_b.t[:], self.tri_f.t[:]), self.tri_f.b, self.tri_b.b)
        P.op("vector", CP(self.mask0_b.t[:], self.mask0.t[:]), self.mask0.b, self.mask0_b.b)
        P.op("vector", MEMSET(self.ones_b.t[:], 1.0), [], self.ones_b.b)

    def load_w(self, dst, src, kchunks, per=4):
        P = self.P
        for k0 in range(0, kchunks, per):
            k1 = min(kchunks, k0 + per)
            P.dma("gpsimd", [(dst.t[:, k0:k1, :], src[k0 * 128:k1 * 128, :].rearrange("(k p) n -> p k n", p=128))],
                  [], dst.b, dst.ds)

    def ln_fm(self, s, out, W, g, b, R):
        P = self.P
        ones = self.ones_b
        sum_ps, ssq_ps = R["sum"], R["ssq"]
        for c in range(8):
            sb = R["sb"].next()
            sq = R["sq"].next()
            P.op("scalar", ACT(sb.t[:, 0:W], s.t[:, c, 0:W], AF.Copy), [s.b[c]], sb.b)
            P.op("scalar", ACT(sq.t[:, 0:W], s.t[:, c, 0:W], AF.Square), [s.b[c]], sq.b)
            P.op("tensor", MM(sum_ps.t[:, 0:W], ones.t[:], sb.t[:, 0:W], c == 0, c == 7), sb.b + ones.b, sum_ps.b,
                 mark=False)
            P.op("tensor", MM(ssq_ps.t[:, 0:W], ones.t[:], sq.t[:, 0:W], c == 0, c == 7), sq.b + ones.b, ssq_ps.b,
                 mark=True)
        mean, var, rstd = R["mean"], R["var"], R["rstd"]
        P.op("vector", TS(mean.t[:, 0:W], sum_ps.t[:, 0:W], 1.0 / D, ALU.mult), sum_ps.b, mean.b)
        P.op("vector", TS(var.t[:, 0:W], ssq_ps.t[:, 0:W], 1.0 / D, ALU.mult), ssq_ps.b, var.b)
        msq = R["msq"]
        P.op("vector", TT(msq.t[:, 0:W], mean.t[:, 0:W], mean.t[:, 0:W], ALU.mult), mean.b, msq.b)
        P.op("vector", TT(var.t[:, 0:W], var.t[:, 0:W], msq.t[:, 0:W], ALU.subtract), var.b + msq.b, var.b)
        P.op("vector", TS(var.t[:, 0:W], var.t[:, 0:W], LN_EPS, ALU.add), var.b, var.b)
        P.op("scalar", ACT(rstd.t[:, 0:W], var.t[:, 0:W], AF.Ln), var.b, rstd.b)
        P.op("scalar", ACT(rstd.t[:, 0:W], rstd.t[:, 0:W], AF.Exp, scale=-0.5), rstd.b, rstd.b)
        for c in range(8):
            tmp = R["tmp"].next()
            P.op("vector", TT(tmp.t[:, 0:W], s.t[:, c, 0:W], mean.t[:, 0:W], ALU.subtract), [s.b[c]] + mean.b, tmp.b)
            P.op("vector", TT(tmp.t[:, 0:W], tmp.t[:, 0:W], rstd.t[:, 0:W], ALU.mult), tmp.b + rstd.b, tmp.b)
            P.op("scalar", ACT(out.t[:, c, 0:W], tmp.t[:, 0:W], AF.Identity, bias=b[:, c:c + 1], scale=g[:, c:c + 1]),
                 tmp.b, [out.b[c]])

    def ln_resources(self, wmax=512):
        P = self.P
        R = {}
        R["sum"] = P.psum([128, 512], F32)
        R["ssq"] = P.psum([128, 512], F32)
        R["sb"] = Ring([P.sbuf([128, wmax], BF16) for _ in range(2)])
        R["sq"] = Ring([P.sbuf([128, wmax], BF16) for _ in range(2)])
        for nm in ("mean", "var", "msq", "rstd"):
            R[nm] = P.sbuf([128, wmax], F32)
        R["tmp"] = Ring([P.sbuf([128, wmax], F32) for _ in range(2)])
        return R

    def phase_E(self):
        P = self.P
        hT = self.scr["hT"]
        with ExitStack() as st:
            P.begin_phase(st)
            xin = self.ext["xin"]
            gb = P.sbuf([128, 16], F32, dma=True)
            P.dma("sync", [(gb.t[:, 0:8], self.ext["emb_g"][:, :]), (gb.t[:, 8:16], self.ext["emb_b"][:, :])],
                  [], gb.b, gb.ds)
            xr = Ring([P.sbuf([128, 4, 1024], F32, dma=True) for _ in range(2)])
            sr = Ring([P.sbuf([128, 8, 512], F32, n=8) for _ in range(2)])
            orr = Ring([P.sbuf([128, 8, 512], F32, n=8, dma=True) for _ in range(2)])
            tpr = Ring([P.psum([128, 512], F32) for _ in range(2)])
            R = self.ln_resources()
            for (c0, W) in tiles_of(512):
                nb = W // 128
                x = xr.next()
                P.dma("sync", [(x.t[:, 0:nb, :], xin[c0:c0 + W, :].rearrange("(b p) f -> p b f", p=128))],
                      [], x.b, x.ds)
                s = sr.next()
                for c in range(8):
                    tp = tpr.next()
                    for bi in range(nb):
                        P.op("tensor", TR(tp.t[:, bi * 128:(bi + 1) * 128], x.t[:, bi, c * 128:(c + 1) * 128],
                                          self.identf.t[:]), x.b + self.identf.b, tp.b, mark=(bi == nb - 1))
                    P.op("vector" if c % 2 else "scalar",
                         CP(s.t[:, c, 0:W], tp.t[:, 0:W]) if c % 2 else ACT(s.t[:, c, 0:W], tp.t[:, 0:W], AF.Copy),
                         tp.b, [s.b[c]])
                o = orr.next()
                self.ln_fm(s, o, W, gb.t[:, 0:8], gb.t[:, 8:16], R)
                P.dma("gpsimd", [(hT.t[:, :, c0:c0 + W].rearrange("c p t -> p c t"), o.t[:, :, 0:W])],
                      o.b, hT.b, o.ss, nowaw=True)
            P.end_phase()


    def phase_A1(self, l):
        P = self.P
        E = self.ext
        hT, qT, kT, kpeT, vS = (self.scr[n] for n in ("hT", "qT", "kT", "kpeT", "v"))
        with ExitStack() as st:
            P.begin_phase(st)
            w_in = E["w_in"]
            Wql = P.sbuf([128, 8, QL], BF16, dma=True)
            Wkvl = P.sbuf([128, 8, KVL], BF16, dma=True)
            Wkpe = P.sbuf([128, 8, 64], BF16, dma=True)
            Wkpes = P.sbuf([128, 8, 64], BF16, dma=True)
            Wqn = P.sbuf([128, 6, 1024], BF16, dma=True)
            Wqp = P.sbuf([128, 6, 512], BF16, dma=True)
            Wqps = P.sbuf([128, 6, 512], BF16, dma=True)
            Wkn = P.sbuf([128, 2, 1024], BF16, dma=True)
            Wv = P.sbuf([128, 2, 1024], BF16, dma=True)
            self.load_w(Wql, w_in[l, :, 0:768], 8)
            self.load_w(Wkvl, w_in[l, :, 768:1024], 8, per=8)
            self.load_w(Wkpe, w_in[l, :, 1024:1088], 8, per=8)
            self.load_w(Wkpes, E["w_kpes"][l, :, :], 8, per=8)
            self.load_w(Wqn, E["wqb_n"][l, :, :], 6, per=3)
            self.load_w(Wqp, E["wqb_p"][l, :, :], 6, per=6)
            self.load_w(Wqps, E["wqb_ps"][l, :, :], 6, per=6)
            self.load_w(Wkn, E["wkvb_kn"][l, :, :], 2)
            self.load_w(Wv, E["wkvb_v"][l, :, :], 2)
            ng = P.sbuf([128, 8], F32, dma=True)
            P.dma("sync", [(ng.t[:, 0:6], E["qng"][l, :, :]), (ng.t[:, 6:8], E["kvng"][l, :, :])], [], ng.b, ng.ds)
            hr = Ring([P.sbuf([128, 8, 512], F32, dma=True) for _ in range(2)])
            hbr = Ring([P.sbuf([128, 8, 512], BF16, n=8) for _ in range(2)])
            csr = Ring([P.sbuf([128, 2, 512], F32, dma=True) for _ in range(2)])
            qlat = P.sbuf([128, 6, 512], F32, n=6)
            qn = P.sbuf([128, 6, 512], BF16, n=6)
            kvlat = P.sbuf([128, 2, 512], F32, n=2)
            kvn = P.sbuf([128, 2, 512], BF16, n=2)
            sqr = Ring([P.sbuf([128, 512], BF16) for _ in range(2)])
            rstd = P.sbuf([128, 512], F32)
            t1r = Ring([P.sbuf([128, 512], F32) for _ in range(2)])
            t2r = Ring([P.sbuf([128, 512], F32) for _ in range(2)])
            stg = Ring([P.sbuf([128, 512], BF16, dma=True) for _ in range(4)])
            vst = Ring([P.sbuf([128, 1024], BF16, n=2, dma=True) for _ in range(2)])
            accr = Ring([P.psum([128, 512], F32) for _ in range(4)])
            ss_ps = P.psum([128, 512], F32)
            ones = self.ones_b
            evi = [0]

            def evac(out_ap, in_ap, reads, writes):
                evi[0] += 1
                if evi[0] % 2:
                    P.op("scalar", ACT(out_ap, in_ap, AF.Copy), reads, writes)
                else:
                    P.op("vector", CP(out_ap, in_ap), reads, writes)

            def rms(acc_list_fn, nch, lat, dst, W, gcol0, nfeat):
                for c in range(nch):
                    acc = acc_list_fn(c)
                    sq = sqr.next()
                    P.op("scalar", ACT(lat.t[:, c, 0:W], acc.t[:, 0:W], AF.Copy), acc.b, [lat.b[c]])
                    P.op("scalar", ACT(sq.t[:, 0:W], acc.t[:, 0:W], AF.Square), acc.b, sq.b)
                    P.op("tensor", MM(ss_ps.t[:, 0:W], ones.t[:], sq.t[:, 0:W], c == 0, c == nch - 1),
                         sq.b + ones.b, ss_ps.b, mark=(c == nch - 1))
                P.op("vector", TS(rstd.t[:, 0:W], ss_ps.t[:, 0:W], 1.0 / nfeat, ALU.mult, RMS_EPS, ALU.add),
                     ss_ps.b, rstd.b)
                P.op("scalar", ACT(rstd.t[:, 0:W], rstd.t[:, 0:W], AF.Ln), rstd.b, rstd.b)
                P.op("scalar", ACT(rstd.t[:, 0:W], rstd.t[:, 0:W], AF.Exp, scale=-0.5), rstd.b, rstd.b)
                for c in range(nch):
                    P.op("vector", STT(dst.t[:, c, 0:W], lat.t[:, c, 0:W], ng.t[:, gcol0 + c:gcol0 + c + 1],
                                       rstd.t[:, 0:W], ALU.mult, ALU.mult), [lat.b[c]] + rstd.b + ng.b, [dst.b[c]])

            def proj(acc, W_tb, ncols0, ncols, K, rhs_tb, W, M=128):
                for k in range(K):
                    P.op("tensor", MM(acc.t[0:M, 0:W], W_tb.t[:, k, ncols0:ncols0 + ncols], rhs_tb.t[:, k, 0:W],
                                      k == 0, k == K - 1), W_tb.b + [rhs_tb.b[k]], acc.b, mark=(k == K - 1))

            def rope(acc1, acc2, cs, W, M, outs):
                t1 = t1r.next()
                t2 = t2r.next()
                P.op("vector", TT(t1.t[0:M, 0:W], acc1.t[0:M, 0:W], cs.t[0:M, 0, 0:W], ALU.mult), acc1.b + cs.b, t1.b)
                P.op("vector", TT(t2.t[0:M, 0:W], acc2.t[0:M, 0:W], cs.t[0:M, 1, 0:W], ALU.mult), acc2.b + cs.b, t2.b)
                sg = stg.next()
                P.op("vector", TT(sg.t[0:M, 0:W], t1.t[0:M, 0:W], t2.t[0:M, 0:W], ALU.add), t1.b + t2.b, sg.b)
                for (dst_tb, dst_ap, p0, p1) in outs:
                    P.dma("gpsimd", [(dst_ap, sg.t[p0:p1, 0:W])], sg.b, dst_tb.b, sg.ss, nowaw=True)

            for (c0, W) in tiles_of(512):
                nb = W // 128
                h = hr.next()
                P.dma("sync", [(h.t[:, :, 0:W], hT.t[:, :, c0:c0 + W].rearrange("c p t -> p c t"))], hT.b, h.b, h.ds)
                cs = csr.next()
                P.dma("sync", [(cs.t[:, 0, 0:W], E["cosT2"][:, c0:c0 + W]), (cs.t[:, 1, 0:W], E["sinT2"][:, c0:c0 + W])],
                      [], cs.b, cs.ds)
                hb = hbr.next()
                for c in range(8):
                    evac(hb.t[:, c, 0:W], h.t[:, c, 0:W], h.b, [hb.b[c]])

                def ql_acc(c):
                    acc = accr.next()
                    proj(acc, Wql, c * 128, 128, 8, hb, W)
                    return acc
                rms(ql_acc, 6, qlat, qn, W, 0, QL)
                for hd in range(NH):
                    acc = accr.next()
                    proj(acc, Wqn, hd * 128, 128, 6, qn, W)
                    sg = stg.next()
                    evac(sg.t[:, 0:W], acc.t[:, 0:W], acc.b, sg.b)
                    P.dma("gpsimd", [(qT.t[hd, 0:128, c0:c0 + W], sg.t[:, 0:W])], sg.b, qT.b, sg.ss, nowaw=True)
                for pr in range(4):
                    a1 = accr.next()
                    proj(a1, Wqp, pr * 128, 128, 6, qn, W)
                    a2 = accr.next()
                    proj(a2, Wqps, pr * 128, 128, 6, qn, W)
                    rope(a1, a2, cs, W, 128, [(qT, qT.t[2 * pr, 128:192, c0:c0 + W], 0, 64),
                                              (qT, qT.t[2 * pr + 1, 128:192, c0:c0 + W], 64, 128)])

                def kv_acc(c):
                    acc = accr.next()
                    proj(acc, Wkvl, c * 128, 128, 8, hb, W)
                    return acc
                rms(kv_acc, 2, kvlat, kvn, W, 6, KVL)
                for hd in range(NH):
                    acc = accr.next()
                    proj(acc, Wkn, hd * 128, 128, 2, kvn, W)
                    sg = stg.next()
                    evac(sg.t[:, 0:W], acc.t[:, 0:W], acc.b, sg.b)
                    P.dma("gpsimd", [(kT.t[hd, :, c0:c0 + W], sg.t[:, 0:W])], sg.b, kT.b, sg.ss, nowaw=True)
                for bi in range(nb):
                    vs = vst.next()
                    for half in range(2):
                        acc = accr.next()
                        for k in range(2):
                            P.op("tensor", MM(acc.t[:, :], kvn.t[:, k, bi * 128:(bi + 1) * 128],
                                              Wv.t[:, k, half * 512:(half + 1) * 512], k == 0, k == 1),
                                 Wv.b + [kvn.b[k]], acc.b, mark=(k == 1))
                        evac(vs.t[:, half * 512:(half + 1) * 512], acc.t[:, :], acc.b, [vs.b[half]])
                    blk = c0 // 128 + bi
                    P.dma("gpsimd", [(vS.t[:, :, blk, :].rearrange("h p d -> p h d"),
                                      vs.t[:, :].rearrange("p (h d) -> p h d", h=NH))], vs.b, vS.b, vs.ss, nowaw=True)
                a1 = accr.next()
                proj(a1, Wkpe, 0, 64, 8, hb, W, M=64)
                a2 = accr.next()
                proj(a2, Wkpes, 0, 64, 8, hb, W, M=64)
                rope(a1, a2, cs, W, 64, [(kpeT, kpeT.t[:, c0:c0 + W], 0, 64)])
            P.end_phase()


    def phase_B(self, l):
        P = self.P
        qT, kT, kpeT, vS, oT = (self.scr[n] for n in ("qT", "kT", "kpeT", "v", "oT"))
        ones = self.ones_b
        with ExitStack() as st:
            P.begin_phase(st)
            kpe = P.sbuf([64, LP], BF16, dma=True)
            P.dma("sync", [(kpe.t[:, :], kpeT.t[:, :])], kpeT.b, kpe.b, kpe.ds)
            knr = Ring([P.sbuf([128, LP], BF16, dma=True) for _ in range(2)])
            vr = Ring([P.sbuf([128, NBLK, 128], BF16, dma=True) for _ in range(2)])
            qr = Ring([P.sbuf([128, 2, 512], BF16, dma=True) for _ in range(3)])
            ptr = Ring([P.sbuf([128, 512], BF16) for _ in range(6)])
            rsr = Ring([P.sbuf([128, 512], F32) for _ in range(2)])
            osr = Ring([P.sbuf([128, 512], BF16, dma=True) for _ in range(2)])
            spr = Ring([P.psum([128, 512], F32) for _ in range(4)])
            opr = Ring([P.psum([128, 512], F32) for _ in range(2)])
            smr = Ring([P.psum([128, 512], F32) for _ in range(2)])
            tiles = tiles_of(512)
            for hd in range(NH):
                kn = knr.next()
                P.dma("sync", [(kn.t[:, :], kT.t[hd, :, :])], kT.b, kn.b, kn.ds)
                v = vr.next()
                P.dma("sync", [(v.t[:, :, :], vS.t[hd, :, :, :])], vS.b, v.b, v.ds)
                for (c0, W) in tiles:
                    q = qr.next()
                    P.dma("sync", [(q.t[:, 0, 0:W], qT.t[hd, 0:128, c0:c0 + W]),
                                   (q.t[0:64, 1, 0:W], qT.t[hd, 128:192, c0:c0 + W])], qT.b, q.b, q.ds)
                    o_ps = opr.next()
                    sm_ps = smr.next()
                    nkb = (c0 + W) // 128
                    units = []
                    for kb in range(nkb):
                        qo = kb * 128 - c0 if kb * 128 >= c0 else 0
                        units.append((kb, qo))

                    def qk(u):
                        kb, qo = u
                        sp = spr.next()
                        P.op("tensor", MM(sp.t[:, qo:W], kn.t[:, kb * 128:(kb + 1) * 128], q.t[:, 0, qo:W], True, False),
                             kn.b + q.b, sp.b, mark=False)
                        P.op("tensor", MM(sp.t[:, qo:W], kpe.t[0:64, kb * 128:(kb + 1) * 128], q.t[0:64, 1, qo:W],
                                          False, True), kpe.b + q.b, sp.b, mark=True)
                        return sp

                    def soft(u, sp):
                        kb, qo = u
                        pt = ptr.next()
                        P.op("scalar", ACT(pt.t[:, qo:W], sp.t[:, qo:W], AF.Exp, scale=SCALE), sp.b, pt.b)
                        if kb == 0 and c0 == 0:
                            P.op("vector", TT(pt.t[:, 0:128], pt.t[:, 0:128], self.mask0_b.t[:], ALU.mult),
                                 pt.b + self.mask0_b.b, pt.b)
                        elif kb == 0:
                            P.op("vector", TS(pt.t[:, 0:W], pt.t[:, 0:W], self.kmask0.t[:, 0:1], ALU.mult),
                                 pt.b + self.kmask0.b, pt.b)
                        elif kb * 128 >= c0:
                            P.op("vector", TT(pt.t[:, qo:qo + 128], pt.t[:, qo:qo + 128], self.tri_b.t[:], ALU.mult),
                                 pt.b + self.tri_b.b, pt.b)
                        return pt

                    def pv(u, pt, first, last):
                        kb, qo = u
                        P.op("tensor", MM(o_ps.t[:, qo:W], v.t[:, kb, :], pt.t[:, qo:W], first, last),
                             v.b + pt.b, o_ps.b, mark=False)
                        P.op("tensor", MM(sm_ps.t[:, qo:W], ones.t[:], pt.t[:, qo:W], first, last),
                             ones.b + pt.b, sm_ps.b, mark=True)

                    LA = 2
                    sps = [qk(units[i]) for i in range(min(LA, len(units)))]
                    for i, u in enumerate(units):
                        if i + LA < len(units):
                            sps.append(qk(units[i + LA]))
                        pt = soft(u, sps[i])
                        pv(u, pt, i == 0, i == len(units) - 1)
                    rs = rsr.next()
                    P.op("vector", lambda e, o=rs.t[:, 0:W], i=sm_ps.t[:, 0:W]: e.reciprocal(out=o, in_=i), sm_ps.b, rs.b)
                    og = osr.next()
                    P.op("vector", TT(og.t[:, 0:W], o_ps.t[:, 0:W], rs.t[:, 0:W], ALU.mult), o_ps.b + rs.b, og.b)
                    P.dma("gpsimd", [(oT.t[hd, :, c0:c0 + W], og.t[:, 0:W])], og.b, oT.b, og.ss, nowaw=True)
            P.end_phase()


    def phase_A2(self, l):
        P = self.P
        E = self.ext
        hT, xsS, BtokS, BTS, CTS, dtS, zsS = (self.scr[n] for n in ("hT", "xs", "Btok", "BT", "CT", "dt", "zs"))
        with ExitStack() as st:
            P.begin_phase(st)
            w_in = E["w_in"]
            Wz = P.sbuf([128, 8, 2048], BF16, dma=True)
            Wx = P.sbuf([128, 8, 3072], BF16, dma=True)
            Wdt = P.sbuf([128, 8, 32], BF16, dma=True)
            self.load_w(Wz, w_in[l, :, 1088:3136], 8, per=2)
            self.load_w(Wx, w_in[l, :, 3136:6208], 8, per=2)
            self.load_w(Wdt, w_in[l, :, 6208:6240], 8, per=8)
            cw = P.sbuf([128, 24, 4], F32, dma=True)
            cb = P.sbuf([128, 24], F32, dma=True)
            dtb = P.sbuf([128, 32], F32, dma=True)
            P.dma("sync", [(cw.t[:], E["ssd_cw"][l, :, :, :])], [], cw.b, cw.ds)
            P.dma("sync", [(cb.t[:], E["ssd_cb"][l, :, :])], [], cb.b, cb.ds)
            P.dma("sync", [(dtb.t[:], E["dtb_bc"][l, :, :])], [], dtb.b, dtb.ds)
            h = P.sbuf([128, 8, 512], F32, dma=True)
            hbr = Ring([P.sbuf([128, 8, 512], BF16, n=8) for _ in range(2)])
            xc = P.sbuf([128, 24, 512], BF16, n=24, dma=True)
            xbr = Ring([P.sbuf([128, 515], F32) for _ in range(2)])
            halo = P.sbuf([128, 24, 3], F32, n=24)
            tcr = Ring([P.sbuf([128, 512], F32) for _ in range(2)])
            zst = Ring([P.sbuf([128, 2048], BF16, n=4, dma=True) for _ in range(2)])
            dtr = Ring([P.sbuf([128, 32], F32, dma=True) for _ in range(2)])
            tst = Ring([P.sbuf([128, 2560], BF16, n=5, dma=True) for _ in range(2)])
            accr = Ring([P.psum([128, 512], F32) for _ in range(3)])
            tpr = Ring([P.psum([128, 512], BF16) for _ in range(2)])
            P.op("vector", MEMSET(halo.t[:], 0.0), [], halo.b)
            evi = [0]

            def evac(out_ap, in_ap, reads, writes):
                evi[0] += 1
                if evi[0] % 2:
                    P.op("scalar", ACT(out_ap, in_ap, AF.Copy), reads, writes)
                else:
                    P.op("vector", CP(out_ap, in_ap), reads, writes)

            for (c0, W) in tiles_of(512):
                nb = W // 128
                P.dma("sync", [(h.t[:, :, 0:W], hT.t[:, :, c0:c0 + W].rearrange("c p t -> p c t"))], hT.b, h.b, h.ds)
                hb = hbr.next()
                for c in range(8):
                    evac(hb.t[:, c, 0:W], h.t[:, c, 0:W], h.b, [hb.b[c]])
                for bi in range(nb):
                    zt = zst.next()
                    for cg in range(4):
                        acc = accr.next()
                        for k in range(8):
                            P.op("tensor", MM(acc.t[:, :], hb.t[:, k, bi * 128:(bi + 1) * 128],
                                              Wz.t[:, k, cg * 512:(cg + 1) * 512], k == 0, k == 7),
                                 Wz.b + [hb.b[k]], acc.b, mark=(k == 7))
                        P.op("scalar", ACT(zt.t[:, cg * 512:(cg + 1) * 512], acc.t[:, :], AF.Silu), acc.b, [zt.b[cg]])
                    r0 = c0 + bi * 128
                    P.dma("gpsimd", [(zsS.t[r0:r0 + 128, :], zt.t[:, :])], zt.b, zsS.b, zt.ss, nowaw=True)
                    acc = accr.next()
                    for k in range(8):
                        P.op("tensor", MM(acc.t[:, 0:32], hb.t[:, k, bi * 128:(bi + 1) * 128], Wdt.t[:, k, 0:32],
                                          k == 0, k == 7), Wdt.b + [hb.b[k]], acc.b, mark=(k == 7))
                    dtt = dtr.next()
                    P.op("vector", TT(dtt.t[:, :], acc.t[:, 0:32], dtb.t[:, :], ALU.add), acc.b + dtb.b, dtt.b)
                    P.op("scalar", ACT(dtt.t[:, :], dtt.t[:, :], AF.Exp), dtt.b, dtt.b)
                    P.op("scalar", ACT(dtt.t[:, :], dtt.t[:, :], AF.Ln, bias=1.0), dtt.b, dtt.b)
                    P.dma("gpsimd", [(dtS.t[r0:r0 + 128, :], dtt.t[:, :])], dtt.b, dtS.b, dtt.ss, nowaw=True)
                for c in range(24):
                    acc = accr.next()
                    for k in range(8):
                        P.op("tensor", MM(acc.t[:, 0:W], Wx.t[:, k, c * 128:(c + 1) * 128], hb.t[:, k, 0:W],
                                          k == 0, k == 7), Wx.b + [hb.b[k]], acc.b, mark=(k == 7))
                    xb = xbr.next()
                    P.op("scalar", ACT(xb.t[:, 3:3 + W], acc.t[:, 0:W], AF.Copy), acc.b, xb.b)
                    P.op("vector", CP(xb.t[:, 0:3], halo.t[:, c, :]), [halo.b[c]], xb.b)
                    if c0 == 0:
                        P.op("vector", MEMSET(xb.t[:, 0:3 + PAD], 0.0), [], xb.b)
                    P.op("vector", CP(halo.t[:, c, :], xb.t[:, W:W + 3]), xb.b, [halo.b[c]])
                    tc_ = tcr.next()
                    P.op("scalar", ACT(tc_.t[:, 0:W], xb.t[:, 0:W], AF.Identity, bias=cb.t[:, c:c + 1], scale=cw.t[:, c, 0:1]),
                         xb.b + cw.b + cb.b, tc_.b)
                    for tap in range(1, 4):
                        P.op("vector", STT(tc_.t[:, 0:W], xb.t[:, tap:tap + W], cw.t[:, c, tap:tap + 1], tc_.t[:, 0:W],
                                           ALU.mult, ALU.add), xb.b + cw.b + tc_.b, tc_.b)
                    P.op("scalar", ACT(xc.t[:, c, 0:W], tc_.t[:, 0:W], AF.Silu), tc_.b, [xc.b[c]])
                    if c0 == 0:
                        P.op("vector", MEMSET(xc.t[:, c, 0:PAD], 0.0), [], [xc.b[c]])
                prs = []
                for g in range(4):
                    prs.append((BTS.t[g, :, c0:c0 + W], xc.t[:, 16 + g, 0:W]))
                    prs.append((CTS.t[g, :, c0:c0 + W], xc.t[:, 20 + g, 0:W]))
                P.dma("gpsimd", prs, xc.b[16:24], BTS.b + CTS.b, xc.ss, nowaw=True)
                for bi in range(nb):
                    ts_ = tst.next()
                    for q4 in range(5):
                        tp = tpr.next()
                        for j in range(4):
                            c = q4 * 4 + j
                            P.op("tensor", TR(tp.t[:, j * 128:(j + 1) * 128], xc.t[:, c, bi * 128:(bi + 1) * 128],
                                              self.identb.t[:]), [xc.b[c]] + self.identb.b, tp.b, mark=(j == 3))
                        evac(ts_.t[:, q4 * 512:(q4 + 1) * 512], tp.t[:, :], tp.b, [ts_.b[q4]])
                    r0 = c0 + bi * 128
                    P.dma("gpsimd", [(xsS.t[r0:r0 + 128, :], ts_.t[:, 0:2048]), (BtokS.t[r0:r0 + 128, :], ts_.t[:, 2048:2560])],
                          ts_.b, xsS.b + BtokS.b, ts_.ss, nowaw=True)
            P.end_phase()

    def phase_S(self, l):
        P = self.P
        E = self.ext
        xsS, BtokS, BTS, CTS, dtS, zsS, ynTS = (self.scr[n] for n in ("xs", "Btok", "BT", "CT", "dt", "zs", "ynT"))
        trib = self.tri_b
        ones = self.ones_b
        identb = self.identb
        with ExitStack() as st:
            P.begin_phase(st)
            abc = P.sbuf([128, 32], F32, dma=True)
            dsk = P.sbuf([128, 2048], F32, dma=True)
            ngb = P.sbuf([128, 2048], F32, dma=True)
            negm_f = P.sbuf([128, 128], F32, dma=True)
            P.dma("sync", [(abc.t[:], E["alog_bc"][l, :, :])], [], abc.b, abc.ds)
            P.dma("sync", [(dsk.t[:], E["dskip_bc"][l, :, :])], [], dsk.b, dsk.ds)
            P.dma("sync", [(ngb.t[:], E["ssdng_bc"][l, :, :])], [], ngb.b, ngb.ds)
            P.dma("sync", [(negm_f.t[:], E["c_negmask"][:, :])], [], negm_f.b, negm_f.ds)
            P.op("scalar", ACT(abc.t[:], abc.t[:], AF.Exp), abc.b, abc.b)
            P.op("vector", TS(abc.t[:], abc.t[:], -1.0, ALU.mult), abc.b, abc.b)
            negm = P.sbuf([128, 128], BF16)
            trin = P.sbuf([128, 128], BF16)
            P.op("vector", CP(negm.t[:], negm_f.t[:]), negm_f.b, negm.b)
            P.op("vector", TS(trin.t[:], self.tri_f.t[:], -1.0, ALU.mult), self.tri_f.b, trin.b)
            NB_ = 3
            xsr = Ring([P.sbuf([128, 2048], BF16, dma=True) for _ in range(NB_)])
            btr = Ring([P.sbuf([128, 512], BF16, dma=True) for _ in range(NB_)])
            bTr = Ring([P.sbuf([128, 4, 128], BF16, dma=True) for _ in range(NB_)])
            cTr = Ring([P.sbuf([128, 4, 128], BF16, dma=True) for _ in range(NB_)])
            dtr = Ring([P.sbuf([128, 32], F32, dma=True) for _ in range(NB_)])
            zsr = Ring([P.sbuf([128, 2048], BF16, dma=True) for _ in range(NB_)])
            prev = P.sbuf([128, 2048], F32, n=4)
            prevb = P.sbuf([128, 2048], BF16, n=4)
            P.op("vector", MEMSET(prev.t[:], 0.0), [], prev.b)
            P.op("vector", MEMSET(prevb.t[:], 0.0), [], prevb.b)
            at = P.sbuf([128, 32], F32)
            ar = P.sbuf([128, 32], F32)
            a3 = [P.sbuf([128, 32], BF16) for _ in range(3)]
            def mk_ctx():
                return dict(abig=[P.sbuf([128, 32, 128], BF16) for _ in range(2)],
                            abm=[P.sbuf([128, 32, 128], BF16) for _ in range(2)],
                            eacs=P.sbuf([128, 32], F32), cdec=P.sbuf([128, 32], F32),
                            xdt=P.sbuf([128, 2048], BF16), xdts=P.sbuf([128, 2048], BF16), xsd=P.sbuf([128, 2048], BF16))
            ctxr = Ring([mk_ctx() for _ in range(2)])
            acs = P.sbuf([128, 32], F32)
            dst = P.sbuf([128, 32], F32)
            y = P.sbuf([128, 2048], F32, n=4)
            yn = P.sbuf([128, 2048], BF16, n=4)
            junkr = Ring([P.sbuf([128, 512], F32) for _ in range(2)])
            ss = P.sbuf([128, 4], F32)
            rstd = P.sbuf([128, 4], F32)
            t1r = Ring([P.sbuf([128, 512], F32) for _ in range(2)])
            cbsr = Ring([P.sbuf([128, 128], F32) for _ in range(2)])
            er = Ring([P.sbuf([128, 4, 128], F32) for _ in range(3)])
            mtr = Ring([P.sbuf([128, 4, 128], BF16) for _ in range(4)])
            ynst = Ring([P.sbuf([128, 16, 128], BF16, n=4, dma=True) for _ in range(2)])
            misc = P.psum([128, 512], F32)
            acs_ps = TB(misc.t[:, 0:32]); acs_ps.b = misc.b
            last_ps = TB(misc.t[:, 32:64]); last_ps.b = misc.b
            cb_ps = TB(misc.t[:, 128:256]); cb_ps.b = misc.b
            dpr = Ring([P.psum([128, 512], F32) for i in range(2)])
            ydr = Ring([P.psum([128, 512], F32) for i in range(2)])
            yo_ps = P.psum([128, 512], F32)
            st_ps = P.psum([128, 512], F32)
            tpr = Ring([P.psum([128, 512], BF16) for _ in range(1)])
            bc3 = lambda ap2, n: ap2.unsqueeze(2).to_broadcast([128, n, 64])
            v3 = lambda ap2: ap2.rearrange("p (h d) -> p h d", d=64)
            nch = getattr(self, 'S_NCH', NBLK)
            loaded = {}
            ctxs = {}

            def load(c):
                r0 = c * 128
                t = dict(xs=xsr.next(), bt=btr.next(), bT=bTr.next(), cT=cTr.next(), dt=dtr.next(), zs=zsr.next())
                P.dma("sync", [(t["xs"].t[:], xsS.t[r0:r0 + 128, :])], xsS.b, t["xs"].b, t["xs"].ds)
                P.dma("sync", [(t["bt"].t[:], BtokS.t[r0:r0 + 128, :])], BtokS.b, t["bt"].b, t["bt"].ds)
                P.dma("sync", [(t["bT"].t[:], BTS.t[:, :, r0:r0 + 128].rearrange("g p t -> p g t"))], BTS.b, t["bT"].b, t["bT"].ds)
                P.dma("sync", [(t["cT"].t[:], CTS.t[:, :, r0:r0 + 128].rearrange("g p t -> p g t"))], CTS.b, t["cT"].b, t["cT"].ds)
                P.dma("sync", [(t["dt"].t[:], dtS.t[r0:r0 + 128, :])], dtS.b, t["dt"].b, t["dt"].ds)
                P.dma("sync", [(t["zs"].t[:], zsS.t[r0:r0 + 128, :])], zsS.b, t["zs"].b, t["zs"].ds)
                loaded[c] = t

            def front(c):
                t = loaded[c]
                xs, dt = t["xs"], t["dt"]
                X = ctxr.next()
                ctxs[c] = X
                abig, abm, eacs, cdec, xdt, xdts, xsd = (X[k_] for k_ in ("abig", "abm", "eacs", "cdec", "xdt", "xdts", "xsd"))
                P.op("vector", TT(at.t[:], dt.t[:], abc.t[:], ALU.mult), dt.b + abc.b, at.b)
                P.op("vector", CP(a3[0].t[:], at.t[:]), at.b, a3[0].b)
                P.op("vector", TT(ar.t[:], at.t[:], a3[0].t[:], ALU.subtract), at.b + a3[0].b, ar.b)
                P.op("vector", CP(a3[1].t[:], ar.t[:]), ar.b, a3[1].b)
                P.op("vector", TT(ar.t[:], ar.t[:], a3[1].t[:], ALU.subtract), a3[1].b, ar.b)
                P.op("vector", CP(a3[2].t[:], ar.t[:]), ar.b, a3[2].b)
                for i3 in range(3):
                    P.op("tensor", MM(acs_ps.t, trib.t[:], a3[i3].t[:], i3 == 0, i3 == 2), trib.b + a3[i3].b, misc.b,
                         mark=(i3 == 2))
                for i3 in range(3):
                    P.op("tensor", MM(last_ps.t, ones.t[:], a3[i3].t[:], i3 == 0, i3 == 2), ones.b + a3[i3].b, misc.b,
                         mark=(i3 == 2))
                P.op("vector", CP(acs.t[:], acs_ps.t), [], acs.b + misc.b)
                for i3 in range(2):
                    P.op("scalar", ACT(abig[i3].t[:], a3[i3].t[:, :].unsqueeze(2).to_broadcast([128, 32, 128]), AF.Copy),
                         a3[i3].b, abig[i3].b)
                    P.op("gpsimd", TT(abm[i3].t[:], a3[i3].t[:, :].unsqueeze(2).to_broadcast([128, 32, 128]),
                                      trin.t[:, :].unsqueeze(1).to_broadcast([128, 32, 128]), ALU.mult),
                         a3[i3].b + trin.b, abm[i3].b)
                P.op("scalar", ACT(eacs.t[:], acs.t[:], AF.Exp), acs.b, eacs.b)
                P.op("scalar", ACT(cdec.t[:], last_ps.t, AF.Exp), [], cdec.b + misc.b)
                P.op("vector", TT(dst.t[:], last_ps.t, acs.t[:], ALU.subtract), acs.b, dst.b + misc.b)
                P.op("scalar", ACT(dst.t[:], dst.t[:], AF.Exp), dst.b, dst.b)
                P.op("vector", TT(v3(xdt.t[:]), v3(xs.t[:]), bc3(dt.t[:, :], 32), ALU.mult), xs.b + dt.b, xdt.b)
                P.op("gpsimd", TT(v3(xdts.t[:]), v3(xdt.t[:]), bc3(dst.t[:, :], 32), ALU.mult), xdt.b + dst.b, xdts.b)
                P.op("gpsimd", TT(xsd.t[:], xs.t[:], dsk.t[:], ALU.mult), xs.b + dsk.b, xsd.b)

            load(0)
            front(0)
            for c in range(nch):
                r0 = c * 128
                if c + 1 < nch:
                    load(c + 1)
                    front(c + 1)
                t = loaded.pop(c)
                X = ctxs.pop(c)
                abig, abm, eacs, cdec, xdt, xdts, xsd = (X[k_] for k_ in ("abig", "abm", "eacs", "cdec", "xdt", "xdts", "xsd"))
                xs, bt, bT, cT, dt, zs = t["xs"], t["bt"], t["bT"], t["cT"], t["dt"], t["zs"]
                def pe_D(g):
                    P.op("tensor", MM(cb_ps.t, bT.t[:, g, :], cT.t[:, g, :]), bT.b + cT.b, misc.b)
                    dd = []
                    for hb4 in range(2):
                        dps = dpr.next()
                        dd.append(dps)
                        for j in range(4):
                            hd = g * 8 + hb4 * 4 + j
                            sl = dps.t[:, j * 128:(j + 1) * 128]
                            P.op("tensor", MM(sl, abig[0].t[:, hd, :], trib.t[:], True, False), abig[0].b + trib.b, dps.b, mark=False)
                            P.op("tensor", MM(sl, abig[1].t[:, hd, :], trib.t[:], False, False), abig[1].b, dps.b, mark=False)
                            P.op("tensor", MM(sl, abm[0].t[:, hd, :], ones.t[:], False, False), abm[0].b + ones.b, dps.b, mark=False)
                            P.op("tensor", MM(sl, abm[1].t[:, hd, :], ones.t[:], False, False), abm[1].b, dps.b, mark=False)
                            P.op("tensor", MM(sl, identb.t[:], negm.t[:], False, True), identb.b + negm.b, dps.b, mark=(j == 3))
                    return dd

                def tail(g, yd):
                    gs = slice(g * 512, (g + 1) * 512)
                    t1 = t1r.next()
                    P.op("vector", TT(v3(t1.t[:]), v3(yo_ps.t[:, :]), bc3(eacs.t[:, g * 8:(g + 1) * 8], 8), ALU.mult),
                         yo_ps.b + eacs.b, t1.b)
                    P.op("vector", TT(y.t[:, gs], yd.t[:, :], t1.t[:], ALU.add), yd.b + t1.b, [y.b[g]])
                    P.op("vector", TT(y.t[:, gs], y.t[:, gs], zs.t[:, gs], ALU.mult), zs.b, [y.b[g]])
                    junk = junkr.next()
                    P.op("gpsimd", TT(junk.t[:], y.t[:, gs], y.t[:, gs], ALU.mult), [y.b[g]], junk.b)
                    P.op("vector", lambda e, o=junk.t[:], acc=ss.t[:, g:g + 1]: e.tensor_scalar(
                        out=o, in0=o, scalar1=1.0, scalar2=0.0, op0=ALU.mult, op1=ALU.add, accum_out=acc),
                        [], junk.b + ss.b)
                    P.op("gpsimd", TT(v3(prev.t[:, gs]), v3(prev.t[:, gs]), bc3(cdec.t[:, g * 8:(g + 1) * 8], 8), ALU.mult),
                         cdec.b, [prev.b[g]])
                    P.op("vector", TT(prev.t[:, gs], prev.t[:, gs], st_ps.t[:, :], ALU.add), st_ps.b, [prev.b[g]])
                    P.op("scalar", ACT(prevb.t[:, gs], prev.t[:, gs], AF.Copy), [prev.b[g]], [prevb.b[g]])

                dcur = pe_D(0)
                yds = {}
                for g in range(4):
                    gs = slice(g * 512, (g + 1) * 512)
                    cbs = cbsr.next()
                    P.op("scalar", ACT(cbs.t[:], cb_ps.t, AF.Copy), [], cbs.b + misc.b)
                    mts = []
                    for hb4 in range(2):
                        dps = dcur[hb4]
                        ee = er.next()
                        P.op("scalar", ACT(ee.t[:].rearrange("p a b -> p (a b)"), dps.t[:, :], AF.Exp), dps.b, ee.b)
                        mt = mtr.next()
                        P.op("vector", TT(mt.t[:], ee.t[:], cbs.t[:, :].unsqueeze(1).to_broadcast([128, 4, 128]), ALU.mult),
                             ee.b + cbs.b, mt.b)
                        mts.append(mt)
                    if g + 1 < 4:
                        dcur = pe_D(g + 1)
                    yd = ydr.next()
                    yds[g] = yd
                    for hb4 in range(2):
                        mt = mts[hb4]
                        for j in range(4):
                            hh = hb4 * 4 + j
                            hd = g * 8 + hh
                            P.op("tensor", MM(yd.t[:, hh * 64:(hh + 1) * 64], mt.t[:, j, :], xdt.t[:, hd * 64:(hd + 1) * 64], True, False),
                                 mt.b + xdt.b, yd.b, mark=False)
                            P.op("tensor", MM(yd.t[:, hh * 64:(hh + 1) * 64], identb.t[:], xsd.t[:, hd * 64:(hd + 1) * 64], False, True),
                                 identb.b + xsd.b, yd.b, mark=(hh == 7))
                    if g >= 1:
                        tail(g - 1, yds.pop(g - 1))
                    P.op("tensor", MM(yo_ps.t[:, :], cT.t[:, g, :], prevb.t[:, gs]), cT.b + [prevb.b[g]], yo_ps.b)
                    P.op("tensor", MM(st_ps.t[:, :], bt.t[:, g * 128:(g + 1) * 128], xdts.t[:, gs]), bt.b + xdts.b, st_ps.b)
                tail(3, yds.pop(3))
                P.op("vector", TS(rstd.t[:], ss.t[:], 1.0 / 512, ALU.mult, RMS_EPS, ALU.add), ss.b, rstd.b)
                P.op("scalar", ACT(rstd.t[:], rstd.t[:], AF.Ln), rstd.b, rstd.b)
                P.op("scalar", ACT(rstd.t[:], rstd.t[:], AF.Exp, scale=-0.5), rstd.b, rstd.b)
                yst = ynst.next()
                for g in range(4):
                    gs = slice(g * 512, (g + 1) * 512)
                    P.op("vector", STT(yn.t[:, gs], y.t[:, gs], rstd.t[:, g:g + 1], ngb.t[:, gs], ALU.mult, ALU.mult),
                         [y.b[g]] + rstd.b + ngb.b, [yn.b[g]])
                    tp = tpr.next()
                    for j in range(4):
                        cc = g * 4 + j
                        P.op("tensor", TR(tp.t[:, j * 128:(j + 1) * 128], yn.t[:, cc * 128:(cc + 1) * 128], identb.t[:]),
                             [yn.b[g]] + identb.b, tp.b, mark=(j == 3))
                    P.op("scalar", ACT(yst.t[:, g * 4:(g + 1) * 4, :], tp.t[:, :].rearrange("p (c t) -> p c t", c=4), AF.Copy),
                         tp.b, [yst.b[g]])
                P.dma("sync", [(ynTS.t[:, :, r0:r0 + 128].rearrange("c p t -> p c t"), yst.t[:, :, :])],
                      yst.b, ynTS.b, yst.ss, nowaw=True)
            P.end_phase()

    def phase_C1(self, l):
        P = self.P
        E = self.ext
        hT, oTS, ynTS, h1T = (self.scr[n] for n in ("hT", "oT", "ynT", "h1T"))
        WT = 256
        with ExitStack() as st:
            P.begin_phase(st)
            w_in = E["w_in"]
            Woa = P.sbuf([128, 8, 1024], BF16, dma=True)
            Wos = P.sbuf([128, 16, 1024], BF16, dma=True)
            Wout = P.sbuf([128, 8, 1024], BF16, dma=True)
            Wga = P.sbuf([128, 8, 1024], BF16, dma=True)
            Wgs = P.sbuf([128, 8, 1024], BF16, dma=True)
            self.load_w(Woa, E["w_o_attn"][l, :, :], 8)
            self.load_w(Wos, E["w_o_ssd"][l, :, :], 16)
            self.load_w(Wout, E["w_out"][l, :, :], 8)
            self.load_w(Wga, w_in[l, :, 6240:7264], 8)
            self.load_w(Wgs, w_in[l, :, 7264:8288], 8)
            gb = P.sbuf([128, 16], F32, dma=True)
            P.dma("sync", [(gb.t[:, 0:8], E["ln1_g"][l, :, :]), (gb.t[:, 8:16], E["ln1_b"][l, :, :])], [], gb.b, gb.ds)
            otr = Ring([P.sbuf([128, 8, WT], BF16, dma=True) for _ in range(2)])
            ynr = Ring([P.sbuf([128, 16, WT], BF16, dma=True) for _ in range(2)])
            hr = Ring([P.sbuf([128, 8, WT], F32, n=8, dma=True) for _ in range(2)])
            hb = P.sbuf([128, 8, WT], BF16, n=8)
            mixed = P.sbuf([128, 8, WT], BF16, n=8)
            orr = Ring([P.sbuf([128, 8, WT], F32, n=8, dma=True) for _ in range(1)])
            sgr = Ring([P.sbuf([128, WT], F32) for _ in range(4)])
            tr_ = Ring([P.sbuf([128, WT], F32) for _ in range(4)])
            accr = Ring([P.psum([128, 512], F32) for _ in range(4)])
            R = self.ln_resources(WT)
            evi = [0]

            def proj(acc, W_tb, oc, K, rhs_tb, W):
                for k in range(K):
                    P.op("tensor", MM(acc.t[:, 0:W], W_tb.t[:, k, oc * 128:(oc + 1) * 128], rhs_tb.t[:, k, 0:W],
                                      k == 0, k == K - 1), W_tb.b + [rhs_tb.b[k % len(rhs_tb.b)]], acc.b, mark=(k == K - 1))

            for (c0, W) in tiles_of(WT):
                ot = otr.next(); yt = ynr.next(); h = hr.next()
                P.dma("sync", [(ot.t[:, :, 0:W], oTS.t[:, :, c0:c0 + W].rearrange("h p t -> p h t"))], oTS.b, ot.b, ot.ds)
                P.dma("sync", [(yt.t[:, :, 0:W], ynTS.t[:, :, c0:c0 + W].rearrange("c p t -> p c t"))], ynTS.b, yt.b, yt.ds)
                P.dma("sync", [(h.t[:, :, 0:W], hT.t[:, :, c0:c0 + W].rearrange("c p t -> p c t"))], hT.b, h.b, h.ds)
                for c in range(8):
                    evi[0] += 1
                    if evi[0] % 2:
                        P.op("scalar", ACT(hb.t[:, c, 0:W], h.t[:, c, 0:W], AF.Copy), h.b, [hb.b[c]])
                    else:
                        P.op("vector", CP(hb.t[:, c, 0:W], h.t[:, c, 0:W]), h.b, [hb.b[c]])
                for oc in range(8):
                    ga = accr.next()
                    proj(ga, Wga, oc, 8, hb, W)
                    sga = sgr.next()
                    P.op("scalar", ACT(sga.t[:, 0:W], ga.t[:, 0:W], AF.Sigmoid), ga.b, sga.b)
                    gs_ = accr.next()
                    proj(gs_, Wgs, oc, 8, hb, W)
                    sgs = sgr.next()
                    P.op("scalar", ACT(sgs.t[:, 0:W], gs_.t[:, 0:W], AF.Sigmoid), gs_.b, sgs.b)
                    ya = accr.next()
                    proj(ya, Woa, oc, 8, ot, W)
                    t1 = tr_.next()
                    P.op("vector", TT(t1.t[:, 0:W], ya.t[:, 0:W], sga.t[:, 0:W], ALU.mult), ya.b + sga.b, t1.b)
                    ys_ = accr.next()
                    proj(ys_, Wos, oc, 16, yt, W)
                    t2 = tr_.next()
                    P.op("vector", TT(t2.t[:, 0:W], ys_.t[:, 0:W], sgs.t[:, 0:W], ALU.mult), ys_.b + sgs.b, t2.b)
                    P.op("vector", TT(mixed.t[:, oc, 0:W], t1.t[:, 0:W], t2.t[:, 0:W], ALU.add), t1.b + t2.b, [mixed.b[oc]])
                for oc in range(8):
                    r = accr.next()
                    proj(r, Wout, oc, 8, mixed, W)
                    P.op("vector", STT(h.t[:, oc, 0:W], h.t[:, oc, 0:W], ALPHA, r.t[:, 0:W], ALU.mult, ALU.add),
                         r.b + [hb.b[oc]], [h.b[oc]])
                o = orr.next()
                self.ln_fm(h, o, W, gb.t[:, 0:8], gb.t[:, 8:16], R)
                P.dma("gpsimd", [(h1T.t[:, :, c0:c0 + W].rearrange("c p t -> p c t"), o.t[:, :, 0:W])],
                      o.b, h1T.b, o.ss, nowaw=True)
            P.end_phase()

    def phase_C2(self, l):
        P = self.P
        E = self.ext
        hT, h1T = (self.scr[n] for n in ("hT", "h1T"))
        last = (l == self.n_layers - 1)
        WT = 256
        with ExitStack() as st:
            P.begin_phase(st)
            Wup = P.sbuf([128, 8, 2 * DFF], BF16, dma=True)
            Wdn = P.sbuf([128, 22, 1024], BF16, dma=True)
            self.load_w(Wup, E["w_up"][l, :, :], 8, per=1)
            self.load_w(Wdn, E["w_down"][l, :, :], 22, per=4)
            gb = P.sbuf([128, 16], F32, dma=True)
            P.dma("sync", [(gb.t[:, 0:8], E["ln2_g"][l, :, :]), (gb.t[:, 8:16], E["ln2_b"][l, :, :])], [], gb.b, gb.ds)
            cw = P.sbuf([128, 44, 3], F32, dma=True)
            cb = P.sbuf([128, 44], F32, dma=True)
            P.dma("sync", [(cw.t[:], E["ffn_cw"][l, :, :, :])], [], cw.b, cw.ds)
            P.dma("sync", [(cb.t[:], E["ffn_cb"][l, :, :])], [], cb.b, cb.ds)
            hr = Ring([P.sbuf([128, 8, WT], F32, n=8, dma=True) for _ in range(2)])
            hb = P.sbuf([128, 8, WT], BF16, n=8)
            a = P.sbuf([128, 22, WT], BF16, n=22)
            orr = Ring([P.sbuf([128, 8, WT], F32, n=8, dma=True) for _ in range(1)])
            xbr = Ring([P.sbuf([128, WT + 2], F32) for _ in range(4)])
            tcr = Ring([P.sbuf([128, WT], F32) for _ in range(4)])
            sgr = Ring([P.sbuf([128, WT], F32) for _ in range(2)])
            halo = P.sbuf([128, 44, 2], F32, n=44)
            ostr = Ring([P.sbuf([128, 1024], F32, n=2, dma=True) for _ in range(1)]) if last else None
            accr = Ring([P.psum([128, 512], F32) for _ in range(4)])
            tpr = Ring([P.psum([128, 512], F32) for _ in range(2)]) if last else None
            R = self.ln_resources(WT)
            P.op("vector", MEMSET(halo.t[:], 0.0), [], halo.b)
            evi = [0]

            def proj(acc, W_tb, col0, K, rhs_tb, W):
                for k in range(K):
                    P.op("tensor", MM(acc.t[:, 0:W], W_tb.t[:, k, col0:col0 + 128], rhs_tb.t[:, k, 0:W],
                                      k == 0, k == K - 1), W_tb.b + [rhs_tb.b[k]], acc.b, mark=(k == K - 1))

            def conv(acc, ci, W, first):
                xb = xbr.next()
                P.op("scalar", ACT(xb.t[:, 2:2 + W], acc.t[:, 0:W], AF.Copy), acc.b, xb.b)
                P.op("vector", CP(xb.t[:, 0:2], halo.t[:, ci, :]), [halo.b[ci]], xb.b)
                if first:
                    P.op("vector", MEMSET(xb.t[:, 0:2 + PAD], 0.0), [], xb.b)
                P.op("vector", CP(halo.t[:, ci, :], xb.t[:, W:W + 2]), xb.b, [halo.b[ci]])
                tc_ = tcr.next()
                P.op("scalar", ACT(tc_.t[:, 0:W], xb.t[:, 0:W], AF.Identity, bias=cb.t[:, ci:ci + 1], scale=cw.t[:, ci, 0:1]),
                     xb.b + cw.b + cb.b, tc_.b)
                for tap in (1, 2):
                    P.op("vector", STT(tc_.t[:, 0:W], xb.t[:, tap:tap + W], cw.t[:, ci, tap:tap + 1], tc_.t[:, 0:W],
                                       ALU.mult, ALU.add), xb.b + cw.b, tc_.b)
                return tc_

            for (c0, W) in tiles_of(WT):
                nb = W // 128
                h = hr.next()
                P.dma("sync", [(h.t[:, :, 0:W], h1T.t[:, :, c0:c0 + W].rearrange("c p t -> p c t"))], h1T.b, h.b, h.ds)
                for c in range(8):
                    evi[0] += 1
                    if evi[0] % 2:
                        P.op("scalar", ACT(hb.t[:, c, 0:W], h.t[:, c, 0:W], AF.Copy), h.b, [hb.b[c]])
                    else:
                        P.op("vector", CP(hb.t[:, c, 0:W], h.t[:, c, 0:W]), h.b, [hb.b[c]])
                for c in range(22):
                    ug = accr.next()
                    proj(ug, Wup, c * 128, 8, hb, W)
                    uv = accr.next()
                    proj(uv, Wup, DFF + c * 128, 8, hb, W)
                    tg = conv(ug, c, W, c0 == 0)
                    tv = conv(uv, 22 + c, W, c0 == 0)
                    sg = sgr.next()
                    P.op("scalar", ACT(sg.t[:, 0:W], tg.t[:, 0:W], AF.Silu), tg.b, sg.b)
                    P.op("vector", TT(a.t[:, c, 0:W], sg.t[:, 0:W], tv.t[:, 0:W], ALU.mult), sg.b + tv.b, [a.b[c]])
                for oc in range(8):
                    f = accr.next()
                    proj(f, Wdn, oc * 128, 22, a, W)
                    P.op("vector", STT(h.t[:, oc, 0:W], h.t[:, oc, 0:W], ALPHA, f.t[:, 0:W], ALU.mult, ALU.add),
                         f.b + [hb.b[oc]], [h.b[oc]])
                o = orr.next()
                self.ln_fm(h, o, W, gb.t[:, 0:8], gb.t[:, 8:16], R)
                if not last:
                    P.dma("gpsimd", [(hT.t[:, :, c0:c0 + W].rearrange("c p t -> p c t"), o.t[:, :, 0:W])],
                          o.b, hT.b, o.ss, nowaw=True)
                elif c0 > 0:
                    for bi in range(nb):
                        os_ = ostr.next()
                        for half in range(2):
                            tp = tpr.next()
                            for j in range(4):
                                cc = half * 4 + j
                                P.op("tensor", TR(tp.t[:, j * 128:(j + 1) * 128], o.t[:, cc, bi * 128:(bi + 1) * 128],
                                                  self.identf.t[:]), [o.b[cc]] + self.identf.b, tp.b, mark=(j == 3))
                            P.op("scalar", ACT(os_.t[:, half * 512:(half + 1) * 512], tp.t[:, :], AF.Copy), tp.b, [os_.b[half]])
                        r0 = c0 + bi * 128 - 128
                        P.dma("gpsimd", [(self.out[r0:r0 + 128, :], os_.t[:, :])], os_.b, [], os_.ss)
            P.end_phase()


INPUT_SHAPES = {
    "xin": [LP, D], "emb_g": [128, 8], "emb_b": [128, 8],
    "c_ident": [128, 128], "c_tri": [128, 128], "c_sellast": [128, 128], "c_mask0": [128, 128], "c_kmask0": [128, 1], "c_negmask": [128, 128],
    "cosT2": [128, LP], "sinT2": [128, LP],
    "w_in": [DEPTH, D, 8288], "w_kpes": [DEPTH, D, 64],
    "wqb_n": [DEPTH, QL, 1024], "wqb_p": [DEPTH, QL, 512], "wqb_ps": [DEPTH, QL, 512],
    "wkvb_kn": [DEPTH, KVL, 1024], "wkvb_v": [DEPTH, KVL, 1024],
    "qng": [DEPTH, 128, 6], "kvng": [DEPTH, 128, 2],
    "w_o_attn": [DEPTH, 1024, D], "w_o_ssd": [DEPTH, 2048, D], "w_out": [DEPTH, D, D],
    "w_up": [DEPTH, D, 2 * DFF], "w_down": [DEPTH, DFF, D],
    "ssd_cw": [DEPTH, 128, 24, 4], "ssd_cb": [DEPTH, 128, 24],
    "dtb_bc": [DEPTH, 128, 32], "alog_bc": [DEPTH, 128, 32], "dskip_bc": [DEPTH, 128, 2048], "ssdng_bc": [DEPTH, 128, 2048],
    "ln1_g": [DEPTH, 128, 8], "ln1_b": [DEPTH, 128, 8], "ln2_g": [DEPTH, 128, 8], "ln2_b": [DEPTH, 128, 8],
    "ffn_cw": [DEPTH, 128, 44, 3], "ffn_cb": [DEPTH, 128, 44],
}

SCRATCH = {
    "hT": ([8, 128, LP], F32), "h1T": ([8, 128, LP], F32),
    "qT": ([NH, 192, LP], BF16), "kT": ([NH, 128, LP], BF16), "kpeT": ([64, LP], BF16),
    "v": ([NH, 128, NBLK, 128], BF16), "oT": ([NH, 128, LP], BF16),
    "xs": ([LP, 2048], BF16), "Btok": ([LP, 512], BF16), "BT": ([4, 128, LP], BF16), "CT": ([4, 128, LP], BF16),
    "dt": ([LP, 32], F32), "zs": ([LP, 2048], BF16), "ynT": ([16, 128, LP], BF16),
    "dbgY": ([LP, 2048], F32), "dbgS": ([LP, 168], F32), "dbgP": ([NBLK, 128, 2048], F32),
}


def build(n_layers=2, dbg=None, stop=None):
    nc = bass.Bass("TRN2", target_bir_lowering=False)
    with ExitStack() as gst:
        P = Prog(nc, gst)
        k = K(nc, P, n_layers, dbg)
        for nm, shp in INPUT_SHAPES.items():
            k.din(nm, shp)
        for nm, (shp, dt) in SCRATCH.items():
            k.dscr(nm, shp, dt)
        k.out = nc.dram_tensor("out", [SEQ, D], F32, kind="ExternalOutput").ap()
        k.load_consts(gst)
        seq = [("E", k.phase_E, ())]
        for l in range(n_layers):
            for nm in ("A1", "A2", "B", "S", "C1", "C2"):
                fn = getattr(k, "phase_" + nm, None)
                if fn is not None:
                    seq.append((f"{nm}_{l}", fn, (l,)))
        import os
        skip = os.environ.get("K_SKIP", "").split(",")
        for (nm, fn, args) in seq:
            if nm.split("_")[0] in skip:
                continue
            fn(*args)
            if stop == nm:
                break
    return nc, k


def host_consts():
    c = {}
    c["c_ident"] = np.eye(128, dtype=np.float32)
    kk = np.arange(128)[:, None]
    qq = np.arange(128)[None, :]
    c["c_tri"] = (kk <= qq).astype(np.float32)
    c["c_sellast"] = np.broadcast_to((kk == 127), (128, 128)).astype(np.float32).copy()
    m0 = ((qq >= PAD) & (kk >= PAD) & (kk <= qq)) | ((qq < PAD) & (kk == qq))
    c["c_mask0"] = m0.astype(np.float32)
    c["c_kmask0"] = (np.arange(128) >= PAD).astype(np.float32)[:, None].copy()
    c["c_negmask"] = np.where(qq < kk, np.float32(-30000.0), np.float32(0.0)).astype(np.float32)
    return c


def pm(v, nchunk):
    return np.ascontiguousarray(np.asarray(v, np.float32).reshape(nchunk, 128).T)


def rope_tables_T():
    inv_freq = (1.0 / (np.float32(10000.0) ** (np.arange(0, 64, 2, dtype=np.float32) / np.float32(64)))).astype(np.float32)
    pos = np.maximum(np.arange(LP, dtype=np.float32) - np.float32(PAD), np.float32(0))
    ang = (pos[:, None] * inv_freq[None, :]).astype(np.float32)
    ang = np.concatenate([ang, ang], axis=-1)
    cos = np.cos(ang).astype(np.float32).T
    sin = np.sin(ang).astype(np.float32).T
    sgn = np.concatenate([-np.ones(32, np.float32), np.ones(32, np.float32)])[:, None]
    sins = sin * sgn
    return (np.ascontiguousarray(np.concatenate([cos, cos], 0)), np.ascontiguousarray(np.concatenate([sins, sins], 0)))


def prep_shared(inp):
    f = lambda a: np.ascontiguousarray(np.asarray(a, np.float32))
    sh = dict(host_consts())
    sh["cosT2"], sh["sinT2"] = rope_tables_T()
    sh["emb_g"] = pm(inp["emb_ln_g"], 8)
    sh["emb_b"] = pm(inp["emb_ln_b"], 8)
    w_in = f(inp["w_in"])
    sh["w_in"] = w_in
    sh["w_kpes"] = f(np.concatenate([w_in[:, :, 1056:1088], w_in[:, :, 1024:1056]], axis=-1))
    wqb = f(inp["w_q_b"]).reshape(DEPTH, QL, NH, 192)
    sh["wqb_n"] = f(wqb[..., :128].reshape(DEPTH, QL, 1024))
    sh["wqb_p"] = f(wqb[..., 128:].reshape(DEPTH, QL, 512))
    sh["wqb_ps"] = f(np.concatenate([wqb[..., 160:192], wqb[..., 128:160]], axis=-1).reshape(DEPTH, QL, 512))
    wkv = f(inp["w_kv_b"]).reshape(DEPTH, KVL, NH, 256)
    sh["wkvb_kn"] = f(wkv[..., :128].reshape(DEPTH, KVL, 1024))
    sh["wkvb_v"] = f(wkv[..., 128:].reshape(DEPTH, KVL, 1024))
    sh["qng"] = f(np.stack([pm(inp["q_norm_g"][l], 6) for l in range(DEPTH)]))
    sh["kvng"] = f(np.stack([pm(inp["kv_norm_g"][l], 2) for l in range(DEPTH)]))
    for nm in ("w_o_attn", "w_o_ssd", "w_out", "w_up", "w_down"):
        sh[nm] = f(inp[nm])
    cw = f(inp["ssd_conv_w"])
    sh["ssd_cw"] = f(cw.reshape(DEPTH, 4, 24, 128).transpose(0, 3, 2, 1))
    sh["ssd_cb"] = f(f(inp["ssd_conv_b"]).reshape(DEPTH, 24, 128).transpose(0, 2, 1))
    bc = lambda a: f(np.broadcast_to(f(a)[:, None, :], (DEPTH, 128, a.shape[-1])))
    sh["dtb_bc"] = bc(inp["dt_bias"])
    sh["alog_bc"] = bc(inp["a_log"])
    sh["dskip_bc"] = bc(np.repeat(f(inp["d_skip"]), 64, axis=-1))
    sh["ssdng_bc"] = bc(inp["ssd_norm_g"])
    for nm in ("ln1_g", "ln1_b", "ln2_g", "ln2_b"):
        sh[nm] = f(np.stack([pm(inp[nm][l], 8) for l in range(DEPTH)]))
    fw = f(inp["ffn_conv_w"])
    sh["ffn_cw"] = f(fw.reshape(DEPTH, 3, 44, 128).transpose(0, 3, 2, 1))
    sh["ffn_cb"] = f(f(inp["ffn_conv_b"]).reshape(DEPTH, 44, 128).transpose(0, 2, 1))
    return sh


def xin_of(inp, b):
    xin = np.zeros((LP, D), np.float32)
    xin[PAD:PAD + NMETA] = inp["meta_tokens"]
    xin[128:] = inp["x"][b]
    return xin


def kernel(**inp):
    sh = prep_shared(inp)
    nc, k = build()
    in_maps = []
    for c in range(8):
        m = dict(sh)
        m["xin"] = xin_of(inp, c % 4)
        in_maps.append(m)
    res = run_bass_kernel_spmd(nc, in_maps, core_ids=list(range(8)))
    out = np.stack([np.asarray(res.results[b]["out"], np.float32) for b in range(4)], axis=0)
    return out
```

```python
import math
from contextlib import ExitStack

import numpy as np
import concourse.bass as bass
import concourse.mybir as mybir
from concourse.bass_utils import run_bass_kernel_spmd

F32 = mybir.dt.float32
BF16 = mybir.dt.bfloat16
AF = mybir.ActivationFunctionType
ALU = mybir.AluOpType

D = 1024
SEQ = 8192
NMETA = 16
PAD = 112
LP = 8320
NBLK = 65
NH = 8
QL = 768
KVL = 256
DFF = 2816
DEPTH = 2
ALPHA = (2 * DEPTH) ** 0.25
LN_EPS = 1e-5
RMS_EPS = 1e-6
SCALE = 192 ** -0.5

ENGS = ("sync", "scalar", "vector", "gpsimd", "tensor")
COMPUTE = ("scalar", "vector", "gpsimd", "tensor")


class Buf:
    __slots__ = ("w", "r")

    def __init__(self):
        self.w = None
        self.r = {}


class DSem:
    def __init__(self, sem):
        self.sem = sem
        self.count = 0


class TB:
    def __init__(self, t, n=1, ds=None, ss=None):
        self.t = t
        self.b = [Buf() for _ in range(n)]
        self.ds = ds
        self.ss = ss


class Ring:
    def __init__(self, items):
        self.items = items
        self.i = 0

    def next(self):
        it = self.items[self.i % len(self.items)]
        self.i += 1
        return it


class Prog:
    def __init__(self, nc, gstack):
        self.nc = nc
        self.gstack = gstack
        self.pstack = None
        self.q = {e: [] for e in ENGS}
        self.cnt = {e: 0 for e in ENGS}
        self.waited = {e: {} for e in ENGS}
        self.pending = {e: [] for e in ENGS}
        self.psem = {}
        self.nsem = 0
        for e in COMPUTE:
            self.psem[e] = self._new_sem()
        self.pool = []
        self.pool_i = 0
        self.ninstr = 0
        self.phase_i = 0
        self.nt = 0

    def _new_sem(self):
        self.nsem += 1
        return self.gstack.enter_context(self.nc.semaphore(f"sem{self.nsem}"))

    def dsem(self):
        if self.pool_i == len(self.pool):
            self.pool.append(DSem(self._new_sem()))
        d = self.pool[self.pool_i]
        self.pool_i += 1
        return d

    def sbuf(self, shape, dtype, n=1, dma=False):
        self.nt += 1
        t = self.pstack.enter_context(self.nc.sbuf_tensor(f"sb{self.nt}", list(shape), dtype))
        return TB(t, n, self.dsem() if dma else None, self.dsem() if dma else None)

    def psum(self, shape, dtype, n=1):
        self.nt += 1
        t = self.pstack.enter_context(self.nc.psum_tensor(f"ps{self.nt}", list(shape), dtype))
        return TB(t, n)

    def _wait(self, eng, tok):
        if tok is None:
            return
        sem, val, src = tok
        if src == eng and eng == "tensor":
            return
        key = id(sem)
        if self.waited[eng].get(key, 0) >= val:
            return
        self.waited[eng][key] = val
        self.q[eng].append(lambda e, s=sem, v=val: e.wait_ge(s, v))

    def _deps(self, eng, reads, writes, nowaw=False):
        for b in reads:
            self._wait(eng, b.w)
        for b in writes:
            if not nowaw:
                self._wait(eng, b.w)
            for t in b.r.values():
                self._wait(eng, t)

    def _commit(self, tok, reads, writes):
        k = id(tok[0])
        for b in reads:
            b.r[k] = tok
        for b in writes:
            b.w = tok
            b.r = {}

    def op(self, eng, fn, reads=(), writes=(), mark=True):
        self.ninstr += 1
        self._deps(eng, reads, writes)
        if not mark:
            self.pending[eng].append((tuple(reads), tuple(writes)))
            self.q[eng].append(fn)
            return None
        self.cnt[eng] += 1
        v = self.cnt[eng]
        sem = self.psem[eng]
        self.q[eng].append(lambda e, f=fn, s=sem: f(e).then_inc(s, 1))
        tok = (sem, v, eng)
        for (r, w) in self.pending[eng]:
            self._commit(tok, r, w)
        self.pending[eng] = []
        self._commit(tok, reads, writes)
        return tok

    def dma(self, eng, pairs, reads, writes, ds, nowaw=False):
        self._deps(eng, reads, writes, nowaw)
        for (o, i) in pairs:
            self.ninstr += 1
            ds.count += 16
            self.q[eng].append(lambda e, o=o, i=i, s=ds.sem: e.dma_start(out=o, in_=i).then_inc(s, 16))
        tok = (ds.sem, ds.count, "dma")
        self._commit(tok, reads, writes)
        return tok

    def begin_phase(self, st):
        self.pstack = st
        self.pool_i = 0

    def end_phase(self):
        for d in self.pool:
            if d.count:
                self._wait("sync", (d.sem, d.count, "dma"))
        for e in COMPUTE:
            assert not self.pending[e], e
            if self.cnt[e]:
                self._wait("sync", (self.psem[e], self.cnt[e], e))
        nc = self.nc
        self.phase_i += 1
        with nc.Block() as block:
            for ename in ENGS:
                lst = self.q[ename]
                if not lst:
                    continue

                def body(e, lst=lst):
                    for f in lst:
                        f(e)
                getattr(block, ename)(body)
        self.q = {e: [] for e in ENGS}


def MM(out, lhsT, rhs, start=True, stop=True):
    return lambda e: e.matmul(out, lhsT=lhsT, rhs=rhs, start=start, stop=stop)


def TR(out, in_, ident):
    return lambda e: e.transpose(out=out, in_=in_, identity=ident)


def ACT(out, in_, func, bias=None, scale=None):
    kw = {}
    if bias is not None:
        kw["bias"] = bias
    if scale is not None:
        kw["scale"] = scale
    return lambda e: e.activation(out=out, in_=in_, func=func, **kw)


def TT(out, a, b, op):
    return lambda e: e.tensor_tensor(out=out, in0=a, in1=b, op=op)


def TS(out, a, s1, op0, s2=None, op1=None):
    if op1 is None:
        return lambda e: e.tensor_scalar(out=out, in0=a, scalar1=s1, scalar2=None, op0=op0)
    return lambda e: e.tensor_scalar(out=out, in0=a, scalar1=s1, scalar2=s2, op0=op0, op1=op1)


def STT(out, in0, scalar, in1, op0, op1):
    return lambda e: e.scalar_tensor_tensor(out=out, in0=in0, scalar=scalar, in1=in1, op0=op0, op1=op1)


def CP(out, in_):
    return lambda e: e.tensor_copy(out=out, in_=in_)


def MEMSET(ap, v):
    return lambda e: e.memset(ap, v)


def tiles_of(width):
    t = [(0, 128)]
    c = 128
    while c < LP:
        t.append((c, width))
        c += width
    return t


class K:
    def __init__(self, nc, P, n_layers, dbg):
        self.nc = nc
        self.P = P
        self.n_layers = n_layers
        self.dbg = dbg or ()
        self.ext = {}
        self.scr = {}

    def din(self, name, shape, dt=F32):
        ap = self.nc.dram_tensor(name, list(shape), dt, kind="ExternalInput").ap()
        self.ext[name] = ap
        return ap

    def dscr(self, name, shape, dt):
        kind = "ExternalOutput" if name in self.dbg else "Internal"
        ap = self.nc.dram_tensor(name, list(shape), dt, kind=kind).ap()
        self.scr[name] = TB(ap, 1)
        return self.scr[name]

    def load_consts(self, st):
        P = self.P
        P.pstack = st
        c = self.ext
        self.identf = P.sbuf([128, 128], F32, dma=True)
        self.tri_f = P.sbuf([128, 128], F32, dma=True)
        self.sellast = P.sbuf([128, 128], F32, dma=True)
        self.mask0 = P.sbuf([128, 128], F32, dma=True)
        self.kmask0 = P.sbuf([128, 1], F32, dma=True)
        self.identb = P.sbuf([128, 128], BF16)
        self.ones_b = P.sbuf([128, 128], BF16)
        self.tri_b = P.sbuf([128, 128], BF16)
        self.mask0_b = P.sbuf([128, 128], BF16)
        for tb, nm in ((self.identf, "c_ident"), (self.tri_f, "c_tri"), (self.sellast, "c_sellast"),
                       (self.mask0, "c_mask0")):
            P.dma("sync", [(tb.t[:], c[nm][:, :])], [], tb.b, tb.ds)
        P.dma("sync", [(self.kmask0.t[:], c["c_kmask0"][:, :])], [], self.kmask0.b, self.kmask0.ds)
        P.op("vector", CP(self.identb.t[:], self.identf.t[:]), self.identf.b, self.identb.b)
        P.op("vector", CP(self.tri_b.t[:], self.tri_f.t[:]), self.tri_f.b, self.tri_b.b)
        P.op("vector", CP(self.mask0_b.t[:], self.mask0.t[:]), self.mask0.b, self.mask0_b.b)
        P.op("vector", MEMSET(self.ones_b.t[:], 1.0), [], self.ones_b.b)

    def load_w(self, dst, src, kchunks, per=4):
        P = self.P
        for k0 in range(0, kchunks, per):
            k1 = min(kchunks, k0 + per)
            P.dma("gpsimd", [(dst.t[:, k0:k1, :], src[k0 * 128:k1 * 128, :].rearrange("(k p) n -> p k n", p=128))],
                  [], dst.b, dst.ds)

    def ln_fm(self, s, out, W, g, b, R):
        P = self.P
        ones = self.ones_b
        sum_ps, ssq_ps = R["sum"], R["ssq"]
        for c in range(8):
            sb = R["sb"].next()
            sq = R["sq"].next()
            P.op("scalar", ACT(sb.t[:, 0:W], s.t[:, c, 0:W], AF.Copy), [s.b[c]], sb.b)
            P.op("scalar", ACT(sq.t[:, 0:W], s.t[:, c, 0:W], AF.Square), [s.b[c]], sq.b)
            P.op("tensor", MM(sum_ps.t[:, 0:W], ones.t[:], sb.t[:, 0:W], c == 0, c == 7), sb.b + ones.b, sum_ps.b,
                 mark=False)
            P.op("tensor", MM(ssq_ps.t[:, 0:W], ones.t[:], sq.t[:, 0:W], c == 0, c == 7), sq.b + ones.b, ssq_ps.b,
                 mark=True)
        mean, var, rstd = R["mean"], R["var"], R["rstd"]
        P.op("vector", TS(mean.t[:, 0:W], sum_ps.t[:, 0:W], 1.0 / D, ALU.mult), sum_ps.b, mean.b)
        P.op("vector", TS(var.t[:, 0:W], ssq_ps.t[:, 0:W], 1.0 / D, ALU.mult), ssq_ps.b, var.b)
        msq = R["msq"]
        P.op("vector", TT(msq.t[:, 0:W], mean.t[:, 0:W], mean.t[:, 0:W], ALU.mult), mean.b, msq.b)
        P.op("vector", TT(var.t[:, 0:W], var.t[:, 0:W], msq.t[:, 0:W], ALU.subtract), var.b + msq.b, var.b)
        P.op("vector", TS(var.t[:, 0:W], var.t[:, 0:W], LN_EPS, ALU.add), var.b, var.b)
        P.op("scalar", ACT(rstd.t[:, 0:W], var.t[:, 0:W], AF.Ln), var.b, rstd.b)
        P.op("scalar", ACT(rstd.t[:, 0:W], rstd.t[:, 0:W], AF.Exp, scale=-0.5), rstd.b, rstd.b)
        for c in range(8):
            tmp = R["tmp"].next()
            P.op("vector", TT(tmp.t[:, 0:W], s.t[:, c, 0:W], mean.t[:, 0:W], ALU.subtract), [s.b[c]] + mean.b, tmp.b)
            P.op("vector", TT(tmp.t[:, 0:W], tmp.t[:, 0:W], rstd.t[:, 0:W], ALU.mult), tmp.b + rstd.b, tmp.b)
            P.op("scalar", ACT(out.t[:, c, 0:W], tmp.t[:, 0:W], AF.Identity, bias=b[:, c:c + 1], scale=g[:, c:c + 1]),
                 tmp.b, [out.b[c]])

    def ln_resources(self, wmax=512):
        P = self.P
        R = {}
        R["sum"] = P.psum([128, 512], F32)
        R["ssq"] = P.psum([128, 512], F32)
        R["sb"] = Ring([P.sbuf([128, wmax], BF16) for _ in range(2)])
        R["sq"] = Ring([P.sbuf([128, wmax], BF16) for _ in range(2)])
        for nm in ("mean", "var", "msq", "rstd"):
            R[nm] = P.sbuf([128, wmax], F32)
        R["tmp"] = Ring([P.sbuf([128, wmax], F32) for _ in range(2)])
        return R

    def phase_E(self):
        P = self.P
        hT = self.scr["hT"]
        with ExitStack() as st:
            P.begin_phase(st)
            xin = self.ext["xin"]
            gb = P.sbuf([128, 16], F32, dma=True)
            P.dma("sync", [(gb.t[:, 0:8], self.ext["emb_g"][:, :]), (gb.t[:, 8:16], self.ext["emb_b"][:, :])],
                  [], gb.b, gb.ds)
            xr = Ring([P.sbuf([128, 4, 1024], F32, dma=True) for _ in range(2)])
            sr = Ring([P.sbuf([128, 8, 512], F32, n=8) for _ in range(2)])
            orr = Ring([P.sbuf([128, 8, 512], F32, n=8, dma=True) for _ in range(2)])
            tpr = Ring([P.psum([128, 512], F32) for _ in range(2)])
            R = self.ln_resources()
            for (c0, W) in tiles_of(512):
                nb = W // 128
                x = xr.next()
                P.dma("sync", [(x.t[:, 0:nb, :], xin[c0:c0 + W, :].rearrange("(b p) f -> p b f", p=128))],
                      [], x.b, x.ds)
                s = sr.next()
                for c in range(8):
                    tp = tpr.next()
                    for bi in range(nb):
                        P.op("tensor", TR(tp.t[:, bi * 128:(bi + 1) * 128], x.t[:, bi, c * 128:(c + 1) * 128],
                                          self.identf.t[:]), x.b + self.identf.b, tp.b, mark=(bi == nb - 1))
                    P.op("vector" if c % 2 else "scalar",
                         CP(s.t[:, c, 0:W], tp.t[:, 0:W]) if c % 2 else ACT(s.t[:, c, 0:W], tp.t[:, 0:W], AF.Copy),
                         tp.b, [s.b[c]])
                o = orr.next()
                self.ln_fm(s, o, W, gb.t[:, 0:8], gb.t[:, 8:16], R)
                P.dma("gpsimd", [(hT.t[:, :, c0:c0 + W].rearrange("c p t -> p c t"), o.t[:, :, 0:W])],
                      o.b, hT.b, o.ss, nowaw=True)
            P.end_phase()


    def phase_A1(self, l):
        P = self.P
        E = self.ext
        hT, qT, kT, kpeT, vS = (self.scr[n] for n in ("hT", "qT", "kT", "kpeT", "v"))
        with ExitStack() as st:
            P.begin_phase(st)
            w_in = E["w_in"]
            Wql = P.sbuf([128, 8, QL], BF16, dma=True)
            Wkvl = P.sbuf([128, 8, KVL], BF16, dma=True)
            Wkpe = P.sbuf([128, 8, 64], BF16, dma=True)
            Wkpes = P.sbuf([128, 8, 64], BF16, dma=True)
            Wqn = P.sbuf([128, 6, 1024], BF16, dma=True)
            Wqp = P.sbuf([128, 6, 512], BF16, dma=True)
            Wqps = P.sbuf([128, 6, 512], BF16, dma=True)
            Wkn = P.sbuf([128, 2, 1024], BF16, dma=True)
            Wv = P.sbuf([128, 2, 1024], BF16, dma=True)
            self.load_w(Wql, w_in[l, :, 0:768], 8)
            self.load_w(Wkvl, w_in[l, :, 768:1024], 8, per=8)
            self.load_w(Wkpe, w_in[l, :, 1024:1088], 8, per=8)
            self.load_w(Wkpes, E["w_kpes"][l, :, :], 8, per=8)
            self.load_w(Wqn, E["wqb_n"][l, :, :], 6, per=3)
            self.load_w(Wqp, E["wqb_p"][l, :, :], 6, per=6)
            self.load_w(Wqps, E["wqb_ps"][l, :, :], 6, per=6)
            self.load_w(Wkn, E["wkvb_kn"][l, :, :], 2)
            self.load_w(Wv, E["wkvb_v"][l, :, :], 2)
            ng = P.sbuf([128, 8], F32, dma=True)
            P.dma("sync", [(ng.t[:, 0:6], E["qng"][l, :, :]), (ng.t[:, 6:8], E["kvng"][l, :, :])], [], ng.b, ng.ds)
            hr = Ring([P.sbuf([128, 8, 512], F32, dma=True) for _ in range(2)])
            hbr = Ring([P.sbuf([128, 8, 512], BF16, n=8) for _ in range(2)])
            csr = Ring([P.sbuf([128, 2, 512], F32, dma=True) for _ in range(2)])
            qlat = P.sbuf([128, 6, 512], F32, n=6)
            qn = P.sbuf([128, 6, 512], BF16, n=6)
            kvlat = P.sbuf([128, 2, 512], F32, n=2)
            kvn = P.sbuf([128, 2, 512], BF16, n=2)
            sqr = Ring([P.sbuf([128, 512], BF16) for _ in range(2)])
            rstd = P.sbuf([128, 512], F32)
            t1r = Ring([P.sbuf([128, 512], F32) for _ in range(2)])
            t2r = Ring([P.sbuf([128, 512], F32) for _ in range(2)])
            stg = Ring([P.sbuf([128, 512], BF16, dma=True) for _ in range(4)])
            vst = Ring([P.sbuf([128, 1024], BF16, n=2, dma=True) for _ in range(2)])
            accr = Ring([P.psum([128, 512], F32) for _ in range(4)])
            ss_ps = P.psum([128, 512], F32)
            ones = self.ones_b
            evi = [0]

            def evac(out_ap, in_ap, reads, writes):
                evi[0] += 1
                if evi[0] % 2:
                    P.op("scalar", ACT(out_ap, in_ap, AF.Copy), reads, writes)
                else:
                    P.op("vector", CP(out_ap, in_ap), reads, writes)

            def rms(acc_list_fn, nch, lat, dst, W, gcol0, nfeat):
                for c in range(nch):
                    acc = acc_list_fn(c)
                    sq = sqr.next()
                    P.op("scalar", ACT(lat.t[:, c, 0:W], acc.t[:, 0:W], AF.Copy), acc.b, [lat.b[c]])
                    P.op("scalar", ACT(sq.t[:, 0:W], acc.t[:, 0:W], AF.Square), acc.b, sq.b)
                    P.op("tensor", MM(ss_ps.t[:, 0:W], ones.t[:], sq.t[:, 0:W], c == 0, c == nch - 1),
                         sq.b + ones.b, ss_ps.b, mark=(c == nch - 1))
                P.op("vector", TS(rstd.t[:, 0:W], ss_ps.t[:, 0:W], 1.0 / nfeat, ALU.mult, RMS_EPS, ALU.add),
                     ss_ps.b, rstd.b)
                P.op("scalar", ACT(rstd.t[:, 0:W], rstd.t[:, 0:W], AF.Ln), rstd.b, rstd.b)
                P.op("scalar", ACT(rstd.t[:, 0:W], rstd.t[:, 0:W], AF.Exp, scale=-0.5), rstd.b, rstd.b)
                for c in range(nch):
                    P.op("vector", STT(dst.t[:, c, 0:W], lat.t[:, c, 0:W], ng.t[:, gcol0 + c:gcol0 + c + 1],
                                       rstd.t[:, 0:W], ALU.mult, ALU.mult), [lat.b[c]] + rstd.b + ng.b, [dst.b[c]])

            def proj(acc, W_tb, ncols0, ncols, K, rhs_tb, W, M=128):
                for k in range(K):
                    P.op("tensor", MM(acc.t[0:M, 0:W], W_tb.t[:, k, ncols0:ncols0 + ncols], rhs_tb.t[:, k, 0:W],
                                      k == 0, k == K - 1), W_tb.b + [rhs_tb.b[k]], acc.b, mark=(k == K - 1))

            def rope(acc1, acc2, cs, W, M, outs):
                t1 = t1r.next()
                t2 = t2r.next()
                P.op("vector", TT(t1.t[0:M, 0:W], acc1.t[0:M, 0:W], cs.t[0:M, 0, 0:W], ALU.mult), acc1.b + cs.b, t1.b)
                P.op("vector", TT(t2.t[0:M, 0:W], acc2.t[0:M, 0:W], cs.t[0:M, 1, 0:W], ALU.mult), acc2.b + cs.b, t2.b)
                sg = stg.next()
                P.op("vector", TT(sg.t[0:M, 0:W], t1.t[0:M, 0:W], t2.t[0:M, 0:W], ALU.add), t1.b + t2.b, sg.b)
                for (dst_tb, dst_ap, p0, p1) in outs:
                    P.dma("gpsimd", [(dst_ap, sg.t[p0:p1, 0:W])], sg.b, dst_tb.b, sg.ss, nowaw=True)

            for (c0, W) in tiles_of(512):
                nb = W // 128
                h = hr.next()
                P.dma("sync", [(h.t[:, :, 0:W], hT.t[:, :, c0:c0 + W].rearrange("c p t -> p c t"))], hT.b, h.b, h.ds)
                cs = csr.next()
                P.dma("sync", [(cs.t[:, 0, 0:W], E["cosT2"][:, c0:c0 + W]), (cs.t[:, 1, 0:W], E["sinT2"][:, c0:c0 + W])],
                      [], cs.b, cs.ds)
                hb = hbr.next()
                for c in range(8):
                    evac(hb.t[:, c, 0:W], h.t[:, c, 0:W], h.b, [hb.b[c]])

                def ql_acc(c):
                    acc = accr.next()
                    proj(acc, Wql, c * 128, 128, 8, hb, W)
                    return acc
                rms(ql_acc, 6, qlat, qn, W, 0, QL)
                for hd in range(NH):
                    acc = accr.next()
                    proj(acc, Wqn, hd * 128, 128, 6, qn, W)
                    sg = stg.next()
                    evac(sg.t[:, 0:W], acc.t[:, 0:W], acc.b, sg.b)
                    P.dma("gpsimd", [(qT.t[hd, 0:128, c0:c0 + W], sg.t[:, 0:W])], sg.b, qT.b, sg.ss, nowaw=True)
                for pr in range(4):
                    a1 = accr.next()
                    proj(a1, Wqp, pr * 128, 128, 6, qn, W)
                    a2 = accr.next()
                    proj(a2, Wqps, pr * 128, 128, 6, qn, W)
                    rope(a1, a2, cs, W, 128, [(qT, qT.t[2 * pr, 128:192, c0:c0 + W], 0, 64),
                                              (qT, qT.t[2 * pr + 1, 128:192, c0:c0 + W], 64, 128)])

                def kv_acc(c):
                    acc = accr.next()
                    proj(acc, Wkvl, c * 128, 128, 8, hb, W)
                    return acc
                rms(kv_acc, 2, kvlat, kvn, W, 6, KVL)
                for hd in range(NH):
                    acc = accr.next()
                    proj(acc, Wkn, hd * 128, 128, 2, kvn, W)
                    sg = stg.next()
                    evac(sg.t[:, 0:W], acc.t[:, 0:W], acc.b, sg.b)
                    P.dma("gpsimd", [(kT.t[hd, :, c0:c0 + W], sg.t[:, 0:W])], sg.b, kT.b, sg.ss, nowaw=True)
                for bi in range(nb):
                    vs = vst.next()
                    for half in range(2):
                        acc = accr.next()
                        for k in range(2):
                            P.op("tensor", MM(acc.t[:, :], kvn.t[:, k, bi * 128:(bi + 1) * 128],
                                              Wv.t[:, k, half * 512:(half + 1) * 512], k == 0, k == 1),
                                 Wv.b + [kvn.b[k]], acc.b, mark=(k == 1))
                        evac(vs.t[:, half * 512:(half + 1) * 512], acc.t[:, :], acc.b, [vs.b[half]])
                    blk = c0 // 128 + bi
                    P.dma("gpsimd", [(vS.t[:, :, blk, :].rearrange("h p d -> p h d"),
                                      vs.t[:, :].rearrange("p (h d) -> p h d", h=NH))], vs.b, vS.b, vs.ss, nowaw=True)
                a1 = accr.next()
                proj(a1, Wkpe, 0, 64, 8, hb, W, M=64)
                a2 = accr.next()
                proj(a2, Wkpes, 0, 64, 8, hb, W, M=64)
                rope(a1, a2, cs, W, 64, [(kpeT, kpeT.t[:, c0:c0 + W], 0, 64)])
            P.end_phase()


    def phase_B(self, l):
        P = self.P
        qT, kT, kpeT, vS, oT = (self.scr[n] for n in ("qT", "kT", "kpeT", "v", "oT"))
        ones = self.ones_b
        with ExitStack() as st:
            P.begin_phase(st)
            kpe = P.sbuf([64, LP], BF16, dma=True)
            P.dma("sync", [(kpe.t[:, :], kpeT.t[:, :])], kpeT.b, kpe.b, kpe.ds)
            knr = Ring([P.sbuf([128, LP], BF16, dma=True) for _ in range(2)])
            vr = Ring([P.sbuf([128, NBLK, 128], BF16, dma=True) for _ in range(2)])
            qr = Ring([P.sbuf([128, 2, 512], BF16, dma=True) for _ in range(3)])
            ptr = Ring([P.sbuf([128, 512], BF16) for _ in range(6)])
            rsr = Ring([P.sbuf([128, 512], F32) for _ in range(2)])
            osr = Ring([P.sbuf([128, 512], BF16, dma=True) for _ in range(2)])
            spr = Ring([P.psum([128, 512], F32) for _ in range(4)])
            opr = Ring([P.psum([128, 512], F32) for _ in range(2)])
            smr = Ring([P.psum([128, 512], F32) for _ in range(2)])
            tiles = tiles_of(512)
            for hd in range(NH):
                kn = knr.next()
                P.dma("sync", [(kn.t[:, :], kT.t[hd, :, :])], kT.b, kn.b, kn.ds)
                v = vr.next()
                P.dma("sync", [(v.t[:, :, :], vS.t[hd, :, :, :])], vS.b, v.b, v.ds)
                for (c0, W) in tiles:
                    q = qr.next()
                    P.dma("sync", [(q.t[:, 0, 0:W], qT.t[hd, 0:128, c0:c0 + W]),
                                   (q.t[0:64, 1, 0:W], qT.t[hd, 128:192, c0:c0 + W])], qT.b, q.b, q.ds)
                    o_ps = opr.next()
                    sm_ps = smr.next()
                    nkb = (c0 + W) // 128
                    units = []
                    for kb in range(nkb):
                        qo = kb * 128 - c0 if kb * 128 >= c0 else 0
                        units.append((kb, qo))

                    def qk(u):
                        kb, qo = u
                        sp = spr.next()
                        P.op("tensor", MM(sp.t[:, qo:W], kn.t[:, kb * 128:(kb + 1) * 128], q.t[:, 0, qo:W], True, False),
                             kn.b + q.b, sp.b, mark=False)
                        P.op("tensor", MM(sp.t[:, qo:W], kpe.t[0:64, kb * 128:(kb + 1) * 128], q.t[0:64, 1, qo:W],
                                          False, True), kpe.b + q.b, sp.b, mark=True)
                        return sp

                    def soft(u, sp):
                        kb, qo = u
                        pt = ptr.next()
                        P.op("scalar", ACT(pt.t[:, qo:W], sp.t[:, qo:W], AF.Exp, scale=SCALE), sp.b, pt.b)
                        if kb == 0 and c0 == 0:
                            P.op("vector", TT(pt.t[:, 0:128], pt.t[:, 0:128], self.mask0_b.t[:], ALU.mult),
                                 pt.b + self.mask0_b.b, pt.b)
                        elif kb == 0:
                            P.op("vector", TS(pt.t[:, 0:W], pt.t[:, 0:W], self.kmask0.t[:, 0:1], ALU.mult),
                                 pt.b + self.kmask0.b, pt.b)
                        elif kb * 128 >= c0:
                            P.op("vector", TT(pt.t[:, qo:qo + 128], pt.t[:, qo:qo + 128], self.tri_b.t[:], ALU.mult),
                                 pt.b + self.tri_b.b, pt.b)
                        return pt

                    def pv(u, pt, first, last):
                        kb, qo = u
                        P.op("tensor", MM(o_ps.t[:, qo:W], v.t[:, kb, :], pt.t[:, qo:W], first, last),
                             v.b + pt.b, o_ps.b, mark=False)
                        P.op("tensor", MM(sm_ps.t[:, qo:W], ones.t[:], pt.t[:, qo:W], first, last),
                             ones.b + pt.b, sm_ps.b, mark=True)

                    LA = 2
                    sps = [qk(units[i]) for i in range(min(LA, len(units)))]
                    for i, u in enumerate(units):
                        if i + LA < len(units):
                            sps.append(qk(units[i + LA]))
                        pt = soft(u, sps[i])
                        pv(u, pt, i == 0, i == len(units) - 1)
                    rs = rsr.next()
                    P.op("vector", lambda e, o=rs.t[:, 0:W], i=sm_ps.t[:, 0:W]: e.reciprocal(out=o, in_=i), sm_ps.b, rs.b)
                    og = osr.next()
                    P.op("vector", TT(og.t[:, 0:W], o_ps.t[:, 0:W], rs.t[:, 0:W], ALU.mult), o_ps.b + rs.b, og.b)
                    P.dma("gpsimd", [(oT.t[hd, :, c0:c0 + W], og.t[:, 0:W])], og.b, oT.b, og.ss, nowaw=True)
            P.end_phase()


    def phase_A2(self, l):
        P = self.P
        E = self.ext
        hT, xsS, BtokS, BTS, CTS, dtS, zsS = (self.scr[n] for n in ("hT", "xs", "Btok", "BT", "CT", "dt", "zs"))
        with ExitStack() as st:
            P.begin_phase(st)
            w_in = E["w_in"]
            Wz = P.sbuf([128, 8, 2048], BF16, dma=True)
            Wx = P.sbuf([128, 8, 3072], BF16, dma=True)
            Wdt = P.sbuf([128, 8, 32], BF16, dma=True)
            self.load_w(Wz, w_in[l, :, 1088:3136], 8, per=2)
            self.load_w(Wx, w_in[l, :, 3136:6208], 8, per=2)
            self.load_w(Wdt, w_in[l, :, 6208:6240], 8, per=8)
            cw = P.sbuf([128, 24, 4], F32, dma=True)
            cb = P.sbuf([128, 24], F32, dma=True)
            dtb = P.sbuf([128, 32], F32, dma=True)
            P.dma("sync", [(cw.t[:], E["ssd_cw"][l, :, :, :])], [], cw.b, cw.ds)
            P.dma("sync", [(cb.t[:], E["ssd_cb"][l, :, :])], [], cb.b, cb.ds)
            P.dma("sync", [(dtb.t[:], E["dtb_bc"][l, :, :])], [], dtb.b, dtb.ds)
            h = P.sbuf([128, 8, 512], F32, dma=True)
            hbr = Ring([P.sbuf([128, 8, 512], BF16, n=8) for _ in range(2)])
            xc = P.sbuf([128, 24, 512], BF16, n=24, dma=True)
            xbr = Ring([P.sbuf([128, 515], F32) for _ in range(2)])
            halo = P.sbuf([128, 24, 3], F32, n=24)
            halo2 = P.sbuf([128, 24, 3], F32, n=24)
            halos = [halo, halo2]
            tcr = Ring([P.sbuf([128, 512], F32) for _ in range(4)])
            zst = Ring([P.sbuf([128, 2048], BF16, n=4, dma=True) for _ in range(2)])
            dtr = Ring([P.sbuf([128, 32], F32, dma=True) for _ in range(2)])
            tst = Ring([P.sbuf([128, 2560], BF16, n=5, dma=True) for _ in range(2)])
            accr = Ring([P.psum([128, 512], F32) for _ in range(6)])
            tpr = Ring([P.psum([128, 512], BF16) for _ in range(2)])
            P.op("vector", MEMSET(halo.t[:], 0.0), [], halo.b)
            evi = [0]

            def evac(out_ap, in_ap, reads, writes):
                evi[0] += 1
                if evi[0] % 2:
                    P.op("scalar", ACT(out_ap, in_ap, AF.Copy), reads, writes)
                else:
                    P.op("vector", CP(out_ap, in_ap), reads, writes)

            for tile_i, (c0, W) in enumerate(tiles_of(512)):
                nb = W // 128
                P.dma("sync", [(h.t[:, :, 0:W], hT.t[:, :, c0:c0 + W].rearrange("c p t -> p c t"))], hT.b, h.b, h.ds)
                hb = hbr.next()
                for c in range(8):
                    evac(hb.t[:, c, 0:W], h.t[:, c, 0:W], h.b, [hb.b[c]])
                for bi in range(nb):
                    zt = zst.next()
                    for cg in range(4):
                        acc = accr.next()
                        for k in range(8):
                            P.op("tensor", MM(acc.t[:, :], hb.t[:, k, bi * 128:(bi + 1) * 128],
                                              Wz.t[:, k, cg * 512:(cg + 1) * 512], k == 0, k == 7),
                                 Wz.b + [hb.b[k]], acc.b, mark=(k == 7))
                        P.op("scalar", ACT(zt.t[:, cg * 512:(cg + 1) * 512], acc.t[:, :], AF.Silu), acc.b, [zt.b[cg]])
                    r0 = c0 + bi * 128
                    P.dma("gpsimd", [(zsS.t[r0:r0 + 128, :], zt.t[:, :])], zt.b, zsS.b, zt.ss, nowaw=True)
                    acc = accr.next()
                    for k in range(8):
                        P.op("tensor", MM(acc.t[:, 0:32], hb.t[:, k, bi * 128:(bi + 1) * 128], Wdt.t[:, k, 0:32],
                                          k == 0, k == 7), Wdt.b + [hb.b[k]], acc.b, mark=(k == 7))
                    dtt = dtr.next()
                    P.op("vector", TT(dtt.t[:, :], acc.t[:, 0:32], dtb.t[:, :], ALU.add), acc.b + dtb.b, dtt.b)
                    P.op("scalar", ACT(dtt.t[:, :], dtt.t[:, :], AF.Exp), dtt.b, dtt.b)
                    P.op("scalar", ACT(dtt.t[:, :], dtt.t[:, :], AF.Ln, bias=1.0), dtt.b, dtt.b)
                    P.dma("gpsimd", [(dtS.t[r0:r0 + 128, :], dtt.t[:, :])], dtt.b, dtS.b, dtt.ss, nowaw=True)
                for c in range(24):
                    acc = accr.next()
                    for k in range(8):
                        P.op("tensor", MM(acc.t[:, 0:W], Wx.t[:, k, c * 128:(c + 1) * 128], hb.t[:, k, 0:W],
                                          k == 0, k == 7), Wx.b + [hb.b[k]], acc.b, mark=(k == 7))
                    hin, hout = halos[tile_i % 2], halos[(tile_i + 1) % 2]
                    tc_ = tcr.next()
                    if c0 == 0:
                        xb = xbr.next()
                        P.op("vector", MEMSET(xb.t[:, 0:3 + PAD], 0.0), [], xb.b)
                        P.op("scalar", ACT(xb.t[:, 3 + PAD:3 + W], acc.t[:, PAD:W], AF.Copy), [], xb.b + acc.b)
                        P.op("scalar", ACT(hout.t[:, c, :], xb.t[:, W:W + 3], AF.Copy), xb.b, [hout.b[c]])
                        P.op("scalar", ACT(tc_.t[:, 0:W], xb.t[:, 0:W], AF.Identity, bias=cb.t[:, c:c + 1], scale=cw.t[:, c, 0:1]),
                             xb.b + cw.b + cb.b, tc_.b)
                        for tap in range(1, 4):
                            P.op("vector", STT(tc_.t[:, 0:W], xb.t[:, tap:tap + W], cw.t[:, c, tap:tap + 1], tc_.t[:, 0:W],
                                               ALU.mult, ALU.add), xb.b + cw.b, tc_.b)
                    else:
                        P.op("scalar", ACT(tc_.t[:, 3:W], acc.t[:, 0:W - 3], AF.Identity, bias=cb.t[:, c:c + 1], scale=cw.t[:, c, 0:1]),
                             cw.b + cb.b, tc_.b + acc.b)
                        P.op("scalar", ACT(hout.t[:, c, :], acc.t[:, W - 3:W], AF.Copy), [], [hout.b[c]] + acc.b)
                        P.op("scalar", ACT(tc_.t[:, 0:3], hin.t[:, c, :], AF.Identity, bias=cb.t[:, c:c + 1], scale=cw.t[:, c, 0:1]),
                             [hin.b[c]], tc_.b)
                        P.op("vector", STT(tc_.t[:, 0:2], hin.t[:, c, 1:3], cw.t[:, c, 1:2], tc_.t[:, 0:2], ALU.mult, ALU.add),
                             [hin.b[c]] + cw.b, tc_.b)
                        P.op("vector", STT(tc_.t[:, 0:1], hin.t[:, c, 2:3], cw.t[:, c, 2:3], tc_.t[:, 0:1], ALU.mult, ALU.add),
                             [hin.b[c]] + cw.b, tc_.b)
                        for tap in range(1, 4):
                            P.op("vector", STT(tc_.t[:, 3 - tap:W], acc.t[:, 0:W - 3 + tap], cw.t[:, c, tap:tap + 1],
                                               tc_.t[:, 3 - tap:W], ALU.mult, ALU.add), cw.b, tc_.b + acc.b)
                    P.op("scalar", ACT(xc.t[:, c, 0:W], tc_.t[:, 0:W], AF.Silu), tc_.b, [xc.b[c]])
                    if c0 == 0:
                        P.op("vector", MEMSET(xc.t[:, c, 0:PAD], 0.0), [], [xc.b[c]])
                prs = []
                for g in range(4):
                    prs.append((BTS.t[g, :, c0:c0 + W], xc.t[:, 16 + g, 0:W]))
                    prs.append((CTS.t[g, :, c0:c0 + W], xc.t[:, 20 + g, 0:W]))
                P.dma("gpsimd", prs, xc.b[16:24], BTS.b + CTS.b, xc.ss, nowaw=True)
                for bi in range(nb):
                    ts_ = tst.next()
                    for q4 in range(5):
                        tp = tpr.next()
                        for j in range(4):
                            c = q4 * 4 + j
                            P.op("tensor", TR(tp.t[:, j * 128:(j + 1) * 128], xc.t[:, c, bi * 128:(bi + 1) * 128],
                                              self.identb.t[:]), [xc.b[c]] + self.identb.b, tp.b, mark=(j == 3))
                        evac(ts_.t[:, q4 * 512:(q4 + 1) * 512], tp.t[:, :], tp.b, [ts_.b[q4]])
                    r0 = c0 + bi * 128
                    P.dma("gpsimd", [(xsS.t[r0:r0 + 128, :], ts_.t[:, 0:2048]), (BtokS.t[r0:r0 + 128, :], ts_.t[:, 2048:2560])],
                          ts_.b, xsS.b + BtokS.b, ts_.ss, nowaw=True)
            P.end_phase()

    def phase_S(self, l):
        P = self.P
        E = self.ext
        xsS, BtokS, BTS, CTS, dtS, zsS, ynTS = (self.scr[n] for n in ("xs", "Btok", "BT", "CT", "dt", "zs", "ynT"))
        trib = self.tri_b
        ones = self.ones_b
        identb = self.identb
        with ExitStack() as st:
            P.begin_phase(st)
            abc = P.sbuf([128, 32], F32, dma=True)
            dsk = P.sbuf([128, 2048], F32, dma=True)
            ngb = P.sbuf([128, 2048], F32, dma=True)
            negm_f = P.sbuf([128, 128], F32, dma=True)
            P.dma("sync", [(abc.t[:], E["alog_bc"][l, :, :])], [], abc.b, abc.ds)
            P.dma("sync", [(dsk.t[:], E["dskip_bc"][l, :, :])], [], dsk.b, dsk.ds)
            P.dma("sync", [(ngb.t[:], E["ssdng_bc"][l, :, :])], [], ngb.b, ngb.ds)
            P.dma("sync", [(negm_f.t[:], E["c_negmask"][:, :])], [], negm_f.b, negm_f.ds)
            P.op("scalar", ACT(abc.t[:], abc.t[:], AF.Exp), abc.b, abc.b)
            P.op("vector", TS(abc.t[:], abc.t[:], -1.0, ALU.mult), abc.b, abc.b)
            negm = P.sbuf([128, 128], BF16)
            trin = P.sbuf([128, 128], BF16)
            P.op("vector", CP(negm.t[:], negm_f.t[:]), negm_f.b, negm.b)
            P.op("vector", TS(trin.t[:], self.tri_f.t[:], -1.0, ALU.mult), self.tri_f.b, trin.b)
            NB_ = 3
            xsr = Ring([P.sbuf([128, 2048], BF16, dma=True) for _ in range(NB_)])
            btr = Ring([P.sbuf([128, 512], BF16, dma=True) for _ in range(NB_)])
            bTr = Ring([P.sbuf([128, 4, 128], BF16, dma=True) for _ in range(NB_)])
            cTr = Ring([P.sbuf([128, 4, 128], BF16, dma=True) for _ in range(NB_)])
            dtr = Ring([P.sbuf([128, 32], F32, dma=True) for _ in range(NB_)])
            zsr = Ring([P.sbuf([128, 2048], BF16, dma=True) for _ in range(NB_)])
            prev = P.sbuf([128, 2048], F32, n=4)
            prevb = P.sbuf([128, 2048], BF16, n=4)
            P.op("vector", MEMSET(prev.t[:], 0.0), [], prev.b)
            P.op("vector", MEMSET(prevb.t[:], 0.0), [], prevb.b)
            at = P.sbuf([128, 32], F32)
            ar = P.sbuf([128, 32], F32)
            a3 = [P.sbuf([128, 32], BF16) for _ in range(3)]
            def mk_ctx():
                return dict(abig=[P.sbuf([128, 32, 128], BF16) for _ in range(2)],
                            abm=[P.sbuf([128, 32, 128], BF16) for _ in range(2)],
                            eacs=P.sbuf([128, 32], F32), cdec=P.sbuf([128, 32], F32),
                            xdt=P.sbuf([128, 2048], BF16), xdts=P.sbuf([128, 2048], BF16), xsd=P.sbuf([128, 2048], BF16))
            ctxr = Ring([mk_ctx() for _ in range(2)])
            acs = P.sbuf([128, 32], F32)
            dst = P.sbuf([128, 32], F32)
            y = P.sbuf([128, 2048], F32, n=4)
            yn = P.sbuf([128, 2048], BF16, n=4)
            junkr = Ring([P.sbuf([128, 512], F32) for _ in range(2)])
            ss = P.sbuf([128, 4], F32)
            rstd = P.sbuf([128, 4], F32)
            t1r = Ring([P.sbuf([128, 512], F32) for _ in range(2)])
            cbsr = Ring([P.sbuf([128, 128], F32) for _ in range(2)])
            er = Ring([P.sbuf([128, 4, 128], F32) for _ in range(3)])
            mtr = Ring([P.sbuf([128, 4, 128], BF16) for _ in range(4)])
            ynst = Ring([P.sbuf([128, 16, 128], BF16, n=4, dma=True) for _ in range(2)])
            misc = P.psum([128, 512], F32)
            acs_ps = TB(misc.t[:, 0:32]); acs_ps.b = misc.b
            last_ps = TB(misc.t[:, 32:64]); last_ps.b = misc.b
            cb_ps = TB(misc.t[:, 128:256]); cb_ps.b = misc.b
            dpr = Ring([P.psum([128, 512], F32) for i in range(2)])
            ydr = Ring([P.psum([128, 512], F32) for i in range(2)])
            yo_ps = P.psum([128, 512], F32)
            st_ps = P.psum([128, 512], F32)
            tpr = Ring([P.psum([128, 512], BF16) for _ in range(1)])
            bc3 = lambda ap2, n: ap2.unsqueeze(2).to_broadcast([128, n, 64])
            v3 = lambda ap2: ap2.rearrange("p (h d) -> p h d", d=64)
            nch = getattr(self, 'S_NCH', NBLK)
            loaded = {}
            ctxs = {}

            def load(c):
                r0 = c * 128
                t = dict(xs=xsr.next(), bt=btr.next(), bT=bTr.next(), cT=cTr.next(), dt=dtr.next(), zs=zsr.next())
                P.dma("sync", [(t["xs"].t[:], xsS.t[r0:r0 + 128, :])], xsS.b, t["xs"].b, t["xs"].ds)
                P.dma("sync", [(t["bt"].t[:], BtokS.t[r0:r0 + 128, :])], BtokS.b, t["bt"].b, t["bt"].ds)
                P.dma("sync", [(t["bT"].t[:], BTS.t[:, :, r0:r0 + 128].rearrange("g p t -> p g t"))], BTS.b, t["bT"].b, t["bT"].ds)
                P.dma("sync", [(t["cT"].t[:], CTS.t[:, :, r0:r0 + 128].rearrange("g p t -> p g t"))], CTS.b, t["cT"].b, t["cT"].ds)
                P.dma("sync", [(t["dt"].t[:], dtS.t[r0:r0 + 128, :])], dtS.b, t["dt"].b, t["dt"].ds)
                P.dma("sync", [(t["zs"].t[:], zsS.t[r0:r0 + 128, :])], zsS.b, t["zs"].b, t["zs"].ds)
                loaded[c] = t

            def front(c):
                t = loaded[c]
                xs, dt = t["xs"], t["dt"]
                X = ctxr.next()
                ctxs[c] = X
                abig, abm, eacs, cdec, xdt, xdts, xsd = (X[k_] for k_ in ("abig", "abm", "eacs", "cdec", "xdt", "xdts", "xsd"))
                P.op("vector", TT(at.t[:], dt.t[:], abc.t[:], ALU.mult), dt.b + abc.b, at.b)
                P.op("vector", CP(a3[0].t[:], at.t[:]), at.b, a3[0].b)
                P.op("vector", TT(ar.t[:], at.t[:], a3[0].t[:], ALU.subtract), at.b + a3[0].b, ar.b)
                P.op("vector", CP(a3[1].t[:], ar.t[:]), ar.b, a3[1].b)
                P.op("vector", TT(ar.t[:], ar.t[:], a3[1].t[:], ALU.subtract), a3[1].b, ar.b)
                P.op("vector", CP(a3[2].t[:], ar.t[:]), ar.b, a3[2].b)
                for i3 in range(3):
                    P.op("tensor", MM(acs_ps.t, trib.t[:], a3[i3].t[:], i3 == 0, i3 == 2), trib.b + a3[i3].b, misc.b,
                         mark=(i3 == 2))
                for i3 in range(3):
                    P.op("tensor", MM(last_ps.t, ones.t[:], a3[i3].t[:], i3 == 0, i3 == 2), ones.b + a3[i3].b, misc.b,
                         mark=(i3 == 2))
                P.op("vector", CP(acs.t[:], acs_ps.t), [], acs.b + misc.b)
                for i3 in range(2):
                    P.op("scalar", ACT(abig[i3].t[:], a3[i3].t[:, :].unsqueeze(2).to_broadcast([128, 32, 128]), AF.Copy),
                         a3[i3].b, abig[i3].b)
                    P.op("gpsimd", TT(abm[i3].t[:], a3[i3].t[:, :].unsqueeze(2).to_broadcast([128, 32, 128]),
                                      trin.t[:, :].unsqueeze(1).to_broadcast([128, 32, 128]), ALU.mult),
                         a3[i3].b + trin.b, abm[i3].b)
                P.op("scalar", ACT(eacs.t[:], acs.t[:], AF.Exp), acs.b, eacs.b)
                P.op("scalar", ACT(cdec.t[:], last_ps.t, AF.Exp), [], cdec.b + misc.b)
                P.op("vector", TT(dst.t[:], last_ps.t, acs.t[:], ALU.subtract), acs.b, dst.b + misc.b)
                P.op("scalar", ACT(dst.t[:], dst.t[:], AF.Exp), dst.b, dst.b)
                P.op("vector", TT(v3(xdt.t[:]), v3(xs.t[:]), bc3(dt.t[:, :], 32), ALU.mult), xs.b + dt.b, xdt.b)
                P.op("gpsimd", TT(v3(xdts.t[:]), v3(xdt.t[:]), bc3(dst.t[:, :], 32), ALU.mult), xdt.b + dst.b, xdts.b)
                P.op("gpsimd", TT(xsd.t[:], xs.t[:], dsk.t[:], ALU.mult), xs.b + dsk.b, xsd.b)

            load(0)
            front(0)
            for c in range(nch):
                r0 = c * 128
                if c + 1 < nch:
                    load(c + 1)
                    front(c + 1)
                t = loaded.pop(c)
                X = ctxs.pop(c)
                abig, abm, eacs, cdec, xdt, xdts, xsd = (X[k_] for k_ in ("abig", "abm", "eacs", "cdec", "xdt", "xdts", "xsd"))
                xs, bt, bT, cT, dt, zs = t["xs"], t["bt"], t["bT"], t["cT"], t["dt"], t["zs"]
                def pe_D(g):
                    P.op("tensor", MM(cb_ps.t, bT.t[:, g, :], cT.t[:, g, :]), bT.b + cT.b, misc.b)
                    dd = []
                    for hb4 in range(2):
                        dps = dpr.next()
                        dd.append(dps)
                        for j in range(4):
                            hd = g * 8 + hb4 * 4 + j
                            sl = dps.t[:, j * 128:(j + 1) * 128]
                            P.op("tensor", MM(sl, abig[0].t[:, hd, :], trib.t[:], True, False), abig[0].b + trib.b, dps.b, mark=False)
                            P.op("tensor", MM(sl, abig[1].t[:, hd, :], trib.t[:], False, False), abig[1].b, dps.b, mark=False)
                            P.op("tensor", MM(sl, abm[0].t[:, hd, :], ones.t[:], False, False), abm[0].b + ones.b, dps.b, mark=False)
                            P.op("tensor", MM(sl, abm[1].t[:, hd, :], ones.t[:], False, False), abm[1].b, dps.b, mark=False)
                            P.op("tensor", MM(sl, identb.t[:], negm.t[:], False, True), identb.b + negm.b, dps.b, mark=(j == 3))
                    return dd

                def tail(g, yd):
                    gs = slice(g * 512, (g + 1) * 512)
                    t1 = t1r.next()
                    P.op("vector", TT(v3(t1.t[:]), v3(yo_ps.t[:, :]), bc3(eacs.t[:, g * 8:(g + 1) * 8], 8), ALU.mult),
                         yo_ps.b + eacs.b, t1.b)
                    P.op("vector", TT(y.t[:, gs], yd.t[:, :], t1.t[:], ALU.add), yd.b + t1.b, [y.b[g]])
                    P.op("vector", TT(y.t[:, gs], y.t[:, gs], zs.t[:, gs], ALU.mult), zs.b, [y.b[g]])
                    junk = junkr.next()
                    P.op("gpsimd", TT(junk.t[:], y.t[:, gs], y.t[:, gs], ALU.mult), [y.b[g]], junk.b)
                    P.op("vector", lambda e, o=junk.t[:], acc=ss.t[:, g:g + 1]: e.tensor_scalar(
                        out=o, in0=o, scalar1=1.0, scalar2=0.0, op0=ALU.mult, op1=ALU.add, accum_out=acc),
                        [], junk.b + ss.b)
                    P.op("gpsimd", TT(v3(prev.t[:, gs]), v3(prev.t[:, gs]), bc3(cdec.t[:, g * 8:(g + 1) * 8], 8), ALU.mult),
                         cdec.b, [prev.b[g]])
                    P.op("vector", TT(prev.t[:, gs], prev.t[:, gs], st_ps.t[:, :], ALU.add), st_ps.b, [prev.b[g]])
                    P.op("scalar", ACT(prevb.t[:, gs], prev.t[:, gs], AF.Copy), [prev.b[g]], [prevb.b[g]])

                dcur = pe_D(0)
                yds = {}
                for g in range(4):
                    gs = slice(g * 512, (g + 1) * 512)
                    cbs = cbsr.next()
                    P.op("scalar", ACT(cbs.t[:], cb_ps.t, AF.Copy), [], cbs.b + misc.b)
                    mts = []
                    for hb4 in range(2):
                        dps = dcur[hb4]
                        ee = er.next()
                        P.op("scalar", ACT(ee.t[:].rearrange("p a b -> p (a b)"), dps.t[:, :], AF.Exp), dps.b, ee.b)
                        mt = mtr.next()
                        P.op("vector", TT(mt.t[:], ee.t[:], cbs.t[:, :].unsqueeze(1).to_broadcast([128, 4, 128]), ALU.mult),
                             ee.b + cbs.b, mt.b)
                        mts.append(mt)
                    if g + 1 < 4:
                        dcur = pe_D(g + 1)
                    yd = ydr.next()
                    yds[g] = yd
                    for hb4 in range(2):
                        mt = mts[hb4]
                        for j in range(4):
                            hh = hb4 * 4 + j
                            hd = g * 8 + hh
                            P.op("tensor", MM(yd.t[:, hh * 64:(hh + 1) * 64], mt.t[:, j, :], xdt.t[:, hd * 64:(hd + 1) * 64], True, False),
                                 mt.b + xdt.b, yd.b, mark=False)
                            P.op("tensor", MM(yd.t[:, hh * 64:(hh + 1) * 64], identb.t[:], xsd.t[:, hd * 64:(hd + 1) * 64], False, True),
                                 identb.b + xsd.b, yd.b, mark=(hh == 7))
                    if g >= 1:
                        tail(g - 1, yds.pop(g - 1))
                    P.op("tensor", MM(yo_ps.t[:, :], cT.t[:, g, :], prevb.t[:, gs]), cT.b + [prevb.b[g]], yo_ps.b)
                    P.op("tensor", MM(st_ps.t[:, :], bt.t[:, g * 128:(g + 1) * 128], xdts.t[:, gs]), bt.b + xdts.b, st_ps.b)
                tail(3, yds.pop(3))
                P.op("vector", TS(rstd.t[:], ss.t[:], 1.0 / 512, ALU.mult, RMS_EPS, ALU.add), ss.b, rstd.b)
                P.op("scalar", ACT(rstd.t[:], rstd.t[:], AF.Ln), rstd.b, rstd.b)
                P.op("scalar", ACT(rstd.t[:], rstd.t[:], AF.Exp, scale=-0.5), rstd.b, rstd.b)
                yst = ynst.next()
                for g in range(4):
                    gs = slice(g * 512, (g + 1) * 512)
                    P.op("vector", STT(yn.t[:, gs], y.t[:, gs], rstd.t[:, g:g + 1], ngb.t[:, gs], ALU.mult, ALU.mult),
                         [y.b[g]] + rstd.b + ngb.b, [yn.b[g]])
                    tp = tpr.next()
                    for j in range(4):
                        cc = g * 4 + j
                        P.op("tensor", TR(tp.t[:, j * 128:(j + 1) * 128], yn.t[:, cc * 128:(cc + 1) * 128], identb.t[:]),
                             [yn.b[g]] + identb.b, tp.b, mark=(j == 3))
                    P.op("scalar", ACT(yst.t[:, g * 4:(g + 1) * 4, :], tp.t[:, :].rearrange("p (c t) -> p c t", c=4), AF.Copy),
                         tp.b, [yst.b[g]])
                P.dma("sync", [(ynTS.t[:, :, r0:r0 + 128].rearrange("c p t -> p c t"), yst.t[:, :, :])],
                      yst.b, ynTS.b, yst.ss, nowaw=True)
            P.end_phase()

    def phase_C1(self, l):
        P = self.P
        E = self.ext
        hT, oTS, ynTS, h1T = (self.scr[n] for n in ("hT", "oT", "ynT", "h1T"))
        WT = 256
        with ExitStack() as st:
            P.begin_phase(st)
            w_in = E["w_in"]
            Woa = P.sbuf([128, 8, 1024], BF16, dma=True)
            Wos = P.sbuf([128, 16, 1024], BF16, dma=True)
            Wout = P.sbuf([128, 8, 1024], BF16, dma=True)
            Wga = P.sbuf([128, 8, 1024], BF16, dma=True)
            Wgs = P.sbuf([128, 8, 1024], BF16, dma=True)
            self.load_w(Woa, E["w_o_attn"][l, :, :], 8)
            self.load_w(Wos, E["w_o_ssd"][l, :, :], 16)
            self.load_w(Wout, E["w_out"][l, :, :], 8)
            self.load_w(Wga, w_in[l, :, 6240:7264], 8)
            self.load_w(Wgs, w_in[l, :, 7264:8288], 8)
            gb = P.sbuf([128, 16], F32, dma=True)
            P.dma("sync", [(gb.t[:, 0:8], E["ln1_g"][l, :, :]), (gb.t[:, 8:16], E["ln1_b"][l, :, :])], [], gb.b, gb.ds)
            otr = Ring([P.sbuf([128, 8, WT], BF16, dma=True) for _ in range(2)])
            ynr = Ring([P.sbuf([128, 16, WT], BF16, dma=True) for _ in range(2)])
            hr = Ring([P.sbuf([128, 8, WT], F32, n=8, dma=True) for _ in range(2)])
            hb = P.sbuf([128, 8, WT], BF16, n=8)
            mixed = P.sbuf([128, 8, WT], BF16, n=8)
            orr = Ring([P.sbuf([128, 8, WT], F32, n=8, dma=True) for _ in range(1)])
            sgr = Ring([P.sbuf([128, WT], F32) for _ in range(4)])
            tr_ = Ring([P.sbuf([128, WT], F32) for _ in range(4)])
            accr = Ring([P.psum([128, 512], F32) for _ in range(4)])
            R = self.ln_resources(WT)
            evi = [0]

            def proj(acc, W_tb, oc, K, rhs_tb, W):
                for k in range(K):
                    P.op("tensor", MM(acc.t[:, 0:W], W_tb.t[:, k, oc * 128:(oc + 1) * 128], rhs_tb.t[:, k, 0:W],
                                      k == 0, k == K - 1), W_tb.b + [rhs_tb.b[k % len(rhs_tb.b)]], acc.b, mark=(k == K - 1))

            for (c0, W) in tiles_of(WT):
                ot = otr.next(); yt = ynr.next(); h = hr.next()
                P.dma("sync", [(ot.t[:, :, 0:W], oTS.t[:, :, c0:c0 + W].rearrange("h p t -> p h t"))], oTS.b, ot.b, ot.ds)
                P.dma("sync", [(yt.t[:, :, 0:W], ynTS.t[:, :, c0:c0 + W].rearrange("c p t -> p c t"))], ynTS.b, yt.b, yt.ds)
                P.dma("sync", [(h.t[:, :, 0:W], hT.t[:, :, c0:c0 + W].rearrange("c p t -> p c t"))], hT.b, h.b, h.ds)
                for c in range(8):
                    evi[0] += 1
                    if evi[0] % 2:
                        P.op("scalar", ACT(hb.t[:, c, 0:W], h.t[:, c, 0:W], AF.Copy), h.b, [hb.b[c]])
                    else:
                        P.op("vector", CP(hb.t[:, c, 0:W], h.t[:, c, 0:W]), h.b, [hb.b[c]])
                for oc in range(8):
                    ga = accr.next()
                    proj(ga, Wga, oc, 8, hb, W)
                    sga = sgr.next()
                    P.op("scalar", ACT(sga.t[:, 0:W], ga.t[:, 0:W], AF.Sigmoid), ga.b, sga.b)
                    gs_ = accr.next()
                    proj(gs_, Wgs, oc, 8, hb, W)
                    sgs = sgr.next()
                    P.op("scalar", ACT(sgs.t[:, 0:W], gs_.t[:, 0:W], AF.Sigmoid), gs_.b, sgs.b)
                    ya = accr.next()
                    proj(ya, Woa, oc, 8, ot, W)
                    t1 = tr_.next()
                    P.op("vector", TT(t1.t[:, 0:W], ya.t[:, 0:W], sga.t[:, 0:W], ALU.mult), ya.b + sga.b, t1.b)
                    ys_ = accr.next()
                    proj(ys_, Wos, oc, 16, yt, W)
                    t2 = tr_.next()
                    P.op("vector", TT(t2.t[:, 0:W], ys_.t[:, 0:W], sgs.t[:, 0:W], ALU.mult), ys_.b + sgs.b, t2.b)
                    P.op("vector", TT(mixed.t[:, oc, 0:W], t1.t[:, 0:W], t2.t[:, 0:W], ALU.add), t1.b + t2.b, [mixed.b[oc]])
                for oc in range(8):
                    r = accr.next()
                    proj(r, Wout, oc, 8, mixed, W)
                    P.op("vector", STT(h.t[:, oc, 0:W], h.t[:, oc, 0:W], ALPHA, r.t[:, 0:W], ALU.mult, ALU.add),
                         r.b + [hb.b[oc]], [h.b[oc]])
                o = orr.next()
                self.ln_fm(h, o, W, gb.t[:, 0:8], gb.t[:, 8:16], R)
                P.dma("gpsimd", [(h1T.t[:, :, c0:c0 + W].rearrange("c p t -> p c t"), o.t[:, :, 0:W])],
                      o.b, h1T.b, o.ss, nowaw=True)
            P.end_phase()

    def phase_C2(self, l):
        P = self.P
        E = self.ext
        hT, h1T = (self.scr[n] for n in ("hT", "h1T"))
        last = (l == self.n_layers - 1)
        WT = 256
        with ExitStack() as st:
            P.begin_phase(st)
            Wup = P.sbuf([128, 8, 2 * DFF], BF16, dma=True)
            Wdn = P.sbuf([128, 22, 1024], BF16, dma=True)
            self.load_w(Wup, E["w_up"][l, :, :], 8, per=1)
            self.load_w(Wdn, E["w_down"][l, :, :], 22, per=4)
            gb = P.sbuf([128, 16], F32, dma=True)
            P.dma("sync", [(gb.t[:, 0:8], E["ln2_g"][l, :, :]), (gb.t[:, 8:16], E["ln2_b"][l, :, :])], [], gb.b, gb.ds)
            cw = P.sbuf([128, 44, 3], F32, dma=True)
            cb = P.sbuf([128, 44], F32, dma=True)
            P.dma("sync", [(cw.t[:], E["ffn_cw"][l, :, :, :])], [], cw.b, cw.ds)
            P.dma("sync", [(cb.t[:], E["ffn_cb"][l, :, :])], [], cb.b, cb.ds)
            hr = Ring([P.sbuf([128, 8, WT], F32, n=8, dma=True) for _ in range(2)])
            hb = P.sbuf([128, 8, WT], BF16, n=8)
            a = P.sbuf([128, 22, WT], BF16, n=22)
            orr = Ring([P.sbuf([128, 8, WT], F32, n=8, dma=True) for _ in range(1)])
            xbr = Ring([P.sbuf([128, WT + 2], F32) for _ in range(2)])
            tcr = Ring([P.sbuf([128, WT], F32) for _ in range(8)])
            sgr = Ring([P.sbuf([128, WT], F32) for _ in range(3)])
            halo = P.sbuf([128, 44, 2], F32, n=44)
            halo2 = P.sbuf([128, 44, 2], F32, n=44)
            ostr = Ring([P.sbuf([128, 1024], F32, n=2, dma=True) for _ in range(1)]) if last else None
            accr = Ring([P.psum([128, 512], F32) for _ in range(5 if last else 6)])
            tpr = Ring([P.psum([128, 512], F32) for _ in range(1)]) if last else None
            R = self.ln_resources(WT)
            P.op("vector", MEMSET(halo.t[:], 0.0), [], halo.b)
            evi = [0]

            def proj(acc, W_tb, col0, K, rhs_tb, W):
                for k in range(K):
                    P.op("tensor", MM(acc.t[:, 0:W], W_tb.t[:, k, col0:col0 + 128], rhs_tb.t[:, k, 0:W],
                                      k == 0, k == K - 1), W_tb.b + [rhs_tb.b[k]], acc.b, mark=(k == K - 1))

            def conv(acc, ci, W, first, hin, hout):
                tc_ = tcr.next()
                if first:
                    xb = xbr.next()
                    P.op("vector", MEMSET(xb.t[:, 0:2 + PAD], 0.0), [], xb.b)
                    P.op("scalar", ACT(xb.t[:, 2 + PAD:2 + W], acc.t[:, PAD:W], AF.Copy), [], xb.b + acc.b)
                    P.op("scalar", ACT(hout.t[:, ci, :], xb.t[:, W:W + 2], AF.Copy), xb.b, [hout.b[ci]])
                    P.op("scalar", ACT(tc_.t[:, 0:W], xb.t[:, 0:W], AF.Identity, bias=cb.t[:, ci:ci + 1], scale=cw.t[:, ci, 0:1]),
                         xb.b + cw.b + cb.b, tc_.b)
                    for tap in (1, 2):
                        P.op("vector", STT(tc_.t[:, 0:W], xb.t[:, tap:tap + W], cw.t[:, ci, tap:tap + 1], tc_.t[:, 0:W],
                                           ALU.mult, ALU.add), xb.b + cw.b, tc_.b)
                    return tc_
                P.op("scalar", ACT(tc_.t[:, 2:W], acc.t[:, 0:W - 2], AF.Identity, bias=cb.t[:, ci:ci + 1], scale=cw.t[:, ci, 0:1]),
                     cw.b + cb.b, tc_.b + acc.b)
                P.op("scalar", ACT(hout.t[:, ci, :], acc.t[:, W - 2:W], AF.Copy), [], [hout.b[ci]] + acc.b)
                P.op("scalar", ACT(tc_.t[:, 0:2], hin.t[:, ci, :], AF.Identity, bias=cb.t[:, ci:ci + 1], scale=cw.t[:, ci, 0:1]),
                     [hin.b[ci]], tc_.b)
                P.op("vector", STT(tc_.t[:, 0:1], hin.t[:, ci, 1:2], cw.t[:, ci, 1:2], tc_.t[:, 0:1], ALU.mult, ALU.add),
                     [hin.b[ci]] + cw.b, tc_.b)
                P.op("vector", STT(tc_.t[:, 1:W], acc.t[:, 0:W - 1], cw.t[:, ci, 1:2], tc_.t[:, 1:W], ALU.mult, ALU.add),
                     cw.b, tc_.b + acc.b)
                P.op("vector", STT(tc_.t[:, 0:W], acc.t[:, 0:W], cw.t[:, ci, 2:3], tc_.t[:, 0:W], ALU.mult, ALU.add),
                     cw.b, tc_.b + acc.b)
                return tc_

            halos = [halo, halo2]
            tile_i = [0]
            for (c0, W) in tiles_of(WT):
                nb = W // 128
                h = hr.next()
                P.dma("sync", [(h.t[:, :, 0:W], h1T.t[:, :, c0:c0 + W].rearrange("c p t -> p c t"))], h1T.b, h.b, h.ds)
                for c in range(8):
                    evi[0] += 1
                    if evi[0] % 2:
                        P.op("scalar", ACT(hb.t[:, c, 0:W], h.t[:, c, 0:W], AF.Copy), h.b, [hb.b[c]])
                    else:
                        P.op("vector", CP(hb.t[:, c, 0:W], h.t[:, c, 0:W]), h.b, [hb.b[c]])
                for c in range(22):
                    ug = accr.next()
                    proj(ug, Wup, c * 128, 8, hb, W)
                    uv = accr.next()
                    proj(uv, Wup, DFF + c * 128, 8, hb, W)
                    hin, hout = halos[tile_i[0] % 2], halos[(tile_i[0] + 1) % 2]
                    tg = conv(ug, c, W, c0 == 0, hin, hout)
                    tv = conv(uv, 22 + c, W, c0 == 0, hin, hout)
                    sg = sgr.next()
                    P.op("scalar", ACT(sg.t[:, 0:W], tg.t[:, 0:W], AF.Silu), tg.b, sg.b)
                    P.op("vector", TT(a.t[:, c, 0:W], sg.t[:, 0:W], tv.t[:, 0:W], ALU.mult), sg.b + tv.b, [a.b[c]])
                tile_i[0] += 1
                for oc in range(8):
                    f = accr.next()
                    proj(f, Wdn, oc * 128, 22, a, W)
                    P.op("vector", STT(h.t[:, oc, 0:W], h.t[:, oc, 0:W], ALPHA, f.t[:, 0:W], ALU.mult, ALU.add),
                         f.b + [hb.b[oc]], [h.b[oc]])
                o = orr.next()
                self.ln_fm(h, o, W, gb.t[:, 0:8], gb.t[:, 8:16], R)
                if not last:
                    P.dma("gpsimd", [(hT.t[:, :, c0:c0 + W].rearrange("c p t -> p c t"), o.t[:, :, 0:W])],
                          o.b, hT.b, o.ss, nowaw=True)
                elif c0 > 0:
                    for bi in range(nb):
                        os_ = ostr.next()
                        for half in range(2):
                            tp = tpr.next()
                            for j in range(4):
                                cc = half * 4 + j
                                P.op("tensor", TR(tp.t[:, j * 128:(j + 1) * 128], o.t[:, cc, bi * 128:(bi + 1) * 128],
                                                  self.identf.t[:]), [o.b[cc]] + self.identf.b, tp.b, mark=(j == 3))
                            P.op("scalar", ACT(os_.t[:, half * 512:(half + 1) * 512], tp.t[:, :], AF.Copy), tp.b, [os_.b[half]])
                        r0 = c0 + bi * 128 - 128
                        P.dma("gpsimd", [(self.out[r0:r0 + 128, :], os_.t[:, :])], os_.b, [], os_.ss)
            P.end_phase()


INPUT_SHAPES = {
    "xin": [LP, D], "emb_g": [128, 8], "emb_b": [128, 8],
    "c_ident": [128, 128], "c_tri": [128, 128], "c_sellast": [128, 128], "c_mask0": [128, 128], "c_kmask0": [128, 1], "c_negmask": [128, 128],
    "cosT2": [128, LP], "sinT2": [128, LP],
    "w_in": [DEPTH, D, 8288], "w_kpes": [DEPTH, D, 64],
    "wqb_n": [DEPTH, QL, 1024], "wqb_p": [DEPTH, QL, 512], "wqb_ps": [DEPTH, QL, 512],
    "wkvb_kn": [DEPTH, KVL, 1024], "wkvb_v": [DEPTH, KVL, 1024],
    "qng": [DEPTH, 128, 6], "kvng": [DEPTH, 128, 2],
    "w_o_attn": [DEPTH, 1024, D], "w_o_ssd": [DEPTH, 2048, D], "w_out": [DEPTH, D, D],
    "w_up": [DEPTH, D, 2 * DFF], "w_down": [DEPTH, DFF, D],
    "ssd_cw": [DEPTH, 128, 24, 4], "ssd_cb": [DEPTH, 128, 24],
    "dtb_bc": [DEPTH, 128, 32], "alog_bc": [DEPTH, 128, 32], "dskip_bc": [DEPTH, 128, 2048], "ssdng_bc": [DEPTH, 128, 2048],
    "ln1_g": [DEPTH, 128, 8], "ln1_b": [DEPTH, 128, 8], "ln2_g": [DEPTH, 128, 8], "ln2_b": [DEPTH, 128, 8],
    "ffn_cw": [DEPTH, 128, 44, 3], "ffn_cb": [DEPTH, 128, 44],
}

SCRATCH = {
    "hT": ([8, 128, LP], F32), "h1T": ([8, 128, LP], F32),
    "qT": ([NH, 192, LP], BF16), "kT": ([NH, 128, LP], BF16), "kpeT": ([64, LP], BF16),
    "v": ([NH, 128, NBLK, 128], BF16), "oT": ([NH, 128, LP], BF16),
    "xs": ([LP, 2048], BF16), "Btok": ([LP, 512], BF16), "BT": ([4, 128, LP], BF16), "CT": ([4, 128, LP], BF16),
    "dt": ([LP, 32], F32), "zs": ([LP, 2048], BF16), "ynT": ([16, 128, LP], BF16),
    "dbgY": ([LP, 2048], F32), "dbgS": ([LP, 168], F32), "dbgP": ([NBLK, 128, 2048], F32),
}


def build(n_layers=2, dbg=None, stop=None):
    nc = bass.Bass("TRN2", target_bir_lowering=False)
    with ExitStack() as gst:
        P = Prog(nc, gst)
        k = K(nc, P, n_layers, dbg)
        for nm, shp in INPUT_SHAPES.items():
            k.din(nm, shp)
        for nm, (shp, dt) in SCRATCH.items():
            k.dscr(nm, shp, dt)
        k.out = nc.dram_tensor("out", [SEQ, D], F32, kind="ExternalOutput").ap()
        k.load_consts(gst)
        seq = [("E", k.phase_E, ())]
        for l in range(n_layers):
            for nm in ("A1", "A2", "B", "S", "C1", "C2"):
                fn = getattr(k, "phase_" + nm, None)
                if fn is not None:
                    seq.append((f"{nm}_{l}", fn, (l,)))
        import os
        skip = os.environ.get("K_SKIP", "").split(",")
        for (nm, fn, args) in seq:
            if nm.split("_")[0] in skip:
                continue
            fn(*args)
            if stop == nm:
                break
    return nc, k


def host_consts():
    c = {}
    c["c_ident"] = np.eye(128, dtype=np.float32)
    kk = np.arange(128)[:, None]
    qq = np.arange(128)[None, :]
    c["c_tri"] = (kk <= qq).astype(np.float32)
    c["c_sellast"] = np.broadcast_to((kk == 127), (128, 128)).astype(np.float32).copy()
    m0 = ((qq >= PAD) & (kk >= PAD) & (kk <= qq)) | ((qq < PAD) & (kk == qq))
    c["c_mask0"] = m0.astype(np.float32)
    c["c_kmask0"] = (np.arange(128) >= PAD).astype(np.float32)[:, None].copy()
    c["c_negmask"] = np.where(qq < kk, np.float32(-30000.0), np.float32(0.0)).astype(np.float32)
    return c


def pm(v, nchunk):
    return np.ascontiguousarray(np.asarray(v, np.float32).reshape(nchunk, 128).T)


def rope_tables_T():
    inv_freq = (1.0 / (np.float32(10000.0) ** (np.arange(0, 64, 2, dtype=np.float32) / np.float32(64)))).astype(np.float32)
    pos = np.maximum(np.arange(LP, dtype=np.float32) - np.float32(PAD), np.float32(0))
    ang = (pos[:, None] * inv_freq[None, :]).astype(np.float32)
    ang = np.concatenate([ang, ang], axis=-1)
    cos = np.cos(ang).astype(np.float32).T
    sin = np.sin(ang).astype(np.float32).T
    sgn = np.concatenate([-np.ones(32, np.float32), np.ones(32, np.float32)])[:, None]
    sins = sin * sgn
    return (np.ascontiguousarray(np.concatenate([cos, cos], 0)), np.ascontiguousarray(np.concatenate([sins, sins], 0)))


def prep_shared(inp):
    f = lambda a: np.ascontiguousarray(np.asarray(a, np.float32))
    sh = dict(host_consts())
    sh["cosT2"], sh["sinT2"] = rope_tables_T()
    sh["emb_g"] = pm(inp["emb_ln_g"], 8)
    sh["emb_b"] = pm(inp["emb_ln_b"], 8)
    w_in = f(inp["w_in"])
    sh["w_in"] = w_in
    sh["w_kpes"] = f(np.concatenate([w_in[:, :, 1056:1088], w_in[:, :, 1024:1056]], axis=-1))
    wqb = f(inp["w_q_b"]).reshape(DEPTH, QL, NH, 192)
    sh["wqb_n"] = f(wqb[..., :128].reshape(DEPTH, QL, 1024))
    sh["wqb_p"] = f(wqb[..., 128:].reshape(DEPTH, QL, 512))
    sh["wqb_ps"] = f(np.concatenate([wqb[..., 160:192], wqb[..., 128:160]], axis=-1).reshape(DEPTH, QL, 512))
    wkv = f(inp["w_kv_b"]).reshape(DEPTH, KVL, NH, 256)
    sh["wkvb_kn"] = f(wkv[..., :128].reshape(DEPTH, KVL, 1024))
    sh["wkvb_v"] = f(wkv[..., 128:].reshape(DEPTH, KVL, 1024))
    sh["qng"] = f(np.stack([pm(inp["q_norm_g"][l], 6) for l in range(DEPTH)]))
    sh["kvng"] = f(np.stack([pm(inp["kv_norm_g"][l], 2) for l in range(DEPTH)]))
    for nm in ("w_o_attn", "w_o_ssd", "w_out", "w_up", "w_down"):
        sh[nm] = f(inp[nm])
    cw = f(inp["ssd_conv_w"])
    sh["ssd_cw"] = f(cw.reshape(DEPTH, 4, 24, 128).transpose(0, 3, 2, 1))
    sh["ssd_cb"] = f(f(inp["ssd_conv_b"]).reshape(DEPTH, 24, 128).transpose(0, 2, 1))
    bc = lambda a: f(np.broadcast_to(f(a)[:, None, :], (DEPTH, 128, a.shape[-1])))
    sh["dtb_bc"] = bc(inp["dt_bias"])
    sh["alog_bc"] = bc(inp["a_log"])
    sh["dskip_bc"] = bc(np.repeat(f(inp["d_skip"]), 64, axis=-1))
    sh["ssdng_bc"] = bc(inp["ssd_norm_g"])
    for nm in ("ln1_g", "ln1_b", "ln2_g", "ln2_b"):
        sh[nm] = f(np.stack([pm(inp[nm][l], 8) for l in range(DEPTH)]))
    fw = f(inp["ffn_conv_w"])
    sh["ffn_cw"] = f(fw.reshape(DEPTH, 3, 44, 128).transpose(0, 3, 2, 1))
    sh["ffn_cb"] = f(f(inp["ffn_conv_b"]).reshape(DEPTH, 44, 128).transpose(0, 2, 1))
    return sh


def xin_of(inp, b):
    xin = np.zeros((LP, D), np.float32)
    xin[PAD:PAD + NMETA] = inp["meta_tokens"]
    xin[128:] = inp["x"][b]
    return xin


def kernel(**inp):
    sh = prep_shared(inp)
    nc, k = build()
    in_maps = []
    for c in range(8):
        m = dict(sh)
        m["xin"] = xin_of(inp, c % 4)
        in_maps.append(m)
    res = run_bass_kernel_spmd(nc, in_maps, core_ids=list(range(8)))
    out = np.stack([np.asarray(res.results[b]["out"], np.float32) for b in range(4)], axis=0)
    return out
```

```python
import math
from contextlib import ExitStack

import numpy as np
import concourse.bass as bass
import concourse.mybir as mybir
from concourse.bass_utils import run_bass_kernel_spmd

F32 = mybir.dt.float32
BF16 = mybir.dt.bfloat16
AF = mybir.ActivationFunctionType
ALU = mybir.AluOpType

D = 1024
SEQ = 8192
NMETA = 16
PAD = 112
LP = 8320
NBLK = 65
NH = 8
QL = 768
KVL = 256
DFF = 2816
DEPTH = 2
ALPHA = (2 * DEPTH) ** 0.25
LN_EPS = 1e-5
RMS_EPS = 1e-6
SCALE = 192 ** -0.5

ENGS = ("sync", "scalar", "vector", "gpsimd", "tensor")
COMPUTE = ("scalar", "vector", "gpsimd", "tensor")


class Buf:
    __slots__ = ("w", "r")

    def __init__(self):
        self.w = None
        self.r = {}


class DSem:
    def __init__(self, sem):
        self.sem = sem
        self.count = 0


class TB:
    def __init__(self, t, n=1, ds=None, ss=None):
        self.t = t
        self.b = [Buf() for _ in range(n)]
        self.ds = ds
        self.ss = ss


class Ring:
    def __init__(self, items):
        self.items = items
        self.i = 0

    def next(self):
        it = self.items[self.i % len(self.items)]
        self.i += 1
        return it


class Prog:
    def __init__(self, nc, gstack):
        self.nc = nc
        self.gstack = gstack
        self.pstack = None
        self.q = {e: [] for e in ENGS}
        self.cnt = {e: 0 for e in ENGS}
        self.waited = {e: {} for e in ENGS}
        self.pending = {e: [] for e in ENGS}
        self.psem = {}
        self.nsem = 0
        for e in COMPUTE:
            self.psem[e] = self._new_sem()
        self.pool = []
        self.pool_i = 0
        self.ninstr = 0
        self.phase_i = 0
        self.nt = 0

    def _new_sem(self):
        self.nsem += 1
        return self.gstack.enter_context(self.nc.semaphore(f"sem{self.nsem}"))

    def dsem(self):
        if self.pool_i == len(self.pool):
            self.pool.append(DSem(self._new_sem()))
        d = self.pool[self.pool_i]
        self.pool_i += 1
        return d

    def sbuf(self, shape, dtype, n=1, dma=False):
        self.nt += 1
        t = self.pstack.enter_context(self.nc.sbuf_tensor(f"sb{self.nt}", list(shape), dtype))
        return TB(t, n, self.dsem() if dma else None, self.dsem() if dma else None)

    def psum(self, shape, dtype, n=1):
        self.nt += 1
        t = self.pstack.enter_context(self.nc.psum_tensor(f"ps{self.nt}", list(shape), dtype))
        return TB(t, n)

    def _wait(self, eng, tok):
        if tok is None:
            return
        sem, val, src = tok
        if src == eng and eng == "tensor":
            return
        key = id(sem)
        if self.waited[eng].get(key, 0) >= val:
            return
        self.waited[eng][key] = val
        self.q[eng].append(lambda e, s=sem, v=val: e.wait_ge(s, v))

    def _deps(self, eng, reads, writes, nowaw=False):
        for b in reads:
            self._wait(eng, b.w)
        for b in writes:
            if not nowaw:
                self._wait(eng, b.w)
            for t in b.r.values():
                self._wait(eng, t)

    def _commit(self, tok, reads, writes):
        k = id(tok[0])
        for b in reads:
            b.r[k] = tok
        for b in writes:
            b.w = tok
            b.r = {}

    def op(self, eng, fn, reads=(), writes=(), mark=True):
        self.ninstr += 1
        self._deps(eng, reads, writes)
        if not mark:
            self.pending[eng].append((tuple(reads), tuple(writes)))
            self.q[eng].append(fn)
            return None
        self.cnt[eng] += 1
        v = self.cnt[eng]
        sem = self.psem[eng]
        self.q[eng].append(lambda e, f=fn, s=sem: f(e).then_inc(s, 1))
        tok = (sem, v, eng)
        for (r, w) in self.pending[eng]:
            self._commit(tok, r, w)
        self.pending[eng] = []
        self._commit(tok, reads, writes)
        return tok

    def dma(self, eng, pairs, reads, writes, ds, nowaw=False):
        self._deps(eng, reads, writes, nowaw)
        for (o, i) in pairs:
            self.ninstr += 1
            ds.count += 16
            self.q[eng].append(lambda e, o=o, i=i, s=ds.sem: e.dma_start(out=o, in_=i).then_inc(s, 16))
        tok = (ds.sem, ds.count, "dma")
        self._commit(tok, reads, writes)
        return tok

    def begin_phase(self, st):
        self.pstack = st
        self.pool_i = 0

    def end_phase(self):
        for d in self.pool:
            if d.count:
                self._wait("sync", (d.sem, d.count, "dma"))
        for e in COMPUTE:
            assert not self.pending[e], e
            if self.cnt[e]:
                self._wait("sync", (self.psem[e], self.cnt[e], e))
        nc = self.nc
        self.phase_i += 1
        with nc.Block() as block:
            for ename in ENGS:
                lst = self.q[ename]
                if not lst:
                    continue

                def body(e, lst=lst):
                    for f in lst:
                        f(e)
                getattr(block, ename)(body)
        self.q = {e: [] for e in ENGS}


def MM(out, lhsT, rhs, start=True, stop=True):
    return lambda e: e.matmul(out, lhsT=lhsT, rhs=rhs, start=start, stop=stop)


def TR(out, in_, ident):
    return lambda e: e.transpose(out=out, in_=in_, identity=ident)


def ACT(out, in_, func, bias=None, scale=None):
    kw = {}
    if bias is not None:
        kw["bias"] = bias
    if scale is not None:
        kw["scale"] = scale
    return lambda e: e.activation(out=out, in_=in_, func=func, **kw)


def TT(out, a, b, op):
    return lambda e: e.tensor_tensor(out=out, in0=a, in1=b, op=op)


def TS(out, a, s1, op0, s2=None, op1=None):
    if op1 is None:
        return lambda e: e.tensor_scalar(out=out, in0=a, scalar1=s1, scalar2=None, op0=op0)
    return lambda e: e.tensor_scalar(out=out, in0=a, scalar1=s1, scalar2=s2, op0=op0, op1=op1)


def STT(out, in0, scalar, in1, op0, op1):
    return lambda e: e.scalar_tensor_tensor(out=out, in0=in0, scalar=scalar, in1=in1, op0=op0, op1=op1)


def CP(out, in_):
    return lambda e: e.tensor_copy(out=out, in_=in_)


def MEMSET(ap, v):
    return lambda e: e.memset(ap, v)


def tiles_of(width):
    t = [(0, 128)]
    c = 128
    while c < LP:
        t.append((c, width))
        c += width
    return t


class K:
    def __init__(self, nc, P, n_layers, dbg):
        self.nc = nc
        self.P = P
        self.n_layers = n_layers
        self.dbg = dbg or ()
        self.ext = {}
        self.scr = {}

    def din(self, name, shape, dt=F32):
        ap = self.nc.dram_tensor(name, list(shape), dt, kind="ExternalInput").ap()
        self.ext[name] = ap
        return ap

    def dscr(self, name, shape, dt):
        kind = "ExternalOutput" if name in self.dbg else "Internal"
        ap = self.nc.dram_tensor(name, list(shape), dt, kind=kind).ap()
        self.scr[name] = TB(ap, 1)
        return self.scr[name]

    def load_consts(self, st):
        P = self.P
        P.pstack = st
        c = self.ext
        self.identf = P.sbuf([128, 128], F32, dma=True)
        self.tri_f = P.sbuf([128, 128], F32, dma=True)
        self.sellast = P.sbuf([128, 128], F32, dma=True)
        self.mask0 = P.sbuf([128, 128], F32, dma=True)
        self.kmask0 = P.sbuf([128, 1], F32, dma=True)
        self.identb = P.sbuf([128, 128], BF16)
        self.ones_b = P.sbuf([128, 128], BF16)
        self.tri_b = P.sbuf([128, 128], BF16)
        self.mask0_b = P.sbuf([128, 128], BF16)
        for tb, nm in ((self.identf, "c_ident"), (self.tri_f, "c_tri"), (self.sellast, "c_sellast"),
                       (self.mask0, "c_mask0")):
            P.dma("sync", [(tb.t[:], c[nm][:, :])], [], tb.b, tb.ds)
        P.dma("sync", [(self.kmask0.t[:], c["c_kmask0"][:, :])], [], self.kmask0.b, self.kmask0.ds)
        P.op("vector", CP(self.identb.t[:], self.identf.t[:]), self.identf.b, self.identb.b)
        P.op("vector", CP(self.tri_b.t[:], self.tri_f.t[:]), self.tri_f.b, self.tri_b.b)
        P.op("vector", CP(self.mask0_b.t[:], self.mask0.t[:]), self.mask0.b, self.mask0_b.b)
        P.op("vector", MEMSET(self.ones_b.t[:], 1.0), [], self.ones_b.b)

    def load_w(self, dst, src, kchunks, per=4):
        P = self.P
        for k0 in range(0, kchunks, per):
            k1 = min(kchunks, k0 + per)
            P.dma("gpsimd", [(dst.t[:, k0:k1, :], src[k0 * 128:k1 * 128, :].rearrange("(k p) n -> p k n", p=128))],
                  [], dst.b, dst.ds)

    def ln_fm(self, s, out, W, g, b, R):
        P = self.P
        ones = self.ones_b
        sum_ps, ssq_ps = R["sum"], R["ssq"]
        for c in range(8):
            sb = R["sb"].next()
            sq = R["sq"].next()
            P.op("scalar", ACT(sb.t[:, 0:W], s.t[:, c, 0:W], AF.Copy), [s.b[c]], sb.b)
            P.op("scalar", ACT(sq.t[:, 0:W], s.t[:, c, 0:W], AF.Square), [s.b[c]], sq.b)
            P.op("tensor", MM(sum_ps.t[:, 0:W], ones.t[:], sb.t[:, 0:W], c == 0, c == 7), sb.b + ones.b, sum_ps.b,
                 mark=False)
            P.op("tensor", MM(ssq_ps.t[:, 0:W], ones.t[:], sq.t[:, 0:W], c == 0, c == 7), sq.b + ones.b, ssq_ps.b,
                 mark=True)
        mean, var, rstd = R["mean"], R["var"], R["rstd"]
        P.op("vector", TS(mean.t[:, 0:W], sum_ps.t[:, 0:W], 1.0 / D, ALU.mult), sum_ps.b, mean.b)
        P.op("vector", TS(var.t[:, 0:W], ssq_ps.t[:, 0:W], 1.0 / D, ALU.mult), ssq_ps.b, var.b)
        msq = R["msq"]
        P.op("vector", TT(msq.t[:, 0:W], mean.t[:, 0:W], mean.t[:, 0:W], ALU.mult), mean.b, msq.b)
        P.op("vector", TT(var.t[:, 0:W], var.t[:, 0:W], msq.t[:, 0:W], ALU.subtract), var.b + msq.b, var.b)
        P.op("vector", TS(var.t[:, 0:W], var.t[:, 0:W], LN_EPS, ALU.add), var.b, var.b)
        P.op("scalar", ACT(rstd.t[:, 0:W], var.t[:, 0:W], AF.Ln), var.b, rstd.b)
        P.op("scalar", ACT(rstd.t[:, 0:W], rstd.t[:, 0:W], AF.Exp, scale=-0.5), rstd.b, rstd.b)
        for c in range(8):
            tmp = R["tmp"].next()
            P.op("vector", TT(tmp.t[:, 0:W], s.t[:, c, 0:W], mean.t[:, 0:W], ALU.subtract), [s.b[c]] + mean.b, tmp.b)
            P.op("vector", TT(tmp.t[:, 0:W], tmp.t[:, 0:W], rstd.t[:, 0:W], ALU.mult), tmp.b + rstd.b, tmp.b)
            P.op("scalar", ACT(out.t[:, c, 0:W], tmp.t[:, 0:W], AF.Identity, bias=b[:, c:c + 1], scale=g[:, c:c + 1]),
                 tmp.b, [out.b[c]])

    def ln_resources(self, wmax=512):
        P = self.P
        R = {}
        R["sum"] = P.psum([128, 512], F32)
        R["ssq"] = P.psum([128, 512], F32)
        R["sb"] = Ring([P.sbuf([128, wmax], BF16) for _ in range(2)])
        R["sq"] = Ring([P.sbuf([128, wmax], BF16) for _ in range(2)])
        for nm in ("mean", "var", "msq", "rstd"):
            R[nm] = P.sbuf([128, wmax], F32)
        R["tmp"] = Ring([P.sbuf([128, wmax], F32) for _ in range(2)])
        return R

    def phase_E(self):
        P = self.P
        hT = self.scr["hT"]
        with ExitStack() as st:
            P.begin_phase(st)
            xin = self.ext["xin"]
            gb = P.sbuf([128, 16], F32, dma=True)
            P.dma("sync", [(gb.t[:, 0:8], self.ext["emb_g"][:, :]), (gb.t[:, 8:16], self.ext["emb_b"][:, :])],
                  [], gb.b, gb.ds)
            xr = Ring([P.sbuf([128, 4, 1024], F32, dma=True) for _ in range(2)])
            sr = Ring([P.sbuf([128, 8, 512], F32, n=8) for _ in range(2)])
            orr = Ring([P.sbuf([128, 8, 512], F32, n=8, dma=True) for _ in range(2)])
            tpr = Ring([P.psum([128, 512], F32) for _ in range(2)])
            R = self.ln_resources()
            for (c0, W) in tiles_of(512):
                nb = W // 128
                x = xr.next()
                P.dma("sync", [(x.t[:, 0:nb, :], xin[c0:c0 + W, :].rearrange("(b p) f -> p b f", p=128))],
                      [], x.b, x.ds)
                s = sr.next()
                for c in range(8):
                    tp = tpr.next()
                    for bi in range(nb):
                        P.op("tensor", TR(tp.t[:, bi * 128:(bi + 1) * 128], x.t[:, bi, c * 128:(c + 1) * 128],
                                          self.identf.t[:]), x.b + self.identf.b, tp.b, mark=(bi == nb - 1))
                    P.op("vector" if c % 2 else "scalar",
                         CP(s.t[:, c, 0:W], tp.t[:, 0:W]) if c % 2 else ACT(s.t[:, c, 0:W], tp.t[:, 0:W], AF.Copy),
                         tp.b, [s.b[c]])
                o = orr.next()
                self.ln_fm(s, o, W, gb.t[:, 0:8], gb.t[:, 8:16], R)
                P.dma("gpsimd", [(hT.t[:, :, c0:c0 + W].rearrange("c p t -> p c t"), o.t[:, :, 0:W])],
                      o.b, hT.b, o.ss, nowaw=True)
            P.end_phase()


    def phase_A1(self, l):
        P = self.P
        E = self.ext
        hT, qT, kT, kpeT, vS = (self.scr[n] for n in ("hT", "qT", "kT", "kpeT", "v"))
        with ExitStack() as st:
            P.begin_phase(st)
            w_in = E["w_in"]
            Wql = P.sbuf([128, 8, QL], BF16, dma=True)
            Wkvl = P.sbuf([128, 8, KVL], BF16, dma=True)
            Wkpe = P.sbuf([128, 8, 64], BF16, dma=True)
            Wkpes = P.sbuf([128, 8, 64], BF16, dma=True)
            Wqn = P.sbuf([128, 6, 1024], BF16, dma=True)
            Wqp = P.sbuf([128, 6, 512], BF16, dma=True)
            Wqps = P.sbuf([128, 6, 512], BF16, dma=True)
            Wkn = P.sbuf([128, 2, 1024], BF16, dma=True)
            Wv = P.sbuf([128, 2, 1024], BF16, dma=True)
            self.load_w(Wql, w_in[l, :, 0:768], 8)
            self.load_w(Wkvl, w_in[l, :, 768:1024], 8, per=8)
            self.load_w(Wkpe, w_in[l, :, 1024:1088], 8, per=8)
            self.load_w(Wkpes, E["w_kpes"][l, :, :], 8, per=8)
            self.load_w(Wqn, E["wqb_n"][l, :, :], 6, per=3)
            self.load_w(Wqp, E["wqb_p"][l, :, :], 6, per=6)
            self.load_w(Wqps, E["wqb_ps"][l, :, :], 6, per=6)
            self.load_w(Wkn, E["wkvb_kn"][l, :, :], 2)
            self.load_w(Wv, E["wkvb_v"][l, :, :], 2)
            ng = P.sbuf([128, 8], F32, dma=True)
            P.dma("sync", [(ng.t[:, 0:6], E["qng"][l, :, :]), (ng.t[:, 6:8], E["kvng"][l, :, :])], [], ng.b, ng.ds)
            hr = Ring([P.sbuf([128, 8, 512], F32, dma=True) for _ in range(2)])
            hbr = Ring([P.sbuf([128, 8, 512], BF16, n=8) for _ in range(2)])
            csr = Ring([P.sbuf([128, 2, 512], F32, dma=True) for _ in range(2)])
            qlat = P.sbuf([128, 6, 512], F32, n=6)
            qn = P.sbuf([128, 6, 512], BF16, n=6)
            kvlat = P.sbuf([128, 2, 512], F32, n=2)
            kvn = P.sbuf([128, 2, 512], BF16, n=2)
            sqr = Ring([P.sbuf([128, 512], BF16) for _ in range(2)])
            rstd = P.sbuf([128, 512], F32)
            t1r = Ring([P.sbuf([128, 512], F32) for _ in range(2)])
            t2r = Ring([P.sbuf([128, 512], F32) for _ in range(2)])
            stg = Ring([P.sbuf([128, 512], BF16, dma=True) for _ in range(4)])
            vst = Ring([P.sbuf([128, 1024], BF16, n=2, dma=True) for _ in range(2)])
            accr = Ring([P.psum([128, 512], F32) for _ in range(4)])
            ss_ps = P.psum([128, 512], F32)
            ones = self.ones_b
            evi = [0]

            def evac(out_ap, in_ap, reads, writes):
                evi[0] += 1
                if evi[0] % 2:
                    P.op("scalar", ACT(out_ap, in_ap, AF.Copy), reads, writes)
                else:
                    P.op("vector", CP(out_ap, in_ap), reads, writes)

            def rms(acc_list_fn, nch, lat, dst, W, gcol0, nfeat):
                for c in range(nch):
                    acc = acc_list_fn(c)
                    sq = sqr.next()
                    P.op("scalar", ACT(lat.t[:, c, 0:W], acc.t[:, 0:W], AF.Copy), acc.b, [lat.b[c]])
                    P.op("scalar", ACT(sq.t[:, 0:W], acc.t[:, 0:W], AF.Square), acc.b, sq.b)
                    P.op("tensor", MM(ss_ps.t[:, 0:W], ones.t[:], sq.t[:, 0:W], c == 0, c == nch - 1),
                         sq.b + ones.b, ss_ps.b, mark=(c == nch - 1))
                P.op("vector", TS(rstd.t[:, 0:W], ss_ps.t[:, 0:W], 1.0 / nfeat, ALU.mult, RMS_EPS, ALU.add),
                     ss_ps.b, rstd.b)
                P.op("scalar", ACT(rstd.t[:, 0:W], rstd.t[:, 0:W], AF.Ln), rstd.b, rstd.b)
                P.op("scalar", ACT(rstd.t[:, 0:W], rstd.t[:, 0:W], AF.Exp, scale=-0.5), rstd.b, rstd.b)
                for c in range(nch):
                    P.op("vector", STT(dst.t[:, c, 0:W], lat.t[:, c, 0:W], ng.t[:, gcol0 + c:gcol0 + c + 1],
                                       rstd.t[:, 0:W], ALU.mult, ALU.mult), [lat.b[c]] + rstd.b + ng.b, [dst.b[c]])

            def proj(acc, W_tb, ncols0, ncols, K, rhs_tb, W, M=128):
                for k in range(K):
                    P.op("tensor", MM(acc.t[0:M, 0:W], W_tb.t[:, k, ncols0:ncols0 + ncols], rhs_tb.t[:, k, 0:W],
                                      k == 0, k == K - 1), W_tb.b + [rhs_tb.b[k]], acc.b, mark=(k == K - 1))

            def rope(acc1, acc2, cs, W, M, outs):
                t1 = t1r.next()
                t2 = t2r.next()
                P.op("vector", TT(t1.t[0:M, 0:W], acc1.t[0:M, 0:W], cs.t[0:M, 0, 0:W], ALU.mult), acc1.b + cs.b, t1.b)
                P.op("vector", TT(t2.t[0:M, 0:W], acc2.t[0:M, 0:W], cs.t[0:M, 1, 0:W], ALU.mult), acc2.b + cs.b, t2.b)
                sg = stg.next()
                P.op("vector", TT(sg.t[0:M, 0:W], t1.t[0:M, 0:W], t2.t[0:M, 0:W], ALU.add), t1.b + t2.b, sg.b)
                for (dst_tb, dst_ap, p0, p1) in outs:
                    P.dma("gpsimd", [(dst_ap, sg.t[p0:p1, 0:W])], sg.b, dst_tb.b, sg.ss, nowaw=True)

            for (c0, W) in tiles_of(512):
                nb = W // 128
                h = hr.next()
                P.dma("sync", [(h.t[:, :, 0:W], hT.t[:, :, c0:c0 + W].rearrange("c p t -> p c t"))], hT.b, h.b, h.ds)
                cs = csr.next()
                P.dma("sync", [(cs.t[:, 0, 0:W], E["cosT2"][:, c0:c0 + W]), (cs.t[:, 1, 0:W], E["sinT2"][:, c0:c0 + W])],
                      [], cs.b, cs.ds)
                hb = hbr.next()
                for c in range(8):
                    evac(hb.t[:, c, 0:W], h.t[:, c, 0:W], h.b, [hb.b[c]])

                def ql_acc(c):
                    acc = accr.next()
                    proj(acc, Wql, c * 128, 128, 8, hb, W)
                    return acc
                rms(ql_acc, 6, qlat, qn, W, 0, QL)
                for hd in range(NH):
                    acc = accr.next()
                    proj(acc, Wqn, hd * 128, 128, 6, qn, W)
                    sg = stg.next()
                    evac(sg.t[:, 0:W], acc.t[:, 0:W], acc.b, sg.b)
                    P.dma("gpsimd", [(qT.t[hd, 0:128, c0:c0 + W], sg.t[:, 0:W])], sg.b, qT.b, sg.ss, nowaw=True)
                for pr in range(4):
                    a1 = accr.next()
                    proj(a1, Wqp, pr * 128, 128, 6, qn, W)
                    a2 = accr.next()
                    proj(a2, Wqps, pr * 128, 128, 6, qn, W)
                    rope(a1, a2, cs, W, 128, [(qT, qT.t[2 * pr, 128:192, c0:c0 + W], 0, 64),
                                              (qT, qT.t[2 * pr + 1, 128:192, c0:c0 + W], 64, 128)])

                def kv_acc(c):
                    acc = accr.next()
                    proj(acc, Wkvl, c * 128, 128, 8, hb, W)
                    return acc
                rms(kv_acc, 2, kvlat, kvn, W, 6, KVL)
                for hd in range(NH):
                    acc = accr.next()
                    proj(acc, Wkn, hd * 128, 128, 2, kvn, W)
                    sg = stg.next()
                    evac(sg.t[:, 0:W], acc.t[:, 0:W], acc.b, sg.b)
                    P.dma("gpsimd", [(kT.t[hd, :, c0:c0 + W], sg.t[:, 0:W])], sg.b, kT.b, sg.ss, nowaw=True)
                for bi in range(nb):
                    vs = vst.next()
                    for half in range(2):
                        acc = accr.next()
                        for k in range(2):
                            P.op("tensor", MM(acc.t[:, :], kvn.t[:, k, bi * 128:(bi + 1) * 128],
                                              Wv.t[:, k, half * 512:(half + 1) * 512], k == 0, k == 1),
                                 Wv.b + [kvn.b[k]], acc.b, mark=(k == 1))
                        evac(vs.t[:, half * 512:(half + 1) * 512], acc.t[:, :], acc.b, [vs.b[half]])
                    blk = c0 // 128 + bi
                    P.dma("gpsimd", [(vS.t[:, :, blk, :].rearrange("h p d -> p h d"),
                                      vs.t[:, :].rearrange("p (h d) -> p h d", h=NH))], vs.b, vS.b, vs.ss, nowaw=True)
                a1 = accr.next()
                proj(a1, Wkpe, 0, 64, 8, hb, W, M=64)
                a2 = accr.next()
                proj(a2, Wkpes, 0, 64, 8, hb, W, M=64)
                rope(a1, a2, cs, W, 64, [(kpeT, kpeT.t[:, c0:c0 + W], 0, 64)])
            P.end_phase()


    def phase_B(self, l):
        P = self.P
        qT, kT, kpeT, vS, oT = (self.scr[n] for n in ("qT", "kT", "kpeT", "v", "oT"))
        ones = self.ones_b
        with ExitStack() as st:
            P.begin_phase(st)
            kpe = P.sbuf([64, LP], BF16, dma=True)
            P.dma("sync", [(kpe.t[:, :], kpeT.t[:, :])], kpeT.b, kpe.b, kpe.ds)
            knr = Ring([P.sbuf([128, LP], BF16, dma=True) for _ in range(2)])
            vr = Ring([P.sbuf([128, NBLK, 128], BF16, dma=True) for _ in range(2)])
            qr = Ring([P.sbuf([128, 2, 512], BF16, dma=True) for _ in range(3)])
            ptr = Ring([P.sbuf([128, 512], BF16) for _ in range(6)])
            rsr = Ring([P.sbuf([128, 512], F32) for _ in range(2)])
            osr = Ring([P.sbuf([128, 512], BF16, dma=True) for _ in range(2)])
            spr = Ring([P.psum([128, 512], F32) for _ in range(4)])
            opr = Ring([P.psum([128, 512], F32) for _ in range(2)])
            smr = Ring([P.psum([128, 512], F32) for _ in range(2)])
            tiles = tiles_of(512)
            for hd in range(NH):
                kn = knr.next()
                P.dma("sync", [(kn.t[:, :], kT.t[hd, :, :])], kT.b, kn.b, kn.ds)
                v = vr.next()
                P.dma("sync", [(v.t[:, :, :], vS.t[hd, :, :, :])], vS.b, v.b, v.ds)
                for (c0, W) in tiles:
                    q = qr.next()
                    P.dma("sync", [(q.t[:, 0, 0:W], qT.t[hd, 0:128, c0:c0 + W]),
                                   (q.t[0:64, 1, 0:W], qT.t[hd, 128:192, c0:c0 + W])], qT.b, q.b, q.ds)
                    o_ps = opr.next()
                    sm_ps = smr.next()
                    nkb = (c0 + W) // 128
                    units = []
                    for kb in range(nkb):
                        qo = kb * 128 - c0 if kb * 128 >= c0 else 0
                        units.append((kb, qo))

                    def qk(u):
                        kb, qo = u
                        sp = spr.next()
                        P.op("tensor", MM(sp.t[:, qo:W], kn.t[:, kb * 128:(kb + 1) * 128], q.t[:, 0, qo:W], True, False),
                             kn.b + q.b, sp.b, mark=False)
                        P.op("tensor", MM(sp.t[:, qo:W], kpe.t[0:64, kb * 128:(kb + 1) * 128], q.t[0:64, 1, qo:W],
                                          False, True), kpe.b + q.b, sp.b, mark=True)
                        return sp

                    def soft(u, sp):
                        kb, qo = u
                        pt = ptr.next()
                        P.op("scalar", ACT(pt.t[:, qo:W], sp.t[:, qo:W], AF.Exp, scale=SCALE), sp.b, pt.b)
                        if kb == 0 and c0 == 0:
                            P.op("vector", TT(pt.t[:, 0:128], pt.t[:, 0:128], self.mask0_b.t[:], ALU.mult),
                                 pt.b + self.mask0_b.b, pt.b)
                        elif kb == 0:
                            P.op("vector", TS(pt.t[:, 0:W], pt.t[:, 0:W], self.kmask0.t[:, 0:1], ALU.mult),
                                 pt.b + self.kmask0.b, pt.b)
                        elif kb * 128 >= c0:
                            P.op("vector", TT(pt.t[:, qo:qo + 128], pt.t[:, qo:qo + 128], self.tri_b.t[:], ALU.mult),
                                 pt.b + self.tri_b.b, pt.b)
                        return pt

                    def pv(u, pt, first, last):
                        kb, qo = u
                        P.op("tensor", MM(o_ps.t[:, qo:W], v.t[:, kb, :], pt.t[:, qo:W], first, last),
                             v.b + pt.b, o_ps.b, mark=False)
                        P.op("tensor", MM(sm_ps.t[:, qo:W], ones.t[:], pt.t[:, qo:W], first, last),
                             ones.b + pt.b, sm_ps.b, mark=True)

                    LA = 2
                    sps = [qk(units[i]) for i in range(min(LA, len(units)))]
                    for i, u in enumerate(units):
                        if i + LA < len(units):
                            sps.append(qk(units[i + LA]))
                        pt = soft(u, sps[i])
                        pv(u, pt, i == 0, i == len(units) - 1)
                    rs = rsr.next()
                    P.op("vector", lambda e, o=rs.t[:, 0:W], i=sm_ps.t[:, 0:W]: e.reciprocal(out=o, in_=i), sm_ps.b, rs.b)
                    og = osr.next()
                    P.op("vector", TT(og.t[:, 0:W], o_ps.t[:, 0:W], rs.t[:, 0:W], ALU.mult), o_ps.b + rs.b, og.b)
                    P.dma("gpsimd", [(oT.t[hd, :, c0:c0 + W], og.t[:, 0:W])], og.b, oT.b, og.ss, nowaw=True)
            P.end_phase()


    def phase_A2(self, l):
        P = self.P
        E = self.ext
        hT, xsS, BtokS, BTS, CTS, dtS, zsS = (self.scr[n] for n in ("hT", "xs", "Btok", "BT", "CT", "dt", "zs"))
        with ExitStack() as st:
            P.begin_phase(st)
            w_in = E["w_in"]
            Wz = P.sbuf([128, 8, 2048], BF16, dma=True)
            Wx = P.sbuf([128, 8, 3072], BF16, dma=True)
            Wdt = P.sbuf([128, 8, 32], BF16, dma=True)
            self.load_w(Wz, w_in[l, :, 1088:3136], 8, per=2)
            self.load_w(Wx, w_in[l, :, 3136:6208], 8, per=2)
            self.load_w(Wdt, w_in[l, :, 6208:6240], 8, per=8)
            cw = P.sbuf([128, 24, 4], F32, dma=True)
            cb = P.sbuf([128, 24], F32, dma=True)
            dtb = P.sbuf([128, 32], F32, dma=True)
            P.dma("sync", [(cw.t[:], E["ssd_cw"][l, :, :, :])], [], cw.b, cw.ds)
            P.dma("sync", [(cb.t[:], E["ssd_cb"][l, :, :])], [], cb.b, cb.ds)
            P.dma("sync", [(dtb.t[:], E["dtb_bc"][l, :, :])], [], dtb.b, dtb.ds)
            h = P.sbuf([128, 8, 512], F32, dma=True)
            hbr = Ring([P.sbuf([128, 8, 512], BF16, n=8) for _ in range(2)])
            xc = P.sbuf([128, 24, 512], BF16, n=24, dma=True)
            xbr = Ring([P.sbuf([128, 515], F32) for _ in range(2)])
            halo = P.sbuf([128, 24, 3], F32, n=24)
            halo2 = P.sbuf([128, 24, 3], F32, n=24)
            halos = [halo, halo2]
            tcr = Ring([P.sbuf([128, 512], F32) for _ in range(4)])
            zst = Ring([P.sbuf([128, 2048], BF16, n=4, dma=True) for _ in range(2)])
            dtr = Ring([P.sbuf([128, 32], F32, dma=True) for _ in range(2)])
            tst = Ring([P.sbuf([128, 2560], BF16, n=5, dma=True) for _ in range(2)])
            accr = Ring([P.psum([128, 512], F32) for _ in range(6)])
            tpr = Ring([P.psum([128, 512], BF16) for _ in range(2)])
            P.op("vector", MEMSET(halo.t[:], 0.0), [], halo.b)
            evi = [0]

            def evac(out_ap, in_ap, reads, writes):
                evi[0] += 1
                if evi[0] % 2:
                    P.op("scalar", ACT(out_ap, in_ap, AF.Copy), reads, writes)
                else:
                    P.op("vector", CP(out_ap, in_ap), reads, writes)

            for tile_i, (c0, W) in enumerate(tiles_of(512)):
                nb = W // 128
                P.dma("sync", [(h.t[:, :, 0:W], hT.t[:, :, c0:c0 + W].rearrange("c p t -> p c t"))], hT.b, h.b, h.ds)
                hb = hbr.next()
                for c in range(8):
                    evac(hb.t[:, c, 0:W], h.t[:, c, 0:W], h.b, [hb.b[c]])
                for bi in range(nb):
                    zt = zst.next()
                    for cg in range(4):
                        acc = accr.next()
                        for k in range(8):
                            P.op("tensor", MM(acc.t[:, :], hb.t[:, k, bi * 128:(bi + 1) * 128],
                                              Wz.t[:, k, cg * 512:(cg + 1) * 512], k == 0, k == 7),
                                 Wz.b + [hb.b[k]], acc.b, mark=(k == 7))
                        P.op("scalar", ACT(zt.t[:, cg * 512:(cg + 1) * 512], acc.t[:, :], AF.Silu), acc.b, [zt.b[cg]])
                    r0 = c0 + bi * 128
                    P.dma("gpsimd", [(zsS.t[r0:r0 + 128, :], zt.t[:, :])], zt.b, zsS.b, zt.ss, nowaw=True)
                    acc = accr.next()
                    for k in range(8):
                        P.op("tensor", MM(acc.t[:, 0:32], hb.t[:, k, bi * 128:(bi + 1) * 128], Wdt.t[:, k, 0:32],
                                          k == 0, k == 7), Wdt.b + [hb.b[k]], acc.b, mark=(k == 7))
                    dtt = dtr.next()
                    P.op("vector", TT(dtt.t[:, :], acc.t[:, 0:32], dtb.t[:, :], ALU.add), acc.b + dtb.b, dtt.b)
                    P.op("scalar", ACT(dtt.t[:, :], dtt.t[:, :], AF.Exp), dtt.b, dtt.b)
                    P.op("scalar", ACT(dtt.t[:, :], dtt.t[:, :], AF.Ln, bias=1.0), dtt.b, dtt.b)
                    P.dma("gpsimd", [(dtS.t[r0:r0 + 128, :], dtt.t[:, :])], dtt.b, dtS.b, dtt.ss, nowaw=True)
                for c in range(24):
                    acc = accr.next()
                    for k in range(8):
                        P.op("tensor", MM(acc.t[:, 0:W], Wx.t[:, k, c * 128:(c + 1) * 128], hb.t[:, k, 0:W],
                                          k == 0, k == 7), Wx.b + [hb.b[k]], acc.b, mark=(k == 7))
                    hin, hout = halos[tile_i % 2], halos[(tile_i + 1) % 2]
                    tc_ = tcr.next()
                    if c0 == 0:
                        xb = xbr.next()
                        P.op("vector", MEMSET(xb.t[:, 0:3 + PAD], 0.0), [], xb.b)
                        P.op("scalar", ACT(xb.t[:, 3 + PAD:3 + W], acc.t[:, PAD:W], AF.Copy), [], xb.b + acc.b)
                        P.op("scalar", ACT(hout.t[:, c, :], xb.t[:, W:W + 3], AF.Copy), xb.b, [hout.b[c]])
                        P.op("scalar", ACT(tc_.t[:, 0:W], xb.t[:, 0:W], AF.Identity, bias=cb.t[:, c:c + 1], scale=cw.t[:, c, 0:1]),
                             xb.b + cw.b + cb.b, tc_.b)
                        for tap in range(1, 4):
                            P.op("vector", STT(tc_.t[:, 0:W], xb.t[:, tap:tap + W], cw.t[:, c, tap:tap + 1], tc_.t[:, 0:W],
                                               ALU.mult, ALU.add), xb.b + cw.b, tc_.b)
                    else:
                        P.op("scalar", ACT(tc_.t[:, 3:W], acc.t[:, 0:W - 3], AF.Identity, bias=cb.t[:, c:c + 1], scale=cw.t[:, c, 0:1]),
                             cw.b + cb.b, tc_.b + acc.b)
                        P.op("scalar", ACT(hout.t[:, c, :], acc.t[:, W - 3:W], AF.Copy), [], [hout.b[c]] + acc.b)
                        P.op("scalar", ACT(tc_.t[:, 0:3], hin.t[:, c, :], AF.Identity, bias=cb.t[:, c:c + 1], scale=cw.t[:, c, 0:1]),
                             [hin.b[c]], tc_.b)
                        P.op("vector", STT(tc_.t[:, 0:2], hin.t[:, c, 1:3], cw.t[:, c, 1:2], tc_.t[:, 0:2], ALU.mult, ALU.add),
                             [hin.b[c]] + cw.b, tc_.b)
                        P.op("vector", STT(tc_.t[:, 0:1], hin.t[:, c, 2:3], cw.t[:, c, 2:3], tc_.t[:, 0:1], ALU.mult, ALU.add),
                             [hin.b[c]] + cw.b, tc_.b)
                        for tap in range(1, 4):
                            P.op("vector", STT(tc_.t[:, 3 - tap:W], acc.t[:, 0:W - 3 + tap], cw.t[:, c, tap:tap + 1],
                                               tc_.t[:, 3 - tap:W], ALU.mult, ALU.add), cw.b, tc_.b + acc.b)
                    P.op("scalar", ACT(xc.t[:, c, 0:W], tc_.t[:, 0:W], AF.Silu), tc_.b, [xc.b[c]])
                    if c0 == 0:
                        P.op("vector", MEMSET(xc.t[:, c, 0:PAD], 0.0), [], [xc.b[c]])
                prs = []
                for g in range(4):
                    prs.append((BTS.t[g, :, c0:c0 + W], xc.t[:, 16 + g, 0:W]))
                    prs.append((CTS.t[g, :, c0:c0 + W], xc.t[:, 20 + g, 0:W]))
                P.dma("gpsimd", prs, xc.b[16:24], BTS.b + CTS.b, xc.ss, nowaw=True)
                for bi in range(nb):
                    ts_ = tst.next()
                    for q4 in range(5):
                        tp = tpr.next()
                        for j in range(4):
                            c = q4 * 4 + j
                            P.op("tensor", TR(tp.t[:, j * 128:(j + 1) * 128], xc.t[:, c, bi * 128:(bi + 1) * 128],
                                              self.identb.t[:]), [xc.b[c]] + self.identb.b, tp.b, mark=(j == 3))
                        evac(ts_.t[:, q4 * 512:(q4 + 1) * 512], tp.t[:, :], tp.b, [ts_.b[q4]])
                    r0 = c0 + bi * 128
                    P.dma("gpsimd", [(xsS.t[r0:r0 + 128, :], ts_.t[:, 0:2048]), (BtokS.t[r0:r0 + 128, :], ts_.t[:, 2048:2560])],
                          ts_.b, xsS.b + BtokS.b, ts_.ss, nowaw=True)
            P.end_phase()

    def phase_S(self, l):
        P = self.P
        E = self.ext
        xsS, BtokS, BTS, CTS, dtS, zsS, ynTS = (self.scr[n] for n in ("xs", "Btok", "BT", "CT", "dt", "zs", "ynT"))
        trib = self.tri_b
        ones = self.ones_b
        identb = self.identb
        with ExitStack() as st:
            P.begin_phase(st)
            abc = P.sbuf([128, 32], F32, dma=True)
            dsk = P.sbuf([128, 2048], F32, dma=True)
            ngb = P.sbuf([128, 2048], F32, dma=True)
            negm_f = P.sbuf([128, 128], F32, dma=True)
            P.dma("sync", [(abc.t[:], E["alog_bc"][l, :, :])], [], abc.b, abc.ds)
            P.dma("sync", [(dsk.t[:], E["dskip_bc"][l, :, :])], [], dsk.b, dsk.ds)
            P.dma("sync", [(ngb.t[:], E["ssdng_bc"][l, :, :])], [], ngb.b, ngb.ds)
            P.dma("sync", [(negm_f.t[:], E["c_negmask"][:, :])], [], negm_f.b, negm_f.ds)
            P.op("scalar", ACT(abc.t[:], abc.t[:], AF.Exp), abc.b, abc.b)
            P.op("vector", TS(abc.t[:], abc.t[:], -1.0, ALU.mult), abc.b, abc.b)
            negm = P.sbuf([128, 128], BF16)
            trin = P.sbuf([128, 128], BF16)
            P.op("vector", CP(negm.t[:], negm_f.t[:]), negm_f.b, negm.b)
            P.op("vector", TS(trin.t[:], self.tri_f.t[:], -1.0, ALU.mult), self.tri_f.b, trin.b)
            NB_ = 3
            xsr = Ring([P.sbuf([128, 2048], BF16, dma=True) for _ in range(NB_)])
            btr = Ring([P.sbuf([128, 512], BF16, dma=True) for _ in range(NB_)])
            bTr = Ring([P.sbuf([128, 4, 128], BF16, dma=True) for _ in range(NB_)])
            cTr = Ring([P.sbuf([128, 4, 128], BF16, dma=True) for _ in range(NB_)])
            dtr = Ring([P.sbuf([128, 32], F32, dma=True) for _ in range(NB_)])
            zsr = Ring([P.sbuf([128, 2048], BF16, dma=True) for _ in range(NB_)])
            prev = P.sbuf([128, 2048], F32, n=4)
            prevb = P.sbuf([128, 2048], BF16, n=4)
            P.op("vector", MEMSET(prev.t[:], 0.0), [], prev.b)
            P.op("vector", MEMSET(prevb.t[:], 0.0), [], prevb.b)
            at = P.sbuf([128, 32], F32)
            ar = P.sbuf([128, 32], F32)
            a3 = [P.sbuf([128, 32], BF16) for _ in range(3)]
            def mk_ctx():
                return dict(abig=[P.sbuf([128, 32, 128], BF16) for _ in range(2)],
                            abm=[P.sbuf([128, 32, 128], BF16) for _ in range(2)],
                            eacs=P.sbuf([128, 32], F32), cdec=P.sbuf([128, 32], F32),
                            xdt=P.sbuf([128, 2048], BF16), xdts=P.sbuf([128, 2048], BF16), xsd=P.sbuf([128, 2048], BF16))
            ctxr = Ring([mk_ctx() for _ in range(2)])
            acs = P.sbuf([128, 32], F32)
            dst = P.sbuf([128, 32], F32)
            y = P.sbuf([128, 2048], F32, n=4)
            yn = P.sbuf([128, 2048], BF16, n=4)
            junkr = Ring([P.sbuf([128, 512], F32) for _ in range(2)])
            ss = P.sbuf([128, 4], F32)
            rstd = P.sbuf([128, 4], F32)
            t1r = Ring([P.sbuf([128, 512], F32) for _ in range(2)])
            cbsr = Ring([P.sbuf([128, 128], F32) for _ in range(2)])
            er = Ring([P.sbuf([128, 4, 128], F32) for _ in range(3)])
            mtr = Ring([P.sbuf([128, 4, 128], BF16) for _ in range(4)])
            ynst = Ring([P.sbuf([128, 16, 128], BF16, n=4, dma=True) for _ in range(2)])
            misc = P.psum([128, 512], F32)
            acs_ps = TB(misc.t[:, 0:32]); acs_ps.b = misc.b
            last_ps = TB(misc.t[:, 32:64]); last_ps.b = misc.b
            cb_ps = TB(misc.t[:, 128:256]); cb_ps.b = misc.b
            dpr = Ring([P.psum([128, 512], F32) for i in range(2)])
            ydr = Ring([P.psum([128, 512], F32) for i in range(2)])
            yo_ps = P.psum([128, 512], F32)
            st_ps = P.psum([128, 512], F32)
            tpr = Ring([P.psum([128, 512], BF16) for _ in range(1)])
            bc3 = lambda ap2, n: ap2.unsqueeze(2).to_broadcast([128, n, 64])
            v3 = lambda ap2: ap2.rearrange("p (h d) -> p h d", d=64)
            nch = getattr(self, 'S_NCH', NBLK)
            loaded = {}
            ctxs = {}

            def load(c):
                r0 = c * 128
                t = dict(xs=xsr.next(), bt=btr.next(), bT=bTr.next(), cT=cTr.next(), dt=dtr.next(), zs=zsr.next())
                P.dma("sync", [(t["xs"].t[:], xsS.t[r0:r0 + 128, :])], xsS.b, t["xs"].b, t["xs"].ds)
                P.dma("sync", [(t["bt"].t[:], BtokS.t[r0:r0 + 128, :])], BtokS.b, t["bt"].b, t["bt"].ds)
                P.dma("sync", [(t["bT"].t[:], BTS.t[:, :, r0:r0 + 128].rearrange("g p t -> p g t"))], BTS.b, t["bT"].b, t["bT"].ds)
                P.dma("sync", [(t["cT"].t[:], CTS.t[:, :, r0:r0 + 128].rearrange("g p t -> p g t"))], CTS.b, t["cT"].b, t["cT"].ds)
                P.dma("sync", [(t["dt"].t[:], dtS.t[r0:r0 + 128, :])], dtS.b, t["dt"].b, t["dt"].ds)
                P.dma("sync", [(t["zs"].t[:], zsS.t[r0:r0 + 128, :])], zsS.b, t["zs"].b, t["zs"].ds)
                loaded[c] = t

            def front(c):
                t = loaded[c]
                xs, dt = t["xs"], t["dt"]
                X = ctxr.next()
                ctxs[c] = X
                abig, abm, eacs, cdec, xdt, xdts, xsd = (X[k_] for k_ in ("abig", "abm", "eacs", "cdec", "xdt", "xdts", "xsd"))
                P.op("vector", TT(at.t[:], dt.t[:], abc.t[:], ALU.mult), dt.b + abc.b, at.b)
                P.op("vector", CP(a3[0].t[:], at.t[:]), at.b, a3[0].b)
                P.op("vector", TT(ar.t[:], at.t[:], a3[0].t[:], ALU.subtract), at.b + a3[0].b, ar.b)
                P.op("vector", CP(a3[1].t[:], ar.t[:]), ar.b, a3[1].b)
                P.op("vector", TT(ar.t[:], ar.t[:], a3[1].t[:], ALU.subtract), a3[1].b, ar.b)
                P.op("vector", CP(a3[2].t[:], ar.t[:]), ar.b, a3[2].b)
                for i3 in range(3):
                    P.op("tensor", MM(acs_ps.t, trib.t[:], a3[i3].t[:], i3 == 0, i3 == 2), trib.b + a3[i3].b, misc.b,
                         mark=(i3 == 2))
                for i3 in range(3):
                    P.op("tensor", MM(last_ps.t, ones.t[:], a3[i3].t[:], i3 == 0, i3 == 2), ones.b + a3[i3].b, misc.b,
                         mark=(i3 == 2))
                P.op("vector", CP(acs.t[:], acs_ps.t), [], acs.b + misc.b)
                for i3 in range(2):
                    P.op("scalar", ACT(abig[i3].t[:], a3[i3].t[:, :].unsqueeze(2).to_broadcast([128, 32, 128]), AF.Copy),
                         a3[i3].b, abig[i3].b)
                    P.op("gpsimd", TT(abm[i3].t[:], a3[i3].t[:, :].unsqueeze(2).to_broadcast([128, 32, 128]),
                                      trin.t[:, :].unsqueeze(1).to_broadcast([128, 32, 128]), ALU.mult),
                         a3[i3].b + trin.b, abm[i3].b)
                P.op("scalar", ACT(eacs.t[:], acs.t[:], AF.Exp), acs.b, eacs.b)
                P.op("scalar", ACT(cdec.t[:], last_ps.t, AF.Exp), [], cdec.b + misc.b)
                P.op("vector", TT(dst.t[:], last_ps.t, acs.t[:], ALU.subtract), acs.b, dst.b + misc.b)
                P.op("scalar", ACT(dst.t[:], dst.t[:], AF.Exp), dst.b, dst.b)
                P.op("vector", TT(v3(xdt.t[:]), v3(xs.t[:]), bc3(dt.t[:, :], 32), ALU.mult), xs.b + dt.b, xdt.b)
                P.op("gpsimd", TT(v3(xdts.t[:]), v3(xdt.t[:]), bc3(dst.t[:, :], 32), ALU.mult), xdt.b + dst.b, xdts.b)
                P.op("gpsimd", TT(xsd.t[:], xs.t[:], dsk.t[:], ALU.mult), xs.b + dsk.b, xsd.b)

            load(0)
            front(0)
            for c in range(nch):
                r0 = c * 128
                if c + 1 < nch:
                    load(c + 1)
                    front(c + 1)
                t = loaded.pop(c)
                X = ctxs.pop(c)
                abig, abm, eacs, cdec, xdt, xdts, xsd = (X[k_] for k_ in ("abig", "abm", "eacs", "cdec", "xdt", "xdts", "xsd"))
                xs, bt, bT, cT, dt, zs = t["xs"], t["bt"], t["bT"], t["cT"], t["dt"], t["zs"]
                def pe_D(g):
                    P.op("tensor", MM(cb_ps.t, bT.t[:, g, :], cT.t[:, g, :]), bT.b + cT.b, misc.b)
                    dd = []
                    for hb4 in range(2):
                        dps = dpr.next()
                        dd.append(dps)
                        for j in range(4):
                            hd = g * 8 + hb4 * 4 + j
                            sl = dps.t[:, j * 128:(j + 1) * 128]
                            P.op("tensor", MM(sl, abig[0].t[:, hd, :], trib.t[:], True, False), abig[0].b + trib.b, dps.b, mark=False)
                            P.op("tensor", MM(sl, abig[1].t[:, hd, :], trib.t[:], False, False), abig[1].b, dps.b, mark=False)
                            P.op("tensor", MM(sl, abm[0].t[:, hd, :], ones.t[:], False, False), abm[0].b + ones.b, dps.b, mark=False)
                            P.op("tensor", MM(sl, abm[1].t[:, hd, :], ones.t[:], False, False), abm[1].b, dps.b, mark=False)
                            P.op("tensor", MM(sl, identb.t[:], negm.t[:], False, True), identb.b + negm.b, dps.b, mark=(j == 3))
                    return dd

                def tail(g, yd):
                    gs = slice(g * 512, (g + 1) * 512)
                    t1 = t1r.next()
                    P.op("vector", TT(v3(t1.t[:]), v3(yo_ps.t[:, :]), bc3(eacs.t[:, g * 8:(g + 1) * 8], 8), ALU.mult),
                         yo_ps.b + eacs.b, t1.b)
                    P.op("vector", TT(y.t[:, gs], yd.t[:, :], t1.t[:], ALU.add), yd.b + t1.b, [y.b[g]])
                    P.op("vector", TT(y.t[:, gs], y.t[:, gs], zs.t[:, gs], ALU.mult), zs.b, [y.b[g]])
                    junk = junkr.next()
                    P.op("gpsimd", TT(junk.t[:], y.t[:, gs], y.t[:, gs], ALU.mult), [y.b[g]], junk.b)
                    P.op("vector", lambda e, o=junk.t[:], acc=ss.t[:, g:g + 1]: e.tensor_scalar(
                        out=o, in0=o, scalar1=1.0, scalar2=0.0, op0=ALU.mult, op1=ALU.add, accum_out=acc),
                        [], junk.b + ss.b)
                    P.op("gpsimd", TT(v3(prev.t[:, gs]), v3(prev.t[:, gs]), bc3(cdec.t[:, g * 8:(g + 1) * 8], 8), ALU.mult),
                         cdec.b, [prev.b[g]])
                    P.op("vector", TT(prev.t[:, gs], prev.t[:, gs], st_ps.t[:, :], ALU.add), st_ps.b, [prev.b[g]])
                    P.op("scalar", ACT(prevb.t[:, gs], prev.t[:, gs], AF.Copy), [prev.b[g]], [prevb.b[g]])

                dcur = pe_D(0)
                yds = {}
                for g in range(4):
                    gs = slice(g * 512, (g + 1) * 512)
                    cbs = cbsr.next()
                    P.op("scalar", ACT(cbs.t[:], cb_ps.t, AF.Copy), [], cbs.b + misc.b)
                    mts = []
                    for hb4 in range(2):
                        dps = dcur[hb4]
                        ee = er.next()
                        P.op("scalar", ACT(ee.t[:].rearrange("p a b -> p (a b)"), dps.t[:, :], AF.Exp), dps.b, ee.b)
                        mt = mtr.next()
                        P.op("vector", TT(mt.t[:], ee.t[:], cbs.t[:, :].unsqueeze(1).to_broadcast([128, 4, 128]), ALU.mult),
                             ee.b + cbs.b, mt.b)
                        mts.append(mt)
                    if g + 1 < 4:
                        dcur = pe_D(g + 1)
                    yd = ydr.next()
                    yds[g] = yd
                    for hb4 in range(2):
                        mt = mts[hb4]
                        for j in range(4):
                            hh = hb4 * 4 + j
                            hd = g * 8 + hh
                            P.op("tensor", MM(yd.t[:, hh * 64:(hh + 1) * 64], mt.t[:, j, :], xdt.t[:, hd * 64:(hd + 1) * 64], True, False),
                                 mt.b + xdt.b, yd.b, mark=False)
                            P.op("tensor", MM(yd.t[:, hh * 64:(hh + 1) * 64], identb.t[:], xsd.t[:, hd * 64:(hd + 1) * 64], False, True),
                                 identb.b + xsd.b, yd.b, mark=(hh == 7))
                    if g >= 1:
                        tail(g - 1, yds.pop(g - 1))
                    P.op("tensor", MM(yo_ps.t[:, :], cT.t[:, g, :], prevb.t[:, gs]), cT.b + [prevb.b[g]], yo_ps.b)
                    P.op("tensor", MM(st_ps.t[:, :], bt.t[:, g * 128:(g + 1) * 128], xdts.t[:, gs]), bt.b + xdts.b, st_ps.b)
                tail(3, yds.pop(3))
                P.op("vector", TS(rstd.t[:], ss.t[:], 1.0 / 512, ALU.mult, RMS_EPS, ALU.add), ss.b, rstd.b)
                P.op("scalar", ACT(rstd.t[:], rstd.t[:], AF.Ln), rstd.b, rstd.b)
                P.op("scalar", ACT(rstd.t[:], rstd.t[:], AF.Exp, scale=-0.5), rstd.b, rstd.b)
                yst = ynst.next()
                for g in range(4):
                    gs = slice(g * 512, (g + 1) * 512)
                    P.op("vector", STT(yn.t[:, gs], y.t[:, gs], rstd.t[:, g:g + 1], ngb.t[:, gs], ALU.mult, ALU.mult),
                         [y.b[g]] + rstd.b + ngb.b, [yn.b[g]])
                    tp = tpr.next()
                    for j in range(4):
                        cc = g * 4 + j
                        P.op("tensor", TR(tp.t[:, j * 128:(j + 1) * 128], yn.t[:, cc * 128:(cc + 1) * 128], identb.t[:]),
                             [yn.b[g]] + identb.b, tp.b, mark=(j == 3))
                    P.op("scalar", ACT(yst.t[:, g * 4:(g + 1) * 4, :], tp.t[:, :].rearrange("p (c t) -> p c t", c=4), AF.Copy),
                         tp.b, [yst.b[g]])
                P.dma("sync", [(ynTS.t[:, :, r0:r0 + 128].rearrange("c p t -> p c t"), yst.t[:, :, :])],
                      yst.b, ynTS.b, yst.ss, nowaw=True)
            P.end_phase()

    def phase_C1(self, l):
        P = self.P
        E = self.ext
        hT, oTS, ynTS, h1T = (self.scr[n] for n in ("hT", "oT", "ynT", "h1T"))
        WT = 256
        with ExitStack() as st:
            P.begin_phase(st)
            w_in = E["w_in"]
            Woa = P.sbuf([128, 8, 1024], BF16, dma=True)
            Wos = P.sbuf([128, 16, 1024], BF16, dma=True)
            Wout = P.sbuf([128, 8, 1024], BF16, dma=True)
            Wga = P.sbuf([128, 8, 1024], BF16, dma=True)
            Wgs = P.sbuf([128, 8, 1024], BF16, dma=True)
            self.load_w(Woa, E["w_o_attn"][l, :, :], 8)
            self.load_w(Wos, E["w_o_ssd"][l, :, :], 16)
            self.load_w(Wout, E["w_out"][l, :, :], 8)
            self.load_w(Wga, w_in[l, :, 6240:7264], 8)
            self.load_w(Wgs, w_in[l, :, 7264:8288], 8)
            gb = P.sbuf([128, 16], F32, dma=True)
            P.dma("sync", [(gb.t[:, 0:8], E["ln1_g"][l, :, :]), (gb.t[:, 8:16], E["ln1_b"][l, :, :])], [], gb.b, gb.ds)
            otr = Ring([P.sbuf([128, 8, WT], BF16, dma=True) for _ in range(2)])
            ynr = Ring([P.sbuf([128, 16, WT], BF16, dma=True) for _ in range(2)])
            hr = Ring([P.sbuf([128, 8, WT], F32, n=8, dma=True) for _ in range(2)])
            hb = P.sbuf([128, 8, WT], BF16, n=8)
            mixed = P.sbuf([128, 8, WT], BF16, n=8)
            orr = Ring([P.sbuf([128, 8, WT], F32, n=8, dma=True) for _ in range(1)])
            sgr = Ring([P.sbuf([128, WT], F32) for _ in range(4)])
            tr_ = Ring([P.sbuf([128, WT], F32) for _ in range(4)])
            accr = Ring([P.psum([128, 512], F32) for _ in range(4)])
            R = self.ln_resources(WT)
            evi = [0]

            def proj(acc, W_tb, oc, K, rhs_tb, W):
                for k in range(K):
                    P.op("tensor", MM(acc.t[:, 0:W], W_tb.t[:, k, oc * 128:(oc + 1) * 128], rhs_tb.t[:, k, 0:W],
                                      k == 0, k == K - 1), W_tb.b + [rhs_tb.b[k % len(rhs_tb.b)]], acc.b, mark=(k == K - 1))

            for (c0, W) in tiles_of(WT):
                ot = otr.next(); yt = ynr.next(); h = hr.next()
                P.dma("sync", [(ot.t[:, :, 0:W], oTS.t[:, :, c0:c0 + W].rearrange("h p t -> p h t"))], oTS.b, ot.b, ot.ds)
                P.dma("sync", [(yt.t[:, :, 0:W], ynTS.t[:, :, c0:c0 + W].rearrange("c p t -> p c t"))], ynTS.b, yt.b, yt.ds)
                P.dma("sync", [(h.t[:, :, 0:W], hT.t[:, :, c0:c0 + W].rearrange("c p t -> p c t"))], hT.b, h.b, h.ds)
                for c in range(8):
                    evi[0] += 1
                    if evi[0] % 2:
                        P.op("scalar", ACT(hb.t[:, c, 0:W], h.t[:, c, 0:W], AF.Copy), h.b, [hb.b[c]])
                    else:
                        P.op("vector", CP(hb.t[:, c, 0:W], h.t[:, c, 0:W]), h.b, [hb.b[c]])
                for oc in range(8):
                    ga = accr.next()
                    proj(ga, Wga, oc, 8, hb, W)
                    sga = sgr.next()
                    P.op("scalar", ACT(sga.t[:, 0:W], ga.t[:, 0:W], AF.Sigmoid), ga.b, sga.b)
                    gs_ = accr.next()
                    proj(gs_, Wgs, oc, 8, hb, W)
                    sgs = sgr.next()
                    P.op("scalar", ACT(sgs.t[:, 0:W], gs_.t[:, 0:W], AF.Sigmoid), gs_.b, sgs.b)
                    ya = accr.next()
                    proj(ya, Woa, oc, 8, ot, W)
                    t1 = tr_.next()
                    P.op("vector", TT(t1.t[:, 0:W], ya.t[:, 0:W], sga.t[:, 0:W], ALU.mult), ya.b + sga.b, t1.b)
                    ys_ = accr.next()
                    proj(ys_, Wos, oc, 16, yt, W)
                    t2 = tr_.next()
                    P.op("vector", TT(t2.t[:, 0:W], ys_.t[:, 0:W], sgs.t[:, 0:W], ALU.mult), ys_.b + sgs.b, t2.b)
                    P.op("vector", TT(mixed.t[:, oc, 0:W], t1.t[:, 0:W], t2.t[:, 0:W], ALU.add), t1.b + t2.b, [mixed.b[oc]])
                for oc in range(8):
                    r = accr.next()
                    proj(r, Wout, oc, 8, mixed, W)
                    P.op("vector", STT(h.t[:, oc, 0:W], h.t[:, oc, 0:W], ALPHA, r.t[:, 0:W], ALU.mult, ALU.add),
                         r.b + [hb.b[oc]], [h.b[oc]])
                o = orr.next()
                self.ln_fm(h, o, W, gb.t[:, 0:8], gb.t[:, 8:16], R)
                P.dma("gpsimd", [(h1T.t[:, :, c0:c0 + W].rearrange("c p t -> p c t"), o.t[:, :, 0:W])],
                      o.b, h1T.b, o.ss, nowaw=True)
            P.end_phase()

    def phase_C2(self, l):
        P = self.P
        E = self.ext
        hT, h1T = (self.scr[n] for n in ("hT", "h1T"))
        last = (l == self.n_layers - 1)
        WT = 256
        with ExitStack() as st:
            P.begin_phase(st)
            Wup = P.sbuf([128, 8, 2 * DFF], BF16, dma=True)
            Wdn = P.sbuf([128, 22, 1024], BF16, dma=True)
            self.load_w(Wup, E["w_up"][l, :, :], 8, per=1)
            self.load_w(Wdn, E["w_down"][l, :, :], 22, per=4)
            gb = P.sbuf([128, 16], F32, dma=True)
            P.dma("sync", [(gb.t[:, 0:8], E["ln2_g"][l, :, :]), (gb.t[:, 8:16], E["ln2_b"][l, :, :])], [], gb.b, gb.ds)
            cw = P.sbuf([128, 44, 3], F32, dma=True)
            cb = P.sbuf([128, 44], F32, dma=True)
            P.dma("sync", [(cw.t[:], E["ffn_cw"][l, :, :, :])], [], cw.b, cw.ds)
            P.dma("sync", [(cb.t[:], E["ffn_cb"][l, :, :])], [], cb.b, cb.ds)
            hr = Ring([P.sbuf([128, 8, WT], F32, n=8, dma=True) for _ in range(2)])
            hbr = Ring([P.sbuf([128, 8, WT], BF16, n=8) for _ in range(1)])
            ar_ = Ring([P.sbuf([128, 22, WT], BF16, n=22) for _ in range(2)])
            orr = Ring([P.sbuf([128, 8, WT], F32, n=8, dma=True) for _ in range(1)])
            xbr = Ring([P.sbuf([128, WT + 2], F32) for _ in range(1)])
            tcr = Ring([P.sbuf([128, WT], F32) for _ in range(3)])
            sgr = Ring([P.sbuf([128, WT], F32) for _ in range(2)])
            halo = P.sbuf([128, 44, 2], F32, n=44)
            halo2 = P.sbuf([128, 44, 2], F32, n=44)
            ostr = Ring([P.sbuf([128, 1024], F32, n=2, dma=True) for _ in range(1)]) if last else None
            accr = Ring([P.psum([128, 512], F32) for _ in range(5 if last else 6)])
            tpr = Ring([P.psum([128, 512], F32) for _ in range(1)]) if last else None
            R = self.ln_resources(WT)
            P.op("vector", MEMSET(halo.t[:], 0.0), [], halo.b)
            evi = [0]

            def proj(acc, W_tb, col0, K, rhs_tb, W):
                for k in range(K):
                    P.op("tensor", MM(acc.t[:, 0:W], W_tb.t[:, k, col0:col0 + 128], rhs_tb.t[:, k, 0:W],
                                      k == 0, k == K - 1), W_tb.b + [rhs_tb.b[k]], acc.b, mark=(k == K - 1))

            def conv(acc, ci, W, first, hin, hout):
                tc_ = tcr.next()
                if first:
                    xb = xbr.next()
                    P.op("vector", MEMSET(xb.t[:, 0:2 + PAD], 0.0), [], xb.b)
                    P.op("scalar", ACT(xb.t[:, 2 + PAD:2 + W], acc.t[:, PAD:W], AF.Copy), [], xb.b + acc.b)
                    P.op("scalar", ACT(hout.t[:, ci, :], xb.t[:, W:W + 2], AF.Copy), xb.b, [hout.b[ci]])
                    P.op("scalar", ACT(tc_.t[:, 0:W], xb.t[:, 0:W], AF.Identity, bias=cb.t[:, ci:ci + 1], scale=cw.t[:, ci, 0:1]),
                         xb.b + cw.b + cb.b, tc_.b)
                    for tap in (1, 2):
                        P.op("vector", STT(tc_.t[:, 0:W], xb.t[:, tap:tap + W], cw.t[:, ci, tap:tap + 1], tc_.t[:, 0:W],
                                           ALU.mult, ALU.add), xb.b + cw.b, tc_.b)
                    return tc_
                P.op("scalar", ACT(tc_.t[:, 2:W], acc.t[:, 0:W - 2], AF.Identity, bias=cb.t[:, ci:ci + 1], scale=cw.t[:, ci, 0:1]),
                     cw.b + cb.b, tc_.b + acc.b)
                P.op("scalar", ACT(hout.t[:, ci, :], acc.t[:, W - 2:W], AF.Copy), [], [hout.b[ci]] + acc.b)
                P.op("scalar", ACT(tc_.t[:, 0:2], hin.t[:, ci, :], AF.Identity, bias=cb.t[:, ci:ci + 1], scale=cw.t[:, ci, 0:1]),
                     [hin.b[ci]], tc_.b)
                P.op("vector", STT(tc_.t[:, 0:1], hin.t[:, ci, 1:2], cw.t[:, ci, 1:2], tc_.t[:, 0:1], ALU.mult, ALU.add),
                     [hin.b[ci]] + cw.b, tc_.b)
                P.op("vector", STT(tc_.t[:, 1:W], acc.t[:, 0:W - 1], cw.t[:, ci, 1:2], tc_.t[:, 1:W], ALU.mult, ALU.add),
                     cw.b, tc_.b + acc.b)
                P.op("vector", STT(tc_.t[:, 0:W], acc.t[:, 0:W], cw.t[:, ci, 2:3], tc_.t[:, 0:W], ALU.mult, ALU.add),
                     cw.b, tc_.b + acc.b)
                return tc_

            halos = [halo, halo2]
            tiles = tiles_of(WT)
            state = {}

            def stage_up(ti):
                c0, W = tiles[ti]
                h = hr.next()
                hb = hbr.next()
                a = ar_.next()
                P.dma("sync", [(h.t[:, :, 0:W], h1T.t[:, :, c0:c0 + W].rearrange("c p t -> p c t"))], h1T.b, h.b, h.ds)
                for c in range(8):
                    evi[0] += 1
                    if evi[0] % 2:
                        P.op("scalar", ACT(hb.t[:, c, 0:W], h.t[:, c, 0:W], AF.Copy), h.b, [hb.b[c]])
                    else:
                        P.op("vector", CP(hb.t[:, c, 0:W], h.t[:, c, 0:W]), h.b, [hb.b[c]])
                hin, hout = halos[ti % 2], halos[(ti + 1) % 2]
                for c in range(22):
                    ug = accr.next()
                    proj(ug, Wup, c * 128, 8, hb, W)
                    uv = accr.next()
                    proj(uv, Wup, DFF + c * 128, 8, hb, W)
                    tg = conv(ug, c, W, c0 == 0, hin, hout)
                    tv = conv(uv, 22 + c, W, c0 == 0, hin, hout)
                    sg = sgr.next()
                    P.op("scalar", ACT(sg.t[:, 0:W], tg.t[:, 0:W], AF.Silu), tg.b, sg.b)
                    P.op("vector", TT(a.t[:, c, 0:W], sg.t[:, 0:W], tv.t[:, 0:W], ALU.mult), sg.b + tv.b, [a.b[c]])
                state[ti] = (h, hb, a)

            def stage_down(ti):
                c0, W = tiles[ti]
                nb = W // 128
                h, hb, a = state.pop(ti)
                for oc in range(8):
                    f = accr.next()
                    proj(f, Wdn, oc * 128, 22, a, W)
                    P.op("vector", STT(h.t[:, oc, 0:W], h.t[:, oc, 0:W], ALPHA, f.t[:, 0:W], ALU.mult, ALU.add),
                         f.b, [h.b[oc]])
                o = orr.next()
                self.ln_fm(h, o, W, gb.t[:, 0:8], gb.t[:, 8:16], R)
                if not last:
                    P.dma("gpsimd", [(hT.t[:, :, c0:c0 + W].rearrange("c p t -> p c t"), o.t[:, :, 0:W])],
                          o.b, hT.b, o.ss, nowaw=True)
                elif c0 > 0:
                    for bi in range(nb):
                        os_ = ostr.next()
                        for half in range(2):
                            tp = tpr.next()
                            for j in range(4):
                                cc = half * 4 + j
                                P.op("tensor", TR(tp.t[:, j * 128:(j + 1) * 128], o.t[:, cc, bi * 128:(bi + 1) * 128],
                                                  self.identf.t[:]), [o.b[cc]] + self.identf.b, tp.b, mark=(j == 3))
                            P.op("scalar", ACT(os_.t[:, half * 512:(half + 1) * 512], tp.t[:, :], AF.Copy), tp.b, [os_.b[half]])
                        r0 = c0 + bi * 128 - 128
                        P.dma("gpsimd", [(self.out[r0:r0 + 128, :], os_.t[:, :])], os_.b, [], os_.ss)

            stage_up(0)
            for ti in range(len(tiles)):
                if ti + 1 < len(tiles):
                    stage_up(ti + 1)
                stage_down(ti)
            P.end_phase()


INPUT_SHAPES = {
    "xin": [LP, D], "emb_g": [128, 8], "emb_b": [128, 8],
    "c_ident": [128, 128], "c_tri": [128, 128], "c_sellast": [128, 128], "c_mask0": [128, 128], "c_kmask0": [128, 1], "c_negmask": [128, 128],
    "cosT2": [128, LP], "sinT2": [128, LP],
    "w_in": [DEPTH, D, 8288], "w_kpes": [DEPTH, D, 64],
    "wqb_n": [DEPTH, QL, 1024], "wqb_p": [DEPTH, QL, 512], "wqb_ps": [DEPTH, QL, 512],
    "wkvb_kn": [DEPTH, KVL, 1024], "wkvb_v": [DEPTH, KVL, 1024],
    "qng": [DEPTH, 128, 6], "kvng": [DEPTH, 128, 2],
    "w_o_attn": [DEPTH, 1024, D], "w_o_ssd": [DEPTH, 2048, D], "w_out": [DEPTH, D, D],
    "w_up": [DEPTH, D, 2 * DFF], "w_down": [DEPTH, DFF, D],
    "ssd_cw": [DEPTH, 128, 24, 4], "ssd_cb": [DEPTH, 128, 24],
    "dtb_bc": [DEPTH, 128, 32], "alog_bc": [DEPTH, 128, 32], "dskip_bc": [DEPTH, 128, 2048], "ssdng_bc": [DEPTH, 128, 2048],
    "ln1_g": [DEPTH, 128, 8], "ln1_b": [DEPTH, 128, 8], "ln2_g": [DEPTH, 128, 8], "ln2_b": [DEPTH, 128, 8],
    "ffn_cw": [DEPTH, 128, 44, 3], "ffn_cb": [DEPTH, 128, 44],
}

SCRATCH = {
    "hT": ([8, 128, LP], F32), "h1T": ([8, 128, LP], F32),
    "qT": ([NH, 192, LP], BF16), "kT": ([NH, 128, LP], BF16), "kpeT": ([64, LP], BF16),
    "v": ([NH, 128, NBLK, 128], BF16), "oT": ([NH, 128, LP], BF16),
    "xs": ([LP, 2048], BF16), "Btok": ([LP, 512], BF16), "BT": ([4, 128, LP], BF16), "CT": ([4, 128, LP], BF16),
    "dt": ([LP, 32], F32), "zs": ([LP, 2048], BF16), "ynT": ([16, 128, LP], BF16),
    "dbgY": ([LP, 2048], F32), "dbgS": ([LP, 168], F32), "dbgP": ([NBLK, 128, 2048], F32),
}


def build(n_layers=2, dbg=None, stop=None):
    nc = bass.Bass("TRN2", target_bir_lowering=False)
    with ExitStack() as gst:
        P = Prog(nc, gst)
        k = K(nc, P, n_layers, dbg)
        for nm, shp in INPUT_SHAPES.items():
            k.din(nm, shp)
        for nm, (shp, dt) in SCRATCH.items():
            k.dscr(nm, shp, dt)
        k.out = nc.dram_tensor("out", [SEQ, D], F32, kind="ExternalOutput").ap()
        k.load_consts(gst)
        seq = [("E", k.phase_E, ())]
        for l in range(n_layers):
            for nm in ("A1", "A2", "B", "S", "C1", "C2"):
                fn = getattr(k, "phase_" + nm, None)
                if fn is not None:
                    seq.append((f"{nm}_{l}", fn, (l,)))
        import os
        skip = os.environ.get("K_SKIP", "").split(",")
        for (nm, fn, args) in seq:
            if nm.split("_")[0] in skip:
                continue
            fn(*args)
            if stop == nm:
                break
    return nc, k


def host_consts():
    c = {}
    c["c_ident"] = np.eye(128, dtype=np.float32)
    kk = np.arange(128)[:, None]
    qq = np.arange(128)[None, :]
    c["c_tri"] = (kk <= qq).astype(np.float32)
    c["c_sellast"] = np.broadcast_to((kk == 127), (128, 128)).astype(np.float32).copy()
    m0 = ((qq >= PAD) & (kk >= PAD) & (kk <= qq)) | ((qq < PAD) & (kk == qq))
    c["c_mask0"] = m0.astype(np.float32)
    c["c_kmask0"] = (np.arange(128) >= PAD).astype(np.float32)[:, None].copy()
    c["c_negmask"] = np.where(qq < kk, np.float32(-30000.0), np.float32(0.0)).astype(np.float32)
    return c


def pm(v, nchunk):
    return np.ascontiguousarray(np.asarray(v, np.float32).reshape(nchunk, 128).T)


def rope_tables_T():
    inv_freq = (1.0 / (np.float32(10000.0) ** (np.arange(0, 64, 2, dtype=np.float32) / np.float32(64)))).astype(np.float32)
    pos = np.maximum(np.arange(LP, dtype=np.float32) - np.float32(PAD), np.float32(0))
    ang = (pos[:, None] * inv_freq[None, :]).astype(np.float32)
    ang = np.concatenate([ang, ang], axis=-1)
    cos = np.cos(ang).astype(np.float32).T
    sin = np.sin(ang).astype(np.float32).T
    sgn = np.concatenate([-np.ones(32, np.float32), np.ones(32, np.float32)])[:, None]
    sins = sin * sgn
    return (np.ascontiguousarray(np.concatenate([cos, cos], 0)), np.ascontiguousarray(np.concatenate([sins, sins], 0)))


def prep_shared(inp):
    f = lambda a: np.ascontiguousarray(np.asarray(a, np.float32))
    sh = dict(host_consts())
    sh["cosT2"], sh["sinT2"] = rope_tables_T()
    sh["emb_g"] = pm(inp["emb_ln_g"], 8)
    sh["emb_b"] = pm(inp["emb_ln_b"], 8)
    w_in = f(inp["w_in"])
    sh["w_in"] = w_in
    sh["w_kpes"] = f(np.concatenate([w_in[:, :, 1056:1088], w_in[:, :, 1024:1056]], axis=-1))
    wqb = f(inp["w_q_b"]).reshape(DEPTH, QL, NH, 192)
    sh["wqb_n"] = f(wqb[..., :128].reshape(DEPTH, QL, 1024))
    sh["wqb_p"] = f(wqb[..., 128:].reshape(DEPTH, QL, 512))
    sh["wqb_ps"] = f(np.concatenate([wqb[..., 160:192], wqb[..., 128:160]], axis=-1).reshape(DEPTH, QL, 512))
    wkv = f(inp["w_kv_b"]).reshape(DEPTH, KVL, NH, 256)
    sh["wkvb_kn"] = f(wkv[..., :128].reshape(DEPTH, KVL, 1024))
    sh["wkvb_v"] = f(wkv[..., 128:].reshape(DEPTH, KVL, 1024))
    sh["qng"] = f(np.stack([pm(inp["q_norm_g"][l], 6) for l in range(DEPTH)]))
    sh["kvng"] = f(np.stack([pm(inp["kv_norm_g"][l], 2) for l in range(DEPTH)]))
    for nm in ("w_o_attn", "w_o_ssd", "w_out", "w_up", "w_down"):
        sh[nm] = f(inp[nm])
    cw = f(inp["ssd_conv_w"])
    sh["ssd_cw"] = f(cw.reshape(DEPTH, 4, 24, 128).transpose(0, 3, 2, 1))
    sh["ssd_cb"] = f(f(inp["ssd_conv_b"]).reshape(DEPTH, 24, 128).transpose(0, 2, 1))
    bc = lambda a: f(np.broadcast_to(f(a)[:, None, :], (DEPTH, 128, a.shape[-1])))
    sh["dtb_bc"] = bc(inp["dt_bias"])
    sh["alog_bc"] = bc(inp["a_log"])
    sh["dskip_bc"] = bc(np.repeat(f(inp["d_skip"]), 64, axis=-1))
    sh["ssdng_bc"] = bc(inp["ssd_norm_g"])
    for nm in ("ln1_g", "ln1_b", "ln2_g", "ln2_b"):
        sh[nm] = f(np.stack([pm(inp[nm][l], 8) for l in range(DEPTH)]))
    fw = f(inp["ffn_conv_w"])
    sh["ffn_cw"] = f(fw.reshape(DEPTH, 3, 44, 128).transpose(0, 3, 2, 1))
    sh["ffn_cb"] = f(f(inp["ffn_conv_b"]).reshape(DEPTH, 44, 128).transpose(0, 2, 1))
    return sh


def xin_of(inp, b):
    xin = np.zeros((LP, D), np.float32)
    xin[PAD:PAD + NMETA] = inp["meta_tokens"]
    xin[128:] = inp["x"][b]
    return xin


def kernel(**inp):
    sh = prep_shared(inp)
    nc, k = build()
    in_maps = []
    for c in range(8):
        m = dict(sh)
        m["xin"] = xin_of(inp, c % 4)
        in_maps.append(m)
    res = run_bass_kernel_spmd(nc, in_maps, core_ids=list(range(8)))
    out = np.stack([np.asarray(res.results[b]["out"], np.float32) for b in range(4)], axis=0)
    return out
```

```python
import math
from contextlib import ExitStack

import numpy as np
import concourse.bass as bass
import concourse.mybir as mybir
from concourse.bass_utils import run_bass_kernel_spmd

F32 = mybir.dt.float32
BF16 = mybir.dt.bfloat16
AF = mybir.ActivationFunctionType
ALU = mybir.AluOpType

D = 1024
SEQ = 8192
NMETA = 16
PAD = 112
LP = 8320
NBLK = 65
NH = 8
QL = 768
KVL = 256
DFF = 2816
DEPTH = 2
ALPHA = (2 * DEPTH) ** 0.25
LN_EPS = 1e-5
RMS_EPS = 1e-6
SCALE = 192 ** -0.5

ENGS = ("sync", "scalar", "vector", "gpsimd", "tensor")
COMPUTE = ("scalar", "vector", "gpsimd", "tensor")


class Buf:
    __slots__ = ("w", "r")

    def __init__(self):
        self.w = None
        self.r = {}


class DSem:
    def __init__(self, sem):
        self.sem = sem
        self.count = 0


class TB:
    def __init__(self, t, n=1, ds=None, ss=None):
        self.t = t
        self.b = [Buf() for _ in range(n)]
        self.ds = ds
        self.ss = ss


class Ring:
    def __init__(self, items):
        self.items = items
        self.i = 0

    def next(self):
        it = self.items[self.i % len(self.items)]
        self.i += 1
        return it


class Prog:
    def __init__(self, nc, gstack):
        self.nc = nc
        self.gstack = gstack
        self.pstack = None
        self.q = {e: [] for e in ENGS}
        self.cnt = {e: 0 for e in ENGS}
        self.waited = {e: {} for e in ENGS}
        self.pending = {e: [] for e in ENGS}
        self.psem = {}
        self.nsem = 0
        for e in COMPUTE:
            self.psem[e] = self._new_sem()
        self.pool = []
        self.pool_i = 0
        self.ninstr = 0
        self.phase_i = 0
        self.nt = 0

    def _new_sem(self):
        self.nsem += 1
        return self.gstack.enter_context(self.nc.semaphore(f"sem{self.nsem}"))

    def dsem(self):
        if self.pool_i == len(self.pool):
            self.pool.append(DSem(self._new_sem()))
        d = self.pool[self.pool_i]
        self.pool_i += 1
        return d

    def sbuf(self, shape, dtype, n=1, dma=False):
        self.nt += 1
        t = self.pstack.enter_context(self.nc.sbuf_tensor(f"sb{self.nt}", list(shape), dtype))
        return TB(t, n, self.dsem() if dma else None, self.dsem() if dma else None)

    def psum(self, shape, dtype, n=1):
        self.nt += 1
        t = self.pstack.enter_context(self.nc.psum_tensor(f"ps{self.nt}", list(shape), dtype))
        return TB(t, n)

    def _wait(self, eng, tok):
        if tok is None:
            return
        sem, val, src = tok
        if src == eng and eng == "tensor":
            return
        key = id(sem)
        if self.waited[eng].get(key, 0) >= val:
            return
        self.waited[eng][key] = val
        self.q[eng].append(lambda e, s=sem, v=val: e.wait_ge(s, v))

    def _deps(self, eng, reads, writes, nowaw=False):
        for b in reads:
            self._wait(eng, b.w)
        for b in writes:
            if not nowaw:
                self._wait(eng, b.w)
            for t in b.r.values():
                self._wait(eng, t)

    def _commit(self, tok, reads, writes):
        k = id(tok[0])
        for b in reads:
            b.r[k] = tok
        for b in writes:
            b.w = tok
            b.r = {}

    def op(self, eng, fn, reads=(), writes=(), mark=True):
        self.ninstr += 1
        self._deps(eng, reads, writes)
        if not mark:
            self.pending[eng].append((tuple(reads), tuple(writes)))
            self.q[eng].append(fn)
            return None
        self.cnt[eng] += 1
        v = self.cnt[eng]
        sem = self.psem[eng]
        self.q[eng].append(lambda e, f=fn, s=sem: f(e).then_inc(s, 1))
        tok = (sem, v, eng)
        for (r, w) in self.pending[eng]:
            self._commit(tok, r, w)
        self.pending[eng] = []
        self._commit(tok, reads, writes)
        return tok

    def dma(self, eng, pairs, reads, writes, ds, nowaw=False):
        self._deps(eng, reads, writes, nowaw)
        for (o, i) in pairs:
            self.ninstr += 1
            ds.count += 16
            self.q[eng].append(lambda e, o=o, i=i, s=ds.sem: e.dma_start(out=o, in_=i).then_inc(s, 16))
        tok = (ds.sem, ds.count, "dma")
        self._commit(tok, reads, writes)
        return tok

    def begin_phase(self, st):
        self.pstack = st
        self.pool_i = 0

    def end_phase(self):
        for d in self.pool:
            if d.count:
                self._wait("sync", (d.sem, d.count, "dma"))
        for e in COMPUTE:
            assert not self.pending[e], e
            if self.cnt[e]:
                self._wait("sync", (self.psem[e], self.cnt[e], e))
        nc = self.nc
        self.phase_i += 1
        with nc.Block() as block:
            for ename in ENGS:
                lst = self.q[ename]
                if not lst:
                    continue

                def body(e, lst=lst):
                    for f in lst:
                        f(e)
                getattr(block, ename)(body)
        self.q = {e: [] for e in ENGS}


def MM(out, lhsT, rhs, start=True, stop=True):
    return lambda e: e.matmul(out, lhsT=lhsT, rhs=rhs, start=start, stop=stop)


def TR(out, in_, ident):
    return lambda e: e.transpose(out=out, in_=in_, identity=ident)


def ACT(out, in_, func, bias=None, scale=None):
    kw = {}
    if bias is not None:
        kw["bias"] = bias
    if scale is not None:
        kw["scale"] = scale
    return lambda e: e.activation(out=out, in_=in_, func=func, **kw)


def TT(out, a, b, op):
    return lambda e: e.tensor_tensor(out=out, in0=a, in1=b, op=op)


def TS(out, a, s1, op0, s2=None, op1=None):
    if op1 is None:
        return lambda e: e.tensor_scalar(out=out, in0=a, scalar1=s1, scalar2=None, op0=op0)
    return lambda e: e.tensor_scalar(out=out, in0=a, scalar1=s1, scalar2=s2, op0=op0, op1=op1)


def STT(out, in0, scalar, in1, op0, op1):
    return lambda e: e.scalar_tensor_tensor(out=out, in0=in0, scalar=scalar, in1=in1, op0=op0, op1=op1)


def CP(out, in_):
    return lambda e: e.tensor_copy(out=out, in_=in_)


def MEMSET(ap, v):
    return lambda e: e.memset(ap, v)


def tiles_of(width):
    t = [(0, 128)]
    c = 128
    while c < LP:
        t.append((c, width))
        c += width
    return t


class K:
    def __init__(self, nc, P, n_layers, dbg):
        self.nc = nc
        self.P = P
        self.n_layers = n_layers
        self.dbg = dbg or ()
        self.ext = {}
        self.scr = {}

    def din(self, name, shape, dt=F32):
        ap = self.nc.dram_tensor(name, list(shape), dt, kind="ExternalInput").ap()
        self.ext[name] = ap
        return ap

    def dscr(self, name, shape, dt):
        kind = "ExternalOutput" if name in self.dbg else "Internal"
        ap = self.nc.dram_tensor(name, list(shape), dt, kind=kind).ap()
        self.scr[name] = TB(ap, 1)
        return self.scr[name]

    def load_consts(self, st):
        P = self.P
        P.pstack = st
        c = self.ext
        self.identf = P.sbuf([128, 128], F32, dma=True)
        self.tri_f = P.sbuf([128, 128], F32, dma=True)
        self.sellast = P.sbuf([128, 128], F32, dma=True)
        self.mask0 = P.sbuf([128, 128], F32, dma=True)
        self.kmask0 = P.sbuf([128, 1], F32, dma=True)
        self.identb = P.sbuf([128, 128], BF16)
        self.ones_b = P.sbuf([128, 128], BF16)
        self.tri_b = P.sbuf([128, 128], BF16)
        self.mask0_b = P.sbuf([128, 128], BF16)
        for tb, nm in ((self.identf, "c_ident"), (self.tri_f, "c_tri"), (self.sellast, "c_sellast"),
                       (self.mask0, "c_mask0")):
            P.dma("sync", [(tb.t[:], c[nm][:, :])], [], tb.b, tb.ds)
        P.dma("sync", [(self.kmask0.t[:], c["c_kmask0"][:, :])], [], self.kmask0.b, self.kmask0.ds)
        P.op("vector", CP(self.identb.t[:], self.identf.t[:]), self.identf.b, self.identb.b)
        P.op("vector", CP(self.tri_b.t[:], self.tri_f.t[:]), self.tri_f.b, self.tri_b.b)
        P.op("vector", CP(self.mask0_b.t[:], self.mask0.t[:]), self.mask0.b, self.mask0_b.b)
        P.op("vector", MEMSET(self.ones_b.t[:], 1.0), [], self.ones_b.b)

    def load_w(self, dst, src, kchunks, per=4):
        P = self.P
        for k0 in range(0, kchunks, per):
            k1 = min(kchunks, k0 + per)
            P.dma("gpsimd", [(dst.t[:, k0:k1, :], src[k0 * 128:k1 * 128, :].rearrange("(k p) n -> p k n", p=128))],
                  [], dst.b, dst.ds)

    def ln_fm(self, s, out, W, g, b, R):
        P = self.P
        ones = self.ones_b
        sum_ps, ssq_ps = R["sum"], R["ssq"]
        for c in range(8):
            sb = R["sb"].next()
            sq = R["sq"].next()
            P.op("scalar", ACT(sb.t[:, 0:W], s.t[:, c, 0:W], AF.Copy), [s.b[c]], sb.b)
            P.op("scalar", ACT(sq.t[:, 0:W], s.t[:, c, 0:W], AF.Square), [s.b[c]], sq.b)
            P.op("tensor", MM(sum_ps.t[:, 0:W], ones.t[:], sb.t[:, 0:W], c == 0, c == 7), sb.b + ones.b, sum_ps.b,
                 mark=False)
            P.op("tensor", MM(ssq_ps.t[:, 0:W], ones.t[:], sq.t[:, 0:W], c == 0, c == 7), sq.b + ones.b, ssq_ps.b,
                 mark=True)
        mean, var, rstd = R["mean"], R["var"], R["rstd"]
        P.op("vector", TS(mean.t[:, 0:W], sum_ps.t[:, 0:W], 1.0 / D, ALU.mult), sum_ps.b, mean.b)
        P.op("vector", TS(var.t[:, 0:W], ssq_ps.t[:, 0:W], 1.0 / D, ALU.mult), ssq_ps.b, var.b)
        msq = R["msq"]
        P.op("vector", TT(msq.t[:, 0:W], mean.t[:, 0:W], mean.t[:, 0:W], ALU.mult), mean.b, msq.b)
        P.op("vector", TT(var.t[:, 0:W], var.t[:, 0:W], msq.t[:, 0:W], ALU.subtract), var.b + msq.b, var.b)
        P.op("vector", TS(var.t[:, 0:W], var.t[:, 0:W], LN_EPS, ALU.add), var.b, var.b)
        P.op("scalar", ACT(rstd.t[:, 0:W], var.t[:, 0:W], AF.Ln), var.b, rstd.b)
        P.op("scalar", ACT(rstd.t[:, 0:W], rstd.t[:, 0:W], AF.Exp, scale=-0.5), rstd.b, rstd.b)
        for c in range(8):
            tmp = R["tmp"].next()
            P.op("vector", TT(tmp.t[:, 0:W], s.t[:, c, 0:W], mean.t[:, 0:W], ALU.subtract), [s.b[c]] + mean.b, tmp.b)
            P.op("vector", TT(tmp.t[:, 0:W], tmp.t[:, 0:W], rstd.t[:, 0:W], ALU.mult), tmp.b + rstd.b, tmp.b)
            P.op("scalar", ACT(out.t[:, c, 0:W], tmp.t[:, 0:W], AF.Identity, bias=b[:, c:c + 1], scale=g[:, c:c + 1]),
                 tmp.b, [out.b[c]])

    def ln_resources(self, wmax=512):
        P = self.P
        R = {}
        R["sum"] = P.psum([128, 512], F32)
        R["ssq"] = P.psum([128, 512], F32)
        R["sb"] = Ring([P.sbuf([128, wmax], BF16) for _ in range(2)])
        R["sq"] = Ring([P.sbuf([128, wmax], BF16) for _ in range(2)])
        for nm in ("mean", "var", "msq", "rstd"):
            R[nm] = P.sbuf([128, wmax], F32)
        R["tmp"] = Ring([P.sbuf([128, wmax], F32) for _ in range(2)])
        return R

    def phase_E(self):
        P = self.P
        hT = self.scr["hT"]
        with ExitStack() as st:
            P.begin_phase(st)
            xin = self.ext["xin"]
            gb = P.sbuf([128, 16], F32, dma=True)
            P.dma("sync", [(gb.t[:, 0:8], self.ext["emb_g"][:, :]), (gb.t[:, 8:16], self.ext["emb_b"][:, :])],
                  [], gb.b, gb.ds)
            xr = Ring([P.sbuf([128, 4, 1024], F32, dma=True) for _ in range(2)])
            sr = Ring([P.sbuf([128, 8, 512], F32, n=8) for _ in range(2)])
            orr = Ring([P.sbuf([128, 8, 512], F32, n=8, dma=True) for _ in range(2)])
            tpr = Ring([P.psum([128, 512], F32) for _ in range(2)])
            R = self.ln_resources()
            for (c0, W) in tiles_of(512):
                nb = W // 128
                x = xr.next()
                P.dma("sync", [(x.t[:, 0:nb, :], xin[c0:c0 + W, :].rearrange("(b p) f -> p b f", p=128))],
                      [], x.b, x.ds)
                s = sr.next()
                for c in range(8):
                    tp = tpr.next()
                    for bi in range(nb):
                        P.op("tensor", TR(tp.t[:, bi * 128:(bi + 1) * 128], x.t[:, bi, c * 128:(c + 1) * 128],
                                          self.identf.t[:]), x.b + self.identf.b, tp.b, mark=(bi == nb - 1))
                    P.op("vector" if c % 2 else "scalar",
                         CP(s.t[:, c, 0:W], tp.t[:, 0:W]) if c % 2 else ACT(s.t[:, c, 0:W], tp.t[:, 0:W], AF.Copy),
                         tp.b, [s.b[c]])
                o = orr.next()
                self.ln_fm(s, o, W, gb.t[:, 0:8], gb.t[:, 8:16], R)
                P.dma("gpsimd", [(hT.t[:, :, c0:c0 + W].rearrange("c p t -> p c t"), o.t[:, :, 0:W])],
                      o.b, hT.b, o.ss, nowaw=True)
            P.end_phase()


    def phase_A1(self, l):
        P = self.P
        E = self.ext
        hT, qT, kT, kpeT, vS = (self.scr[n] for n in ("hT", "qT", "kT", "kpeT", "v"))
        with ExitStack() as st:
            P.begin_phase(st)
            w_in = E["w_in"]
            Wql = P.sbuf([128, 8, QL], BF16, dma=True)
            Wkvl = P.sbuf([128, 8, KVL], BF16, dma=True)
            Wkpe = P.sbuf([128, 8, 64], BF16, dma=True)
            Wkpes = P.sbuf([128, 8, 64], BF16, dma=True)
            Wqn = P.sbuf([128, 6, 1024], BF16, dma=True)
            Wqp = P.sbuf([128, 6, 512], BF16, dma=True)
            Wqps = P.sbuf([128, 6, 512], BF16, dma=True)
            Wkn = P.sbuf([128, 2, 1024], BF16, dma=True)
            Wv = P.sbuf([128, 2, 1024], BF16, dma=True)
            self.load_w(Wql, w_in[l, :, 0:768], 8)
            self.load_w(Wkvl, w_in[l, :, 768:1024], 8, per=8)
            self.load_w(Wkpe, w_in[l, :, 1024:1088], 8, per=8)
            self.load_w(Wkpes, E["w_kpes"][l, :, :], 8, per=8)
            self.load_w(Wqn, E["wqb_n"][l, :, :], 6, per=3)
            self.load_w(Wqp, E["wqb_p"][l, :, :], 6, per=6)
            self.load_w(Wqps, E["wqb_ps"][l, :, :], 6, per=6)
            self.load_w(Wkn, E["wkvb_kn"][l, :, :], 2)
            self.load_w(Wv, E["wkvb_v"][l, :, :], 2)
            ng = P.sbuf([128, 8], F32, dma=True)
            P.dma("sync", [(ng.t[:, 0:6], E["qng"][l, :, :]), (ng.t[:, 6:8], E["kvng"][l, :, :])], [], ng.b, ng.ds)
            hr = Ring([P.sbuf([128, 8, 512], F32, dma=True) for _ in range(2)])
            hbr = Ring([P.sbuf([128, 8, 512], BF16, n=8) for _ in range(2)])
            csr = Ring([P.sbuf([128, 2, 512], F32, dma=True) for _ in range(2)])
            qlat = P.sbuf([128, 6, 512], F32, n=6)
            qn = P.sbuf([128, 6, 512], BF16, n=6)
            kvlat = P.sbuf([128, 2, 512], F32, n=2)
            kvn = P.sbuf([128, 2, 512], BF16, n=2)
            sqr = Ring([P.sbuf([128, 512], BF16) for _ in range(2)])
            rstd = P.sbuf([128, 512], F32)
            t1r = Ring([P.sbuf([128, 512], F32) for _ in range(2)])
            t2r = Ring([P.sbuf([128, 512], F32) for _ in range(2)])
            stg = Ring([P.sbuf([128, 512], BF16, dma=True) for _ in range(4)])
            vst = Ring([P.sbuf([128, 1024], BF16, n=2, dma=True) for _ in range(2)])
            accr = Ring([P.psum([128, 512], F32) for _ in range(4)])
            ss_ps = P.psum([128, 512], F32)
            ones = self.ones_b
            evi = [0]

            def evac(out_ap, in_ap, reads, writes):
                evi[0] += 1
                if evi[0] % 2:
                    P.op("scalar", ACT(out_ap, in_ap, AF.Copy), reads, writes)
                else:
                    P.op("vector", CP(out_ap, in_ap), reads, writes)

            def rms(acc_list_fn, nch, lat, dst, W, gcol0, nfeat):
                for c in range(nch):
                    acc = acc_list_fn(c)
                    sq = sqr.next()
                    P.op("scalar", ACT(lat.t[:, c, 0:W], acc.t[:, 0:W], AF.Copy), acc.b, [lat.b[c]])
                    P.op("scalar", ACT(sq.t[:, 0:W], acc.t[:, 0:W], AF.Square), acc.b, sq.b)
                    P.op("tensor", MM(ss_ps.t[:, 0:W], ones.t[:], sq.t[:, 0:W], c == 0, c == nch - 1),
                         sq.b + ones.b, ss_ps.b, mark=(c == nch - 1))
                P.op("vector", TS(rstd.t[:, 0:W], ss_ps.t[:, 0:W], 1.0 / nfeat, ALU.mult, RMS_EPS, ALU.add),
                     ss_ps.b, rstd.b)
                P.op("scalar", ACT(rstd.t[:, 0:W], rstd.t[:, 0:W], AF.Ln), rstd.b, rstd.b)
                P.op("scalar", ACT(rstd.t[:, 0:W], rstd.t[:, 0:W], AF.Exp, scale=-0.5), rstd.b, rstd.b)
                for c in range(nch):
                    P.op("vector", STT(dst.t[:, c, 0:W], lat.t[:, c, 0:W], ng.t[:, gcol0 + c:gcol0 + c + 1],
                                       rstd.t[:, 0:W], ALU.mult, ALU.mult), [lat.b[c]] + rstd.b + ng.b, [dst.b[c]])

            def proj(acc, W_tb, ncols0, ncols, K, rhs_tb, W, M=128):
                for k in range(K):
                    P.op("tensor", MM(acc.t[0:M, 0:W], W_tb.t[:, k, ncols0:ncols0 + ncols], rhs_tb.t[:, k, 0:W],
                                      k == 0, k == K - 1), W_tb.b + [rhs_tb.b[k]], acc.b, mark=(k == K - 1))

            def rope(acc1, acc2, cs, W, M, outs):
                t1 = t1r.next()
                t2 = t2r.next()
                P.op("vector", TT(t1.t[0:M, 0:W], acc1.t[0:M, 0:W], cs.t[0:M, 0, 0:W], ALU.mult), acc1.b + cs.b, t1.b)
                P.op("vector", TT(t2.t[0:M, 0:W], acc2.t[0:M, 0:W], cs.t[0:M, 1, 0:W], ALU.mult), acc2.b + cs.b, t2.b)
                sg = stg.next()
                P.op("vector", TT(sg.t[0:M, 0:W], t1.t[0:M, 0:W], t2.t[0:M, 0:W], ALU.add), t1.b + t2.b, sg.b)
                for (dst_tb, dst_ap, p0, p1) in outs:
                    P.dma("gpsimd", [(dst_ap, sg.t[p0:p1, 0:W])], sg.b, dst_tb.b, sg.ss, nowaw=True)

            for (c0, W) in tiles_of(512):
                nb = W // 128
                h = hr.next()
                P.dma("sync", [(h.t[:, :, 0:W], hT.t[:, :, c0:c0 + W].rearrange("c p t -> p c t"))], hT.b, h.b, h.ds)
                cs = csr.next()
                P.dma("sync", [(cs.t[:, 0, 0:W], E["cosT2"][:, c0:c0 + W]), (cs.t[:, 1, 0:W], E["sinT2"][:, c0:c0 + W])],
                      [], cs.b, cs.ds)
                hb = hbr.next()
                for c in range(8):
                    evac(hb.t[:, c, 0:W], h.t[:, c, 0:W], h.b, [hb.b[c]])

                def ql_acc(c):
                    acc = accr.next()
                    proj(acc, Wql, c * 128, 128, 8, hb, W)
                    return acc
                rms(ql_acc, 6, qlat, qn, W, 0, QL)
                for hd in range(NH):
                    acc = accr.next()
                    proj(acc, Wqn, hd * 128, 128, 6, qn, W)
                    sg = stg.next()
                    evac(sg.t[:, 0:W], acc.t[:, 0:W], acc.b, sg.b)
                    P.dma("gpsimd", [(qT.t[hd, 0:128, c0:c0 + W], sg.t[:, 0:W])], sg.b, qT.b, sg.ss, nowaw=True)
                for pr in range(4):
                    a1 = accr.next()
                    proj(a1, Wqp, pr * 128, 128, 6, qn, W)
                    a2 = accr.next()
                    proj(a2, Wqps, pr * 128, 128, 6, qn, W)
                    rope(a1, a2, cs, W, 128, [(qT, qT.t[2 * pr, 128:192, c0:c0 + W], 0, 64),
                                              (qT, qT.t[2 * pr + 1, 128:192, c0:c0 + W], 64, 128)])

                def kv_acc(c):
                    acc = accr.next()
                    proj(acc, Wkvl, c * 128, 128, 8, hb, W)
                    return acc
                rms(kv_acc, 2, kvlat, kvn, W, 6, KVL)
                for hd in range(NH):
                    acc = accr.next()
                    proj(acc, Wkn, hd * 128, 128, 2, kvn, W)
                    sg = stg.next()
                    evac(sg.t[:, 0:W], acc.t[:, 0:W], acc.b, sg.b)
                    P.dma("gpsimd", [(kT.t[hd, :, c0:c0 + W], sg.t[:, 0:W])], sg.b, kT.b, sg.ss, nowaw=True)
                for bi in range(nb):
                    vs = vst.next()
                    for half in range(2):
                        acc = accr.next()
                        for k in range(2):
                            P.op("tensor", MM(acc.t[:, :], kvn.t[:, k, bi * 128:(bi + 1) * 128],
                                              Wv.t[:, k, half * 512:(half + 1) * 512], k == 0, k == 1),
                                 Wv.b + [kvn.b[k]], acc.b, mark=(k == 1))
                        evac(vs.t[:, half * 512:(half + 1) * 512], acc.t[:, :], acc.b, [vs.b[half]])
                    blk = c0 // 128 + bi
                    P.dma("gpsimd", [(vS.t[:, :, blk, :].rearrange("h p d -> p h d"),
                                      vs.t[:, :].rearrange("p (h d) -> p h d", h=NH))], vs.b, vS.b, vs.ss, nowaw=True)
                a1 = accr.next()
                proj(a1, Wkpe, 0, 64, 8, hb, W, M=64)
                a2 = accr.next()
                proj(a2, Wkpes, 0, 64, 8, hb, W, M=64)
                rope(a1, a2, cs, W, 64, [(kpeT, kpeT.t[:, c0:c0 + W], 0, 64)])
            P.end_phase()


    def phase_B(self, l):
        P = self.P
        qT, kT, kpeT, vS, oT = (self.scr[n] for n in ("qT", "kT", "kpeT", "v", "oT"))
        ones = self.ones_b
        with ExitStack() as st:
            P.begin_phase(st)
            kpe = P.sbuf([64, LP], BF16, dma=True)
            P.dma("sync", [(kpe.t[:, :], kpeT.t[:, :])], kpeT.b, kpe.b, kpe.ds)
            knr = Ring([P.sbuf([128, LP], BF16, dma=True) for _ in range(2)])
            vr = Ring([P.sbuf([128, NBLK, 128], BF16, dma=True) for _ in range(2)])
            qr = Ring([P.sbuf([128, 2, 512], BF16, dma=True) for _ in range(3)])
            ptr = Ring([P.sbuf([128, 512], BF16) for _ in range(6)])
            rsr = Ring([P.sbuf([128, 512], F32) for _ in range(2)])
            osr = Ring([P.sbuf([128, 512], BF16, dma=True) for _ in range(2)])
            spr = Ring([P.psum([128, 512], F32) for _ in range(4)])
            opr = Ring([P.psum([128, 512], F32) for _ in range(2)])
            smr = Ring([P.psum([128, 512], F32) for _ in range(2)])
            tiles = tiles_of(512)
            for hd in range(NH):
                kn = knr.next()
                P.dma("sync", [(kn.t[:, :], kT.t[hd, :, :])], kT.b, kn.b, kn.ds)
                v = vr.next()
                P.dma("sync", [(v.t[:, :, :], vS.t[hd, :, :, :])], vS.b, v.b, v.ds)
                for (c0, W) in tiles:
                    q = qr.next()
                    P.dma("sync", [(q.t[:, 0, 0:W], qT.t[hd, 0:128, c0:c0 + W]),
                                   (q.t[0:64, 1, 0:W], qT.t[hd, 128:192, c0:c0 + W])], qT.b, q.b, q.ds)
                    o_ps = opr.next()
                    sm_ps = smr.next()
                    nkb = (c0 + W) // 128
                    units = []
                    for kb in range(nkb):
                        qo = kb * 128 - c0 if kb * 128 >= c0 else 0
                        units.append((kb, qo))

                    def qk(u):
                        kb, qo = u
                        sp = spr.next()
                        P.op("tensor", MM(sp.t[:, qo:W], kn.t[:, kb * 128:(kb + 1) * 128], q.t[:, 0, qo:W], True, False),
                             kn.b + q.b, sp.b, mark=False)
                        P.op("tensor", MM(sp.t[:, qo:W], kpe.t[0:64, kb * 128:(kb + 1) * 128], q.t[0:64, 1, qo:W],
                                          False, True), kpe.b + q.b, sp.b, mark=True)
                        return sp

                    def soft(u, sp):
                        kb, qo = u
                        pt = ptr.next()
                        P.op("scalar", ACT(pt.t[:, qo:W], sp.t[:, qo:W], AF.Exp, scale=SCALE), sp.b, pt.b)
                        if kb == 0 and c0 == 0:
                            P.op("vector", TT(pt.t[:, 0:128], pt.t[:, 0:128], self.mask0_b.t[:], ALU.mult),
                                 pt.b + self.mask0_b.b, pt.b)
                        elif kb == 0:
                            P.op("vector", TS(pt.t[:, 0:W], pt.t[:, 0:W], self.kmask0.t[:, 0:1], ALU.mult),
                                 pt.b + self.kmask0.b, pt.b)
                        elif kb * 128 >= c0:
                            P.op("vector", TT(pt.t[:, qo:qo + 128], pt.t[:, qo:qo + 128], self.tri_b.t[:], ALU.mult),
                                 pt.b + self.tri_b.b, pt.b)
                        return pt

                    def pv(u, pt, first, last):
                        kb, qo = u
                        P.op("tensor", MM(o_ps.t[:, qo:W], v.t[:, kb, :], pt.t[:, qo:W], first, last),
                             v.b + pt.b, o_ps.b, mark=False)
                        P.op("tensor", MM(sm_ps.t[:, qo:W], ones.t[:], pt.t[:, qo:W], first, last),
                             ones.b + pt.b, sm_ps.b, mark=True)

                    LA = 3
                    sps = [qk(units[i]) for i in range(min(LA, len(units)))]
                    for i, u in enumerate(units):
                        if i + LA < len(units):
                            sps.append(qk(units[i + LA]))
                        pt = soft(u, sps[i])
                        pv(u, pt, i == 0, i == len(units) - 1)
                    rs = rsr.next()
                    P.op("vector", lambda e, o=rs.t[:, 0:W], i=sm_ps.t[:, 0:W]: e.reciprocal(out=o, in_=i), sm_ps.b, rs.b)
                    og = osr.next()
                    P.op("vector", TT(og.t[:, 0:W], o_ps.t[:, 0:W], rs.t[:, 0:W], ALU.mult), o_ps.b + rs.b, og.b)
                    P.dma("gpsimd", [(oT.t[hd, :, c0:c0 + W], og.t[:, 0:W])], og.b, oT.b, og.ss, nowaw=True)
            P.end_phase()


    def phase_A2(self, l):
        P = self.P
        E = self.ext
        hT, xsS, BtokS, BTS, CTS, dtS, zsS = (self.scr[n] for n in ("hT", "xs", "Btok", "BT", "CT", "dt", "zs"))
        with ExitStack() as st:
            P.begin_phase(st)
            w_in = E["w_in"]
            Wz = P.sbuf([128, 8, 2048], BF16, dma=True)
            Wx = P.sbuf([128, 8, 3072], BF16, dma=True)
            Wdt = P.sbuf([128, 8, 32], BF16, dma=True)
            self.load_w(Wz, w_in[l, :, 1088:3136], 8, per=2)
            self.load_w(Wx, w_in[l, :, 3136:6208], 8, per=2)
            self.load_w(Wdt, w_in[l, :, 6208:6240], 8, per=8)
            cw = P.sbuf([128, 24, 4], F32, dma=True)
            cb = P.sbuf([128, 24], F32, dma=True)
            dtb = P.sbuf([128, 32], F32, dma=True)
            P.dma("sync", [(cw.t[:], E["ssd_cw"][l, :, :, :])], [], cw.b, cw.ds)
            P.dma("sync", [(cb.t[:], E["ssd_cb"][l, :, :])], [], cb.b, cb.ds)
            P.dma("sync", [(dtb.t[:], E["dtb_bc"][l, :, :])], [], dtb.b, dtb.ds)
            h = P.sbuf([128, 8, 512], F32, dma=True)
            hbr = Ring([P.sbuf([128, 8, 512], BF16, n=8) for _ in range(2)])
            xc = P.sbuf([128, 24, 512], BF16, n=24, dma=True)
            xbr = Ring([P.sbuf([128, 515], F32) for _ in range(2)])
            halo = P.sbuf([128, 24, 3], F32, n=24)
            halo2 = P.sbuf([128, 24, 3], F32, n=24)
            halos = [halo, halo2]
            tcr = Ring([P.sbuf([128, 512], F32) for _ in range(4)])
            zst = Ring([P.sbuf([128, 2048], BF16, n=4, dma=True) for _ in range(2)])
            dtr = Ring([P.sbuf([128, 32], F32, dma=True) for _ in range(2)])
            tst = Ring([P.sbuf([128, 2560], BF16, n=5, dma=True) for _ in range(2)])
            accr = Ring([P.psum([128, 512], F32) for _ in range(6)])
            tpr = Ring([P.psum([128, 512], BF16) for _ in range(2)])
            P.op("vector", MEMSET(halo.t[:], 0.0), [], halo.b)
            evi = [0]

            def evac(out_ap, in_ap, reads, writes):
                evi[0] += 1
                if evi[0] % 2:
                    P.op("scalar", ACT(out_ap, in_ap, AF.Copy), reads, writes)
                else:
                    P.op("vector", CP(out_ap, in_ap), reads, writes)

            for tile_i, (c0, W) in enumerate(tiles_of(512)):
                nb = W // 128
                P.dma("sync", [(h.t[:, :, 0:W], hT.t[:, :, c0:c0 + W].rearrange("c p t -> p c t"))], hT.b, h.b, h.ds)
                hb = hbr.next()
                for c in range(8):
                    evac(hb.t[:, c, 0:W], h.t[:, c, 0:W], h.b, [hb.b[c]])
                for bi in range(nb):
                    zt = zst.next()
                    for cg in range(4):
                        acc = accr.next()
                        for k in range(8):
                            P.op("tensor", MM(acc.t[:, :], hb.t[:, k, bi * 128:(bi + 1) * 128],
                                              Wz.t[:, k, cg * 512:(cg + 1) * 512], k == 0, k == 7),
                                 Wz.b + [hb.b[k]], acc.b, mark=(k == 7))
                        P.op("scalar", ACT(zt.t[:, cg * 512:(cg + 1) * 512], acc.t[:, :], AF.Silu), acc.b, [zt.b[cg]])
                    r0 = c0 + bi * 128
                    P.dma("gpsimd", [(zsS.t[r0:r0 + 128, :], zt.t[:, :])], zt.b, zsS.b, zt.ss, nowaw=True)
                    acc = accr.next()
                    for k in range(8):
                        P.op("tensor", MM(acc.t[:, 0:32], hb.t[:, k, bi * 128:(bi + 1) * 128], Wdt.t[:, k, 0:32],
                                          k == 0, k == 7), Wdt.b + [hb.b[k]], acc.b, mark=(k == 7))
                    dtt = dtr.next()
                    P.op("vector", TT(dtt.t[:, :], acc.t[:, 0:32], dtb.t[:, :], ALU.add), acc.b + dtb.b, dtt.b)
                    P.op("scalar", ACT(dtt.t[:, :], dtt.t[:, :], AF.Exp), dtt.b, dtt.b)
                    P.op("scalar", ACT(dtt.t[:, :], dtt.t[:, :], AF.Ln, bias=1.0), dtt.b, dtt.b)
                    P.dma("gpsimd", [(dtS.t[r0:r0 + 128, :], dtt.t[:, :])], dtt.b, dtS.b, dtt.ss, nowaw=True)
                for c in range(24):
                    acc = accr.next()
                    for k in range(8):
                        P.op("tensor", MM(acc.t[:, 0:W], Wx.t[:, k, c * 128:(c + 1) * 128], hb.t[:, k, 0:W],
                                          k == 0, k == 7), Wx.b + [hb.b[k]], acc.b, mark=(k == 7))
                    hin, hout = halos[tile_i % 2], halos[(tile_i + 1) % 2]
                    tc_ = tcr.next()
                    if c0 == 0:
                        xb = xbr.next()
                        P.op("vector", MEMSET(xb.t[:, 0:3 + PAD], 0.0), [], xb.b)
                        P.op("scalar", ACT(xb.t[:, 3 + PAD:3 + W], acc.t[:, PAD:W], AF.Copy), [], xb.b + acc.b)
                        P.op("scalar", ACT(hout.t[:, c, :], xb.t[:, W:W + 3], AF.Copy), xb.b, [hout.b[c]])
                        P.op("scalar", ACT(tc_.t[:, 0:W], xb.t[:, 0:W], AF.Identity, bias=cb.t[:, c:c + 1], scale=cw.t[:, c, 0:1]),
                             xb.b + cw.b + cb.b, tc_.b)
                        for tap in range(1, 4):
                            P.op("vector", STT(tc_.t[:, 0:W], xb.t[:, tap:tap + W], cw.t[:, c, tap:tap + 1], tc_.t[:, 0:W],
                                               ALU.mult, ALU.add), xb.b + cw.b, tc_.b)
                    else:
                        P.op("scalar", ACT(tc_.t[:, 3:W], acc.t[:, 0:W - 3], AF.Identity, bias=cb.t[:, c:c + 1], scale=cw.t[:, c, 0:1]),
                             cw.b + cb.b, tc_.b + acc.b)
                        P.op("scalar", ACT(hout.t[:, c, :], acc.t[:, W - 3:W], AF.Copy), [], [hout.b[c]] + acc.b)
                        P.op("scalar", ACT(tc_.t[:, 0:3], hin.t[:, c, :], AF.Identity, bias=cb.t[:, c:c + 1], scale=cw.t[:, c, 0:1]),
                             [hin.b[c]], tc_.b)
                        P.op("vector", STT(tc_.t[:, 0:2], hin.t[:, c, 1:3], cw.t[:, c, 1:2], tc_.t[:, 0:2], ALU.mult, ALU.add),
                             [hin.b[c]] + cw.b, tc_.b)
                        P.op("vector", STT(tc_.t[:, 0:1], hin.t[:, c, 2:3], cw.t[:, c, 2:3], tc_.t[:, 0:1], ALU.mult, ALU.add),
                             [hin.b[c]] + cw.b, tc_.b)
                        for tap in range(1, 4):
                            P.op("vector", STT(tc_.t[:, 3 - tap:W], acc.t[:, 0:W - 3 + tap], cw.t[:, c, tap:tap + 1],
                                               tc_.t[:, 3 - tap:W], ALU.mult, ALU.add), cw.b, tc_.b + acc.b)
                    P.op("scalar", ACT(xc.t[:, c, 0:W], tc_.t[:, 0:W], AF.Silu), tc_.b, [xc.b[c]])
                    if c0 == 0:
                        P.op("vector", MEMSET(xc.t[:, c, 0:PAD], 0.0), [], [xc.b[c]])
                prs = []
                for g in range(4):
                    prs.append((BTS.t[g, :, c0:c0 + W], xc.t[:, 16 + g, 0:W]))
                    prs.append((CTS.t[g, :, c0:c0 + W], xc.t[:, 20 + g, 0:W]))
                P.dma("gpsimd", prs, xc.b[16:24], BTS.b + CTS.b, xc.ss, nowaw=True)
                for bi in range(nb):
                    ts_ = tst.next()
                    for q4 in range(5):
                        tp = tpr.next()
                        for j in range(4):
                            c = q4 * 4 + j
                            P.op("tensor", TR(tp.t[:, j * 128:(j + 1) * 128], xc.t[:, c, bi * 128:(bi + 1) * 128],
                                              self.identb.t[:]), [xc.b[c]] + self.identb.b, tp.b, mark=(j == 3))
                        evac(ts_.t[:, q4 * 512:(q4 + 1) * 512], tp.t[:, :], tp.b, [ts_.b[q4]])
                    r0 = c0 + bi * 128
                    P.dma("gpsimd", [(xsS.t[r0:r0 + 128, :], ts_.t[:, 0:2048]), (BtokS.t[r0:r0 + 128, :], ts_.t[:, 2048:2560])],
                          ts_.b, xsS.b + BtokS.b, ts_.ss, nowaw=True)
            P.end_phase()

    def phase_S(self, l):
        P = self.P
        E = self.ext
        xsS, BtokS, BTS, CTS, dtS, zsS, ynTS = (self.scr[n] for n in ("xs", "Btok", "BT", "CT", "dt", "zs", "ynT"))
        trib = self.tri_b
        ones = self.ones_b
        identb = self.identb
        with ExitStack() as st:
            P.begin_phase(st)
            abc = P.sbuf([128, 32], F32, dma=True)
            dsk = P.sbuf([128, 2048], F32, dma=True)
            ngb = P.sbuf([128, 2048], F32, dma=True)
            negm_f = P.sbuf([128, 128], F32, dma=True)
            P.dma("sync", [(abc.t[:], E["alog_bc"][l, :, :])], [], abc.b, abc.ds)
            P.dma("sync", [(dsk.t[:], E["dskip_bc"][l, :, :])], [], dsk.b, dsk.ds)
            P.dma("sync", [(ngb.t[:], E["ssdng_bc"][l, :, :])], [], ngb.b, ngb.ds)
            P.dma("sync", [(negm_f.t[:], E["c_negmask"][:, :])], [], negm_f.b, negm_f.ds)
            P.op("scalar", ACT(abc.t[:], abc.t[:], AF.Exp), abc.b, abc.b)
            P.op("vector", TS(abc.t[:], abc.t[:], -1.0, ALU.mult), abc.b, abc.b)
            negm = P.sbuf([128, 128], BF16)
            trin = P.sbuf([128, 128], BF16)
            P.op("vector", CP(negm.t[:], negm_f.t[:]), negm_f.b, negm.b)
            P.op("vector", TS(trin.t[:], self.tri_f.t[:], -1.0, ALU.mult), self.tri_f.b, trin.b)
            NB_ = 3
            xsr = Ring([P.sbuf([128, 2048], BF16, dma=True) for _ in range(NB_)])
            btr = Ring([P.sbuf([128, 512], BF16, dma=True) for _ in range(NB_)])
            bTr = Ring([P.sbuf([128, 4, 128], BF16, dma=True) for _ in range(NB_)])
            cTr = Ring([P.sbuf([128, 4, 128], BF16, dma=True) for _ in range(NB_)])
            dtr = Ring([P.sbuf([128, 32], F32, dma=True) for _ in range(NB_)])
            zsr = Ring([P.sbuf([128, 2048], BF16, dma=True) for _ in range(NB_)])
            prev = P.sbuf([128, 2048], F32, n=4)
            prevb = P.sbuf([128, 2048], BF16, n=4)
            P.op("vector", MEMSET(prev.t[:], 0.0), [], prev.b)
            P.op("vector", MEMSET(prevb.t[:], 0.0), [], prevb.b)
            at = P.sbuf([128, 32], F32)
            ar = P.sbuf([128, 32], F32)
            a3 = [P.sbuf([128, 32], BF16) for _ in range(3)]
            def mk_ctx():
                return dict(abig=[P.sbuf([128, 32, 128], BF16) for _ in range(2)],
                            abm=[P.sbuf([128, 32, 128], BF16) for _ in range(2)],
                            eacs=P.sbuf([128, 32], F32), cdec=P.sbuf([128, 32], F32),
                            xdt=P.sbuf([128, 2048], BF16), xdts=P.sbuf([128, 2048], BF16), xsd=P.sbuf([128, 2048], BF16))
            ctxr = Ring([mk_ctx() for _ in range(2)])
            acs = P.sbuf([128, 32], F32)
            dst = P.sbuf([128, 32], F32)
            y = P.sbuf([128, 2048], F32, n=4)
            yn = P.sbuf([128, 2048], BF16, n=4)
            junkr = Ring([P.sbuf([128, 512], F32) for _ in range(2)])
            ss = P.sbuf([128, 4], F32)
            rstd = P.sbuf([128, 4], F32)
            t1r = Ring([P.sbuf([128, 512], F32) for _ in range(2)])
            cbsr = Ring([P.sbuf([128, 128], F32) for _ in range(2)])
            er = Ring([P.sbuf([128, 4, 128], F32) for _ in range(3)])
            mtr = Ring([P.sbuf([128, 4, 128], BF16) for _ in range(4)])
            ynst = Ring([P.sbuf([128, 16, 128], BF16, n=4, dma=True) for _ in range(2)])
            misc = P.psum([128, 512], F32)
            acs_ps = TB(misc.t[:, 0:32]); acs_ps.b = misc.b
            last_ps = TB(misc.t[:, 32:64]); last_ps.b = misc.b
            cb_ps = TB(misc.t[:, 128:256]); cb_ps.b = misc.b
            dpr = Ring([P.psum([128, 512], F32) for i in range(2)])
            ydr = Ring([P.psum([128, 512], F32) for i in range(2)])
            yo_ps = P.psum([128, 512], F32)
            st_ps = P.psum([128, 512], F32)
            tpr = Ring([P.psum([128, 512], BF16) for _ in range(1)])
            bc3 = lambda ap2, n: ap2.unsqueeze(2).to_broadcast([128, n, 64])
            v3 = lambda ap2: ap2.rearrange("p (h d) -> p h d", d=64)
            nch = getattr(self, 'S_NCH', NBLK)
            loaded = {}
            ctxs = {}

            def load(c):
                r0 = c * 128
                t = dict(xs=xsr.next(), bt=btr.next(), bT=bTr.next(), cT=cTr.next(), dt=dtr.next(), zs=zsr.next())
                P.dma("sync", [(t["xs"].t[:], xsS.t[r0:r0 + 128, :])], xsS.b, t["xs"].b, t["xs"].ds)
                P.dma("sync", [(t["bt"].t[:], BtokS.t[r0:r0 + 128, :])], BtokS.b, t["bt"].b, t["bt"].ds)
                P.dma("sync", [(t["bT"].t[:], BTS.t[:, :, r0:r0 + 128].rearrange("g p t -> p g t"))], BTS.b, t["bT"].b, t["bT"].ds)
                P.dma("sync", [(t["cT"].t[:], CTS.t[:, :, r0:r0 + 128].rearrange("g p t -> p g t"))], CTS.b, t["cT"].b, t["cT"].ds)
                P.dma("sync", [(t["dt"].t[:], dtS.t[r0:r0 + 128, :])], dtS.b, t["dt"].b, t["dt"].ds)
                P.dma("sync", [(t["zs"].t[:], zsS.t[r0:r0 + 128, :])], zsS.b, t["zs"].b, t["zs"].ds)
                loaded[c] = t

            def front(c):
                t = loaded[c]
                xs, dt = t["xs"], t["dt"]
                X = ctxr.next()
                ctxs[c] = X
                abig, abm, eacs, cdec, xdt, xdts, xsd = (X[k_] for k_ in ("abig", "abm", "eacs", "cdec", "xdt", "xdts", "xsd"))
                P.op("vector", TT(at.t[:], dt.t[:], abc.t[:], ALU.mult), dt.b + abc.b, at.b)
                P.op("vector", CP(a3[0].t[:], at.t[:]), at.b, a3[0].b)
                P.op("vector", TT(ar.t[:], at.t[:], a3[0].t[:], ALU.subtract), at.b + a3[0].b, ar.b)
                P.op("vector", CP(a3[1].t[:], ar.t[:]), ar.b, a3[1].b)
                P.op("vector", TT(ar.t[:], ar.t[:], a3[1].t[:], ALU.subtract), a3[1].b, ar.b)
                P.op("vector", CP(a3[2].t[:], ar.t[:]), ar.b, a3[2].b)
                for i3 in range(3):
                    P.op("tensor", MM(acs_ps.t, trib.t[:], a3[i3].t[:], i3 == 0, i3 == 2), trib.b + a3[i3].b, misc.b,
                         mark=(i3 == 2))
                for i3 in range(3):
                    P.op("tensor", MM(last_ps.t, ones.t[:], a3[i3].t[:], i3 == 0, i3 == 2), ones.b + a3[i3].b, misc.b,
                         mark=(i3 == 2))
                P.op("vector", CP(acs.t[:], acs_ps.t), [], acs.b + misc.b)
                for i3 in range(2):
                    P.op("scalar", ACT(abig[i3].t[:], a3[i3].t[:, :].unsqueeze(2).to_broadcast([128, 32, 128]), AF.Copy),
                         a3[i3].b, abig[i3].b)
                    P.op("gpsimd", TT(abm[i3].t[:], a3[i3].t[:, :].unsqueeze(2).to_broadcast([128, 32, 128]),
                                      trin.t[:, :].unsqueeze(1).to_broadcast([128, 32, 128]), ALU.mult),
                         a3[i3].b + trin.b, abm[i3].b)
                P.op("scalar", ACT(eacs.t[:], acs.t[:], AF.Exp), acs.b, eacs.b)
                P.op("scalar", ACT(cdec.t[:], last_ps.t, AF.Exp), [], cdec.b + misc.b)
                P.op("vector", TT(dst.t[:], last_ps.t, acs.t[:], ALU.subtract), acs.b, dst.b + misc.b)
                P.op("scalar", ACT(dst.t[:], dst.t[:], AF.Exp), dst.b, dst.b)
                P.op("vector", TT(v3(xdt.t[:]), v3(xs.t[:]), bc3(dt.t[:, :], 32), ALU.mult), xs.b + dt.b, xdt.b)
                P.op("gpsimd", TT(v3(xdts.t[:]), v3(xdt.t[:]), bc3(dst.t[:, :], 32), ALU.mult), xdt.b + dst.b, xdts.b)
                P.op("gpsimd", TT(xsd.t[:], xs.t[:], dsk.t[:], ALU.mult), xs.b + dsk.b, xsd.b)

            load(0)
            front(0)
            for c in range(nch):
                r0 = c * 128
                if c + 1 < nch:
                    load(c + 1)
                    front(c + 1)
                t = loaded.pop(c)
                X = ctxs.pop(c)
                abig, abm, eacs, cdec, xdt, xdts, xsd = (X[k_] for k_ in ("abig", "abm", "eacs", "cdec", "xdt", "xdts", "xsd"))
                xs, bt, bT, cT, dt, zs = t["xs"], t["bt"], t["bT"], t["cT"], t["dt"], t["zs"]
                def pe_D(g):
                    P.op("tensor", MM(cb_ps.t, bT.t[:, g, :], cT.t[:, g, :]), bT.b + cT.b, misc.b)
                    dd = []
                    for hb4 in range(2):
                        dps = dpr.next()
                        dd.append(dps)
                        for j in range(4):
                            hd = g * 8 + hb4 * 4 + j
                            sl = dps.t[:, j * 128:(j + 1) * 128]
                            P.op("tensor", MM(sl, abig[0].t[:, hd, :], trib.t[:], True, False), abig[0].b + trib.b, dps.b, mark=False)
                            P.op("tensor", MM(sl, abig[1].t[:, hd, :], trib.t[:], False, False), abig[1].b, dps.b, mark=False)
                            P.op("tensor", MM(sl, abm[0].t[:, hd, :], ones.t[:], False, False), abm[0].b + ones.b, dps.b, mark=False)
                            P.op("tensor", MM(sl, abm[1].t[:, hd, :], ones.t[:], False, False), abm[1].b, dps.b, mark=False)
                            P.op("tensor", MM(sl, identb.t[:], negm.t[:], False, True), identb.b + negm.b, dps.b, mark=(j == 3))
                    return dd

                def tail(g, yd):
                    gs = slice(g * 512, (g + 1) * 512)
                    t1 = t1r.next()
                    P.op("vector", TT(v3(t1.t[:]), v3(yo_ps.t[:, :]), bc3(eacs.t[:, g * 8:(g + 1) * 8], 8), ALU.mult),
                         yo_ps.b + eacs.b, t1.b)
                    P.op("vector", TT(y.t[:, gs], yd.t[:, :], t1.t[:], ALU.add), yd.b + t1.b, [y.b[g]])
                    P.op("vector", TT(y.t[:, gs], y.t[:, gs], zs.t[:, gs], ALU.mult), zs.b, [y.b[g]])
                    junk = junkr.next()
                    P.op("gpsimd", TT(junk.t[:], y.t[:, gs], y.t[:, gs], ALU.mult), [y.b[g]], junk.b)
                    P.op("vector", lambda e, o=junk.t[:], acc=ss.t[:, g:g + 1]: e.tensor_scalar(
                        out=o, in0=o, scalar1=1.0, scalar2=0.0, op0=ALU.mult, op1=ALU.add, accum_out=acc),
                        [], junk.b + ss.b)
                    P.op("gpsimd", TT(v3(prev.t[:, gs]), v3(prev.t[:, gs]), bc3(cdec.t[:, g * 8:(g + 1) * 8], 8), ALU.mult),
                         cdec.b, [prev.b[g]])
                    P.op("vector", TT(prev.t[:, gs], prev.t[:, gs], st_ps.t[:, :], ALU.add), st_ps.b, [prev.b[g]])
                    P.op("scalar", ACT(prevb.t[:, gs], prev.t[:, gs], AF.Copy), [prev.b[g]], [prevb.b[g]])

                dcur = pe_D(0)
                yds = {}
                for g in range(4):
                    gs = slice(g * 512, (g + 1) * 512)
                    cbs = cbsr.next()
                    P.op("scalar", ACT(cbs.t[:], cb_ps.t, AF.Copy), [], cbs.b + misc.b)
                    mts = []
                    for hb4 in range(2):
                        dps = dcur[hb4]
                        ee = er.next()
                        P.op("scalar", ACT(ee.t[:].rearrange("p a b -> p (a b)"), dps.t[:, :], AF.Exp), dps.b, ee.b)
                        mt = mtr.next()
                        P.op("vector", TT(mt.t[:], ee.t[:], cbs.t[:, :].unsqueeze(1).to_broadcast([128, 4, 128]), ALU.mult),
                             ee.b + cbs.b, mt.b)
                        mts.append(mt)
                    if g + 1 < 4:
                        dcur = pe_D(g + 1)
                    yd = ydr.next()
                    yds[g] = yd
                    for hb4 in range(2):
                        mt = mts[hb4]
                        for j in range(4):
                            hh = hb4 * 4 + j
                            hd = g * 8 + hh
                            P.op("tensor", MM(yd.t[:, hh * 64:(hh + 1) * 64], mt.t[:, j, :], xdt.t[:, hd * 64:(hd + 1) * 64], True, False),
                                 mt.b + xdt.b, yd.b, mark=False)
                            P.op("tensor", MM(yd.t[:, hh * 64:(hh + 1) * 64], identb.t[:], xsd.t[:, hd * 64:(hd + 1) * 64], False, True),
                                 identb.b + xsd.b, yd.b, mark=(hh == 7))
                    if g >= 1:
                        tail(g - 1, yds.pop(g - 1))
                    P.op("tensor", MM(yo_ps.t[:, :], cT.t[:, g, :], prevb.t[:, gs]), cT.b + [prevb.b[g]], yo_ps.b)
                    P.op("tensor", MM(st_ps.t[:, :], bt.t[:, g * 128:(g + 1) * 128], xdts.t[:, gs]), bt.b + xdts.b, st_ps.b)
                tail(3, yds.pop(3))
                P.op("vector", TS(rstd.t[:], ss.t[:], 1.0 / 512, ALU.mult, RMS_EPS, ALU.add), ss.b, rstd.b)
                P.op("scalar", ACT(rstd.t[:], rstd.t[:], AF.Ln), rstd.b, rstd.b)
                P.op("scalar", ACT(rstd.t[:], rstd.t[:], AF.Exp, scale=-0.5), rstd.b, rstd.b)
                yst = ynst.next()
                for g in range(4):
                    gs = slice(g * 512, (g + 1) * 512)
                    P.op("vector", STT(yn.t[:, gs], y.t[:, gs], rstd.t[:, g:g + 1], ngb.t[:, gs], ALU.mult, ALU.mult),
                         [y.b[g]] + rstd.b + ngb.b, [yn.b[g]])
                    tp = tpr.next()
                    for j in range(4):
                        cc = g * 4 + j
                        P.op("tensor", TR(tp.t[:, j * 128:(j + 1) * 128], yn.t[:, cc * 128:(cc + 1) * 128], identb.t[:]),
                             [yn.b[g]] + identb.b, tp.b, mark=(j == 3))
                    P.op("scalar", ACT(yst.t[:, g * 4:(g + 1) * 4, :], tp.t[:, :].rearrange("p (c t) -> p c t", c=4), AF.Copy),
                         tp.b, [yst.b[g]])
                P.dma("sync", [(ynTS.t[:, :, r0:r0 + 128].rearrange("c p t -> p c t"), yst.t[:, :, :])],
                      yst.b, ynTS.b, yst.ss, nowaw=True)
            P.end_phase()

    def phase_C1(self, l):
        P = self.P
        E = self.ext
        hT, oTS, ynTS, h1T = (self.scr[n] for n in ("hT", "oT", "ynT", "h1T"))
        WT = 256
        with ExitStack() as st:
            P.begin_phase(st)
            w_in = E["w_in"]
            Woa = P.sbuf([128, 8, 1024], BF16, dma=True)
            Wos = P.sbuf([128, 16, 1024], BF16, dma=True)
            Wout = P.sbuf([128, 8, 1024], BF16, dma=True)
            Wga = P.sbuf([128, 8, 1024], BF16, dma=True)
            Wgs = P.sbuf([128, 8, 1024], BF16, dma=True)
            self.load_w(Woa, E["w_o_attn"][l, :, :], 8)
            self.load_w(Wos, E["w_o_ssd"][l, :, :], 16)
            self.load_w(Wout, E["w_out"][l, :, :], 8)
            self.load_w(Wga, w_in[l, :, 6240:7264], 8)
            self.load_w(Wgs, w_in[l, :, 7264:8288], 8)
            gb = P.sbuf([128, 16], F32, dma=True)
            P.dma("sync", [(gb.t[:, 0:8], E["ln1_g"][l, :, :]), (gb.t[:, 8:16], E["ln1_b"][l, :, :])], [], gb.b, gb.ds)
            otr = Ring([P.sbuf([128, 8, WT], BF16, dma=True) for _ in range(2)])
            ynr = Ring([P.sbuf([128, 16, WT], BF16, dma=True) for _ in range(2)])
            hr = Ring([P.sbuf([128, 8, WT], F32, n=8, dma=True) for _ in range(2)])
            hb = P.sbuf([128, 8, WT], BF16, n=8)
            mixed = P.sbuf([128, 8, WT], BF16, n=8)
            orr = Ring([P.sbuf([128, 8, WT], F32, n=8, dma=True) for _ in range(1)])
            sgr = Ring([P.sbuf([128, WT], F32) for _ in range(4)])
            tr_ = Ring([P.sbuf([128, WT], F32) for _ in range(4)])
            accr = Ring([P.psum([128, 512], F32) for _ in range(4)])
            R = self.ln_resources(WT)
            evi = [0]

            def proj(acc, W_tb, oc, K, rhs_tb, W):
                for k in range(K):
                    P.op("tensor", MM(acc.t[:, 0:W], W_tb.t[:, k, oc * 128:(oc + 1) * 128], rhs_tb.t[:, k, 0:W],
                                      k == 0, k == K - 1), W_tb.b + [rhs_tb.b[k % len(rhs_tb.b)]], acc.b, mark=(k == K - 1))

            for (c0, W) in tiles_of(WT):
                ot = otr.next(); yt = ynr.next(); h = hr.next()
                P.dma("sync", [(ot.t[:, :, 0:W], oTS.t[:, :, c0:c0 + W].rearrange("h p t -> p h t"))], oTS.b, ot.b, ot.ds)
                P.dma("sync", [(yt.t[:, :, 0:W], ynTS.t[:, :, c0:c0 + W].rearrange("c p t -> p c t"))], ynTS.b, yt.b, yt.ds)
                P.dma("sync", [(h.t[:, :, 0:W], hT.t[:, :, c0:c0 + W].rearrange("c p t -> p c t"))], hT.b, h.b, h.ds)
                for c in range(8):
                    evi[0] += 1
                    if evi[0] % 2:
                        P.op("scalar", ACT(hb.t[:, c, 0:W], h.t[:, c, 0:W], AF.Copy), h.b, [hb.b[c]])
                    else:
                        P.op("vector", CP(hb.t[:, c, 0:W], h.t[:, c, 0:W]), h.b, [hb.b[c]])
                for oc in range(8):
                    ga = accr.next()
                    proj(ga, Wga, oc, 8, hb, W)
                    sga = sgr.next()
                    P.op("scalar", ACT(sga.t[:, 0:W], ga.t[:, 0:W], AF.Sigmoid), ga.b, sga.b)
                    gs_ = accr.next()
                    proj(gs_, Wgs, oc, 8, hb, W)
                    sgs = sgr.next()
                    P.op("scalar", ACT(sgs.t[:, 0:W], gs_.t[:, 0:W], AF.Sigmoid), gs_.b, sgs.b)
                    ya = accr.next()
                    proj(ya, Woa, oc, 8, ot, W)
                    t1 = tr_.next()
                    P.op("vector", TT(t1.t[:, 0:W], ya.t[:, 0:W], sga.t[:, 0:W], ALU.mult), ya.b + sga.b, t1.b)
                    ys_ = accr.next()
                    proj(ys_, Wos, oc, 16, yt, W)
                    t2 = tr_.next()
                    P.op("vector", TT(t2.t[:, 0:W], ys_.t[:, 0:W], sgs.t[:, 0:W], ALU.mult), ys_.b + sgs.b, t2.b)
                    P.op("vector", TT(mixed.t[:, oc, 0:W], t1.t[:, 0:W], t2.t[:, 0:W], ALU.add), t1.b + t2.b, [mixed.b[oc]])
                for oc in range(8):
                    r = accr.next()
                    proj(r, Wout, oc, 8, mixed, W)
                    P.op("vector", STT(h.t[:, oc, 0:W], h.t[:, oc, 0:W], ALPHA, r.t[:, 0:W], ALU.mult, ALU.add),
                         r.b + [hb.b[oc]], [h.b[oc]])
                o = orr.next()
                self.ln_fm(h, o, W, gb.t[:, 0:8], gb.t[:, 8:16], R)
                P.dma("gpsimd", [(h1T.t[:, :, c0:c0 + W].rearrange("c p t -> p c t"), o.t[:, :, 0:W])],
                      o.b, h1T.b, o.ss, nowaw=True)
            P.end_phase()

    def phase_C2(self, l):
        P = self.P
        E = self.ext
        hT, h1T = (self.scr[n] for n in ("hT", "h1T"))
        last = (l == self.n_layers - 1)
        WT = 256
        with ExitStack() as st:
            P.begin_phase(st)
            Wup = P.sbuf([128, 8, 2 * DFF], BF16, dma=True)
            Wdn = P.sbuf([128, 22, 1024], BF16, dma=True)
            self.load_w(Wup, E["w_up"][l, :, :], 8, per=1)
            self.load_w(Wdn, E["w_down"][l, :, :], 22, per=4)
            gb = P.sbuf([128, 16], F32, dma=True)
            P.dma("sync", [(gb.t[:, 0:8], E["ln2_g"][l, :, :]), (gb.t[:, 8:16], E["ln2_b"][l, :, :])], [], gb.b, gb.ds)
            cw = P.sbuf([128, 44, 3], F32, dma=True)
            cb = P.sbuf([128, 44], F32, dma=True)
            P.dma("sync", [(cw.t[:], E["ffn_cw"][l, :, :, :])], [], cw.b, cw.ds)
            P.dma("sync", [(cb.t[:], E["ffn_cb"][l, :, :])], [], cb.b, cb.ds)
            hr = Ring([P.sbuf([128, 8, WT], F32, n=8, dma=True) for _ in range(2)])
            hbr = Ring([P.sbuf([128, 8, WT], BF16, n=8) for _ in range(1)])
            ar_ = Ring([P.sbuf([128, 22, WT], BF16, n=22) for _ in range(2)])
            orr = Ring([P.sbuf([128, 8, WT], F32, n=8, dma=True) for _ in range(1)])
            xbr = Ring([P.sbuf([128, WT + 2], F32) for _ in range(1)])
            tcr = Ring([P.sbuf([128, WT], F32) for _ in range(3)])
            sgr = Ring([P.sbuf([128, WT], F32) for _ in range(2)])
            halo = P.sbuf([128, 44, 2], F32, n=44)
            halo2 = P.sbuf([128, 44, 2], F32, n=44)
            ostr = Ring([P.sbuf([128, 1024], F32, n=2, dma=True) for _ in range(1)]) if last else None
            accr = Ring([P.psum([128, 512], F32) for _ in range(5 if last else 6)])
            tpr = Ring([P.psum([128, 512], F32) for _ in range(1)]) if last else None
            R = self.ln_resources(WT)
            P.op("vector", MEMSET(halo.t[:], 0.0), [], halo.b)
            evi = [0]

            def proj(acc, W_tb, col0, K, rhs_tb, W):
                for k in range(K):
                    P.op("tensor", MM(acc.t[:, 0:W], W_tb.t[:, k, col0:col0 + 128], rhs_tb.t[:, k, 0:W],
                                      k == 0, k == K - 1), W_tb.b + [rhs_tb.b[k]], acc.b, mark=(k == K - 1))

            def conv(acc, ci, W, first, hin, hout):
                tc_ = tcr.next()
                if first:
                    xb = xbr.next()
                    P.op("vector", MEMSET(xb.t[:, 0:2 + PAD], 0.0), [], xb.b)
                    P.op("scalar", ACT(xb.t[:, 2 + PAD:2 + W], acc.t[:, PAD:W], AF.Copy), [], xb.b + acc.b)
                    P.op("scalar", ACT(hout.t[:, ci, :], xb.t[:, W:W + 2], AF.Copy), xb.b, [hout.b[ci]])
                    P.op("scalar", ACT(tc_.t[:, 0:W], xb.t[:, 0:W], AF.Identity, bias=cb.t[:, ci:ci + 1], scale=cw.t[:, ci, 0:1]),
                         xb.b + cw.b + cb.b, tc_.b)
                    for tap in (1, 2):
                        P.op("vector", STT(tc_.t[:, 0:W], xb.t[:, tap:tap + W], cw.t[:, ci, tap:tap + 1], tc_.t[:, 0:W],
                                           ALU.mult, ALU.add), xb.b + cw.b, tc_.b)
                    return tc_
                P.op("scalar", ACT(tc_.t[:, 2:W], acc.t[:, 0:W - 2], AF.Identity, bias=cb.t[:, ci:ci + 1], scale=cw.t[:, ci, 0:1]),
                     cw.b + cb.b, tc_.b + acc.b)
                P.op("scalar", ACT(hout.t[:, ci, :], acc.t[:, W - 2:W], AF.Copy), [], [hout.b[ci]] + acc.b)
                P.op("scalar", ACT(tc_.t[:, 0:2], hin.t[:, ci, :], AF.Identity, bias=cb.t[:, ci:ci + 1], scale=cw.t[:, ci, 0:1]),
                     [hin.b[ci]], tc_.b)
                P.op("vector", STT(tc_.t[:, 0:1], hin.t[:, ci, 1:2], cw.t[:, ci, 1:2], tc_.t[:, 0:1], ALU.mult, ALU.add),
                     [hin.b[ci]] + cw.b, tc_.b)
                P.op("vector", STT(tc_.t[:, 1:W], acc.t[:, 0:W - 1], cw.t[:, ci, 1:2], tc_.t[:, 1:W], ALU.mult, ALU.add),
                     cw.b, tc_.b + acc.b)
                P.op("vector", STT(tc_.t[:, 0:W], acc.t[:, 0:W], cw.t[:, ci, 2:3], tc_.t[:, 0:W], ALU.mult, ALU.add),
                     cw.b, tc_.b + acc.b)
                return tc_

            halos = [halo, halo2]
            tiles = tiles_of(WT)
            state = {}

            def stage_up(ti):
                c0, W = tiles[ti]
                h = hr.next()
                hb = hbr.next()
                a = ar_.next()
                P.dma("sync", [(h.t[:, :, 0:W], h1T.t[:, :, c0:c0 + W].rearrange("c p t -> p c t"))], h1T.b, h.b, h.ds)
                for c in range(8):
                    evi[0] += 1
                    if evi[0] % 2:
                        P.op("scalar", ACT(hb.t[:, c, 0:W], h.t[:, c, 0:W], AF.Copy), h.b, [hb.b[c]])
                    else:
                        P.op("vector", CP(hb.t[:, c, 0:W], h.t[:, c, 0:W]), h.b, [hb.b[c]])
                hin, hout = halos[ti % 2], halos[(ti + 1) % 2]
                for c in range(22):
                    ug = accr.next()
                    proj(ug, Wup, c * 128, 8, hb, W)
                    uv = accr.next()
                    proj(uv, Wup, DFF + c * 128, 8, hb, W)
                    tg = conv(ug, c, W, c0 == 0, hin, hout)
                    tv = conv(uv, 22 + c, W, c0 == 0, hin, hout)
                    sg = sgr.next()
                    P.op("scalar", ACT(sg.t[:, 0:W], tg.t[:, 0:W], AF.Silu), tg.b, sg.b)
                    P.op("vector", TT(a.t[:, c, 0:W], sg.t[:, 0:W], tv.t[:, 0:W], ALU.mult), sg.b + tv.b, [a.b[c]])
                state[ti] = (h, hb, a)

            def stage_down(ti):
                c0, W = tiles[ti]
                nb = W // 128
                h, hb, a = state.pop(ti)
                for oc in range(8):
                    f = accr.next()
                    proj(f, Wdn, oc * 128, 22, a, W)
                    P.op("vector", STT(h.t[:, oc, 0:W], h.t[:, oc, 0:W], ALPHA, f.t[:, 0:W], ALU.mult, ALU.add),
                         f.b, [h.b[oc]])
                o = orr.next()
                self.ln_fm(h, o, W, gb.t[:, 0:8], gb.t[:, 8:16], R)
                if not last:
                    P.dma("gpsimd", [(hT.t[:, :, c0:c0 + W].rearrange("c p t -> p c t"), o.t[:, :, 0:W])],
                          o.b, hT.b, o.ss, nowaw=True)
                elif c0 > 0:
                    for bi in range(nb):
                        os_ = ostr.next()
                        for half in range(2):
                            tp = tpr.next()
                            for j in range(4):
                                cc = half * 4 + j
                                P.op("tensor", TR(tp.t[:, j * 128:(j + 1) * 128], o.t[:, cc, bi * 128:(bi + 1) * 128],
                                                  self.identf.t[:]), [o.b[cc]] + self.identf.b, tp.b, mark=(j == 3))
                            P.op("scalar", ACT(os_.t[:, half * 512:(half + 1) * 512], tp.t[:, :], AF.Copy), tp.b, [os_.b[half]])
                        r0 = c0 + bi * 128 - 128
                        P.dma("gpsimd", [(self.out[r0:r0 + 128, :], os_.t[:, :])], os_.b, [], os_.ss)

            stage_up(0)
            for ti in range(len(tiles)):
                if ti + 1 < len(tiles):
                    stage_up(ti + 1)
                stage_down(ti)
            P.end_phase()


INPUT_SHAPES = {
    "xin": [LP, D], "emb_g": [128, 8], "emb_b": [128, 8],
    "c_ident": [128, 128], "c_tri": [128, 128], "c_sellast": [128, 128], "c_mask0": [128, 128], "c_kmask0": [128, 1], "c_negmask": [128, 128],
    "cosT2": [128, LP], "sinT2": [128, LP],
    "w_in": [DEPTH, D, 8288], "w_kpes": [DEPTH, D, 64],
    "wqb_n": [DEPTH, QL, 1024], "wqb_p": [DEPTH, QL, 512], "wqb_ps": [DEPTH, QL, 512],
    "wkvb_kn": [DEPTH, KVL, 1024], "wkvb_v": [DEPTH, KVL, 1024],
    "qng": [DEPTH, 128, 6], "kvng": [DEPTH, 128, 2],
    "w_o_attn": [DEPTH, 1024, D], "w_o_ssd": [DEPTH, 2048, D], "w_out": [DEPTH, D, D],
    "w_up": [DEPTH, D, 2 * DFF], "w_down": [DEPTH, DFF, D],
    "ssd_cw": [DEPTH, 128, 24, 4], "ssd_cb": [DEPTH, 128, 24],
    "dtb_bc": [DEPTH, 128, 32], "alog_bc": [DEPTH, 128, 32], "dskip_bc": [DEPTH, 128, 2048], "ssdng_bc": [DEPTH, 128, 2048],
    "ln1_g": [DEPTH, 128, 8], "ln1_b": [DEPTH, 128, 8], "ln2_g": [DEPTH, 128, 8], "ln2_b": [DEPTH, 128, 8],
    "ffn_cw": [DEPTH, 128, 44, 3], "ffn_cb": [DEPTH, 128, 44],
}

SCRATCH = {
    "hT": ([8, 128, LP], F32), "h1T": ([8, 128, LP], F32),
    "qT": ([NH, 192, LP], BF16), "kT": ([NH, 128, LP], BF16), "kpeT": ([64, LP], BF16),
    "v": ([NH, 128, NBLK, 128], BF16), "oT": ([NH, 128, LP], BF16),
    "xs": ([LP, 2048], BF16), "Btok": ([LP, 512], BF16), "BT": ([4, 128, LP], BF16), "CT": ([4, 128, LP], BF16),
    "dt": ([LP, 32], F32), "zs": ([LP, 2048], BF16), "ynT": ([16, 128, LP], BF16),
    "dbgY": ([LP, 2048], F32), "dbgS": ([LP, 168], F32), "dbgP": ([NBLK, 128, 2048], F32),
}


def build(n_layers=2, dbg=None, stop=None):
    nc = bass.Bass("TRN2", target_bir_lowering=False)
    with ExitStack() as gst:
        P = Prog(nc, gst)
        k = K(nc, P, n_layers, dbg)
        for nm, shp in INPUT_SHAPES.items():
            k.din(nm, shp)
        for nm, (shp, dt) in SCRATCH.items():
            k.dscr(nm, shp, dt)
        k.out = nc.dram_tensor("out", [SEQ, D], F32, kind="ExternalOutput").ap()
        k.load_consts(gst)
        seq = [("E", k.phase_E, ())]
        for l in range(n_layers):
            for nm in ("A1", "A2", "B", "S", "C1", "C2"):
                fn = getattr(k, "phase_" + nm, None)
                if fn is not None:
                    seq.append((f"{nm}_{l}", fn, (l,)))
        import os
        skip = os.environ.get("K_SKIP", "").split(",")
        for (nm, fn, args) in seq:
            if nm.split("_")[0] in skip:
                continue
            fn(*args)
            if stop == nm:
                break
    return nc, k


def host_consts():
    c = {}
    c["c_ident"] = np.eye(128, dtype=np.float32)
    kk = np.arange(128)[:, None]
    qq = np.arange(128)[None, :]
    c["c_tri"] = (kk <= qq).astype(np.float32)
    c["c_sellast"] = np.broadcast_to((kk == 127), (128, 128)).astype(np.float32).copy()
    m0 = ((qq >= PAD) & (kk >= PAD) & (kk <= qq)) | ((qq < PAD) & (kk == qq))
    c["c_mask0"] = m0.astype(np.float32)
    c["c_kmask0"] = (np.arange(128) >= PAD).astype(np.float32)[:, None].copy()
    c["c_negmask"] = np.where(qq < kk, np.float32(-30000.0), np.float32(0.0)).astype(np.float32)
    return c


def pm(v, nchunk):
    return np.ascontiguousarray(np.asarray(v, np.float32).reshape(nchunk, 128).T)


def rope_tables_T():
    inv_freq = (1.0 / (np.float32(10000.0) ** (np.arange(0, 64, 2, dtype=np.float32) / np.float32(64)))).astype(np.float32)
    pos = np.maximum(np.arange(LP, dtype=np.float32) - np.float32(PAD), np.float32(0))
    ang = (pos[:, None] * inv_freq[None, :]).astype(np.float32)
    ang = np.concatenate([ang, ang], axis=-1)
    cos = np.cos(ang).astype(np.float32).T
    sin = np.sin(ang).astype(np.float32).T
    sgn = np.concatenate([-np.ones(32, np.float32), np.ones(32, np.float32)])[:, None]
    sins = sin * sgn
    return (np.ascontiguousarray(np.concatenate([cos, cos], 0)), np.ascontiguousarray(np.concatenate([sins, sins], 0)))


def prep_shared(inp):
    f = lambda a: np.ascontiguousarray(np.asarray(a, np.float32))
    sh = dict(host_consts())
    sh["cosT2"], sh["sinT2"] = rope_tables_T()
    sh["emb_g"] = pm(inp["emb_ln_g"], 8)
    sh["emb_b"] = pm(inp["emb_ln_b"], 8)
    w_in = f(inp["w_in"])
    sh["w_in"] = w_in
    sh["w_kpes"] = f(np.concatenate([w_in[:, :, 1056:1088], w_in[:, :, 1024:1056]], axis=-1))
    wqb = f(inp["w_q_b"]).reshape(DEPTH, QL, NH, 192)
    sh["wqb_n"] = f(wqb[..., :128].reshape(DEPTH, QL, 1024))
    sh["wqb_p"] = f(wqb[..., 128:].reshape(DEPTH, QL, 512))
    sh["wqb_ps"] = f(np.concatenate([wqb[..., 160:192], wqb[..., 128:160]], axis=-1).reshape(DEPTH, QL, 512))
    wkv = f(inp["w_kv_b"]).reshape(DEPTH, KVL, NH, 256)
    sh["wkvb_kn"] = f(wkv[..., :128].reshape(DEPTH, KVL, 1024))
    sh["wkvb_v"] = f(wkv[..., 128:].reshape(DEPTH, KVL, 1024))
    sh["qng"] = f(np.stack([pm(inp["q_norm_g"][l], 6) for l in range(DEPTH)]))
    sh["kvng"] = f(np.stack([pm(inp["kv_norm_g"][l], 2) for l in range(DEPTH)]))
    for nm in ("w_o_attn", "w_o_ssd", "w_out", "w_up", "w_down"):
        sh[nm] = f(inp[nm])
    cw = f(inp["ssd_conv_w"])
    sh["ssd_cw"] = f(cw.reshape(DEPTH, 4, 24, 128).transpose(0, 3, 2, 1))
    sh["ssd_cb"] = f(f(inp["ssd_conv_b"]).reshape(DEPTH, 24, 128).transpose(0, 2, 1))
    bc = lambda a: f(np.broadcast_to(f(a)[:, None, :], (DEPTH, 128, a.shape[-1])))
    sh["dtb_bc"] = bc(inp["dt_bias"])
    sh["alog_bc"] = bc(inp["a_log"])
    sh["dskip_bc"] = bc(np.repeat(f(inp["d_skip"]), 64, axis=-1))
    sh["ssdng_bc"] = bc(inp["ssd_norm_g"])
    for nm in ("ln1_g", "ln1_b", "ln2_g", "ln2_b"):
        sh[nm] = f(np.stack([pm(inp[nm][l], 8) for l in range(DEPTH)]))
    fw = f(inp["ffn_conv_w"])
    sh["ffn_cw"] = f(fw.reshape(DEPTH, 3, 44, 128).transpose(0, 3, 2, 1))
    sh["ffn_cb"] = f(f(inp["ffn_conv_b"]).reshape(DEPTH, 44, 128).transpose(0, 2, 1))
    return sh


def xin_of(inp, b):
    xin = np.zeros((LP, D), np.float32)
    xin[PAD:PAD + NMETA] = inp["meta_tokens"]
    xin[128:] = inp["x"][b]
    return xin


def kernel(**inp):
    sh = prep_shared(inp)
    nc, k = build()
    in_maps = []
    for c in range(8):
        m = dict(sh)
        m["xin"] = xin_of(inp, c % 4)
        in_maps.append(m)
    res = run_bass_kernel_spmd(nc, in_maps, core_ids=list(range(8)))
    out = np.stack([np.asarray(res.results[b]["out"], np.float32) for b in range(4)], axis=0)
    return out
```
